# Optimizing a Trainium2 kernel written in Bass

```python
import jax, jax.numpy as jnp
from jax import lax
import numpy as np

D_MODEL = 1024
BATCH = 16
SEQ = 2048
DEPTH = 4

GRID_W = 64
CTX_LEN = 256
HEAD_DIM = 64
ROPE_THETA = 10000.0
EPS = 1e-6
ATTN_SCALE = HEAD_DIM ** -0.5
Q_BLOCK = 128

ATT_WIDTH = D_MODEL // 2
ATT_HEADS = ATT_WIDTH // HEAD_DIM
ATT_KV_HEADS = ATT_HEADS // 4
ATT_GROUP = ATT_HEADS // ATT_KV_HEADS
ATT_KV_WIDTH = ATT_KV_HEADS * HEAD_DIM
POOL_WINDOWS = (2, 4, 8, 16)
POOL_WIDTH = D_MODEL // 4
POOL_GROUP_DIM = POOL_WIDTH // len(POOL_WINDOWS)
NA_WIDTH = D_MODEL // 4
NA_HEADS = NA_WIDTH // HEAD_DIM
NA_KH_MAX = 8
NA_KW = 16

MIX_WIDTH = ATT_WIDTH + POOL_WIDTH + NA_WIDTH
IN_SPLITS = (ATT_WIDTH, ATT_KV_WIDTH, ATT_KV_WIDTH, ATT_WIDTH,
             POOL_WIDTH, POOL_WIDTH,
             NA_WIDTH, NA_WIDTH, NA_WIDTH, NA_WIDTH)
IN_OFFSETS = tuple(int(o) for o in np.cumsum((0,) + IN_SPLITS))
IN_WIDTH = IN_OFFSETS[-1]
SPLIT_POINTS = IN_OFFSETS[1:-1]

kernel_name = "hybrid_parallel_heads_diffusion_block"


def rms_norm(x, g):
    xf = x.astype(jnp.float32)
    y = xf * lax.rsqrt(jnp.mean(xf * xf, axis=-1, keepdims=True) + EPS)
    return (y * g.astype(jnp.float32)).astype(x.dtype)


def rope_axis(x, pos):
    half = x.shape[-1] // 2
    inv_freq = ROPE_THETA ** (-jnp.arange(half, dtype=jnp.float32) / half)
    ang = pos.astype(jnp.float32)[:, None] * inv_freq[None, :]
    cos = jnp.cos(ang)[None, :, None, :]
    sin = jnp.sin(ang)[None, :, None, :]
    xf = x.astype(jnp.float32)
    x1, x2 = xf[..., :half], xf[..., half:]
    return jnp.concatenate([x1 * cos - x2 * sin, x2 * cos + x1 * sin], axis=-1).astype(x.dtype)


def rope_2d(x, row, col):
    h = HEAD_DIM // 2
    return jnp.concatenate([rope_axis(x[..., :h], row), rope_axis(x[..., h:], col)], axis=-1)


def dense_attn(q, k, v):
    s = jnp.einsum("bqkgd,bskd->bkgqs", q, k).astype(jnp.float32)
    p = jax.nn.softmax(s, axis=-1).astype(v.dtype)
    o = jnp.einsum("bkgqs,bskd->bqkgd", p, v)
    return o.reshape(o.shape[0], o.shape[1], -1)


def global_gqa(q, k, v, k_ctx, v_ctx):
    bsz, seq = q.shape[:2]
    n_blk = seq // Q_BLOCK
    kk = jnp.concatenate([k, k_ctx], axis=1)
    vv = jnp.concatenate([v, v_ctx], axis=1)
    qb = jnp.moveaxis(q.reshape(bsz, n_blk, Q_BLOCK, ATT_KV_HEADS, ATT_GROUP, HEAD_DIM), 1, 0)
    o = lax.map(lambda qi: dense_attn(qi, kk, vv), qb)
    return jnp.moveaxis(o, 0, 1).reshape(bsz, seq, ATT_WIDTH)


def multi_pool(z):
    bsz, length, width = z.shape
    zf = z.astype(jnp.float32)
    cs = jnp.concatenate([jnp.zeros((bsz, 1, width), jnp.float32), jnp.cumsum(zf, axis=1)], axis=1)
    t = np.arange(length)
    outs = []
    for g, w in enumerate(POOL_WINDOWS):
        lo = np.maximum(t - w // 2, 0)
        hi = np.minimum(t + w // 2 - 1, length - 1)
        cnt = jnp.asarray((hi - lo + 1).astype(np.float32))[None, :, None]
        sl = slice(g * POOL_GROUP_DIM, (g + 1) * POOL_GROUP_DIM)
        seg = cs[:, :, sl]
        outs.append((seg[:, hi + 1] - seg[:, lo]) / cnt - zf[:, :, sl])
    return jnp.concatenate(outs, axis=-1).astype(z.dtype)


def pool_branch(z, w, s):
    bsz, length = z.shape[:2]
    pooled = multi_pool(z).reshape(bsz, length, len(POOL_WINDOWS), POOL_GROUP_DIM)
    y = jnp.einsum("blgi,gio->blgo", pooled, w).reshape(bsz, length, POOL_WIDTH)
    return y * s


def neighbourhood_attn(q, k, v, k_ctx, v_ctx, rpb, rows):
    bsz = q.shape[0]
    kh = min(NA_KH_MAX, rows)
    qg = q.reshape(bsz, rows, GRID_W, NA_HEADS, HEAD_DIM)
    kg = k.reshape(bsz, rows, GRID_W, NA_HEADS, HEAD_DIM)
    vg = v.reshape(bsz, rows, GRID_W, NA_HEADS, HEAD_DIM)
    c_idx = np.arange(GRID_W)
    c0 = np.clip(c_idx - NA_KW // 2, 0, GRID_W - NA_KW)
    cols = c0[:, None] + np.arange(NA_KW)[None, :]
    dcol = cols - c_idx[:, None] + NA_KW - 1
    n_nb = kh * NA_KW

    def row_block(r):
        r0 = jnp.clip(r - kh // 2, 0, rows - kh)
        k_band = lax.dynamic_slice_in_dim(kg, r0, kh, axis=1)
        v_band = lax.dynamic_slice_in_dim(vg, r0, kh, axis=1)
        k_nb = k_band[:, :, cols]
        v_nb = v_band[:, :, cols]
        q_r = lax.dynamic_index_in_dim(qg, r, axis=1, keepdims=False)
        s_nb = jnp.einsum("bqhd,bmqnhd->bhqmn", q_r, k_nb).astype(jnp.float32)
        drow = r0 + jnp.arange(kh) - r + NA_KH_MAX - 1
        bias = rpb[:, drow[:, None, None], dcol[None, :, :]]
        s_nb = s_nb + jnp.transpose(bias, (0, 2, 1, 3)).astype(jnp.float32)[None]
        s_ctx = jnp.einsum("bqhd,bchd->bhqc", q_r, k_ctx).astype(jnp.float32)
        s = jnp.concatenate([s_nb.reshape(bsz, NA_HEADS, GRID_W, n_nb), s_ctx], axis=-1)
        p = jax.nn.softmax(s, axis=-1).astype(v.dtype)
        p_nb = p[..., :n_nb].reshape(bsz, NA_HEADS, GRID_W, kh, NA_KW)
        p_ctx = p[..., n_nb:]
        return (jnp.einsum("bhqmn,bmqnhd->bqhd", p_nb, v_nb)
                + jnp.einsum("bhqc,bchd->bqhd", p_ctx, v_ctx))

    o = lax.map(row_block, jnp.arange(rows))
    return jnp.moveaxis(o, 0, 1).reshape(bsz, rows * GRID_W, NA_WIDTH)


def setup_inputs(seed: int = 0) -> dict:
    key = jax.random.key(seed)
    ks = jax.random.split(key, 16)
    f32 = jnp.float32
    n = lambda k, shape: jax.random.normal(k, shape, f32)
    return {
        "x": n(ks[0], (BATCH, SEQ, D_MODEL)),
        "c": n(ks[1], (BATCH, D_MODEL)),
        "ctx": n(ks[2], (BATCH, CTX_LEN, D_MODEL)),
        "c_ctx": n(ks[3], (D_MODEL,)),
        "norm_gain": 1.0 + 0.02 * n(ks[4], (DEPTH, D_MODEL)),
        "w_mod": 0.5 * D_MODEL ** -0.5 * n(ks[5], (DEPTH, D_MODEL, 3 * D_MODEL)),
        "b_mod": 0.02 * n(ks[6], (DEPTH, 3 * D_MODEL)),
        "w_in": D_MODEL ** -0.5 * n(ks[7], (DEPTH, D_MODEL, IN_WIDTH)),
        "att_q_gain": 1.0 + 0.02 * n(ks[8], (DEPTH, HEAD_DIM)),
        "att_k_gain": 1.0 + 0.02 * n(ks[9], (DEPTH, HEAD_DIM)),
        "pool_w": POOL_GROUP_DIM ** -0.5 * n(ks[10], (DEPTH, len(POOL_WINDOWS), POOL_GROUP_DIM, POOL_GROUP_DIM)),
        "pool_scale": 1.0 + 0.02 * n(ks[11], (DEPTH, POOL_WIDTH)),
        "na_q_gain": 1.0 + 0.02 * n(ks[12], (DEPTH, HEAD_DIM)),
        "na_k_gain": 1.0 + 0.02 * n(ks[13], (DEPTH, HEAD_DIM)),
        "na_rpb": 0.1 * n(ks[14], (DEPTH, NA_HEADS, 2 * NA_KH_MAX - 1, 2 * NA_KW - 1)),
        "w_out": MIX_WIDTH ** -0.5 * n(ks[15], (DEPTH, MIX_WIDTH, D_MODEL)),
    }


def reference(x, c, ctx, c_ctx, norm_gain, w_mod, b_mod, w_in, att_q_gain, att_k_gain,
              pool_w, pool_scale, na_q_gain, na_k_gain, na_rpb, w_out):
    bsz, seq, _ = x.shape
    n_ctx = ctx.shape[1]
    rows = seq // GRID_W
    t = jnp.arange(seq)
    row, col = t // GRID_W, t % GRID_W
    silu = jax.nn.silu
    for l in range(DEPTH):
        last = l == DEPTH - 1
        shift, scale, gate = jnp.split(silu(c) @ w_mod[l] + b_mod[l], 3, axis=-1)
        shift_c, scale_c, gate_c = jnp.split(silu(c_ctx) @ w_mod[l] + b_mod[l], 3, axis=-1)
        h = rms_norm(x, norm_gain[l]) * (1 + scale[:, None]) + shift[:, None]
        hc = rms_norm(ctx, norm_gain[l]) * (1 + scale_c) + shift_c

        aq, ak, av, ag, bz, bg, nq, nk, nv, ng = jnp.split(h @ w_in[l], SPLIT_POINTS, axis=-1)
        if last:
            ak_c, av_c = jnp.split(hc @ w_in[l][:, IN_OFFSETS[1]:IN_OFFSETS[3]], 2, axis=-1)
            nk_c, nv_c = jnp.split(hc @ w_in[l][:, IN_OFFSETS[7]:IN_OFFSETS[9]], 2, axis=-1)
        else:
            (aq_c, ak_c, av_c, ag_c, bz_c, bg_c,
             nq_c, nk_c, nv_c, ng_c) = jnp.split(hc @ w_in[l], SPLIT_POINTS, axis=-1)

        ak_c = rms_norm(ak_c.reshape(bsz, n_ctx, ATT_KV_HEADS, HEAD_DIM), att_k_gain[l])
        av_c = av_c.reshape(bsz, n_ctx, ATT_KV_HEADS, HEAD_DIM)
        nk_c = rms_norm(nk_c.reshape(bsz, n_ctx, NA_HEADS, HEAD_DIM), na_k_gain[l])
        nv_c = nv_c.reshape(bsz, n_ctx, NA_HEADS, HEAD_DIM)

        aq = rope_2d(rms_norm(aq.reshape(bsz, seq, ATT_HEADS, HEAD_DIM), att_q_gain[l]), row, col) * ATTN_SCALE
        aq = aq.reshape(bsz, seq, ATT_KV_HEADS, ATT_GROUP, HEAD_DIM)
        ak = rope_2d(rms_norm(ak.reshape(bsz, seq, ATT_KV_HEADS, HEAD_DIM), att_k_gain[l]), row, col)
        av = av.reshape(bsz, seq, ATT_KV_HEADS, HEAD_DIM)
        a_out = global_gqa(aq, ak, av, ak_c, av_c)
        b_out = pool_branch(bz, pool_w[l], pool_scale[l])
        nq = rms_norm(nq.reshape(bsz, seq, NA_HEADS, HEAD_DIM), na_q_gain[l]) * ATTN_SCALE
        nk = rms_norm(nk.reshape(bsz, seq, NA_HEADS, HEAD_DIM), na_k_gain[l])
        nv = nv.reshape(bsz, seq, NA_HEADS, HEAD_DIM)
        n_out = neighbourhood_attn(nq, nk, nv, nk_c, nv_c, na_rpb[l], rows)

        y = jnp.concatenate([a_out * silu(ag), b_out * silu(bg), n_out * silu(ng)], axis=-1) @ w_out[l]

        if not last:
            aq_c = rms_norm(aq_c.reshape(bsz, n_ctx, ATT_HEADS, HEAD_DIM), att_q_gain[l]) * ATTN_SCALE
            a_out_c = dense_attn(aq_c.reshape(bsz, n_ctx, ATT_KV_HEADS, ATT_GROUP, HEAD_DIM), ak_c, av_c)
            b_out_c = pool_branch(bz_c, pool_w[l], pool_scale[l])
            nq_c = rms_norm(nq_c.reshape(bsz, n_ctx, NA_HEADS, HEAD_DIM), na_q_gain[l]) * ATTN_SCALE
            n_out_c = dense_attn(nq_c[:, :, :, None, :], nk_c, nv_c)
            yc = jnp.concatenate([a_out_c * silu(ag_c), b_out_c * silu(bg_c), n_out_c * silu(ng_c)],
                                 axis=-1) @ w_out[l]
            ctx = ctx + gate_c * yc

        x = x + gate[:, None] * y
    return x
```

```python
import contextlib
import numpy as np
import concourse.bass as bass
import concourse.mybir as mybir
from concourse.bass_utils import run_bass_kernel_spmd

F32 = mybir.dt.float32
BF16 = mybir.dt.bfloat16
ALU = mybir.AluOpType
AF = mybir.ActivationFunctionType

D = 1024
NL = 4
SEQ = 2048
NCTX = 256
T = SEQ + NCTX
EPS = 1e-6
NEG = -30000.0
TCH = [(0, 512), (512, 512), (1024, 512), (1536, 512), (2048, 256)]
ENGS = ("pe", "act", "dve", "pool", "sp")
PSUM_KEYS = ("PS",)

P_K0, P_K1, P_V = 0, 1, 2
P_Q, P_G, P_Z, P_BG, P_NK, P_NV, P_NQ, P_NG = 3, 7, 11, 13, 15, 17, 19, 21
NPAN = 23


class Op:
    __slots__ = ("eng", "idx", "fn", "waits", "signal", "dma_grp", "dma_cnt", "sigval")

    def __init__(self, eng, idx, fn):
        self.eng = eng
        self.idx = idx
        self.fn = fn
        self.waits = []
        self.signal = False
        self.dma_grp = None
        self.dma_cnt = 0
        self.sigval = 0


class Sched:
    def __init__(self):
        self.q = {e: [] for e in ENGS}
        self.last_w = {}
        self.readers = {}
        self.maxwait = {e: {} for e in ENGS}
        self.dma_cnt = {}

    def add(self, eng, fn, reads=(), writes=(), dma=None):
        op = Op(eng, len(self.q[eng]), fn)
        if dma is not None:
            op.dma_grp = dma
            self.dma_cnt[dma] = self.dma_cnt.get(dma, 0) + 1
            op.dma_cnt = self.dma_cnt[dma]
        best = {}
        for k in reads:
            w = self.last_w.get(k)
            if w is not None:
                self._cand(best, w, eng)
            if isinstance(k, tuple) and k[0] in PSUM_KEYS:
                for r in self.readers.get(k, ()):
                    if r.eng != eng:
                        self._cand(best, r, eng)
        for k in writes:
            w = self.last_w.get(k)
            if w is not None:
                self._cand(best, w, eng)
            for r in self.readers.get(k, ()):
                self._cand(best, r, eng)
        mw = self.maxwait[eng]
        for src, (pos, d) in best.items():
            if mw.get(src, -1) >= pos:
                continue
            mw[src] = pos
            d.signal = True
            op.waits.append(d)
        for k in writes:
            self.last_w[k] = op
            self.readers[k] = []
        for k in reads:
            self.readers.setdefault(k, []).append(op)
        self.q[eng].append(op)
        return op

    @staticmethod
    def _cand(best, d, eng):
        if d.dma_grp is not None:
            src = ("dma", d.dma_grp)
            pos = d.dma_cnt
        else:
            if d.eng == eng and eng == "pe":
                return
            src = d.eng
            pos = d.idx
        cur = best.get(src)
        if cur is None or cur[0] < pos:
            best[src] = (pos, d)

    def emit(self, nc, final_waits=()):
        for op in final_waits:
            op.signal = True
        for e in ENGS:
            n = 0
            for op in self.q[e]:
                if op.dma_grp is None and op.signal:
                    n += 1
                    op.sigval = n
        grps = sorted(self.dma_cnt.keys())
        with contextlib.ExitStack() as st:
            sems = {}
            for e in ENGS:
                sems[e] = st.enter_context(nc.semaphore("sem_" + e))
            for g in grps:
                sems[("dma", g)] = st.enter_context(nc.semaphore("semd_" + str(g)))
            block = st.enter_context(nc.Block())

            def tok(d):
                if d.dma_grp is not None:
                    return sems[("dma", d.dma_grp)], 16 * d.dma_cnt
                return sems[d.eng], d.sigval

            def run(engname, e):
                for op in self.q[engname]:
                    for d in op.waits:
                        s, v = tok(d)
                        e.wait_ge(s, v)
                    ins = op.fn(e)
                    if op.dma_grp is not None:
                        ins.then_inc(sems[("dma", op.dma_grp)], 16)
                    elif op.signal:
                        ins.then_inc(sems[engname], 1)
                if engname == "sp":
                    for d in final_waits:
                        s, v = tok(d)
                        e.wait_ge(s, v)

            @block.tensor
            def _(e):
                run("pe", e)

            @block.scalar
            def _(e):
                run("act", e)

            @block.vector
            def _(e):
                run("dve", e)

            @block.gpsimd
            def _(e):
                run("pool", e)

            @block.sync
            def _(e):
                run("sp", e)


class Ring:
    def __init__(self, name, tiles):
        self.name = name
        self.tiles = tiles
        self.i = 0

    def get(self):
        j = self.i % len(self.tiles)
        self.i += 1
        return self.tiles[j], (self.name, j)


def build_nc(n_layers=NL, n_batch=2, phases="ABC", stage=99):
    nc = bass.Bass("TRN2", target_bir_lowering=False)

    def din(name, shape):
        return nc.dram_tensor(name, list(shape), F32, kind="ExternalInput").ap()

    x_d = din("x", [2, SEQ, D])
    ctx_d = din("ctx", [2, NCTX, D])
    cT_d = din("cT", [128, 8, 3])
    wmod_d = din("wmod", [NL, 6, 128, 8, 512])
    bmod_d = din("bmod", [128, NL, 24])
    ngain_d = din("ngain", [128, NL, 8])
    win_d = din("win", [NL, NPAN, 128, 8, 128])
    wout_d = din("wout", [NL, 8, 128, 8, 128])
    gains_d = din("gains", [128, NL, 4])
    rope_d = din("ropetab", [128, 192])
    rmat_d = din("rmat", [128, 128])
    ident_d = din("ident", [128, 128])
    poolw_d = din("poolw", [NL, 128, 2, 128])
    pscale_d = din("pscale", [128, NL, 2])
    pooltab_d = din("pooltab", [128, 2, 17])
    ebias_d = din("ebias", [NL, 2, 128, 2, 14, 64])
    out_d = nc.dram_tensor("out", [2, SEQ, D], F32, kind="ExternalOutput").ap()

    S = Sched()
    with contextlib.ExitStack() as st:
        def sb(name, shape, dt=F32):
            return st.enter_context(nc.sbuf_tensor(name, list(shape), dt))

        xT = sb("xT", [128, 8, T])
        hT = sb("hT", [128, 8, T], BF16)
        mix = sb("mix", [128, 4, T], BF16)
        PH = sb("PH", [128, 8320])
        stg = [sb("stg%d" % i, [128, 8, 128]) for i in range(2)]
        wb = [sb("wb%d" % i, [128, 8, 128], BF16) for i in range(4)]
        Wt = Ring("W", [sb("wk%d" % i, [128, 512]) for i in range(8)])
        PT = Ring("PT", [sb("pt%d" % i, [128, 1024], BF16) for i in range(3)])
        ident = sb("ident_s", [128, 128])
        rmat = sb("rmat_s", [128, 128])
        ones128 = sb("ones128", [128, 128])
        bones = sb("bones", [128, 128])
        ropet = sb("ropet", [128, 192])
        gains = sb("gains_s", [128, NL, 4])
        ngain = sb("ngain_s", [128, NL, 8])
        bmod = sb("bmod_s", [128, NL, 24])
        pscale = sb("pscale_s", [128, NL, 2])
        pooltab = sb("pooltab_s", [128, 2, 17])
        cT = sb("cT_s", [128, 8, 3])
        scT = sb("scT", [128, 8, 3])
        mod_all = sb("mod_all", [128, NL, 24, 3])
        gs_all = sb("gs_all", [128, NL, 8, 3])
        poolw32 = sb("poolw32", [128, 2, 128])
        poolwb = sb("poolwb", [128, 2, 128], BF16)
        dummy = sb("fence_dummy", [128, 8])

        def ps(name):
            return st.enter_context(nc.psum_tensor(name, [128, 512], F32))

        PSALL = st.enter_context(nc.psum_tensor("PSALL", [128, 4096], F32))
        BK = [PSALL[:, 512 * i:512 * (i + 1)] for i in range(8)]
        SC = [BK[2], BK[3]]
        SCK = [("PS", 2), ("PS", 3)]
        OA = [BK[4], BK[5]]
        OAK = [("PS", 4), ("PS", 5)]
        SCD = [PSALL[:, 0:1024], PSALL[:, 1024:2048]]
        SCDK = [[("PS", 0), ("PS", 1)], [("PS", 2), ("PS", 3)]]
        pj_list = [[6, 0]]
        ms_list = [[7, 1]]

        PHb = PH[:, :].bitcast(BF16)
        kz = [[PHb[:, (2 * kv + hh) * T:(2 * kv + hh + 1) * T] for hh in range(2)] for kv in range(2)]
        vaugA = PHb[:, 4 * T:4 * T + 18 * 256].rearrange("p (t k a d) -> p t k a d", t=18, k=2, a=2)
        nkz = [PHb[:, hh * T:(hh + 1) * T] for hh in range(2)]
        nvaug = PHb[:, 2 * T:2 * T + 33 * 256].rearrange("p (t k a d) -> p t k a d", t=33, k=2, a=2)
        e_off = (2 * T + 33 * 256 + 1) // 2
        ebuf = PH[:, e_off:e_off + 2 * 14 * 64].rearrange("p (h d c) -> p h d c", h=2, d=14)
        assert e_off + 2 * 14 * 64 <= 8320
        ZN = 2352
        zbuf = PH[:, 0:ZN]
        tA = PH[:, ZN:2 * ZN]
        tB = PH[:, 2 * ZN:3 * ZN]
        assert 3 * ZN <= 8320
        NXS = 6
        xstage = [PH[:, i * 1024:(i + 1) * 1024] for i in range(NXS)]

        def MM(out, lhsT, rhs, start=True, stop=True, r=(), w=()):
            S.add("pe", lambda e: e.matmul(out, lhsT=lhsT, rhs=rhs, start=start, stop=stop), r, w)

        def ACT(out, in_, func, r=(), w=(), scale=None, bias=None):
            kw = {}
            if scale is not None:
                kw["scale"] = scale
            if bias is not None:
                kw["bias"] = bias
            S.add("act", lambda e: e.activation(out=out, in_=in_, func=func, **kw), r, w)

        def TT(eng, out, in0, in1, op, r=(), w=()):
            S.add(eng, lambda e: e.tensor_tensor(out=out, in0=in0, in1=in1, op=op), r, w)

        def TS(eng, out, in0, s1, op0, r=(), w=(), s2=None, op1=None):
            if op1 is None:
                S.add(eng, lambda e: e.tensor_scalar(out=out, in0=in0, scalar1=s1, scalar2=None, op0=op0), r, w)
            else:
                S.add(eng, lambda e: e.tensor_scalar(out=out, in0=in0, scalar1=s1, scalar2=s2, op0=op0, op1=op1), r, w)

        def STT(out, in0, scalar, in1, op0, op1, r=(), w=()):
            S.add("dve", lambda e: e.scalar_tensor_tensor(out=out, in0=in0, scalar=scalar, in1=in1, op0=op0, op1=op1), r, w)

        def CP(eng, out, in_, r=(), w=()):
            if eng == "act":
                S.add("act", lambda e: e.activation(out=out, in_=in_, func=AF.Copy), r, w)
            else:
                S.add(eng, lambda e: e.tensor_copy(out=out, in_=in_), r, w)

        def MSET(eng, ap, val, r=(), w=()):
            S.add(eng, lambda e: e.memset(ap, val), r, w)

        def DMA(out, in_, r=(), w=(), grp="m"):
            return S.add("sp", lambda e: e.dma_start(out=out, in_=in_), r, w, dma=grp)

        def fence():
            MSET("pool", dummy[:, :], 0.0, r=(), w=["PH", "dummy"])

        class WStream:
            def __init__(self):
                self.specs = []
                self.issued = 0
                self.consumed = 0

            def _issue(self):
                n = self.issued
                ap, nk, _tag = self.specs[n]
                si, wi = n % 2, n % 4
                DMA(stg[si][:, 0:nk, :], ap, w=[("stg", si)], grp="stg%d" % si)
                CP("pool", wb[wi][:, 0:nk, :], stg[si][:, 0:nk, :], r=[("stg", si)], w=[("wb", wi)])
                self.issued += 1

            def next(self, check=None):
                while self.issued < min(len(self.specs), self.consumed + 3 - 1):
                    self._issue()
                n = self.consumed
                if check is not None:
                    assert self.specs[n][2] == check, (self.specs[n][2], check)
                self.consumed += 1
                return wb[n % 4], ("wb", n % 4)

        WS = WStream()

        def add_spec(ap, nk, tag):
            WS.specs.append((ap, nk, tag))

        for b in range(n_batch):
            for l in range(n_layers):
                if "A" in phases:
                    for pid in (P_K0, P_K1, P_V):
                        add_spec(win_d[l, pid], 8, ("in", l, pid))
                    add_spec(win_d[l, P_Q], 8, ("in", l, P_Q))
                    for c in range(4):
                        add_spec(win_d[l, P_G + c], 8, ("in", l, P_G + c))
                        if c < 3:
                            add_spec(win_d[l, P_Q + c + 1], 8, ("in", l, P_Q + c + 1))
                    for m in range(8):
                        add_spec(wout_d[l, m, :, 0:4, :], 4, ("outA", l, m))
                if "B" in phases:
                    for cz in range(2):
                        add_spec(win_d[l, P_Z + cz], 8, ("in", l, P_Z + cz))
                        add_spec(win_d[l, P_BG + cz], 8, ("in", l, P_BG + cz))
                if "C" in phases:
                    for c2 in range(2):
                        for pid in (P_NK, P_NQ, P_NV, P_NG):
                            add_spec(win_d[l, pid + c2], 8, ("in", l, pid + c2))
                    for m in range(8):
                        if "B" in phases:
                            add_spec(wout_d[l, m, :, 4:8, :], 4, ("outC", l, m))
                        else:
                            add_spec(wout_d[l, m, :, 6:8, :], 2, ("outC", l, m))

        DMA(ident[:], ident_d[:], w=["ident"], grp="c_ident")
        DMA(rmat[:], rmat_d[:], w=["rmat"], grp="c_rmat")
        DMA(ropet[:], rope_d[:], w=["ropet"], grp="c_ropet")
        DMA(gains[:], gains_d[:], w=["gains"], grp="c_gains")
        DMA(ngain[:], ngain_d[:], w=["ngain"], grp="c_ngain")
        DMA(bmod[:], bmod_d[:], w=["bmod"], grp="c_bmod")
        DMA(pscale[:], pscale_d[:], w=["pscale"], grp="c_pscale")
        DMA(pooltab[:], pooltab_d[:], w=["pooltab"], grp="c_pooltab")
        DMA(cT[:], cT_d[:], w=["cT"], grp="c_cT")
        MSET("pool", ones128[:], 1.0, w=["ones"])
        MSET("pool", bones[:], 0.0, w=["bones"])
        MSET("pool", bones[0:64, 0:64], 1.0, w=["bones"])
        MSET("pool", bones[64:128, 64:128], 1.0, w=["bones"])
        ACT(scT[:], cT[:], AF.Exp, r=["cT"], w=["scT"], scale=-1.0)
        ACT(scT[:], scT[:], AF.Ln, r=["scT"], w=["scT"], bias=1.0)
        ACT(scT[:], scT[:], AF.Exp, r=["scT"], w=["scT"], scale=-1.0)
        TT("dve", scT[:], scT[:], cT[:], ALU.mult, r=["scT", "cT"], w=["scT"])
        for l in range(n_layers):
            rows = []
            for cc in range(6):
                i = l * 6 + cc
                si = i % 2
                wslot = PH[:, si * 4096:(si + 1) * 4096].rearrange("p (k c) -> p k c", k=8)
                DMA(wslot, wmod_d[l, cc], r=["PH"], w=[("wms", si)], grp="wms%d" % si)
                bank, bkey = BK[7 if cc % 2 == 0 else 6], ("PS", 7 if cc % 2 == 0 else 6)
                for k in range(8):
                    MM(bank[0:3, :], lhsT=scT[:, k, :], rhs=wslot[:, k, :],
                       start=(k == 0), stop=(k == 7), r=[("wms", si), "scT", "PH"], w=[bkey])
                rt, rtk = Wt.get()
                CP("dve", rt[0:3, :], bank[0:3, :], r=[bkey], w=[rtk])
                rows.append((rt, rtk))
            for m in range(24):
                rt, rtk = rows[m // 4]
                off = (m % 4) * 128
                S.add("pe", (lambda o, i_: (lambda e: e.transpose(out=o, in_=i_, identity=ident[0:3, 0:3])))(
                    BK[1][:, m * 4:m * 4 + 3], rt[0:3, off:off + 128]), [rtk, "ident"], [("PS", 1)])
            msv = BK[1][:, 0:96].rearrange("p (m f) -> p m f", f=4)
            for v in range(3):
                TT("dve", mod_all[:, l, :, v], msv[:, :, v], bmod[:, l, :], ALU.add,
                   r=[("PS", 1), "bmod"], w=["mod"])
            for v in range(3):
                STT(gs_all[:, l, :, v], mod_all[:, l, 8:16, v], 1.0, ngain[:, l, :], ALU.add, ALU.mult,
                    r=["mod", "ngain"], w=["mod"])

        pjc = [0]

        def next_pj():
            lst = pj_list[0]
            i = lst[pjc[0] % len(lst)]
            pjc[0] += 1
            return BK[i], ("PS", i)

        msc = [0]

        def next_ms():
            lst = ms_list[0]
            i = lst[msc[0] % len(lst)]
            msc[0] += 1
            return BK[i], ("PS", i)

        def proj(panel, pkey, tc, bank, bkey):
            t0, wd = TCH[tc]
            for k in range(8):
                MM(bank[:, :wd], lhsT=panel[:, k, :], rhs=hT[:, k, t0:t0 + wd],
                   start=(k == 0), stop=(k == 7), r=[pkey, ("hT", tc)], w=[bkey])

        def xnorm(l, b):
            state = {}

            def stage1(tc):
                t0, wd = TCH[tc]
                ms, msk = next_ms()
                for k in range(8):
                    sq, sqk = Wt.get()
                    if k in (0, 3, 6):
                        TT("pool", sq[:, :wd], xT[:, k, t0:t0 + wd], xT[:, k, t0:t0 + wd], ALU.mult,
                           r=[("xT", tc)], w=[sqk])
                    else:
                        ACT(sq[:, :wd], xT[:, k, t0:t0 + wd], AF.Square, r=[("xT", tc)], w=[sqk])
                    MM(ms[:, :wd], lhsT=ones128[:], rhs=sq[:, :wd], start=(k == 0), stop=(k == 7),
                       r=[sqk, "ones"], w=[msk])
                state[tc] = (ms, msk)

            def stage2(tc):
                t0, wd = TCH[tc]
                v = b if tc < 4 else 2
                ms, msk = state[tc]
                ln, lnk = Wt.get()
                ACT(ln[:, :wd], ms[:, :wd], AF.Ln, r=[msk], w=[lnk], scale=1.0 / D, bias=EPS)
                ACT(ms[:, :wd], ln[:, :wd], AF.Exp, r=[lnk], w=[msk], scale=-0.5)
                for k in range(8):
                    t, tk = Wt.get()
                    TT("dve", t[:, :wd], xT[:, k, t0:t0 + wd], ms[:, :wd], ALU.mult,
                       r=[("xT", tc), msk], w=[tk])
                    if k in (1, 4, 7):
                        TS("pool", hT[:, k, t0:t0 + wd], t[:, :wd], gs_all[:, l, k, v:v + 1], ALU.mult,
                           s2=mod_all[:, l, k, v:v + 1], op1=ALU.add, r=[tk, "mod"], w=[("hT", tc)])
                    else:
                        ACT(hT[:, k, t0:t0 + wd], t[:, :wd], AF.Identity, r=[tk, "mod"], w=[("hT", tc)],
                            scale=gs_all[:, l, k, v:v + 1], bias=mod_all[:, l, k, v:v + 1])

            for i in range(6):
                if i < 5:
                    stage1(i)
                if i >= 1:
                    stage2(i - 1)

        def nr_stages(panel, pkey, tc, gcol, dst, dkey, rope, extra_r=(), micro=False):
            t0, wd = TCH[tc]
            st_ = {}

            hw_ = wd // 2

            def s0():
                st_["bank"], st_["bkey"] = next_pj()
                proj(panel, pkey, tc, st_["bank"], st_["bkey"])

            def pk(k):
                def f():
                    if k == 0:
                        st_["bank"], st_["bkey"] = next_pj()
                    MM(st_["bank"][:, :wd], lhsT=panel[:, k, :], rhs=hT[:, k, t0:t0 + wd],
                       start=(k == 0), stop=(k == 7), r=[pkey, ("hT", tc)], w=[st_["bkey"]])
                return f

            def s1e():
                bank, bkey = st_["bank"], st_["bkey"]
                qg, qgk = Wt.get()
                sq, sqk = Wt.get()
                TS("dve", qg[:, :wd], bank[:, :wd], gcol, ALU.mult, r=[bkey, "gains"], w=[qgk])
                ACT(sq[:, :wd], bank[:, :wd], AF.Square, r=[bkey], w=[sqk])
                st_.update(qg=qg, qgk=qgk, sq=sq, sqk=sqk)

            def s1m():
                ms, msk = next_ms()
                MM(ms[:, :hw_], lhsT=bones[:], rhs=st_["sq"][:, :hw_], r=[st_["sqk"], "bones"], w=[msk])
                st_.update(ms=ms, msk=msk)

            def s1a():
                s1e()
                s1m()

            def s1b():
                MM(st_["ms"][:, hw_:wd], lhsT=bones[:], rhs=st_["sq"][:, hw_:wd], r=[st_["sqk"], "bones"], w=[st_["msk"]])

            def s1():
                s1a()
                s1b()

            def s2e():
                ms, msk = st_["ms"], st_["msk"]
                rs, rsk = st_["sq"], st_["sqk"]
                ACT(rs[:, :wd], ms[:, :wd], AF.Ln, r=[msk], w=[rsk], scale=1.0 / 64, bias=EPS)
                ACT(rs[:, :wd], rs[:, :wd], AF.Exp, r=[rsk], w=[rsk], scale=-0.5)
                st_.update(rs=rs, rsk=rsk)

            def s2m():
                if rope and tc < 4:
                    qg, qgk = st_["qg"], st_["qgk"]
                    ms2, ms2k = next_ms()
                    MM(ms2[:, :hw_], lhsT=rmat[:], rhs=qg[:, :hw_], r=[qgk, "rmat"], w=[ms2k])
                    st_.update(ms2=ms2, ms2k=ms2k)

            def s2a():
                s2e()
                s2m()

            def s2b():
                if rope and tc < 4:
                    MM(st_["ms2"][:, hw_:wd], lhsT=rmat[:], rhs=st_["qg"][:, hw_:wd], r=[st_["qgk"], "rmat"], w=[st_["ms2k"]])

            def s2():
                s2a()
                s2b()

            def s3():
                qg, qgk, rs, rsk = st_["qg"], st_["qgk"], st_["rs"], st_["rsk"]
                if rope and tc < 4:
                    ms2, ms2k = st_["ms2"], st_["ms2k"]
                    nr = wd // 64
                    r0 = t0 // 64
                    cosA = ropet[:, r0:r0 + nr].unsqueeze(2).broadcast_to([128, nr, 64])
                    cosB = ropet[:, 32:96].unsqueeze(1).broadcast_to([128, nr, 64])
                    sinA = ropet[:, 96 + r0:96 + r0 + nr].unsqueeze(2).broadcast_to([128, nr, 64])
                    sinB = ropet[:, 128:192].unsqueeze(1).broadcast_to([128, nr, 64])

                    def v3(ap):
                        return ap.rearrange("p (r c) -> p r c", c=64)
                    t1, t1k = Wt.get()
                    t2, t2k = Wt.get()
                    TT("pool", v3(t1[:, :wd]), v3(qg[:, :wd]), cosA, ALU.mult, r=[qgk, "ropet"], w=[t1k])
                    TT("pool", v3(t1[:, :wd]), v3(t1[:, :wd]), cosB, ALU.mult, r=[t1k, "ropet"], w=[t1k])
                    TT("dve", v3(t2[:, :wd]), v3(ms2[:, :wd]), sinA, ALU.mult, r=[ms2k, "ropet"], w=[t2k])
                    TT("dve", v3(t2[:, :wd]), v3(t2[:, :wd]), sinB, ALU.mult, r=[t2k, "ropet"], w=[t2k])
                    TT("pool", t1[:, :wd], t1[:, :wd], t2[:, :wd], ALU.add, r=[t1k, t2k], w=[t1k])
                    src, srck, eng = t1, t1k, "dve"
                else:
                    src, srck, eng = qg, qgk, "pool"
                dsts = dst if isinstance(dst, list) else [(dst, slice(0, 128))]
                for (dap, prr) in dsts:
                    TT(eng, dap, src[prr, :wd], rs[prr, :wd], ALU.mult, r=[srck, rsk] + list(extra_r), w=[dkey])

            if micro:
                if rope and tc < 4:
                    return [pk(k) for k in range(8)] + [s1e, s1m, s1b, s2e, s2m, s2b, None, s3]
                return [pk(k) for k in range(8)] + [s1e, s1m, s1b, s2e, None, s3]
            return [s0, s1, s2, s3]

        def place(sched, start, ops, per_step=1):
            for i, f in enumerate(ops):
                if f is not None:
                    sched.setdefault(start + i // per_step, []).append(f)

        def run_all(stages):
            for f in stages:
                f()

        def run_pipelined(units):
            n = len(units)
            ns = len(units[0])
            for step in range(n + ns - 1):
                for si in range(ns - 1, -1, -1):
                    ui = step - si
                    if 0 <= ui < n:
                        units[ui][si]()

        def silu_gate(bank, bkey, wd):
            e1, e1k = Wt.get()
            ACT(e1[:, :wd], bank[:, :wd], AF.Exp, r=[bkey], w=[e1k], scale=-1.0)
            ACT(e1[:, :wd], e1[:, :wd], AF.Ln, r=[e1k], w=[e1k], bias=1.0)
            ACT(e1[:, :wd], e1[:, :wd], AF.Exp, r=[e1k], w=[e1k], scale=-1.0)
            g, gk = Wt.get()
            TT("dve", g[:, :wd], bank[:, :wd], e1[:, :wd], ALU.mult, r=[bkey, e1k], w=[gk])
            return g, gk

        def gate_stages(gp, gpk, tc, holder, micro=False):
            t0, wd = TCH[tc]

            def g0():
                holder["bank"], holder["bkey"] = next_pj()
                proj(gp, gpk, tc, holder["bank"], holder["bkey"])

            def gk_(k):
                def f():
                    if k == 0:
                        holder["bank"], holder["bkey"] = next_pj()
                    MM(holder["bank"][:, :wd], lhsT=gp[:, k, :], rhs=hT[:, k, t0:t0 + wd],
                       start=(k == 0), stop=(k == 7), r=[gpk, ("hT", tc)], w=[holder["bkey"]])
                return f

            def g1():
                holder["g"], holder["gk"] = silu_gate(holder["bank"], holder["bkey"], wd)

            if micro:
                return [gk_(k) for k in range(8)] + [g1]
            return [g0, g1]

        def finalize_norm(tc, mc, g, gk):
            t0, wd = TCH[tc]
            ao, aok = Wt.get()
            for hh in range(2):
                rr, rrk = Wt.get()
                ACT(rr[64:128, :wd], OA[hh][64:128, :wd], AF.Ln, r=[OAK[hh]], w=[rrk])
                ACT(rr[64:128, :wd], rr[64:128, :wd], AF.Exp, r=[rrk], w=[rrk], scale=-1.0)
                TT("dve", ao[64 * hh:64 * hh + 64, :wd], OA[hh][0:64, :wd], rr[64:128, :wd], ALU.mult,
                   r=[OAK[hh], rrk], w=[aok])
            TT("pool", mix[:, mc, t0:t0 + wd], ao[:, :wd], g[:, :wd], ALU.mult, r=[aok, gk], w=[("mix", mc, tc)])

        def finalize_attn(gp, gpk, tc, mc):
            h = {}
            run_all(gate_stages(gp, gpk, tc, h))
            finalize_norm(tc, mc, h["g"], h["gk"])

        def y_update(l, b, mcs, tag, last):
            for m in range(8):
                wp, wpk = WS.next((tag, l, m))
                for tc, (t0, wd) in enumerate(TCH):
                    if tc == 4 and last:
                        continue
                    bank, bkey = next_pj()
                    for i, mc in enumerate(mcs):
                        MM(bank[:, :wd], lhsT=wp[:, i, :], rhs=mix[:, mc, t0:t0 + wd],
                           start=(i == 0), stop=(i == len(mcs) - 1), r=[wpk, ("mix", mc, tc)], w=[bkey])
                    v = b if tc < 4 else 2
                    STT(xT[:, m, t0:t0 + wd], bank[:, :wd], mod_all[:, l, 16 + m, v:v + 1], xT[:, m, t0:t0 + wd],
                        ALU.mult, ALU.add, r=[bkey, "mod", ("xT", tc)], w=[("xT", tc)])

        def phase_A(l, b, last):
            fence()
            pj_list[0] = [6]
            ms_list[0] = [7]
            MSET("pool", vaugA[:, :, :, 1, :], 1.0, r=["PH"], w=["vaug"])
            for kv in range(2):
                for tc_ in range(5):
                    t0_, wd_ = TCH[tc_]
                    MSET("pool", kz[kv][0][64:128, t0_:t0_ + wd_], 0.0, r=["PH"], w=[("kdup", kv, tc_)])
                    MSET("pool", kz[kv][1][0:64, t0_:t0_ + wd_], 0.0, r=["PH"], w=[("kdup", kv, tc_)])
            pj_list[0] = [6, 0]
            ms_list[0] = [7, 1]
            kunits = []
            for kv in range(2):
                kp, kpk = WS.next(("in", l, P_K0 + kv))
                for tc in range(5):
                    t0, wd = TCH[tc]
                    kunits.append(nr_stages(kp, kpk, tc, gains[:, l, 1:2],
                                            [(kz[kv][0][0:64, t0:t0 + wd], slice(0, 64)),
                                             (kz[kv][1][64:128, t0:t0 + wd], slice(64, 128))],
                                            ("kdup", kv, tc), True, extra_r=["PH"]))
            run_pipelined(kunits)
            vp, vpk = WS.next(("in", l, P_V))
            for g4 in range(5):
                bank, bkey = next_pj()
                tl = list(range(g4 * 4, min(18, g4 * 4 + 4)))
                for qi, ti in enumerate(tl):
                    for k in range(8):
                        MM(bank[:, qi * 128:(qi + 1) * 128], lhsT=hT[:, k, ti * 128:(ti + 1) * 128], rhs=vp[:, k, :],
                           start=(k == 0), stop=(k == 7), r=[vpk, ("hT", min(ti // 4, 4))], w=[bkey])
                n = len(tl)
                CP("act", vaugA[:, tl[0]:tl[0] + n, :, 0, :],
                   bank[:, 0:n * 128].rearrange("p (t k d) -> p t k d", t=n, k=2),
                   r=[bkey, "PH"], w=["vaug"])

            def q_units(c, qp, qpk, micro=False):
                return [nr_stages(qp, qpk, tc, gains[:, l, 0:1], mix[:, c, TCH[tc][0]:TCH[tc][0] + TCH[tc][1]],
                                  ("mix", c, tc), True, micro=micro) for tc in range(5)]

            qp, qpk = WS.next(("in", l, P_Q + 0))
            run_pipelined(q_units(0, qp, qpk))
            pj_list[0] = [6]
            ms_list[0] = [7]
            for c in range(4):
                kv = c // 2
                gp, gpk = WS.next(("in", l, P_G + c))
                side = []
                if c < 3:
                    qpn, qpnk = WS.next(("in", l, P_Q + c + 1))
                    side = q_units(c + 1, qpn, qpnk, micro=True)
                for tc in range(5):
                    t0, wd = TCH[tc]
                    tiles = list(range(18)) if tc < 4 else [16, 17]
                    nst = len(tiles)
                    sched = {}
                    gh = {}
                    gst = gate_stages(gp, gpk, tc, gh, micro=True)
                    if tc < 4:
                        if side and tc < 3:
                            place(sched, 0, side[tc])
                            place(sched, 9, gst[0:8], per_step=2)
                            place(sched, 14, gst[8:])
                        elif side:
                            place(sched, 0, side[3])
                            place(sched, 9, side[4][0:8], per_step=2)
                            place(sched, 13, side[4][8:9])
                            place(sched, 16, side[4][9:])
                            place(sched, 14, gst[0:8], per_step=2)
                            place(sched, 18, gst[8:])
                        else:
                            place(sched, 4, gst[0:8])
                            place(sched, 13, gst[8:])
                    else:
                        place(sched, 0, gst[0:8], per_step=8)
                        place(sched, 1, gst[8:])
                    LAG = 2
                    pq = []
                    for s_ in range(nst + LAG):
                        cur = None
                        if s_ < nst:
                            t = tiles[s_]
                            scd, sk = SCD[s_ % 2], SCDK[s_ % 2]
                            for hh in range(2):
                                MM(scd[:, 512 * hh:512 * hh + wd], lhsT=kz[kv][hh][:, t * 128:(t + 1) * 128],
                                   rhs=mix[:, c, t0:t0 + wd],
                                   r=[("kdup", kv, min(t // 4, 4)), ("mix", c, tc), "PH"], w=[sk[hh]])
                            ptd, ptk = PT.get()
                            if wd == 512:
                                ACT(ptd[:, :], scd[:, :], AF.Exp, r=sk, w=[ptk], scale=0.125)
                            else:
                                ACT(ptd[:, :].rearrange("p (h w) -> p h w", h=2)[:, :, 0:wd],
                                    scd.rearrange("p (h w) -> p h w", h=2)[:, :, 0:wd], AF.Exp, r=sk, w=[ptk], scale=0.125)
                            cur = (t, ptd, ptk)
                        pq.append(cur)
                        if len(pq) > LAG and pq[0] is not None:
                            t_, ptd_, ptk_ = pq[0]
                            for hh in range(2):
                                MM(OA[hh][:, :wd], lhsT=vaugA[:, t_, kv, :, :].rearrange("p a d -> p (a d)"),
                                   rhs=ptd_[:, 512 * hh:512 * hh + wd], start=(t_ == tiles[0]), stop=(t_ == tiles[-1]),
                                   r=[ptk_, "vaug", "PH"], w=[OAK[hh]])
                        if len(pq) > LAG:
                            pq.pop(0)
                        for f in sched.get(s_, ()):
                            f()
                    for s_x in sorted(k_ for k_ in sched if k_ >= nst + LAG):
                        for f in sched[s_x]:
                            f()
                    finalize_norm(tc, c, gh["g"], gh["gk"])
            pj_list[0] = [6, 0]
            ms_list[0] = [7, 1]
            y_update(l, b, [0, 1, 2, 3], "outA", last)

        def phase_B(l, b, last):
            fence()
            DMA(poolw32[:], poolw_d[l], w=["poolw32"], grp="pw")
            CP("pool", poolwb[:], poolw32[:], r=["poolw32"], w=["poolwb"])
            for ap in (zbuf[:, 0:16], zbuf[:, 2064:2080], zbuf[:, 2336:2352]):
                MSET("pool", ap, 0.0, r=["PH"], w=["zbuf"])
            segs = [(16, 0, SEQ), (2080, SEQ, NCTX)]
            for cz in range(2):
                zp, zpk = WS.next(("in", l, P_Z + cz))
                for tc in range(5):
                    t0, wd = TCH[tc]
                    zo = 16 + t0 if tc < 4 else 2080
                    bank, bkey = next_pj()
                    proj(zp, zpk, tc, bank, bkey)
                    CP("act", zbuf[:, zo:zo + wd], bank[:, :wd], r=[bkey, "PH"], w=["zbuf"])
                N = ZN
                TT("pool", tA[:, 1:N], zbuf[:, 1:N], zbuf[:, 0:N - 1], ALU.add, r=["zbuf", "PH"], w=["tA"])
                TT("pool", tB[:, 2:N - 1], tA[:, 3:N], tA[:, 1:N - 2], ALU.add, r=["tA", "PH"], w=["tB"])
                if cz == 1:
                    TT("pool", tA[:, 4:N - 3], tB[:, 6:N - 1], tB[:, 2:N - 5], ALU.add, r=["tB", "PH"], w=["tA"])
                    TT("pool", tB[:, 8:N - 7], tA[:, 12:N - 3], tA[:, 4:N - 11], ALU.add, r=["tA", "PH"], w=["tB"])
                srcs = [(tA, "tA", 0), (tB, "tB", 64)]
                for (src, skey, p0) in srcs:
                    pr = slice(p0, p0 + 64)
                    for (zo, to, ln_) in segs:
                        tcs = [0, 1, 2, 3] if to == 0 else [4]
                        wkeys = [("mix", cz, tc) for tc in tcs]
                        STT(mix[pr, cz, to:to + ln_], src[pr, zo:zo + ln_], pooltab[pr, cz, 0:1], zbuf[pr, zo:zo + ln_],
                            ALU.mult, ALU.subtract, r=[skey, "zbuf", "pooltab", "PH"], w=wkeys)
                        for (eo, tab0) in ((0, 1), (ln_ - 8, 9)):
                            et, etk = Wt.get()
                            TT("pool", et[pr, 0:8], src[pr, zo + eo:zo + eo + 8], pooltab[pr, cz, tab0:tab0 + 8], ALU.mult,
                               r=[skey, "pooltab", "PH"], w=[etk])
                            TT("pool", mix[pr, cz, to + eo:to + eo + 8], et[pr, 0:8], zbuf[pr, zo + eo:zo + eo + 8],
                               ALU.subtract, r=[etk, "zbuf", "PH"], w=[wkeys[0] if eo == 0 else wkeys[-1]])
                gp, gpk = WS.next(("in", l, P_BG + cz))
                for tc in range(5):
                    t0, wd = TCH[tc]
                    bank, bkey = next_pj()
                    MM(bank[:, :wd], lhsT=poolwb[:, cz, :], rhs=mix[:, cz, t0:t0 + wd], r=["poolwb", ("mix", cz, tc)], w=[bkey])
                    bank2, bkey2 = next_pj()
                    proj(gp, gpk, tc, bank2, bkey2)
                    g, gk = silu_gate(bank2, bkey2, wd)
                    STT(mix[:, cz, t0:t0 + wd], bank[:, :wd], pscale[:, l, cz:cz + 1], g[:, :wd], ALU.mult, ALU.mult,
                        r=[bkey, gk, "pscale"], w=[("mix", cz, tc)])

        def phase_C(l, b, last):
            for c2 in range(2):
                mc = 2 + c2
                fence()
                MSET("pool", nvaug[:, :, :, 1, :], 1.0, r=["PH"], w=["nvaug"])
                for tc_ in range(5):
                    t0_, wd_ = TCH[tc_]
                    MSET("pool", nkz[0][64:128, t0_:t0_ + wd_], 0.0, r=["PH"], w=[("nkT", tc_)])
                    MSET("pool", nkz[1][0:64, t0_:t0_ + wd_], 0.0, r=["PH"], w=[("nkT", tc_)])
                DMA(ebuf[:], ebias_d[l, c2], r=["PH"], w=["ebuf"], grp="eb")
                TS("dve", ebuf[:], ebuf[:], 8.0, ALU.mult, r=["ebuf", "PH"], w=["ebuf"])
                kp, kpk = WS.next(("in", l, P_NK + c2))
                qp, qpk = WS.next(("in", l, P_NQ + c2))
                units = []
                for tc in range(5):
                    t0, wd = TCH[tc]
                    units.append(nr_stages(kp, kpk, tc, gains[:, l, 3:4],
                                           [(nkz[0][0:64, t0:t0 + wd], slice(0, 64)), (nkz[1][64:128, t0:t0 + wd], slice(64, 128))],
                                           ("nkT", tc), False, extra_r=["PH"]))
                for tc in range(5):
                    t0, wd = TCH[tc]
                    units.append(nr_stages(qp, qpk, tc, gains[:, l, 2:3], mix[:, mc, t0:t0 + wd], ("mix", mc, tc), False))
                run_pipelined(units)
                vp, vpk = WS.next(("in", l, P_NV + c2))
                toks = [ti * 128 for ti in range(18)] + [64 + 128 * o for o in range(15)]
                for g4 in range(9):
                    tl = list(range(g4 * 4, min(33, g4 * 4 + 4)))
                    bank, bkey = next_pj()
                    for qi, ti in enumerate(tl):
                        tk0 = toks[ti]
                        rk = sorted(set([("hT", min(tk0 // 512, 4)), ("hT", min((tk0 + 127) // 512, 4))]))
                        for k in range(8):
                            MM(bank[:, qi * 128:(qi + 1) * 128], lhsT=hT[:, k, tk0:tk0 + 128], rhs=vp[:, k, :],
                               start=(k == 0), stop=(k == 7), r=[vpk] + rk, w=[bkey])
                    n = len(tl)
                    CP("act", nvaug[:, tl[0]:tl[0] + n, :, 0, :],
                       bank[:, 0:n * 128].rearrange("p (t k d) -> p t k d", t=n, k=2),
                       r=[bkey, "PH"], w=["nvaug"])
                gp, gpk = WS.next(("in", l, P_NG + c2))
                ev = ebuf.rearrange("p h (q two) c -> p h q two c", two=2)
                pj_list[0] = [6]
                ms_list[0] = [7]
                LA = 3
                ptc = [0]
                pt_all = [("PT", j_) for j_ in range(3)] + [("PT", j_, h_) for j_ in range(3) for h_ in range(2)]
                MSET("pool", dummy[:, :], 0.0, w=pt_all + ["dummy"])
                for tc in range(4):
                    items = [(hh, rr_) for rr_ in range(8) for hh in (0, 1)]
                    pendq = []
                    for s in range(len(items) + LA):
                        cur = None
                        if s < len(items):
                            hh, rr_ = items[s]
                            r = tc * 8 + rr_
                            r0 = min(max(r - 4, 0), 24)
                            kb = 64 * r0
                            pr = slice(64 * hh, 64 * hh + 64)
                            sc, sck = BK[s % 4], ("PS", s % 4)
                            qap = mix[:, mc, 64 * r:64 * r + 64]
                            rkeys = sorted(set(("nkT", min((kb + 128 * i) // 512, 3)) for i in range(4)) |
                                           set(("nkT", min((kb + 128 * i + 127) // 512, 3)) for i in range(4)))
                            for i in range(4):
                                MM(sc[:, 64 * i:64 * i + 64], lhsT=nkz[hh][:, kb + 128 * i:kb + 128 * i + 128], rhs=qap,
                                   r=list(rkeys) + [("mix", mc, tc), "PH"], w=[sck])
                            for i2 in range(2):
                                MM(sc[:, 256 + 64 * i2:256 + 64 * i2 + 64],
                                   lhsT=nkz[hh][:, SEQ + 128 * i2:SEQ + 128 * i2 + 128], rhs=qap,
                                   r=[("nkT", 4), ("mix", mc, tc), "PH"], w=[sck])
                            s0 = r0 - r + 7
                            esl = ev[:, hh, s0 // 2:s0 // 2 + 4, s0 % 2, :]
                            sc3 = sc[:, 0:256].rearrange("p (i c) -> p i c", c=64)
                            TT("dve", sc3, sc3, esl, ALU.add, r=[sck, "ebuf", "PH"], w=[sck])
                            pj_ = ptc[0] % 6
                            ptc[0] += 1
                            pt, ptk = PT.tiles[pj_ // 2][:, 512 * (pj_ % 2):512 * (pj_ % 2) + 512], ("PT", pj_ // 2, pj_ % 2)
                            ACT(pt[:, 0:384], sc[:, 0:384], AF.Exp, r=[sck], w=[ptk], scale=0.125)
                            if r0 % 2 == 0:
                                vts = [r0 // 2 + i for i in range(4)]
                            else:
                                vts = [18 + (r0 - 1) // 2 + i for i in range(4)]
                            vts += [16, 17]
                            cur = (hh, rr_, pt, ptk, vts)
                        pendq.append(cur)
                        if len(pendq) > LA and pendq[0] is not None:
                            hh_, rr2, pt_, ptk_, vts_ = pendq[0]
                            for idx, vt in enumerate(vts_):
                                MM(OA[hh_][:, 64 * rr2:64 * rr2 + 64],
                                   lhsT=nvaug[:, vt, hh_, :, :].rearrange("p a d -> p (a d)"),
                                   rhs=pt_[:, 64 * idx:64 * idx + 64], start=(idx == 0), stop=(idx == 5),
                                   r=[ptk_, "nvaug", "PH"], w=[OAK[hh_]])
                        if len(pendq) > LA:
                            pendq.pop(0)
                    finalize_attn(gp, gpk, tc, mc)
                pj_list[0] = [6, 0]
                ms_list[0] = [7, 1]
                for hh in range(2):
                    pr = slice(64 * hh, 64 * hh + 64)
                    sc, sck = SC[hh], SCK[hh]
                    for i2 in range(2):
                        MM(sc[:, 256 * i2:256 * i2 + 256], lhsT=nkz[hh][:, SEQ + 128 * i2:SEQ + 128 * i2 + 128],
                           rhs=mix[:, mc, SEQ:SEQ + NCTX], r=[("nkT", 4), ("mix", mc, 4), "PH"], w=[sck])
                    pj_ = ptc[0] % 6
                    ptc[0] += 1
                    pt, ptk = PT.tiles[pj_ // 2][:, 512 * (pj_ % 2):512 * (pj_ % 2) + 512], ("PT", pj_ // 2, pj_ % 2)
                    ACT(pt[:, 0:512], sc[:, :], AF.Exp, r=[sck], w=[ptk], scale=0.125)
                    for i2 in range(2):
                        MM(OA[hh][:, 0:256], lhsT=nvaug[:, 16 + i2, hh, :, :].rearrange("p a d -> p (a d)"),
                           rhs=pt[:, 256 * i2:256 * i2 + 256], start=(i2 == 0), stop=(i2 == 1),
                           r=[ptk, "nvaug", "PH"], w=[OAK[hh]])
                finalize_attn(gp, gpk, 4, mc)
                MSET("pool", dummy[:, :], 0.0, w=pt_all + ["dummy"])
            y_update(l, b, ([0, 1, 2, 3] if "B" in phases else [2, 3]), "outC", last)

        final = []
        for b in range(n_batch):
            fence()
            for i in range(18):
                src = x_d[b, i * 128:(i + 1) * 128, :] if i < 16 else ctx_d[b, (i - 16) * 128:(i - 15) * 128, :]
                xs, xsk = xstage[i % NXS], ("xs", i % NXS)
                DMA(xs, src, r=["PH"], w=[xsk], grp="xs%d" % (i % NXS))
                tc = min(i // 4, 4)
                for half in range(2):
                    bank, bkey = next_pj()
                    for kk in range(4):
                        k = half * 4 + kk
                        S.add("pe", (lambda o, i_: (lambda e: e.transpose(out=o, in_=i_, identity=ident[:])))(
                            bank[:, kk * 128:(kk + 1) * 128], xs[:, k * 128:(k + 1) * 128]),
                            [xsk, "ident", "PH"], [bkey])
                    CP("dve" if half == 0 else "act", xT[:, half * 4:half * 4 + 4, i * 128:(i + 1) * 128],
                       bank[:, :].rearrange("p (k t) -> p k t", t=128), r=[bkey], w=[("xT", tc)])
            for l in range(n_layers):
                last = (l == NL - 1)
                if stage >= 1:
                    xnorm(l, b)
                if "A" in phases and stage >= 2:
                    phase_A(l, b, last)
                if "B" in phases:
                    phase_B(l, b, last)
                if "C" in phases:
                    phase_C(l, b, last)
            fence()
            for i in range(16):
                xs, xsk = xstage[i % NXS], ("xs", i % NXS)
                tc = i // 4
                for half in range(2):
                    bank, bkey = next_pj()
                    for kk in range(4):
                        k = half * 4 + kk
                        S.add("pe", (lambda o, i_: (lambda e: e.transpose(out=o, in_=i_, identity=ident[:])))(
                            bank[:, kk * 128:(kk + 1) * 128], xT[:, k, i * 128:(i + 1) * 128]),
                            [("xT", tc), "ident"], [bkey])
                    CP("dve" if half == 0 else "act", xs[:, half * 512:(half + 1) * 512], bank[:, :],
                       r=[bkey, "PH"], w=[xsk])
                final.append(DMA(out_d[b, i * 128:(i + 1) * 128, :], xs, r=[xsk, "PH"], grp="o%d" % (i % NXS)))
        assert stage < 99 or WS.consumed == len(WS.specs), (WS.consumed, len(WS.specs))
        S.emit(nc, final_waits=final[-NXS:])
    return nc


def _f32(a):
    return np.ascontiguousarray(np.asarray(a, dtype=np.float32))


def prep_shared(c_ctx, norm_gain, w_mod, b_mod, w_in, att_q_gain, att_k_gain, pool_w, pool_scale,
                na_q_gain, na_k_gain, na_rpb, w_out):
    sh = {}
    w_mod = _f32(w_mod)
    sh["wmod"] = _f32(w_mod.reshape(NL, 8, 128, 6, 512).transpose(0, 3, 2, 1, 4))
    sh["bmod"] = _f32(_f32(b_mod).reshape(NL, 24, 128).transpose(2, 0, 1))
    sh["ngain"] = _f32(_f32(norm_gain).reshape(NL, 8, 128).transpose(2, 0, 1))
    w_in = _f32(w_in)
    cols = []
    cols.append(np.r_[512:576, 512:576])
    cols.append(np.r_[576:640, 576:640])
    cols.append(np.r_[640:768])
    for c in range(4):
        cols.append(np.r_[c * 128:(c + 1) * 128])
    for c in range(4):
        cols.append(np.r_[768 + c * 128:768 + (c + 1) * 128])
    for base in (1280, 1536, 2048, 2304, 1792, 2560):
        for c in range(2):
            cols.append(np.r_[base + c * 128:base + (c + 1) * 128])
    assert len(cols) == NPAN
    win = np.empty((NL, NPAN, 128, 8, 128), np.float32)
    for pi, cc in enumerate(cols):
        win[:, pi] = w_in[:, :, cc].reshape(NL, 8, 128, 128).transpose(0, 2, 1, 3)
    sh["win"] = win
    sh["wout"] = _f32(_f32(w_out).reshape(NL, 8, 128, 8, 128).transpose(0, 3, 2, 1, 4))
    g = np.stack([_f32(att_q_gain), _f32(att_k_gain), _f32(na_q_gain), _f32(na_k_gain)], axis=-1)
    sh["gains"] = _f32(np.tile(g, (1, 2, 1)).transpose(1, 0, 2))
    p = np.arange(128)
    d = p % 64
    inv_freq = (np.float32(10000.0) ** (-np.arange(16, dtype=np.float32) / np.float32(16))).astype(np.float32)
    f = inv_freq[d % 16]
    sign = np.where((d % 32) < 16, -1.0, 1.0).astype(np.float32)
    isrow = d < 32
    rows = np.arange(32, dtype=np.float32)
    colsv = np.arange(64, dtype=np.float32)
    angA = (rows[None, :] * f[:, None]).astype(np.float32)
    angB = (colsv[None, :] * f[:, None]).astype(np.float32)
    cosA = np.where(isrow[:, None], np.cos(angA), 1.0)
    cosB = np.where(isrow[:, None], 1.0, np.cos(angB))
    sinA = np.where(isrow[:, None], sign[:, None] * np.sin(angA), 1.0)
    sinB = np.where(isrow[:, None], 1.0, sign[:, None] * np.sin(angB))
    sh["ropetab"] = _f32(np.concatenate([cosA, cosB, sinA, sinB], axis=1))
    partner = np.where((p % 32) < 16, p + 16, p - 16)
    rm = np.zeros((128, 128), np.float32)
    rm[partner, p] = 1.0
    sh["rmat"] = rm
    sh["ident"] = np.eye(128, dtype=np.float32)
    pw = _f32(pool_w)
    poolw = np.zeros((NL, 128, 2, 128), np.float32)
    for cz in range(2):
        for j in range(2):
            poolw[:, j * 64:(j + 1) * 64, cz, j * 64:(j + 1) * 64] = pw[:, 2 * cz + j]
    sh["poolw"] = poolw
    sh["pscale"] = _f32(_f32(pool_scale).reshape(NL, 2, 128).transpose(2, 0, 1))
    pt = np.zeros((128, 2, 17), np.float32)
    for cz in range(2):
        for j in range(2):
            w = (2, 4, 8, 16)[2 * cz + j]
            t = np.arange(8)
            cs = np.minimum(w, t + w // 2).astype(np.float32)
            ce = np.minimum(w, (8 - t) + w // 2 - 1 + 0).astype(np.float32)
            ce = np.minimum(w, 8 - t + w // 2).astype(np.float32)
            pt[j * 64:(j + 1) * 64, cz, 0] = 1.0 / w
            pt[j * 64:(j + 1) * 64, cz, 1:9] = 1.0 / cs
            pt[j * 64:(j + 1) * 64, cz, 9:17] = 1.0 / ce
    sh["pooltab"] = pt
    rpb = _f32(na_rpb)
    kc = np.arange(64)[:, None]
    cq = np.arange(64)[None, :]
    c0 = np.clip(cq - 8, 0, 48)
    valid = (kc >= c0) & (kc < c0 + 16)
    dcol = np.clip(kc - cq + 15, 0, 30)
    Dh = np.where(valid[None, None, None], rpb[:, :, :, dcol], np.float32(NEG))
    eb = np.empty((NL, 2, 128, 2, 14, 64), np.float32)
    for c2 in range(2):
        for hh in range(2):
            h = 2 * c2 + hh
            for j in range(2):
                for di in range(14):
                    eb[:, c2, j * 64:(j + 1) * 64, hh, di, :] = Dh[:, h, di + j]
    sh["ebias"] = eb
    return sh


_NC_CACHE = {}


def kernel(x, c, ctx, c_ctx, norm_gain, w_mod, b_mod, w_in, att_q_gain, att_k_gain,
           pool_w, pool_scale, na_q_gain, na_k_gain, na_rpb, w_out):
    x = _f32(x)
    c = _f32(c)
    ctx = _f32(ctx)
    c_ctx = _f32(c_ctx)
    sh = prep_shared(c_ctx, norm_gain, w_mod, b_mod, w_in, att_q_gain, att_k_gain, pool_w, pool_scale,
                     na_q_gain, na_k_gain, na_rpb, w_out)
    if "nc" not in _NC_CACHE:
        _NC_CACHE["nc"] = build_nc()
    nc = _NC_CACHE["nc"]
    in_maps = []
    for i in range(8):
        m = dict(sh)
        m["x"] = np.ascontiguousarray(x[2 * i:2 * i + 2])
        m["ctx"] = np.ascontiguousarray(ctx[2 * i:2 * i + 2])
        vecs = np.stack([c[2 * i], c[2 * i + 1], c_ctx], axis=-1)
        m["cT"] = _f32(vecs.reshape(8, 128, 3).transpose(1, 0, 2))
        in_maps.append(m)
    res = run_bass_kernel_spmd(nc, in_maps, core_ids=list(range(8)))
    return np.concatenate([np.asarray(r["out"], dtype=np.float32) for r in res.results], axis=0)
```

```python
import contextlib
import numpy as np
import concourse.bass as bass
import concourse.mybir as mybir
from concourse.bass_utils import run_bass_kernel_spmd

F32 = mybir.dt.float32
BF16 = mybir.dt.bfloat16
ALU = mybir.AluOpType
AF = mybir.ActivationFunctionType

D = 1024
NL = 4
SEQ = 2048
NCTX = 256
T = SEQ + NCTX
EPS = 1e-6
NEG = -30000.0
TCH = [(0, 512), (512, 512), (1024, 512), (1536, 512), (2048, 256)]
ENGS = ("pe", "act", "dve", "pool", "sp")
PSUM_KEYS = ("PS",)

P_K0, P_K1, P_V = 0, 1, 2
P_Q, P_G, P_Z, P_BG, P_NK, P_NV, P_NQ, P_NG = 3, 7, 11, 13, 15, 17, 19, 21
NPAN = 23


class Op:
    __slots__ = ("eng", "idx", "fn", "waits", "signal", "dma_grp", "dma_cnt", "sigval")

    def __init__(self, eng, idx, fn):
        self.eng = eng
        self.idx = idx
        self.fn = fn
        self.waits = []
        self.signal = False
        self.dma_grp = None
        self.dma_cnt = 0
        self.sigval = 0


class Sched:
    def __init__(self):
        self.q = {e: [] for e in ENGS}
        self.last_w = {}
        self.readers = {}
        self.maxwait = {e: {} for e in ENGS}
        self.dma_cnt = {}

    def add(self, eng, fn, reads=(), writes=(), dma=None):
        op = Op(eng, len(self.q[eng]), fn)
        if dma is not None:
            op.dma_grp = dma
            self.dma_cnt[dma] = self.dma_cnt.get(dma, 0) + 1
            op.dma_cnt = self.dma_cnt[dma]
        best = {}
        for k in reads:
            w = self.last_w.get(k)
            if w is not None:
                self._cand(best, w, eng)
            if isinstance(k, tuple) and k[0] in PSUM_KEYS:
                for r in self.readers.get(k, ()):
                    if r.eng != eng:
                        self._cand(best, r, eng)
        for k in writes:
            w = self.last_w.get(k)
            if w is not None:
                self._cand(best, w, eng)
            for r in self.readers.get(k, ()):
                self._cand(best, r, eng)
        mw = self.maxwait[eng]
        for src, (pos, d) in best.items():
            if mw.get(src, -1) >= pos:
                continue
            mw[src] = pos
            d.signal = True
            op.waits.append(d)
        for k in writes:
            self.last_w[k] = op
            self.readers[k] = []
        for k in reads:
            self.readers.setdefault(k, []).append(op)
        self.q[eng].append(op)
        return op

    @staticmethod
    def _cand(best, d, eng):
        if d.dma_grp is not None:
            src = ("dma", d.dma_grp)
            pos = d.dma_cnt
        else:
            if d.eng == eng and eng == "pe":
                return
            src = d.eng
            pos = d.idx
        cur = best.get(src)
        if cur is None or cur[0] < pos:
            best[src] = (pos, d)

    def emit(self, nc, final_waits=()):
        for op in final_waits:
            op.signal = True
        for e in ENGS:
            n = 0
            for op in self.q[e]:
                if op.dma_grp is None and op.signal:
                    n += 1
                    op.sigval = n
        grps = sorted(self.dma_cnt.keys())
        with contextlib.ExitStack() as st:
            sems = {}
            for e in ENGS:
                sems[e] = st.enter_context(nc.semaphore("sem_" + e))
            for g in grps:
                sems[("dma", g)] = st.enter_context(nc.semaphore("semd_" + str(g)))
            block = st.enter_context(nc.Block())

            def tok(d):
                if d.dma_grp is not None:
                    return sems[("dma", d.dma_grp)], 16 * d.dma_cnt
                return sems[d.eng], d.sigval

            def run(engname, e):
                for op in self.q[engname]:
                    for d in op.waits:
                        s, v = tok(d)
                        e.wait_ge(s, v)
                    ins = op.fn(e)
                    if op.dma_grp is not None:
                        ins.then_inc(sems[("dma", op.dma_grp)], 16)
                    elif op.signal:
                        ins.then_inc(sems[engname], 1)
                if engname == "sp":
                    for d in final_waits:
                        s, v = tok(d)
                        e.wait_ge(s, v)

            @block.tensor
            def _(e):
                run("pe", e)

            @block.scalar
            def _(e):
                run("act", e)

            @block.vector
            def _(e):
                run("dve", e)

            @block.gpsimd
            def _(e):
                run("pool", e)

            @block.sync
            def _(e):
                run("sp", e)


class Ring:
    def __init__(self, name, tiles):
        self.name = name
        self.tiles = tiles
        self.i = 0

    def get(self):
        j = self.i % len(self.tiles)
        self.i += 1
        return self.tiles[j], (self.name, j)


def build_nc(n_layers=NL, n_batch=2, phases="ABC", stage=99):
    nc = bass.Bass("TRN2", target_bir_lowering=False)

    def din(name, shape):
        return nc.dram_tensor(name, list(shape), F32, kind="ExternalInput").ap()

    x_d = din("x", [2, SEQ, D])
    ctx_d = din("ctx", [2, NCTX, D])
    cT_d = din("cT", [128, 8, 3])
    wmod_d = din("wmod", [NL, 6, 128, 8, 512])
    bmod_d = din("bmod", [128, NL, 24])
    ngain_d = din("ngain", [128, NL, 8])
    win_d = din("win", [NL, NPAN, 128, 8, 128])
    wout_d = din("wout", [NL, 8, 128, 8, 128])
    gains_d = din("gains", [128, NL, 4])
    rope_d = din("ropetab", [128, 192])
    rmat_d = din("rmat", [128, 128])
    ident_d = din("ident", [128, 128])
    poolw_d = din("poolw", [NL, 128, 2, 128])
    pscale_d = din("pscale", [128, NL, 2])
    pooltab_d = din("pooltab", [128, 2, 17])
    ebias_d = din("ebias", [NL, 2, 128, 2, 14, 64])
    out_d = nc.dram_tensor("out", [2, SEQ, D], F32, kind="ExternalOutput").ap()

    S = Sched()
    with contextlib.ExitStack() as st:
        def sb(name, shape, dt=F32):
            return st.enter_context(nc.sbuf_tensor(name, list(shape), dt))

        xT = sb("xT", [128, 8, T])
        hT = sb("hT", [128, 8, T], BF16)
        mix = sb("mix", [128, 4, T], BF16)
        PH = sb("PH", [128, 8320])
        stg = [sb("stg%d" % i, [128, 8, 128]) for i in range(2)]
        wb = [sb("wb%d" % i, [128, 8, 128], BF16) for i in range(4)]
        Wt = Ring("W", [sb("wk%d" % i, [128, 512]) for i in range(8)])
        PT = Ring("PT", [sb("pt%d" % i, [128, 1024], BF16) for i in range(3)])
        ident = sb("ident_s", [128, 128])
        rmat = sb("rmat_s", [128, 128])
        ones128 = sb("ones128", [128, 128])
        bones = sb("bones", [128, 128])
        ropet = sb("ropet", [128, 192])
        gains = sb("gains_s", [128, NL, 4])
        ngain = sb("ngain_s", [128, NL, 8])
        bmod = sb("bmod_s", [128, NL, 24])
        pscale = sb("pscale_s", [128, NL, 2])
        pooltab = sb("pooltab_s", [128, 2, 17])
        cT = sb("cT_s", [128, 8, 3])
        scT = sb("scT", [128, 8, 3])
        mod_all = sb("mod_all", [128, NL, 24, 3])
        gs_all = sb("gs_all", [128, NL, 8, 3])
        poolw32 = sb("poolw32", [128, 2, 128])
        poolwb = sb("poolwb", [128, 2, 128], BF16)
        dummy = sb("fence_dummy", [128, 8])

        def ps(name):
            return st.enter_context(nc.psum_tensor(name, [128, 512], F32))

        PSALL = st.enter_context(nc.psum_tensor("PSALL", [128, 4096], F32))
        BK = [PSALL[:, 512 * i:512 * (i + 1)] for i in range(8)]
        SC = [BK[2], BK[3]]
        SCK = [("PS", 2), ("PS", 3)]
        OA = [BK[4], BK[5]]
        OAK = [("PS", 4), ("PS", 5)]
        SCD = [PSALL[:, 0:1024], PSALL[:, 1024:2048]]
        SCDK = [[("PS", 0), ("PS", 1)], [("PS", 2), ("PS", 3)]]
        pj_list = [[6, 0]]
        ms_list = [[7, 1]]

        PHb = PH[:, :].bitcast(BF16)
        kz = [[PHb[:, (2 * kv + hh) * T:(2 * kv + hh + 1) * T] for hh in range(2)] for kv in range(2)]
        vaugA = PHb[:, 4 * T:4 * T + 18 * 256].rearrange("p (t k a d) -> p t k a d", t=18, k=2, a=2)
        nkz = [PHb[:, hh * T:(hh + 1) * T] for hh in range(2)]
        nvaug = PHb[:, 2 * T:2 * T + 33 * 256].rearrange("p (t k a d) -> p t k a d", t=33, k=2, a=2)
        e_off = (2 * T + 33 * 256 + 1) // 2
        ebuf = PH[:, e_off:e_off + 2 * 14 * 64].rearrange("p (h d c) -> p h d c", h=2, d=14)
        assert e_off + 2 * 14 * 64 <= 8320
        ZN = 2352
        zbuf = PH[:, 0:ZN]
        tA = PH[:, ZN:2 * ZN]
        tB = PH[:, 2 * ZN:3 * ZN]
        assert 3 * ZN <= 8320
        NXS = 6
        xstage = [PH[:, i * 1024:(i + 1) * 1024] for i in range(NXS)]

        def MM(out, lhsT, rhs, start=True, stop=True, r=(), w=()):
            S.add("pe", lambda e: e.matmul(out, lhsT=lhsT, rhs=rhs, start=start, stop=stop), r, w)

        def ACT(out, in_, func, r=(), w=(), scale=None, bias=None):
            kw = {}
            if scale is not None:
                kw["scale"] = scale
            if bias is not None:
                kw["bias"] = bias
            S.add("act", lambda e: e.activation(out=out, in_=in_, func=func, **kw), r, w)

        def TT(eng, out, in0, in1, op, r=(), w=()):
            S.add(eng, lambda e: e.tensor_tensor(out=out, in0=in0, in1=in1, op=op), r, w)

        def TS(eng, out, in0, s1, op0, r=(), w=(), s2=None, op1=None):
            if op1 is None:
                S.add(eng, lambda e: e.tensor_scalar(out=out, in0=in0, scalar1=s1, scalar2=None, op0=op0), r, w)
            else:
                S.add(eng, lambda e: e.tensor_scalar(out=out, in0=in0, scalar1=s1, scalar2=s2, op0=op0, op1=op1), r, w)

        def STT(out, in0, scalar, in1, op0, op1, r=(), w=()):
            S.add("dve", lambda e: e.scalar_tensor_tensor(out=out, in0=in0, scalar=scalar, in1=in1, op0=op0, op1=op1), r, w)

        def CP(eng, out, in_, r=(), w=()):
            if eng == "act":
                S.add("act", lambda e: e.activation(out=out, in_=in_, func=AF.Copy), r, w)
            else:
                S.add(eng, lambda e: e.tensor_copy(out=out, in_=in_), r, w)

        def MSET(eng, ap, val, r=(), w=()):
            S.add(eng, lambda e: e.memset(ap, val), r, w)

        def DMA(out, in_, r=(), w=(), grp="m"):
            return S.add("sp", lambda e: e.dma_start(out=out, in_=in_), r, w, dma=grp)

        def fence():
            MSET("pool", dummy[:, :], 0.0, r=(), w=["PH", "dummy"])

        class WStream:
            def __init__(self):
                self.specs = []
                self.issued = 0
                self.consumed = 0

            def _issue(self):
                n = self.issued
                ap, nk, _tag = self.specs[n]
                si, wi = n % 2, n % 4
                DMA(stg[si][:, 0:nk, :], ap, w=[("stg", si)], grp="stg%d" % si)
                CP("pool", wb[wi][:, 0:nk, :], stg[si][:, 0:nk, :], r=[("stg", si)], w=[("wb", wi)])
                self.issued += 1

            def next(self, check=None):
                while self.issued < min(len(self.specs), self.consumed + 3 - 1):
                    self._issue()
                n = self.consumed
                if check is not None:
                    assert self.specs[n][2] == check, (self.specs[n][2], check)
                self.consumed += 1
                return wb[n % 4], ("wb", n % 4)

        WS = WStream()

        def add_spec(ap, nk, tag):
            WS.specs.append((ap, nk, tag))

        for b in range(n_batch):
            for l in range(n_layers):
                if "A" in phases:
                    for pid in (P_K0, P_K1, P_V):
                        add_spec(win_d[l, pid], 8, ("in", l, pid))
                    add_spec(win_d[l, P_Q], 8, ("in", l, P_Q))
                    for c in range(4):
                        add_spec(win_d[l, P_G + c], 8, ("in", l, P_G + c))
                        if c < 3:
                            add_spec(win_d[l, P_Q + c + 1], 8, ("in", l, P_Q + c + 1))
                    for m in range(8):
                        add_spec(wout_d[l, m, :, 0:4, :], 4, ("outA", l, m))
                if "B" in phases:
                    for cz in range(2):
                        add_spec(win_d[l, P_Z + cz], 8, ("in", l, P_Z + cz))
                        add_spec(win_d[l, P_BG + cz], 8, ("in", l, P_BG + cz))
                if "C" in phases:
                    for c2 in range(2):
                        for pid in (P_NK, P_NQ, P_NV, P_NG):
                            add_spec(win_d[l, pid + c2], 8, ("in", l, pid + c2))
                    for m in range(8):
                        if "B" in phases:
                            add_spec(wout_d[l, m, :, 4:8, :], 4, ("outC", l, m))
                        else:
                            add_spec(wout_d[l, m, :, 6:8, :], 2, ("outC", l, m))

        DMA(ident[:], ident_d[:], w=["ident"], grp="c_ident")
        DMA(rmat[:], rmat_d[:], w=["rmat"], grp="c_rmat")
        DMA(ropet[:], rope_d[:], w=["ropet"], grp="c_ropet")
        DMA(gains[:], gains_d[:], w=["gains"], grp="c_gains")
        DMA(ngain[:], ngain_d[:], w=["ngain"], grp="c_ngain")
        DMA(bmod[:], bmod_d[:], w=["bmod"], grp="c_bmod")
        DMA(pscale[:], pscale_d[:], w=["pscale"], grp="c_pscale")
        DMA(pooltab[:], pooltab_d[:], w=["pooltab"], grp="c_pooltab")
        DMA(cT[:], cT_d[:], w=["cT"], grp="c_cT")
        MSET("pool", ones128[:], 1.0, w=["ones"])
        MSET("pool", bones[:], 0.0, w=["bones"])
        MSET("pool", bones[0:64, 0:64], 1.0, w=["bones"])
        MSET("pool", bones[64:128, 64:128], 1.0, w=["bones"])
        ACT(scT[:], cT[:], AF.Exp, r=["cT"], w=["scT"], scale=-1.0)
        ACT(scT[:], scT[:], AF.Ln, r=["scT"], w=["scT"], bias=1.0)
        ACT(scT[:], scT[:], AF.Exp, r=["scT"], w=["scT"], scale=-1.0)
        TT("dve", scT[:], scT[:], cT[:], ALU.mult, r=["scT", "cT"], w=["scT"])
        for l in range(n_layers):
            rows = []
            for cc in range(6):
                i = l * 6 + cc
                si = i % 2
                wslot = PH[:, si * 4096:(si + 1) * 4096].rearrange("p (k c) -> p k c", k=8)
                DMA(wslot, wmod_d[l, cc], r=["PH"], w=[("wms", si)], grp="wms%d" % si)
                bank, bkey = BK[7 if cc % 2 == 0 else 6], ("PS", 7 if cc % 2 == 0 else 6)
                for k in range(8):
                    MM(bank[0:3, :], lhsT=scT[:, k, :], rhs=wslot[:, k, :],
                       start=(k == 0), stop=(k == 7), r=[("wms", si), "scT", "PH"], w=[bkey])
                rt, rtk = Wt.get()
                CP("dve", rt[0:3, :], bank[0:3, :], r=[bkey], w=[rtk])
                rows.append((rt, rtk))
            for m in range(24):
                rt, rtk = rows[m // 4]
                off = (m % 4) * 128
                S.add("pe", (lambda o, i_: (lambda e: e.transpose(out=o, in_=i_, identity=ident[0:3, 0:3])))(
                    BK[1][:, m * 4:m * 4 + 3], rt[0:3, off:off + 128]), [rtk, "ident"], [("PS", 1)])
            msv = BK[1][:, 0:96].rearrange("p (m f) -> p m f", f=4)
            for v in range(3):
                TT("dve", mod_all[:, l, :, v], msv[:, :, v], bmod[:, l, :], ALU.add,
                   r=[("PS", 1), "bmod"], w=["mod"])
            for v in range(3):
                STT(gs_all[:, l, :, v], mod_all[:, l, 8:16, v], 1.0, ngain[:, l, :], ALU.add, ALU.mult,
                    r=["mod", "ngain"], w=["mod"])

        pjc = [0]

        def next_pj():
            lst = pj_list[0]
            i = lst[pjc[0] % len(lst)]
            pjc[0] += 1
            return BK[i], ("PS", i)

        msc = [0]

        def next_ms():
            lst = ms_list[0]
            i = lst[msc[0] % len(lst)]
            msc[0] += 1
            return BK[i], ("PS", i)

        def proj(panel, pkey, tc, bank, bkey):
            t0, wd = TCH[tc]
            for k in range(8):
                MM(bank[:, :wd], lhsT=panel[:, k, :], rhs=hT[:, k, t0:t0 + wd],
                   start=(k == 0), stop=(k == 7), r=[pkey, ("hT", tc)], w=[bkey])

        def xnorm(l, b):
            state = {}

            def stage1(tc):
                t0, wd = TCH[tc]
                ms, msk = next_ms()
                for k in range(8):
                    sq, sqk = Wt.get()
                    if k in (0, 3, 6):
                        TT("pool", sq[:, :wd], xT[:, k, t0:t0 + wd], xT[:, k, t0:t0 + wd], ALU.mult,
                           r=[("xT", tc)], w=[sqk])
                    else:
                        ACT(sq[:, :wd], xT[:, k, t0:t0 + wd], AF.Square, r=[("xT", tc)], w=[sqk])
                    MM(ms[:, :wd], lhsT=ones128[:], rhs=sq[:, :wd], start=(k == 0), stop=(k == 7),
                       r=[sqk, "ones"], w=[msk])
                state[tc] = (ms, msk)

            def stage2(tc):
                t0, wd = TCH[tc]
                v = b if tc < 4 else 2
                ms, msk = state[tc]
                ln, lnk = Wt.get()
                ACT(ln[:, :wd], ms[:, :wd], AF.Ln, r=[msk], w=[lnk], scale=1.0 / D, bias=EPS)
                ACT(ms[:, :wd], ln[:, :wd], AF.Exp, r=[lnk], w=[msk], scale=-0.5)
                for k in range(8):
                    t, tk = Wt.get()
                    TT("dve", t[:, :wd], xT[:, k, t0:t0 + wd], ms[:, :wd], ALU.mult,
                       r=[("xT", tc), msk], w=[tk])
                    if k in (1, 4, 7):
                        TS("pool", hT[:, k, t0:t0 + wd], t[:, :wd], gs_all[:, l, k, v:v + 1], ALU.mult,
                           s2=mod_all[:, l, k, v:v + 1], op1=ALU.add, r=[tk, "mod"], w=[("hT", tc)])
                    else:
                        ACT(hT[:, k, t0:t0 + wd], t[:, :wd], AF.Identity, r=[tk, "mod"], w=[("hT", tc)],
                            scale=gs_all[:, l, k, v:v + 1], bias=mod_all[:, l, k, v:v + 1])

            for i in range(6):
                if i < 5:
                    stage1(i)
                if i >= 1:
                    stage2(i - 1)

        def nr_stages(panel, pkey, tc, gcol, dst, dkey, rope, extra_r=(), micro=False):
            t0, wd = TCH[tc]
            st_ = {}

            hw_ = wd // 2

            def s0():
                st_["bank"], st_["bkey"] = next_pj()
                proj(panel, pkey, tc, st_["bank"], st_["bkey"])

            def pk(k):
                def f():
                    if k == 0:
                        st_["bank"], st_["bkey"] = next_pj()
                    MM(st_["bank"][:, :wd], lhsT=panel[:, k, :], rhs=hT[:, k, t0:t0 + wd],
                       start=(k == 0), stop=(k == 7), r=[pkey, ("hT", tc)], w=[st_["bkey"]])
                return f

            def s1e():
                bank, bkey = st_["bank"], st_["bkey"]
                qg, qgk = Wt.get()
                sq, sqk = Wt.get()
                TS("dve", qg[:, :wd], bank[:, :wd], gcol, ALU.mult, r=[bkey, "gains"], w=[qgk])
                ACT(sq[:, :wd], bank[:, :wd], AF.Square, r=[bkey], w=[sqk])
                st_.update(qg=qg, qgk=qgk, sq=sq, sqk=sqk)

            def s1m():
                ms, msk = next_ms()
                MM(ms[:, :hw_], lhsT=bones[:], rhs=st_["sq"][:, :hw_], r=[st_["sqk"], "bones"], w=[msk])
                st_.update(ms=ms, msk=msk)

            def s1a():
                s1e()
                s1m()

            def s1b():
                MM(st_["ms"][:, hw_:wd], lhsT=bones[:], rhs=st_["sq"][:, hw_:wd], r=[st_["sqk"], "bones"], w=[st_["msk"]])

            def s1():
                s1a()
                s1b()

            def s2e():
                ms, msk = st_["ms"], st_["msk"]
                rs, rsk = st_["sq"], st_["sqk"]
                ACT(rs[:, :wd], ms[:, :wd], AF.Ln, r=[msk], w=[rsk], scale=1.0 / 64, bias=EPS)
                ACT(rs[:, :wd], rs[:, :wd], AF.Exp, r=[rsk], w=[rsk], scale=-0.5)
                st_.update(rs=rs, rsk=rsk)

            def s2m():
                if rope and tc < 4:
                    qg, qgk = st_["qg"], st_["qgk"]
                    ms2, ms2k = next_ms()
                    MM(ms2[:, :hw_], lhsT=rmat[:], rhs=qg[:, :hw_], r=[qgk, "rmat"], w=[ms2k])
                    st_.update(ms2=ms2, ms2k=ms2k)

            def s2a():
                s2e()
                s2m()

            def s2b():
                if rope and tc < 4:
                    MM(st_["ms2"][:, hw_:wd], lhsT=rmat[:], rhs=st_["qg"][:, hw_:wd], r=[st_["qgk"], "rmat"], w=[st_["ms2k"]])

            def s2():
                s2a()
                s2b()

            def s3():
                qg, qgk, rs, rsk = st_["qg"], st_["qgk"], st_["rs"], st_["rsk"]
                if rope and tc < 4:
                    ms2, ms2k = st_["ms2"], st_["ms2k"]
                    nr = wd // 64
                    r0 = t0 // 64
                    cosA = ropet[:, r0:r0 + nr].unsqueeze(2).broadcast_to([128, nr, 64])
                    cosB = ropet[:, 32:96].unsqueeze(1).broadcast_to([128, nr, 64])
                    sinA = ropet[:, 96 + r0:96 + r0 + nr].unsqueeze(2).broadcast_to([128, nr, 64])
                    sinB = ropet[:, 128:192].unsqueeze(1).broadcast_to([128, nr, 64])

                    def v3(ap):
                        return ap.rearrange("p (r c) -> p r c", c=64)
                    t1, t1k = Wt.get()
                    t2, t2k = Wt.get()
                    TT("pool", v3(t1[:, :wd]), v3(qg[:, :wd]), cosA, ALU.mult, r=[qgk, "ropet"], w=[t1k])
                    TT("pool", v3(t1[:, :wd]), v3(t1[:, :wd]), cosB, ALU.mult, r=[t1k, "ropet"], w=[t1k])
                    TT("dve", v3(t2[:, :wd]), v3(ms2[:, :wd]), sinA, ALU.mult, r=[ms2k, "ropet"], w=[t2k])
                    TT("dve", v3(t2[:, :wd]), v3(t2[:, :wd]), sinB, ALU.mult, r=[t2k, "ropet"], w=[t2k])
                    TT("pool", t1[:, :wd], t1[:, :wd], t2[:, :wd], ALU.add, r=[t1k, t2k], w=[t1k])
                    src, srck, eng = t1, t1k, "dve"
                else:
                    src, srck, eng = qg, qgk, "pool"
                dsts = dst if isinstance(dst, list) else [(dst, slice(0, 128))]
                for (dap, prr) in dsts:
                    TT(eng, dap, src[prr, :wd], rs[prr, :wd], ALU.mult, r=[srck, rsk] + list(extra_r), w=[dkey])

            if micro:
                if rope and tc < 4:
                    return [pk(k) for k in range(8)] + [s1e, s1m, s1b, s2e, s2m, s2b, None, s3]
                return [pk(k) for k in range(8)] + [s1e, s1m, s1b, s2e, None, s3]
            return [s0, s1, s2, s3]

        def place(sched, start, ops, per_step=1):
            for i, f in enumerate(ops):
                if f is not None:
                    sched.setdefault(start + i // per_step, []).append(f)

        def run_all(stages):
            for f in stages:
                f()

        def run_pipelined(units):
            n = len(units)
            ns = len(units[0])
            for step in range(n + ns - 1):
                for si in range(ns - 1, -1, -1):
                    ui = step - si
                    if 0 <= ui < n:
                        units[ui][si]()

        def silu_gate(bank, bkey, wd):
            e1, e1k = Wt.get()
            ACT(e1[:, :wd], bank[:, :wd], AF.Exp, r=[bkey], w=[e1k], scale=-1.0)
            ACT(e1[:, :wd], e1[:, :wd], AF.Ln, r=[e1k], w=[e1k], bias=1.0)
            ACT(e1[:, :wd], e1[:, :wd], AF.Exp, r=[e1k], w=[e1k], scale=-1.0)
            g, gk = Wt.get()
            TT("dve", g[:, :wd], bank[:, :wd], e1[:, :wd], ALU.mult, r=[bkey, e1k], w=[gk])
            return g, gk

        def gate_stages(gp, gpk, tc, holder, micro=False):
            t0, wd = TCH[tc]

            def g0():
                holder["bank"], holder["bkey"] = next_pj()
                proj(gp, gpk, tc, holder["bank"], holder["bkey"])

            def gk_(k):
                def f():
                    if k == 0:
                        holder["bank"], holder["bkey"] = next_pj()
                    MM(holder["bank"][:, :wd], lhsT=gp[:, k, :], rhs=hT[:, k, t0:t0 + wd],
                       start=(k == 0), stop=(k == 7), r=[gpk, ("hT", tc)], w=[holder["bkey"]])
                return f

            def g1():
                holder["g"], holder["gk"] = silu_gate(holder["bank"], holder["bkey"], wd)

            if micro:
                return [gk_(k) for k in range(8)] + [g1]
            return [g0, g1]

        def finalize_norm(tc, mc, g, gk):
            t0, wd = TCH[tc]
            ao, aok = Wt.get()
            for hh in range(2):
                rr, rrk = Wt.get()
                ACT(rr[64:128, :wd], OA[hh][64:128, :wd], AF.Ln, r=[OAK[hh]], w=[rrk])
                ACT(rr[64:128, :wd], rr[64:128, :wd], AF.Exp, r=[rrk], w=[rrk], scale=-1.0)
                TT("dve", ao[64 * hh:64 * hh + 64, :wd], OA[hh][0:64, :wd], rr[64:128, :wd], ALU.mult,
                   r=[OAK[hh], rrk], w=[aok])
            TT("pool", mix[:, mc, t0:t0 + wd], ao[:, :wd], g[:, :wd], ALU.mult, r=[aok, gk], w=[("mix", mc, tc)])

        def finalize_attn(gp, gpk, tc, mc):
            h = {}
            run_all(gate_stages(gp, gpk, tc, h))
            finalize_norm(tc, mc, h["g"], h["gk"])

        def y_update(l, b, mcs, tag, last):
            for m in range(8):
                wp, wpk = WS.next((tag, l, m))
                for tc, (t0, wd) in enumerate(TCH):
                    if tc == 4 and last:
                        continue
                    bank, bkey = next_pj()
                    for i, mc in enumerate(mcs):
                        MM(bank[:, :wd], lhsT=wp[:, i, :], rhs=mix[:, mc, t0:t0 + wd],
                           start=(i == 0), stop=(i == len(mcs) - 1), r=[wpk, ("mix", mc, tc)], w=[bkey])
                    v = b if tc < 4 else 2
                    STT(xT[:, m, t0:t0 + wd], bank[:, :wd], mod_all[:, l, 16 + m, v:v + 1], xT[:, m, t0:t0 + wd],
                        ALU.mult, ALU.add, r=[bkey, "mod", ("xT", tc)], w=[("xT", tc)])

        def phase_A(l, b, last):
            fence()
            pj_list[0] = [6]
            ms_list[0] = [7]
            MSET("pool", vaugA[:, :, :, 1, :], 1.0, r=["PH"], w=["vaug"])
            for kv in range(2):
                for tc_ in range(5):
                    t0_, wd_ = TCH[tc_]
                    MSET("pool", kz[kv][0][64:128, t0_:t0_ + wd_], 0.0, r=["PH"], w=[("kdup", kv, tc_)])
                    MSET("pool", kz[kv][1][0:64, t0_:t0_ + wd_], 0.0, r=["PH"], w=[("kdup", kv, tc_)])
            pj_list[0] = [6, 0]
            ms_list[0] = [7, 1]
            kunits = []
            for kv in range(2):
                kp, kpk = WS.next(("in", l, P_K0 + kv))
                for tc in range(5):
                    t0, wd = TCH[tc]
                    kunits.append(nr_stages(kp, kpk, tc, gains[:, l, 1:2],
                                            [(kz[kv][0][0:64, t0:t0 + wd], slice(0, 64)),
                                             (kz[kv][1][64:128, t0:t0 + wd], slice(64, 128))],
                                            ("kdup", kv, tc), True, extra_r=["PH"]))
            run_pipelined(kunits)
            vp, vpk = WS.next(("in", l, P_V))
            for g4 in range(5):
                bank, bkey = next_pj()
                tl = list(range(g4 * 4, min(18, g4 * 4 + 4)))
                for qi, ti in enumerate(tl):
                    for k in range(8):
                        MM(bank[:, qi * 128:(qi + 1) * 128], lhsT=hT[:, k, ti * 128:(ti + 1) * 128], rhs=vp[:, k, :],
                           start=(k == 0), stop=(k == 7), r=[vpk, ("hT", min(ti // 4, 4))], w=[bkey])
                n = len(tl)
                CP("act", vaugA[:, tl[0]:tl[0] + n, :, 0, :],
                   bank[:, 0:n * 128].rearrange("p (t k d) -> p t k d", t=n, k=2),
                   r=[bkey, "PH"], w=["vaug"])

            def q_units(c, qp, qpk, micro=False):
                return [nr_stages(qp, qpk, tc, gains[:, l, 0:1], mix[:, c, TCH[tc][0]:TCH[tc][0] + TCH[tc][1]],
                                  ("mix", c, tc), True, micro=micro) for tc in range(5)]

            qp, qpk = WS.next(("in", l, P_Q + 0))
            run_pipelined(q_units(0, qp, qpk)[:(4 if last else 5)])
            pj_list[0] = [6]
            ms_list[0] = [7]
            for c in range(4):
                kv = c // 2
                gp, gpk = WS.next(("in", l, P_G + c))
                side = []
                if c < 3:
                    qpn, qpnk = WS.next(("in", l, P_Q + c + 1))
                    side = q_units(c + 1, qpn, qpnk, micro=True)
                for tc in range(4 if last else 5):
                    t0, wd = TCH[tc]
                    tiles = list(range(18)) if tc < 4 else [16, 17]
                    nst = len(tiles)
                    sched = {}
                    gh = {}
                    gst = gate_stages(gp, gpk, tc, gh, micro=True)
                    if tc < 4:
                        if side and tc < 3:
                            place(sched, 0, side[tc])
                            place(sched, 9, gst[0:8], per_step=2)
                            place(sched, 14, gst[8:])
                        elif side:
                            place(sched, 0, side[3])
                            if not last:
                                place(sched, 9, side[4][0:8], per_step=2)
                                place(sched, 13, side[4][8:9])
                                place(sched, 16, side[4][9:])
                            place(sched, 14, gst[0:8], per_step=2)
                            place(sched, 18, gst[8:])
                        else:
                            place(sched, 4, gst[0:8])
                            place(sched, 13, gst[8:])
                    else:
                        place(sched, 0, gst[0:8], per_step=8)
                        place(sched, 1, gst[8:])
                    LAG = 2
                    pq = []
                    for s_ in range(nst + LAG):
                        cur = None
                        if s_ < nst:
                            t = tiles[s_]
                            scd, sk = SCD[s_ % 2], SCDK[s_ % 2]
                            for hh in range(2):
                                MM(scd[:, 512 * hh:512 * hh + wd], lhsT=kz[kv][hh][:, t * 128:(t + 1) * 128],
                                   rhs=mix[:, c, t0:t0 + wd],
                                   r=[("kdup", kv, min(t // 4, 4)), ("mix", c, tc), "PH"], w=[sk[hh]])
                            ptd, ptk = PT.get()
                            if wd == 512:
                                ACT(ptd[:, :], scd[:, :], AF.Exp, r=sk, w=[ptk], scale=0.125)
                            else:
                                ACT(ptd[:, :].rearrange("p (h w) -> p h w", h=2)[:, :, 0:wd],
                                    scd.rearrange("p (h w) -> p h w", h=2)[:, :, 0:wd], AF.Exp, r=sk, w=[ptk], scale=0.125)
                            cur = (t, ptd, ptk)
                        pq.append(cur)
                        if len(pq) > LAG and pq[0] is not None:
                            t_, ptd_, ptk_ = pq[0]
                            for hh in range(2):
                                MM(OA[hh][:, :wd], lhsT=vaugA[:, t_, kv, :, :].rearrange("p a d -> p (a d)"),
                                   rhs=ptd_[:, 512 * hh:512 * hh + wd], start=(t_ == tiles[0]), stop=(t_ == tiles[-1]),
                                   r=[ptk_, "vaug", "PH"], w=[OAK[hh]])
                        if len(pq) > LAG:
                            pq.pop(0)
                        for f in sched.get(s_, ()):
                            f()
                    for s_x in sorted(k_ for k_ in sched if k_ >= nst + LAG):
                        for f in sched[s_x]:
                            f()
                    finalize_norm(tc, c, gh["g"], gh["gk"])
            pj_list[0] = [6, 0]
            ms_list[0] = [7, 1]
            y_update(l, b, [0, 1, 2, 3], "outA", last)

        def phase_B(l, b, last):
            fence()
            DMA(poolw32[:], poolw_d[l], w=["poolw32"], grp="pw")
            CP("pool", poolwb[:], poolw32[:], r=["poolw32"], w=["poolwb"])
            for ap in (zbuf[:, 0:16], zbuf[:, 2064:2080], zbuf[:, 2336:2352]):
                MSET("pool", ap, 0.0, r=["PH"], w=["zbuf"])
            segs = [(16, 0, SEQ), (2080, SEQ, NCTX)]
            for cz in range(2):
                zp, zpk = WS.next(("in", l, P_Z + cz))
                for tc in range(5):
                    t0, wd = TCH[tc]
                    zo = 16 + t0 if tc < 4 else 2080
                    bank, bkey = next_pj()
                    proj(zp, zpk, tc, bank, bkey)
                    CP("act", zbuf[:, zo:zo + wd], bank[:, :wd], r=[bkey, "PH"], w=["zbuf"])
                N = ZN
                TT("pool", tA[:, 1:N], zbuf[:, 1:N], zbuf[:, 0:N - 1], ALU.add, r=["zbuf", "PH"], w=["tA"])
                TT("pool", tB[:, 2:N - 1], tA[:, 3:N], tA[:, 1:N - 2], ALU.add, r=["tA", "PH"], w=["tB"])
                if cz == 1:
                    TT("pool", tA[:, 4:N - 3], tB[:, 6:N - 1], tB[:, 2:N - 5], ALU.add, r=["tB", "PH"], w=["tA"])
                    TT("pool", tB[:, 8:N - 7], tA[:, 12:N - 3], tA[:, 4:N - 11], ALU.add, r=["tA", "PH"], w=["tB"])
                srcs = [(tA, "tA", 0), (tB, "tB", 64)]
                for (src, skey, p0) in srcs:
                    pr = slice(p0, p0 + 64)
                    for (zo, to, ln_) in segs:
                        tcs = [0, 1, 2, 3] if to == 0 else [4]
                        wkeys = [("mix", cz, tc) for tc in tcs]
                        STT(mix[pr, cz, to:to + ln_], src[pr, zo:zo + ln_], pooltab[pr, cz, 0:1], zbuf[pr, zo:zo + ln_],
                            ALU.mult, ALU.subtract, r=[skey, "zbuf", "pooltab", "PH"], w=wkeys)
                        for (eo, tab0) in ((0, 1), (ln_ - 8, 9)):
                            et, etk = Wt.get()
                            TT("pool", et[pr, 0:8], src[pr, zo + eo:zo + eo + 8], pooltab[pr, cz, tab0:tab0 + 8], ALU.mult,
                               r=[skey, "pooltab", "PH"], w=[etk])
                            TT("pool", mix[pr, cz, to + eo:to + eo + 8], et[pr, 0:8], zbuf[pr, zo + eo:zo + eo + 8],
                               ALU.subtract, r=[etk, "zbuf", "PH"], w=[wkeys[0] if eo == 0 else wkeys[-1]])
                gp, gpk = WS.next(("in", l, P_BG + cz))
                for tc in range(5):
                    t0, wd = TCH[tc]
                    bank, bkey = next_pj()
                    MM(bank[:, :wd], lhsT=poolwb[:, cz, :], rhs=mix[:, cz, t0:t0 + wd], r=["poolwb", ("mix", cz, tc)], w=[bkey])
                    bank2, bkey2 = next_pj()
                    proj(gp, gpk, tc, bank2, bkey2)
                    g, gk = silu_gate(bank2, bkey2, wd)
                    STT(mix[:, cz, t0:t0 + wd], bank[:, :wd], pscale[:, l, cz:cz + 1], g[:, :wd], ALU.mult, ALU.mult,
                        r=[bkey, gk, "pscale"], w=[("mix", cz, tc)])

        def phase_C(l, b, last):
            for c2 in range(2):
                mc = 2 + c2
                fence()
                MSET("pool", nvaug[:, :, :, 1, :], 1.0, r=["PH"], w=["nvaug"])
                for tc_ in range(5):
                    t0_, wd_ = TCH[tc_]
                    MSET("pool", nkz[0][64:128, t0_:t0_ + wd_], 0.0, r=["PH"], w=[("nkT", tc_)])
                    MSET("pool", nkz[1][0:64, t0_:t0_ + wd_], 0.0, r=["PH"], w=[("nkT", tc_)])
                DMA(ebuf[:], ebias_d[l, c2], r=["PH"], w=["ebuf"], grp="eb")
                TS("dve", ebuf[:], ebuf[:], 8.0, ALU.mult, r=["ebuf", "PH"], w=["ebuf"])
                kp, kpk = WS.next(("in", l, P_NK + c2))
                qp, qpk = WS.next(("in", l, P_NQ + c2))
                units = []
                for tc in range(5):
                    t0, wd = TCH[tc]
                    units.append(nr_stages(kp, kpk, tc, gains[:, l, 3:4],
                                           [(nkz[0][0:64, t0:t0 + wd], slice(0, 64)), (nkz[1][64:128, t0:t0 + wd], slice(64, 128))],
                                           ("nkT", tc), False, extra_r=["PH"]))
                for tc in range(4 if last else 5):
                    t0, wd = TCH[tc]
                    units.append(nr_stages(qp, qpk, tc, gains[:, l, 2:3], mix[:, mc, t0:t0 + wd], ("mix", mc, tc), False))
                run_pipelined(units)
                vp, vpk = WS.next(("in", l, P_NV + c2))
                toks = [ti * 128 for ti in range(18)] + [64 + 128 * o for o in range(15)]
                for g4 in range(9):
                    tl = list(range(g4 * 4, min(33, g4 * 4 + 4)))
                    bank, bkey = next_pj()
                    for qi, ti in enumerate(tl):
                        tk0 = toks[ti]
                        rk = sorted(set([("hT", min(tk0 // 512, 4)), ("hT", min((tk0 + 127) // 512, 4))]))
                        for k in range(8):
                            MM(bank[:, qi * 128:(qi + 1) * 128], lhsT=hT[:, k, tk0:tk0 + 128], rhs=vp[:, k, :],
                               start=(k == 0), stop=(k == 7), r=[vpk] + rk, w=[bkey])
                    n = len(tl)
                    CP("act", nvaug[:, tl[0]:tl[0] + n, :, 0, :],
                       bank[:, 0:n * 128].rearrange("p (t k d) -> p t k d", t=n, k=2),
                       r=[bkey, "PH"], w=["nvaug"])
                gp, gpk = WS.next(("in", l, P_NG + c2))
                ev = ebuf.rearrange("p h (q two) c -> p h q two c", two=2)
                pj_list[0] = [6]
                ms_list[0] = [7]
                LA = 3
                ptc = [0]
                pt_all = [("PT", j_) for j_ in range(3)] + [("PT", j_, h_) for j_ in range(3) for h_ in range(2)]
                MSET("pool", dummy[:, :], 0.0, w=pt_all + ["dummy"])
                for tc in range(4):
                    items = [(hh, rr_) for rr_ in range(8) for hh in (0, 1)]
                    pendq = []
                    for s in range(len(items) + LA):
                        cur = None
                        if s < len(items):
                            hh, rr_ = items[s]
                            r = tc * 8 + rr_
                            r0 = min(max(r - 4, 0), 24)
                            kb = 64 * r0
                            pr = slice(64 * hh, 64 * hh + 64)
                            sc, sck = BK[s % 4], ("PS", s % 4)
                            qap = mix[:, mc, 64 * r:64 * r + 64]
                            rkeys = sorted(set(("nkT", min((kb + 128 * i) // 512, 3)) for i in range(4)) |
                                           set(("nkT", min((kb + 128 * i + 127) // 512, 3)) for i in range(4)))
                            for i in range(4):
                                MM(sc[:, 64 * i:64 * i + 64], lhsT=nkz[hh][:, kb + 128 * i:kb + 128 * i + 128], rhs=qap,
                                   r=list(rkeys) + [("mix", mc, tc), "PH"], w=[sck])
                            for i2 in range(2):
                                MM(sc[:, 256 + 64 * i2:256 + 64 * i2 + 64],
                                   lhsT=nkz[hh][:, SEQ + 128 * i2:SEQ + 128 * i2 + 128], rhs=qap,
                                   r=[("nkT", 4), ("mix", mc, tc), "PH"], w=[sck])
                            s0 = r0 - r + 7
                            esl = ev[:, hh, s0 // 2:s0 // 2 + 4, s0 % 2, :]
                            sc3 = sc[:, 0:256].rearrange("p (i c) -> p i c", c=64)
                            TT("dve", sc3, sc3, esl, ALU.add, r=[sck, "ebuf", "PH"], w=[sck])
                            pj_ = ptc[0] % 6
                            ptc[0] += 1
                            pt, ptk = PT.tiles[pj_ // 2][:, 512 * (pj_ % 2):512 * (pj_ % 2) + 512], ("PT", pj_ // 2, pj_ % 2)
                            ACT(pt[:, 0:384], sc[:, 0:384], AF.Exp, r=[sck], w=[ptk], scale=0.125)
                            if r0 % 2 == 0:
                                vts = [r0 // 2 + i for i in range(4)]
                            else:
                                vts = [18 + (r0 - 1) // 2 + i for i in range(4)]
                            vts += [16, 17]
                            cur = (hh, rr_, pt, ptk, vts)
                        pendq.append(cur)
                        if len(pendq) > LA and pendq[0] is not None:
                            hh_, rr2, pt_, ptk_, vts_ = pendq[0]
                            for idx, vt in enumerate(vts_):
                                MM(OA[hh_][:, 64 * rr2:64 * rr2 + 64],
                                   lhsT=nvaug[:, vt, hh_, :, :].rearrange("p a d -> p (a d)"),
                                   rhs=pt_[:, 64 * idx:64 * idx + 64], start=(idx == 0), stop=(idx == 5),
                                   r=[ptk_, "nvaug", "PH"], w=[OAK[hh_]])
                        if len(pendq) > LA:
                            pendq.pop(0)
                    finalize_attn(gp, gpk, tc, mc)
                pj_list[0] = [6, 0]
                ms_list[0] = [7, 1]
                for hh in (range(2) if not last else ()):
                    pr = slice(64 * hh, 64 * hh + 64)
                    sc, sck = SC[hh], SCK[hh]
                    for i2 in range(2):
                        MM(sc[:, 256 * i2:256 * i2 + 256], lhsT=nkz[hh][:, SEQ + 128 * i2:SEQ + 128 * i2 + 128],
                           rhs=mix[:, mc, SEQ:SEQ + NCTX], r=[("nkT", 4), ("mix", mc, 4), "PH"], w=[sck])
                    pj_ = ptc[0] % 6
                    ptc[0] += 1
                    pt, ptk = PT.tiles[pj_ // 2][:, 512 * (pj_ % 2):512 * (pj_ % 2) + 512], ("PT", pj_ // 2, pj_ % 2)
                    ACT(pt[:, 0:512], sc[:, :], AF.Exp, r=[sck], w=[ptk], scale=0.125)
                    for i2 in range(2):
                        MM(OA[hh][:, 0:256], lhsT=nvaug[:, 16 + i2, hh, :, :].rearrange("p a d -> p (a d)"),
                           rhs=pt[:, 256 * i2:256 * i2 + 256], start=(i2 == 0), stop=(i2 == 1),
                           r=[ptk, "nvaug", "PH"], w=[OAK[hh]])
                if not last:
                    finalize_attn(gp, gpk, 4, mc)
                MSET("pool", dummy[:, :], 0.0, w=pt_all + ["dummy"])
            y_update(l, b, ([0, 1, 2, 3] if "B" in phases else [2, 3]), "outC", last)

        final = []
        for b in range(n_batch):
            fence()
            for i in range(18):
                src = x_d[b, i * 128:(i + 1) * 128, :] if i < 16 else ctx_d[b, (i - 16) * 128:(i - 15) * 128, :]
                xs, xsk = xstage[i % NXS], ("xs", i % NXS)
                DMA(xs, src, r=["PH"], w=[xsk], grp="xs%d" % (i % NXS))
                tc = min(i // 4, 4)
                for half in range(2):
                    bank, bkey = next_pj()
                    for kk in range(4):
                        k = half * 4 + kk
                        S.add("pe", (lambda o, i_: (lambda e: e.transpose(out=o, in_=i_, identity=ident[:])))(
                            bank[:, kk * 128:(kk + 1) * 128], xs[:, k * 128:(k + 1) * 128]),
                            [xsk, "ident", "PH"], [bkey])
                    CP("dve" if half == 0 else "act", xT[:, half * 4:half * 4 + 4, i * 128:(i + 1) * 128],
                       bank[:, :].rearrange("p (k t) -> p k t", t=128), r=[bkey], w=[("xT", tc)])
            for l in range(n_layers):
                last = (l == NL - 1)
                if stage >= 1:
                    xnorm(l, b)
                if "A" in phases and stage >= 2:
                    phase_A(l, b, last)
                if "B" in phases:
                    phase_B(l, b, last)
                if "C" in phases:
                    phase_C(l, b, last)
            fence()
            for i in range(16):
                xs, xsk = xstage[i % NXS], ("xs", i % NXS)
                tc = i // 4
                for half in range(2):
                    bank, bkey = next_pj()
                    for kk in range(4):
                        k = half * 4 + kk
                        S.add("pe", (lambda o, i_: (lambda e: e.transpose(out=o, in_=i_, identity=ident[:])))(
                            bank[:, kk * 128:(kk + 1) * 128], xT[:, k, i * 128:(i + 1) * 128]),
                            [("xT", tc), "ident"], [bkey])
                    CP("dve" if half == 0 else "act", xs[:, half * 512:(half + 1) * 512], bank[:, :],
                       r=[bkey, "PH"], w=[xsk])
                final.append(DMA(out_d[b, i * 128:(i + 1) * 128, :], xs, r=[xsk, "PH"], grp="o%d" % (i % NXS)))
        assert stage < 99 or WS.consumed == len(WS.specs), (WS.consumed, len(WS.specs))
        S.emit(nc, final_waits=final[-NXS:])
    return nc


def _f32(a):
    return np.ascontiguousarray(np.asarray(a, dtype=np.float32))


def prep_shared(c_ctx, norm_gain, w_mod, b_mod, w_in, att_q_gain, att_k_gain, pool_w, pool_scale,
                na_q_gain, na_k_gain, na_rpb, w_out):
    sh = {}
    w_mod = _f32(w_mod)
    sh["wmod"] = _f32(w_mod.reshape(NL, 8, 128, 6, 512).transpose(0, 3, 2, 1, 4))
    sh["bmod"] = _f32(_f32(b_mod).reshape(NL, 24, 128).transpose(2, 0, 1))
    sh["ngain"] = _f32(_f32(norm_gain).reshape(NL, 8, 128).transpose(2, 0, 1))
    w_in = _f32(w_in)
    cols = []
    cols.append(np.r_[512:576, 512:576])
    cols.append(np.r_[576:640, 576:640])
    cols.append(np.r_[640:768])
    for c in range(4):
        cols.append(np.r_[c * 128:(c + 1) * 128])
    for c in range(4):
        cols.append(np.r_[768 + c * 128:768 + (c + 1) * 128])
    for base in (1280, 1536, 2048, 2304, 1792, 2560):
        for c in range(2):
            cols.append(np.r_[base + c * 128:base + (c + 1) * 128])
    assert len(cols) == NPAN
    win = np.empty((NL, NPAN, 128, 8, 128), np.float32)
    for pi, cc in enumerate(cols):
        win[:, pi] = w_in[:, :, cc].reshape(NL, 8, 128, 128).transpose(0, 2, 1, 3)
    sh["win"] = win
    sh["wout"] = _f32(_f32(w_out).reshape(NL, 8, 128, 8, 128).transpose(0, 3, 2, 1, 4))
    g = np.stack([_f32(att_q_gain), _f32(att_k_gain), _f32(na_q_gain), _f32(na_k_gain)], axis=-1)
    sh["gains"] = _f32(np.tile(g, (1, 2, 1)).transpose(1, 0, 2))
    p = np.arange(128)
    d = p % 64
    inv_freq = (np.float32(10000.0) ** (-np.arange(16, dtype=np.float32) / np.float32(16))).astype(np.float32)
    f = inv_freq[d % 16]
    sign = np.where((d % 32) < 16, -1.0, 1.0).astype(np.float32)
    isrow = d < 32
    rows = np.arange(32, dtype=np.float32)
    colsv = np.arange(64, dtype=np.float32)
    angA = (rows[None, :] * f[:, None]).astype(np.float32)
    angB = (colsv[None, :] * f[:, None]).astype(np.float32)
    cosA = np.where(isrow[:, None], np.cos(angA), 1.0)
    cosB = np.where(isrow[:, None], 1.0, np.cos(angB))
    sinA = np.where(isrow[:, None], sign[:, None] * np.sin(angA), 1.0)
    sinB = np.where(isrow[:, None], 1.0, sign[:, None] * np.sin(angB))
    sh["ropetab"] = _f32(np.concatenate([cosA, cosB, sinA, sinB], axis=1))
    partner = np.where((p % 32) < 16, p + 16, p - 16)
    rm = np.zeros((128, 128), np.float32)
    rm[partner, p] = 1.0
    sh["rmat"] = rm
    sh["ident"] = np.eye(128, dtype=np.float32)
    pw = _f32(pool_w)
    poolw = np.zeros((NL, 128, 2, 128), np.float32)
    for cz in range(2):
        for j in range(2):
            poolw[:, j * 64:(j + 1) * 64, cz, j * 64:(j + 1) * 64] = pw[:, 2 * cz + j]
    sh["poolw"] = poolw
    sh["pscale"] = _f32(_f32(pool_scale).reshape(NL, 2, 128).transpose(2, 0, 1))
    pt = np.zeros((128, 2, 17), np.float32)
    for cz in range(2):
        for j in range(2):
            w = (2, 4, 8, 16)[2 * cz + j]
            t = np.arange(8)
            cs = np.minimum(w, t + w // 2).astype(np.float32)
            ce = np.minimum(w, (8 - t) + w // 2 - 1 + 0).astype(np.float32)
            ce = np.minimum(w, 8 - t + w // 2).astype(np.float32)
            pt[j * 64:(j + 1) * 64, cz, 0] = 1.0 / w
            pt[j * 64:(j + 1) * 64, cz, 1:9] = 1.0 / cs
            pt[j * 64:(j + 1) * 64, cz, 9:17] = 1.0 / ce
    sh["pooltab"] = pt
    rpb = _f32(na_rpb)
    kc = np.arange(64)[:, None]
    cq = np.arange(64)[None, :]
    c0 = np.clip(cq - 8, 0, 48)
    valid = (kc >= c0) & (kc < c0 + 16)
    dcol = np.clip(kc - cq + 15, 0, 30)
    Dh = np.where(valid[None, None, None], rpb[:, :, :, dcol], np.float32(NEG))
    eb = np.empty((NL, 2, 128, 2, 14, 64), np.float32)
    for c2 in range(2):
        for hh in range(2):
            h = 2 * c2 + hh
            for j in range(2):
                for di in range(14):
                    eb[:, c2, j * 64:(j + 1) * 64, hh, di, :] = Dh[:, h, di + j]
    sh["ebias"] = eb
    return sh


_NC_CACHE = {}


def kernel(x, c, ctx, c_ctx, norm_gain, w_mod, b_mod, w_in, att_q_gain, att_k_gain,
           pool_w, pool_scale, na_q_gain, na_k_gain, na_rpb, w_out):
    x = _f32(x)
    c = _f32(c)
    ctx = _f32(ctx)
    c_ctx = _f32(c_ctx)
    sh = prep_shared(c_ctx, norm_gain, w_mod, b_mod, w_in, att_q_gain, att_k_gain, pool_w, pool_scale,
                     na_q_gain, na_k_gain, na_rpb, w_out)
    if "nc" not in _NC_CACHE:
        _NC_CACHE["nc"] = build_nc()
    nc = _NC_CACHE["nc"]
    in_maps = []
    for i in range(8):
        m = dict(sh)
        m["x"] = np.ascontiguousarray(x[2 * i:2 * i + 2])
        m["ctx"] = np.ascontiguousarray(ctx[2 * i:2 * i + 2])
        vecs = np.stack([c[2 * i], c[2 * i + 1], c_ctx], axis=-1)
        m["cT"] = _f32(vecs.reshape(8, 128, 3).transpose(1, 0, 2))
        in_maps.append(m)
    res = run_bass_kernel_spmd(nc, in_maps, core_ids=list(range(8)))
    return np.concatenate([np.asarray(r["out"], dtype=np.float32) for r in res.results], axis=0)
```

```python
import contextlib
import numpy as np
import concourse.bass as bass
import concourse.mybir as mybir
from concourse.bass_utils import run_bass_kernel_spmd

F32 = mybir.dt.float32
BF16 = mybir.dt.bfloat16
ALU = mybir.AluOpType
AF = mybir.ActivationFunctionType

D = 1024
NL = 4
SEQ = 2048
NCTX = 256
T = SEQ + NCTX
EPS = 1e-6
NEG = -30000.0
TCH = [(0, 512), (512, 512), (1024, 512), (1536, 512), (2048, 256)]
ENGS = ("pe", "act", "dve", "pool", "sp")
PSUM_KEYS = ("PS",)

P_K0, P_K1, P_V = 0, 1, 2
P_Q, P_G, P_Z, P_BG, P_NK, P_NV, P_NQ, P_NG = 3, 7, 11, 13, 15, 17, 19, 21
NPAN = 23


class Op:
    __slots__ = ("eng", "idx", "fn", "waits", "signal", "dma_grp", "dma_cnt", "sigval")

    def __init__(self, eng, idx, fn):
        self.eng = eng
        self.idx = idx
        self.fn = fn
        self.waits = []
        self.signal = False
        self.dma_grp = None
        self.dma_cnt = 0
        self.sigval = 0


class Sched:
    def __init__(self):
        self.q = {e: [] for e in ENGS}
        self.last_w = {}
        self.readers = {}
        self.maxwait = {e: {} for e in ENGS}
        self.dma_cnt = {}

    def add(self, eng, fn, reads=(), writes=(), dma=None):
        op = Op(eng, len(self.q[eng]), fn)
        if dma is not None:
            op.dma_grp = dma
            self.dma_cnt[dma] = self.dma_cnt.get(dma, 0) + 1
            op.dma_cnt = self.dma_cnt[dma]
        best = {}
        for k in reads:
            w = self.last_w.get(k)
            if w is not None:
                self._cand(best, w, eng)
            if isinstance(k, tuple) and k[0] in PSUM_KEYS:
                for r in self.readers.get(k, ()):
                    if r.eng != eng:
                        self._cand(best, r, eng)
        for k in writes:
            w = self.last_w.get(k)
            if w is not None:
                self._cand(best, w, eng)
            for r in self.readers.get(k, ()):
                self._cand(best, r, eng)
        mw = self.maxwait[eng]
        for src, (pos, d) in best.items():
            if mw.get(src, -1) >= pos:
                continue
            mw[src] = pos
            d.signal = True
            op.waits.append(d)
        for k in writes:
            self.last_w[k] = op
            self.readers[k] = []
        for k in reads:
            self.readers.setdefault(k, []).append(op)
        self.q[eng].append(op)
        return op

    @staticmethod
    def _cand(best, d, eng):
        if d.dma_grp is not None:
            src = ("dma", d.dma_grp)
            pos = d.dma_cnt
        else:
            if d.eng == eng and eng == "pe":
                return
            src = d.eng
            pos = d.idx
        cur = best.get(src)
        if cur is None or cur[0] < pos:
            best[src] = (pos, d)

    def emit(self, nc, final_waits=()):
        for op in final_waits:
            op.signal = True
        for e in ENGS:
            n = 0
            for op in self.q[e]:
                if op.dma_grp is None and op.signal:
                    n += 1
                    op.sigval = n
        grps = sorted(self.dma_cnt.keys())
        with contextlib.ExitStack() as st:
            sems = {}
            for e in ENGS:
                sems[e] = st.enter_context(nc.semaphore("sem_" + e))
            for g in grps:
                sems[("dma", g)] = st.enter_context(nc.semaphore("semd_" + str(g)))
            block = st.enter_context(nc.Block())

            def tok(d):
                if d.dma_grp is not None:
                    return sems[("dma", d.dma_grp)], 16 * d.dma_cnt
                return sems[d.eng], d.sigval

            def run(engname, e):
                for op in self.q[engname]:
                    for d in op.waits:
                        s, v = tok(d)
                        e.wait_ge(s, v)
                    ins = op.fn(e)
                    if op.dma_grp is not None:
                        ins.then_inc(sems[("dma", op.dma_grp)], 16)
                    elif op.signal:
                        ins.then_inc(sems[engname], 1)
                if engname == "sp":
                    for d in final_waits:
                        s, v = tok(d)
                        e.wait_ge(s, v)

            @block.tensor
            def _(e):
                run("pe", e)

            @block.scalar
            def _(e):
                run("act", e)

            @block.vector
            def _(e):
                run("dve", e)

            @block.gpsimd
            def _(e):
                run("pool", e)

            @block.sync
            def _(e):
                run("sp", e)


class Ring:
    def __init__(self, name, tiles):
        self.name = name
        self.tiles = tiles
        self.i = 0

    def get(self):
        j = self.i % len(self.tiles)
        self.i += 1
        return self.tiles[j], (self.name, j)


def build_nc(n_layers=NL, n_batch=2, phases="ABC", stage=99):
    nc = bass.Bass("TRN2", target_bir_lowering=False)

    def din(name, shape):
        return nc.dram_tensor(name, list(shape), F32, kind="ExternalInput").ap()

    x_d = din("x", [2, SEQ, D])
    ctx_d = din("ctx", [2, NCTX, D])
    cT_d = din("cT", [128, 8, 3])
    wmod_d = din("wmod", [NL, 6, 128, 8, 512])
    bmod_d = din("bmod", [128, NL, 24])
    ngain_d = din("ngain", [128, NL, 8])
    win_d = din("win", [NL, NPAN, 128, 8, 128])
    wout_d = din("wout", [NL, 8, 128, 8, 128])
    gains_d = din("gains", [128, NL, 4])
    rope_d = din("ropetab", [128, 192])
    rmat_d = din("rmat", [128, 128])
    ident_d = din("ident", [128, 128])
    poolw_d = din("poolw", [NL, 128, 2, 128])
    pscale_d = din("pscale", [128, NL, 2])
    pooltab_d = din("pooltab", [128, 2, 17])
    ebias_d = din("ebias", [NL, 2, 128, 2, 14, 64])
    out_d = nc.dram_tensor("out", [2, SEQ, D], F32, kind="ExternalOutput").ap()

    S = Sched()
    with contextlib.ExitStack() as st:
        def sb(name, shape, dt=F32):
            return st.enter_context(nc.sbuf_tensor(name, list(shape), dt))

        xT = sb("xT", [128, 8, T])
        hT = sb("hT", [128, 8, T], BF16)
        mix = sb("mix", [128, 4, T], BF16)
        PH = sb("PH", [128, 8320])
        stg = [sb("stg%d" % i, [128, 8, 128]) for i in range(2)]
        wb = [sb("wb%d" % i, [128, 8, 128], BF16) for i in range(4)]
        Wt = Ring("W", [sb("wk%d" % i, [128, 512]) for i in range(8)])
        PT = Ring("PT", [sb("pt%d" % i, [128, 1024], BF16) for i in range(3)])
        ident = sb("ident_s", [128, 128])
        rmat = sb("rmat_s", [128, 128])
        ones128 = sb("ones128", [128, 128])
        bones = sb("bones", [128, 128])
        ropet = sb("ropet", [128, 192])
        gains = sb("gains_s", [128, NL, 4])
        ngain = sb("ngain_s", [128, NL, 8])
        bmod = sb("bmod_s", [128, NL, 24])
        pscale = sb("pscale_s", [128, NL, 2])
        pooltab = sb("pooltab_s", [128, 2, 17])
        cT = sb("cT_s", [128, 8, 3])
        scT = sb("scT", [128, 8, 3])
        mod_all = sb("mod_all", [128, NL, 24, 3])
        gs_all = sb("gs_all", [128, NL, 8, 3])
        poolw32 = sb("poolw32", [128, 2, 128])
        poolwb = sb("poolwb", [128, 2, 128], BF16)
        dummy = sb("fence_dummy", [128, 8])

        def ps(name):
            return st.enter_context(nc.psum_tensor(name, [128, 512], F32))

        PSALL = st.enter_context(nc.psum_tensor("PSALL", [128, 4096], F32))
        BK = [PSALL[:, 512 * i:512 * (i + 1)] for i in range(8)]
        SC = [BK[2], BK[3]]
        SCK = [("PS", 2), ("PS", 3)]
        OA = [BK[4], BK[5]]
        OAK = [("PS", 4), ("PS", 5)]
        SCD = [PSALL[:, 0:1024], PSALL[:, 1024:2048]]
        SCDK = [[("PS", 0), ("PS", 1)], [("PS", 2), ("PS", 3)]]
        pj_list = [[6, 0]]
        ms_list = [[7, 1]]

        PHb = PH[:, :].bitcast(BF16)
        kz = [[PHb[:, (2 * kv + hh) * T:(2 * kv + hh + 1) * T] for hh in range(2)] for kv in range(2)]
        vaugA = PHb[:, 4 * T:4 * T + 18 * 256].rearrange("p (t k a d) -> p t k a d", t=18, k=2, a=2)
        nkz = [PHb[:, hh * T:(hh + 1) * T] for hh in range(2)]
        nvaug = PHb[:, 2 * T:2 * T + 33 * 256].rearrange("p (t k a d) -> p t k a d", t=33, k=2, a=2)
        e_off = (2 * T + 33 * 256 + 1) // 2
        ebuf = PH[:, e_off:e_off + 2 * 14 * 64].rearrange("p (h d c) -> p h d c", h=2, d=14)
        assert e_off + 2 * 14 * 64 <= 8320
        ZN = 2352
        zbuf = PH[:, 0:ZN]
        tA = PH[:, ZN:2 * ZN]
        tB = PH[:, 2 * ZN:3 * ZN]
        assert 3 * ZN <= 8320
        NXS = 6
        xstage = [PH[:, i * 1024:(i + 1) * 1024] for i in range(NXS)]

        def MM(out, lhsT, rhs, start=True, stop=True, r=(), w=()):
            S.add("pe", lambda e: e.matmul(out, lhsT=lhsT, rhs=rhs, start=start, stop=stop), r, w)

        def ACT(out, in_, func, r=(), w=(), scale=None, bias=None):
            kw = {}
            if scale is not None:
                kw["scale"] = scale
            if bias is not None:
                kw["bias"] = bias
            S.add("act", lambda e: e.activation(out=out, in_=in_, func=func, **kw), r, w)

        def TT(eng, out, in0, in1, op, r=(), w=()):
            S.add(eng, lambda e: e.tensor_tensor(out=out, in0=in0, in1=in1, op=op), r, w)

        def TS(eng, out, in0, s1, op0, r=(), w=(), s2=None, op1=None):
            if op1 is None:
                S.add(eng, lambda e: e.tensor_scalar(out=out, in0=in0, scalar1=s1, scalar2=None, op0=op0), r, w)
            else:
                S.add(eng, lambda e: e.tensor_scalar(out=out, in0=in0, scalar1=s1, scalar2=s2, op0=op0, op1=op1), r, w)

        def STT(out, in0, scalar, in1, op0, op1, r=(), w=()):
            S.add("dve", lambda e: e.scalar_tensor_tensor(out=out, in0=in0, scalar=scalar, in1=in1, op0=op0, op1=op1), r, w)

        def CP(eng, out, in_, r=(), w=()):
            if eng == "act":
                S.add("act", lambda e: e.activation(out=out, in_=in_, func=AF.Copy), r, w)
            else:
                S.add(eng, lambda e: e.tensor_copy(out=out, in_=in_), r, w)

        def MSET(eng, ap, val, r=(), w=()):
            S.add(eng, lambda e: e.memset(ap, val), r, w)

        def DMA(out, in_, r=(), w=(), grp="m"):
            return S.add("sp", lambda e: e.dma_start(out=out, in_=in_), r, w, dma=grp)

        def fence():
            MSET("pool", dummy[:, :], 0.0, r=(), w=["PH", "dummy"])

        class WStream:
            def __init__(self):
                self.specs = []
                self.issued = 0
                self.consumed = 0

            def _issue(self):
                n = self.issued
                ap, nk, _tag = self.specs[n]
                si, wi = n % 2, n % 4
                DMA(stg[si][:, 0:nk, :], ap, w=[("stg", si)], grp="stg%d" % si)
                CP("pool", wb[wi][:, 0:nk, :], stg[si][:, 0:nk, :], r=[("stg", si)], w=[("wb", wi)])
                self.issued += 1

            def next(self, check=None):
                while self.issued < min(len(self.specs), self.consumed + 3 - 1):
                    self._issue()
                n = self.consumed
                if check is not None:
                    assert self.specs[n][2] == check, (self.specs[n][2], check)
                self.consumed += 1
                return wb[n % 4], ("wb", n % 4)

        WS = WStream()

        def add_spec(ap, nk, tag):
            WS.specs.append((ap, nk, tag))

        for b in range(n_batch):
            for l in range(n_layers):
                if "A" in phases:
                    for pid in (P_K0, P_K1, P_V):
                        add_spec(win_d[l, pid], 8, ("in", l, pid))
                    add_spec(win_d[l, P_Q], 8, ("in", l, P_Q))
                    for c in range(4):
                        add_spec(win_d[l, P_G + c], 8, ("in", l, P_G + c))
                        if c < 3:
                            add_spec(win_d[l, P_Q + c + 1], 8, ("in", l, P_Q + c + 1))
                    for m in range(8):
                        add_spec(wout_d[l, m, :, 0:4, :], 4, ("outA", l, m))
                if "B" in phases:
                    for cz in range(2):
                        add_spec(win_d[l, P_Z + cz], 8, ("in", l, P_Z + cz))
                        add_spec(win_d[l, P_BG + cz], 8, ("in", l, P_BG + cz))
                if "C" in phases:
                    for c2 in range(2):
                        for pid in (P_NK, P_NQ, P_NV, P_NG):
                            add_spec(win_d[l, pid + c2], 8, ("in", l, pid + c2))
                    for m in range(8):
                        if "B" in phases:
                            add_spec(wout_d[l, m, :, 4:8, :], 4, ("outC", l, m))
                        else:
                            add_spec(wout_d[l, m, :, 6:8, :], 2, ("outC", l, m))

        DMA(ident[:], ident_d[:], w=["ident"], grp="c_ident")
        DMA(rmat[:], rmat_d[:], w=["rmat"], grp="c_rmat")
        DMA(ropet[:], rope_d[:], w=["ropet"], grp="c_ropet")
        DMA(gains[:], gains_d[:], w=["gains"], grp="c_gains")
        DMA(ngain[:], ngain_d[:], w=["ngain"], grp="c_ngain")
        DMA(bmod[:], bmod_d[:], w=["bmod"], grp="c_bmod")
        DMA(pscale[:], pscale_d[:], w=["pscale"], grp="c_pscale")
        DMA(pooltab[:], pooltab_d[:], w=["pooltab"], grp="c_pooltab")
        DMA(cT[:], cT_d[:], w=["cT"], grp="c_cT")
        MSET("pool", ones128[:], 1.0, w=["ones"])
        MSET("pool", bones[:], 0.0, w=["bones"])
        MSET("pool", bones[0:64, 0:64], 1.0, w=["bones"])
        MSET("pool", bones[64:128, 64:128], 1.0, w=["bones"])
        ACT(scT[:], cT[:], AF.Exp, r=["cT"], w=["scT"], scale=-1.0)
        ACT(scT[:], scT[:], AF.Ln, r=["scT"], w=["scT"], bias=1.0)
        ACT(scT[:], scT[:], AF.Exp, r=["scT"], w=["scT"], scale=-1.0)
        TT("dve", scT[:], scT[:], cT[:], ALU.mult, r=["scT", "cT"], w=["scT"])
        for l in range(n_layers):
            rows = []
            for cc in range(6):
                i = l * 6 + cc
                si = i % 2
                wslot = PH[:, si * 4096:(si + 1) * 4096].rearrange("p (k c) -> p k c", k=8)
                DMA(wslot, wmod_d[l, cc], r=["PH"], w=[("wms", si)], grp="wms%d" % si)
                bank, bkey = BK[7 if cc % 2 == 0 else 6], ("PS", 7 if cc % 2 == 0 else 6)
                for k in range(8):
                    MM(bank[0:3, :], lhsT=scT[:, k, :], rhs=wslot[:, k, :],
                       start=(k == 0), stop=(k == 7), r=[("wms", si), "scT", "PH"], w=[bkey])
                rt, rtk = Wt.get()
                CP("dve", rt[0:3, :], bank[0:3, :], r=[bkey], w=[rtk])
                rows.append((rt, rtk))
            for m in range(24):
                rt, rtk = rows[m // 4]
                off = (m % 4) * 128
                S.add("pe", (lambda o, i_: (lambda e: e.transpose(out=o, in_=i_, identity=ident[0:3, 0:3])))(
                    BK[1][:, m * 4:m * 4 + 3], rt[0:3, off:off + 128]), [rtk, "ident"], [("PS", 1)])
            msv = BK[1][:, 0:96].rearrange("p (m f) -> p m f", f=4)
            for v in range(3):
                TT("dve", mod_all[:, l, :, v], msv[:, :, v], bmod[:, l, :], ALU.add,
                   r=[("PS", 1), "bmod"], w=["mod"])
            for v in range(3):
                STT(gs_all[:, l, :, v], mod_all[:, l, 8:16, v], 1.0, ngain[:, l, :], ALU.add, ALU.mult,
                    r=["mod", "ngain"], w=["mod"])

        pjc = [0]

        def next_pj():
            lst = pj_list[0]
            i = lst[pjc[0] % len(lst)]
            pjc[0] += 1
            return BK[i], ("PS", i)

        msc = [0]

        def next_ms():
            lst = ms_list[0]
            i = lst[msc[0] % len(lst)]
            msc[0] += 1
            return BK[i], ("PS", i)

        def proj(panel, pkey, tc, bank, bkey):
            t0, wd = TCH[tc]
            for k in range(8):
                MM(bank[:, :wd], lhsT=panel[:, k, :], rhs=hT[:, k, t0:t0 + wd],
                   start=(k == 0), stop=(k == 7), r=[pkey, ("hT", tc)], w=[bkey])

        def xnorm(l, b):
            state = {}

            def stage1(tc):
                t0, wd = TCH[tc]
                ms, msk = next_ms()
                for k in range(8):
                    sq, sqk = Wt.get()
                    if k in (0, 3, 6):
                        TT("pool", sq[:, :wd], xT[:, k, t0:t0 + wd], xT[:, k, t0:t0 + wd], ALU.mult,
                           r=[("xT", tc)], w=[sqk])
                    else:
                        ACT(sq[:, :wd], xT[:, k, t0:t0 + wd], AF.Square, r=[("xT", tc)], w=[sqk])
                    MM(ms[:, :wd], lhsT=ones128[:], rhs=sq[:, :wd], start=(k == 0), stop=(k == 7),
                       r=[sqk, "ones"], w=[msk])
                state[tc] = (ms, msk)

            def stage2(tc):
                t0, wd = TCH[tc]
                v = b if tc < 4 else 2
                ms, msk = state[tc]
                ln, lnk = Wt.get()
                ACT(ln[:, :wd], ms[:, :wd], AF.Ln, r=[msk], w=[lnk], scale=1.0 / D, bias=EPS)
                ACT(ms[:, :wd], ln[:, :wd], AF.Exp, r=[lnk], w=[msk], scale=-0.5)
                for k in range(8):
                    t, tk = Wt.get()
                    TT("dve", t[:, :wd], xT[:, k, t0:t0 + wd], ms[:, :wd], ALU.mult,
                       r=[("xT", tc), msk], w=[tk])
                    if k in (1, 4, 7):
                        TS("pool", hT[:, k, t0:t0 + wd], t[:, :wd], gs_all[:, l, k, v:v + 1], ALU.mult,
                           s2=mod_all[:, l, k, v:v + 1], op1=ALU.add, r=[tk, "mod"], w=[("hT", tc)])
                    else:
                        ACT(hT[:, k, t0:t0 + wd], t[:, :wd], AF.Identity, r=[tk, "mod"], w=[("hT", tc)],
                            scale=gs_all[:, l, k, v:v + 1], bias=mod_all[:, l, k, v:v + 1])

            for i in range(6):
                if i < 5:
                    stage1(i)
                if i >= 1:
                    stage2(i - 1)

        def nr_stages(panel, pkey, tc, gcol, dst, dkey, rope, extra_r=(), micro=False, five=False):
            t0, wd = TCH[tc]
            st_ = {}

            hw_ = wd // 2

            def s0():
                st_["bank"], st_["bkey"] = next_pj()
                proj(panel, pkey, tc, st_["bank"], st_["bkey"])

            def pk(k):
                def f():
                    if k == 0:
                        st_["bank"], st_["bkey"] = next_pj()
                    MM(st_["bank"][:, :wd], lhsT=panel[:, k, :], rhs=hT[:, k, t0:t0 + wd],
                       start=(k == 0), stop=(k == 7), r=[pkey, ("hT", tc)], w=[st_["bkey"]])
                return f

            def s1e():
                bank, bkey = st_["bank"], st_["bkey"]
                qg, qgk = Wt.get()
                sq, sqk = Wt.get()
                TS("dve", qg[:, :wd], bank[:, :wd], gcol, ALU.mult, r=[bkey, "gains"], w=[qgk])
                ACT(sq[:, :wd], bank[:, :wd], AF.Square, r=[bkey], w=[sqk])
                st_.update(qg=qg, qgk=qgk, sq=sq, sqk=sqk)

            def s1m():
                ms, msk = next_ms()
                MM(ms[:, :hw_], lhsT=bones[:], rhs=st_["sq"][:, :hw_], r=[st_["sqk"], "bones"], w=[msk])
                st_.update(ms=ms, msk=msk)

            def s1a():
                s1e()
                s1m()

            def s1b():
                MM(st_["ms"][:, hw_:wd], lhsT=bones[:], rhs=st_["sq"][:, hw_:wd], r=[st_["sqk"], "bones"], w=[st_["msk"]])

            def s1():
                s1a()
                s1b()

            def s2e():
                ms, msk = st_["ms"], st_["msk"]
                rs, rsk = st_["sq"], st_["sqk"]
                ACT(rs[:, :wd], ms[:, :wd], AF.Ln, r=[msk], w=[rsk], scale=1.0 / 64, bias=EPS)
                ACT(rs[:, :wd], rs[:, :wd], AF.Exp, r=[rsk], w=[rsk], scale=-0.5)
                st_.update(rs=rs, rsk=rsk)

            def s2m():
                if rope and tc < 4:
                    qg, qgk = st_["qg"], st_["qgk"]
                    ms2, ms2k = next_ms()
                    MM(ms2[:, :hw_], lhsT=rmat[:], rhs=qg[:, :hw_], r=[qgk, "rmat"], w=[ms2k])
                    st_.update(ms2=ms2, ms2k=ms2k)

            def s2a():
                s2e()
                s2m()

            def s2b():
                if rope and tc < 4:
                    MM(st_["ms2"][:, hw_:wd], lhsT=rmat[:], rhs=st_["qg"][:, hw_:wd], r=[st_["qgk"], "rmat"], w=[st_["ms2k"]])

            def s2():
                s2a()
                s2b()

            def s3():
                qg, qgk, rs, rsk = st_["qg"], st_["qgk"], st_["rs"], st_["rsk"]
                if rope and tc < 4:
                    ms2, ms2k = st_["ms2"], st_["ms2k"]
                    nr = wd // 64
                    r0 = t0 // 64
                    cosA = ropet[:, r0:r0 + nr].unsqueeze(2).broadcast_to([128, nr, 64])
                    cosB = ropet[:, 32:96].unsqueeze(1).broadcast_to([128, nr, 64])
                    sinA = ropet[:, 96 + r0:96 + r0 + nr].unsqueeze(2).broadcast_to([128, nr, 64])
                    sinB = ropet[:, 128:192].unsqueeze(1).broadcast_to([128, nr, 64])

                    def v3(ap):
                        return ap.rearrange("p (r c) -> p r c", c=64)
                    t1, t1k = Wt.get()
                    t2, t2k = Wt.get()
                    TT("pool", v3(t1[:, :wd]), v3(qg[:, :wd]), cosA, ALU.mult, r=[qgk, "ropet"], w=[t1k])
                    TT("pool", v3(t1[:, :wd]), v3(t1[:, :wd]), cosB, ALU.mult, r=[t1k, "ropet"], w=[t1k])
                    TT("dve", v3(t2[:, :wd]), v3(ms2[:, :wd]), sinA, ALU.mult, r=[ms2k, "ropet"], w=[t2k])
                    TT("dve", v3(t2[:, :wd]), v3(t2[:, :wd]), sinB, ALU.mult, r=[t2k, "ropet"], w=[t2k])
                    TT("pool", t1[:, :wd], t1[:, :wd], t2[:, :wd], ALU.add, r=[t1k, t2k], w=[t1k])
                    src, srck, eng = t1, t1k, "dve"
                else:
                    src, srck, eng = qg, qgk, "pool"
                dsts = dst if isinstance(dst, list) else [(dst, slice(0, 128))]
                for (dap, prr) in dsts:
                    TT(eng, dap, src[prr, :wd], rs[prr, :wd], ALU.mult, r=[srck, rsk] + list(extra_r), w=[dkey])

            if micro:
                if rope and tc < 4:
                    return [pk(k) for k in range(8)] + [s1e, s1m, s1b, s2e, s2m, s2b, None, s3]
                return [pk(k) for k in range(8)] + [s1e, s1m, s1b, s2e, None, s3]
            if five:
                def s1mb():
                    s1m()
                    s1b()
                return [s0, s1e, s1mb, s2, s3]
            return [s0, s1, s2, s3]

        def place(sched, start, ops, per_step=1):
            for i, f in enumerate(ops):
                if f is not None:
                    sched.setdefault(start + i // per_step, []).append(f)

        def run_all(stages):
            for f in stages:
                f()

        def run_pipelined(units):
            n = len(units)
            ns = len(units[0])
            for step in range(n + ns - 1):
                for si in range(ns - 1, -1, -1):
                    ui = step - si
                    if 0 <= ui < n:
                        units[ui][si]()

        def silu_gate(bank, bkey, wd):
            e1, e1k = Wt.get()
            ACT(e1[:, :wd], bank[:, :wd], AF.Exp, r=[bkey], w=[e1k], scale=-1.0)
            ACT(e1[:, :wd], e1[:, :wd], AF.Ln, r=[e1k], w=[e1k], bias=1.0)
            ACT(e1[:, :wd], e1[:, :wd], AF.Exp, r=[e1k], w=[e1k], scale=-1.0)
            g, gk = Wt.get()
            TT("dve", g[:, :wd], bank[:, :wd], e1[:, :wd], ALU.mult, r=[bkey, e1k], w=[gk])
            return g, gk

        def gate_stages(gp, gpk, tc, holder, micro=False):
            t0, wd = TCH[tc]

            def g0():
                holder["bank"], holder["bkey"] = next_pj()
                proj(gp, gpk, tc, holder["bank"], holder["bkey"])

            def gk_(k):
                def f():
                    if k == 0:
                        holder["bank"], holder["bkey"] = next_pj()
                    MM(holder["bank"][:, :wd], lhsT=gp[:, k, :], rhs=hT[:, k, t0:t0 + wd],
                       start=(k == 0), stop=(k == 7), r=[gpk, ("hT", tc)], w=[holder["bkey"]])
                return f

            def g1():
                holder["g"], holder["gk"] = silu_gate(holder["bank"], holder["bkey"], wd)

            if micro:
                return [gk_(k) for k in range(8)] + [g1]
            return [g0, g1]

        def finalize_norm(tc, mc, g, gk):
            t0, wd = TCH[tc]
            ao, aok = Wt.get()
            for hh in range(2):
                rr, rrk = Wt.get()
                ACT(rr[64:128, :wd], OA[hh][64:128, :wd], AF.Ln, r=[OAK[hh]], w=[rrk])
                ACT(rr[64:128, :wd], rr[64:128, :wd], AF.Exp, r=[rrk], w=[rrk], scale=-1.0)
                TT("dve", ao[64 * hh:64 * hh + 64, :wd], OA[hh][0:64, :wd], rr[64:128, :wd], ALU.mult,
                   r=[OAK[hh], rrk], w=[aok])
            TT("pool", mix[:, mc, t0:t0 + wd], ao[:, :wd], g[:, :wd], ALU.mult, r=[aok, gk], w=[("mix", mc, tc)])

        def finalize_attn(gp, gpk, tc, mc):
            h = {}
            run_all(gate_stages(gp, gpk, tc, h))
            finalize_norm(tc, mc, h["g"], h["gk"])

        def y_update(l, b, mcs, tag, last):
            for m in range(8):
                wp, wpk = WS.next((tag, l, m))
                for tc, (t0, wd) in enumerate(TCH):
                    if tc == 4 and last:
                        continue
                    bank, bkey = next_pj()
                    for i, mc in enumerate(mcs):
                        MM(bank[:, :wd], lhsT=wp[:, i, :], rhs=mix[:, mc, t0:t0 + wd],
                           start=(i == 0), stop=(i == len(mcs) - 1), r=[wpk, ("mix", mc, tc)], w=[bkey])
                    v = b if tc < 4 else 2
                    STT(xT[:, m, t0:t0 + wd], bank[:, :wd], mod_all[:, l, 16 + m, v:v + 1], xT[:, m, t0:t0 + wd],
                        ALU.mult, ALU.add, r=[bkey, "mod", ("xT", tc)], w=[("xT", tc)])

        def phase_A(l, b, last):
            fence()
            pj_list[0] = [6]
            ms_list[0] = [7]
            MSET("pool", vaugA[:, :, :, 1, :], 1.0, r=["PH"], w=["vaug"])
            for kv in range(2):
                for tc_ in range(5):
                    t0_, wd_ = TCH[tc_]
                    MSET("pool", kz[kv][0][64:128, t0_:t0_ + wd_], 0.0, r=["PH"], w=[("kdup", kv, tc_)])
                    MSET("pool", kz[kv][1][0:64, t0_:t0_ + wd_], 0.0, r=["PH"], w=[("kdup", kv, tc_)])
            pj_list[0] = [6, 0]
            ms_list[0] = [7, 1]
            kunits = []
            for kv in range(2):
                kp, kpk = WS.next(("in", l, P_K0 + kv))
                for tc in range(5):
                    t0, wd = TCH[tc]
                    kunits.append(nr_stages(kp, kpk, tc, gains[:, l, 1:2],
                                            [(kz[kv][0][0:64, t0:t0 + wd], slice(0, 64)),
                                             (kz[kv][1][64:128, t0:t0 + wd], slice(64, 128))],
                                            ("kdup", kv, tc), True, extra_r=["PH"]))
            run_pipelined(kunits)
            vp, vpk = WS.next(("in", l, P_V))
            for g4 in range(5):
                bank, bkey = next_pj()
                tl = list(range(g4 * 4, min(18, g4 * 4 + 4)))
                for qi, ti in enumerate(tl):
                    for k in range(8):
                        MM(bank[:, qi * 128:(qi + 1) * 128], lhsT=hT[:, k, ti * 128:(ti + 1) * 128], rhs=vp[:, k, :],
                           start=(k == 0), stop=(k == 7), r=[vpk, ("hT", min(ti // 4, 4))], w=[bkey])
                n = len(tl)
                CP("act", vaugA[:, tl[0]:tl[0] + n, :, 0, :],
                   bank[:, 0:n * 128].rearrange("p (t k d) -> p t k d", t=n, k=2),
                   r=[bkey, "PH"], w=["vaug"])

            def q_units(c, qp, qpk, micro=False):
                return [nr_stages(qp, qpk, tc, gains[:, l, 0:1], mix[:, c, TCH[tc][0]:TCH[tc][0] + TCH[tc][1]],
                                  ("mix", c, tc), True, micro=micro) for tc in range(5)]

            qp, qpk = WS.next(("in", l, P_Q + 0))
            run_pipelined(q_units(0, qp, qpk)[:(4 if last else 5)])
            pj_list[0] = [6]
            ms_list[0] = [7]
            for c in range(4):
                kv = c // 2
                gp, gpk = WS.next(("in", l, P_G + c))
                side = []
                if c < 3:
                    qpn, qpnk = WS.next(("in", l, P_Q + c + 1))
                    side = q_units(c + 1, qpn, qpnk, micro=True)
                for tc in range(4 if last else 5):
                    t0, wd = TCH[tc]
                    tiles = list(range(18)) if tc < 4 else [16, 17]
                    nst = len(tiles)
                    sched = {}
                    gh = {}
                    gst = gate_stages(gp, gpk, tc, gh, micro=True)
                    if tc < 4:
                        if side and tc < 3:
                            place(sched, 0, side[tc])
                            place(sched, 9, gst[0:8], per_step=2)
                            place(sched, 14, gst[8:])
                        elif side:
                            place(sched, 0, side[3])
                            if not last:
                                place(sched, 9, side[4][0:8], per_step=2)
                                place(sched, 13, side[4][8:9])
                                place(sched, 16, side[4][9:])
                            place(sched, 14, gst[0:8], per_step=2)
                            place(sched, 18, gst[8:])
                        else:
                            place(sched, 4, gst[0:8])
                            place(sched, 13, gst[8:])
                    else:
                        place(sched, 0, gst[0:8], per_step=8)
                        place(sched, 1, gst[8:])
                    LAG = 2
                    pq = []
                    for s_ in range(nst + LAG):
                        cur = None
                        if s_ < nst:
                            t = tiles[s_]
                            scd, sk = SCD[s_ % 2], SCDK[s_ % 2]
                            for hh in range(2):
                                MM(scd[:, 512 * hh:512 * hh + wd], lhsT=kz[kv][hh][:, t * 128:(t + 1) * 128],
                                   rhs=mix[:, c, t0:t0 + wd],
                                   r=[("kdup", kv, min(t // 4, 4)), ("mix", c, tc), "PH"], w=[sk[hh]])
                            ptd, ptk = PT.get()
                            if wd == 512:
                                ACT(ptd[:, :], scd[:, :], AF.Exp, r=sk, w=[ptk], scale=0.125)
                            else:
                                ACT(ptd[:, :].rearrange("p (h w) -> p h w", h=2)[:, :, 0:wd],
                                    scd.rearrange("p (h w) -> p h w", h=2)[:, :, 0:wd], AF.Exp, r=sk, w=[ptk], scale=0.125)
                            cur = (t, ptd, ptk)
                        pq.append(cur)
                        if len(pq) > LAG and pq[0] is not None:
                            t_, ptd_, ptk_ = pq[0]
                            for hh in range(2):
                                MM(OA[hh][:, :wd], lhsT=vaugA[:, t_, kv, :, :].rearrange("p a d -> p (a d)"),
                                   rhs=ptd_[:, 512 * hh:512 * hh + wd], start=(t_ == tiles[0]), stop=(t_ == tiles[-1]),
                                   r=[ptk_, "vaug", "PH"], w=[OAK[hh]])
                        if len(pq) > LAG:
                            pq.pop(0)
                        for f in sched.get(s_, ()):
                            f()
                    for s_x in sorted(k_ for k_ in sched if k_ >= nst + LAG):
                        for f in sched[s_x]:
                            f()
                    finalize_norm(tc, c, gh["g"], gh["gk"])
            pj_list[0] = [6, 0]
            ms_list[0] = [7, 1]
            y_update(l, b, [0, 1, 2, 3], "outA", last)

        def phase_B(l, b, last):
            fence()
            DMA(poolw32[:], poolw_d[l], w=["poolw32"], grp="pw")
            CP("pool", poolwb[:], poolw32[:], r=["poolw32"], w=["poolwb"])
            for ap in (zbuf[:, 0:16], zbuf[:, 2064:2080], zbuf[:, 2336:2352]):
                MSET("pool", ap, 0.0, r=["PH"], w=["zbuf"])
            segs = [(16, 0, SEQ), (2080, SEQ, NCTX)]
            for cz in range(2):
                zp, zpk = WS.next(("in", l, P_Z + cz))
                for tc in range(5):
                    t0, wd = TCH[tc]
                    zo = 16 + t0 if tc < 4 else 2080
                    bank, bkey = next_pj()
                    proj(zp, zpk, tc, bank, bkey)
                    CP("act", zbuf[:, zo:zo + wd], bank[:, :wd], r=[bkey, "PH"], w=["zbuf"])
                N = ZN
                TT("pool", tA[:, 1:N], zbuf[:, 1:N], zbuf[:, 0:N - 1], ALU.add, r=["zbuf", "PH"], w=["tA"])
                TT("pool", tB[:, 2:N - 1], tA[:, 3:N], tA[:, 1:N - 2], ALU.add, r=["tA", "PH"], w=["tB"])
                if cz == 1:
                    TT("pool", tA[:, 4:N - 3], tB[:, 6:N - 1], tB[:, 2:N - 5], ALU.add, r=["tB", "PH"], w=["tA"])
                    TT("pool", tB[:, 8:N - 7], tA[:, 12:N - 3], tA[:, 4:N - 11], ALU.add, r=["tA", "PH"], w=["tB"])
                srcs = [(tA, "tA", 0), (tB, "tB", 64)]
                for (src, skey, p0) in srcs:
                    pr = slice(p0, p0 + 64)
                    for (zo, to, ln_) in segs:
                        tcs = [0, 1, 2, 3] if to == 0 else [4]
                        wkeys = [("mix", cz, tc) for tc in tcs]
                        STT(mix[pr, cz, to:to + ln_], src[pr, zo:zo + ln_], pooltab[pr, cz, 0:1], zbuf[pr, zo:zo + ln_],
                            ALU.mult, ALU.subtract, r=[skey, "zbuf", "pooltab", "PH"], w=wkeys)
                        for (eo, tab0) in ((0, 1), (ln_ - 8, 9)):
                            et, etk = Wt.get()
                            TT("pool", et[pr, 0:8], src[pr, zo + eo:zo + eo + 8], pooltab[pr, cz, tab0:tab0 + 8], ALU.mult,
                               r=[skey, "pooltab", "PH"], w=[etk])
                            TT("pool", mix[pr, cz, to + eo:to + eo + 8], et[pr, 0:8], zbuf[pr, zo + eo:zo + eo + 8],
                               ALU.subtract, r=[etk, "zbuf", "PH"], w=[wkeys[0] if eo == 0 else wkeys[-1]])
                gp, gpk = WS.next(("in", l, P_BG + cz))
                for tc in range(5):
                    t0, wd = TCH[tc]
                    bank, bkey = next_pj()
                    MM(bank[:, :wd], lhsT=poolwb[:, cz, :], rhs=mix[:, cz, t0:t0 + wd], r=["poolwb", ("mix", cz, tc)], w=[bkey])
                    bank2, bkey2 = next_pj()
                    proj(gp, gpk, tc, bank2, bkey2)
                    g, gk = silu_gate(bank2, bkey2, wd)
                    STT(mix[:, cz, t0:t0 + wd], bank[:, :wd], pscale[:, l, cz:cz + 1], g[:, :wd], ALU.mult, ALU.mult,
                        r=[bkey, gk, "pscale"], w=[("mix", cz, tc)])

        def phase_C(l, b, last):
            for c2 in range(2):
                mc = 2 + c2
                fence()
                MSET("pool", nvaug[:, :, :, 1, :], 1.0, r=["PH"], w=["nvaug"])
                for tc_ in range(5):
                    t0_, wd_ = TCH[tc_]
                    MSET("pool", nkz[0][64:128, t0_:t0_ + wd_], 0.0, r=["PH"], w=[("nkT", tc_)])
                    MSET("pool", nkz[1][0:64, t0_:t0_ + wd_], 0.0, r=["PH"], w=[("nkT", tc_)])
                DMA(ebuf[:], ebias_d[l, c2], r=["PH"], w=["ebuf"], grp="eb")
                TS("dve", ebuf[:], ebuf[:], 8.0, ALU.mult, r=["ebuf", "PH"], w=["ebuf"])
                kp, kpk = WS.next(("in", l, P_NK + c2))
                qp, qpk = WS.next(("in", l, P_NQ + c2))
                units = []
                for tc in range(5):
                    t0, wd = TCH[tc]
                    units.append(nr_stages(kp, kpk, tc, gains[:, l, 3:4],
                                           [(nkz[0][0:64, t0:t0 + wd], slice(0, 64)), (nkz[1][64:128, t0:t0 + wd], slice(64, 128))],
                                           ("nkT", tc), False, extra_r=["PH"], five=True))
                for tc in range(4 if last else 5):
                    t0, wd = TCH[tc]
                    units.append(nr_stages(qp, qpk, tc, gains[:, l, 2:3], mix[:, mc, t0:t0 + wd], ("mix", mc, tc), False,
                                           five=True))
                run_pipelined(units)
                vp, vpk = WS.next(("in", l, P_NV + c2))
                toks = [ti * 128 for ti in range(18)] + [64 + 128 * o for o in range(15)]
                for g4 in range(9):
                    tl = list(range(g4 * 4, min(33, g4 * 4 + 4)))
                    bank, bkey = next_pj()
                    for qi, ti in enumerate(tl):
                        tk0 = toks[ti]
                        rk = sorted(set([("hT", min(tk0 // 512, 4)), ("hT", min((tk0 + 127) // 512, 4))]))
                        for k in range(8):
                            MM(bank[:, qi * 128:(qi + 1) * 128], lhsT=hT[:, k, tk0:tk0 + 128], rhs=vp[:, k, :],
                               start=(k == 0), stop=(k == 7), r=[vpk] + rk, w=[bkey])
                    n = len(tl)
                    CP("act", nvaug[:, tl[0]:tl[0] + n, :, 0, :],
                       bank[:, 0:n * 128].rearrange("p (t k d) -> p t k d", t=n, k=2),
                       r=[bkey, "PH"], w=["nvaug"])
                gp, gpk = WS.next(("in", l, P_NG + c2))
                ev = ebuf.rearrange("p h (q two) c -> p h q two c", two=2)
                pj_list[0] = [6]
                ms_list[0] = [7]
                LA = 3
                ptc = [0]
                pt_all = [("PT", j_) for j_ in range(3)] + [("PT", j_, h_) for j_ in range(3) for h_ in range(2)]
                MSET("pool", dummy[:, :], 0.0, w=pt_all + ["dummy"])
                for tc in range(4):
                    items = [(hh, rr_) for rr_ in range(8) for hh in (0, 1)]
                    pendq = []
                    for s in range(len(items) + LA):
                        cur = None
                        if s < len(items):
                            hh, rr_ = items[s]
                            r = tc * 8 + rr_
                            r0 = min(max(r - 4, 0), 24)
                            kb = 64 * r0
                            pr = slice(64 * hh, 64 * hh + 64)
                            sc, sck = BK[s % 4], ("PS", s % 4)
                            qap = mix[:, mc, 64 * r:64 * r + 64]
                            rkeys = sorted(set(("nkT", min((kb + 128 * i) // 512, 3)) for i in range(4)) |
                                           set(("nkT", min((kb + 128 * i + 127) // 512, 3)) for i in range(4)))
                            for i in range(4):
                                MM(sc[:, 64 * i:64 * i + 64], lhsT=nkz[hh][:, kb + 128 * i:kb + 128 * i + 128], rhs=qap,
                                   r=list(rkeys) + [("mix", mc, tc), "PH"], w=[sck])
                            for i2 in range(2):
                                MM(sc[:, 256 + 64 * i2:256 + 64 * i2 + 64],
                                   lhsT=nkz[hh][:, SEQ + 128 * i2:SEQ + 128 * i2 + 128], rhs=qap,
                                   r=[("nkT", 4), ("mix", mc, tc), "PH"], w=[sck])
                            s0 = r0 - r + 7
                            esl = ev[:, hh, s0 // 2:s0 // 2 + 4, s0 % 2, :]
                            sc3 = sc[:, 0:256].rearrange("p (i c) -> p i c", c=64)
                            TT("dve", sc3, sc3, esl, ALU.add, r=[sck, "ebuf", "PH"], w=[sck])
                            pj_ = ptc[0] % 6
                            ptc[0] += 1
                            pt, ptk = PT.tiles[pj_ // 2][:, 512 * (pj_ % 2):512 * (pj_ % 2) + 512], ("PT", pj_ // 2, pj_ % 2)
                            ACT(pt[:, 0:384], sc[:, 0:384], AF.Exp, r=[sck], w=[ptk], scale=0.125)
                            if r0 % 2 == 0:
                                vts = [r0 // 2 + i for i in range(4)]
                            else:
                                vts = [18 + (r0 - 1) // 2 + i for i in range(4)]
                            vts += [16, 17]
                            cur = (hh, rr_, pt, ptk, vts)
                        pendq.append(cur)
                        if len(pendq) > LA and pendq[0] is not None:
                            hh_, rr2, pt_, ptk_, vts_ = pendq[0]
                            for idx, vt in enumerate(vts_):
                                MM(OA[hh_][:, 64 * rr2:64 * rr2 + 64],
                                   lhsT=nvaug[:, vt, hh_, :, :].rearrange("p a d -> p (a d)"),
                                   rhs=pt_[:, 64 * idx:64 * idx + 64], start=(idx == 0), stop=(idx == 5),
                                   r=[ptk_, "nvaug", "PH"], w=[OAK[hh_]])
                        if len(pendq) > LA:
                            pendq.pop(0)
                    finalize_attn(gp, gpk, tc, mc)
                pj_list[0] = [6, 0]
                ms_list[0] = [7, 1]
                for hh in (range(2) if not last else ()):
                    pr = slice(64 * hh, 64 * hh + 64)
                    sc, sck = SC[hh], SCK[hh]
                    for i2 in range(2):
                        MM(sc[:, 256 * i2:256 * i2 + 256], lhsT=nkz[hh][:, SEQ + 128 * i2:SEQ + 128 * i2 + 128],
                           rhs=mix[:, mc, SEQ:SEQ + NCTX], r=[("nkT", 4), ("mix", mc, 4), "PH"], w=[sck])
                    pj_ = ptc[0] % 6
                    ptc[0] += 1
                    pt, ptk = PT.tiles[pj_ // 2][:, 512 * (pj_ % 2):512 * (pj_ % 2) + 512], ("PT", pj_ // 2, pj_ % 2)
                    ACT(pt[:, 0:512], sc[:, :], AF.Exp, r=[sck], w=[ptk], scale=0.125)
                    for i2 in range(2):
                        MM(OA[hh][:, 0:256], lhsT=nvaug[:, 16 + i2, hh, :, :].rearrange("p a d -> p (a d)"),
                           rhs=pt[:, 256 * i2:256 * i2 + 256], start=(i2 == 0), stop=(i2 == 1),
                           r=[ptk, "nvaug", "PH"], w=[OAK[hh]])
                if not last:
                    finalize_attn(gp, gpk, 4, mc)
                MSET("pool", dummy[:, :], 0.0, w=pt_all + ["dummy"])
            y_update(l, b, ([0, 1, 2, 3] if "B" in phases else [2, 3]), "outC", last)

        final = []
        for b in range(n_batch):
            fence()
            for i in range(18):
                src = x_d[b, i * 128:(i + 1) * 128, :] if i < 16 else ctx_d[b, (i - 16) * 128:(i - 15) * 128, :]
                xs, xsk = xstage[i % NXS], ("xs", i % NXS)
                DMA(xs, src, r=["PH"], w=[xsk], grp="xs%d" % (i % NXS))
                tc = min(i // 4, 4)
                for half in range(2):
                    bank, bkey = next_pj()
                    for kk in range(4):
                        k = half * 4 + kk
                        S.add("pe", (lambda o, i_: (lambda e: e.transpose(out=o, in_=i_, identity=ident[:])))(
                            bank[:, kk * 128:(kk + 1) * 128], xs[:, k * 128:(k + 1) * 128]),
                            [xsk, "ident", "PH"], [bkey])
                    CP("dve" if half == 0 else "act", xT[:, half * 4:half * 4 + 4, i * 128:(i + 1) * 128],
                       bank[:, :].rearrange("p (k t) -> p k t", t=128), r=[bkey], w=[("xT", tc)])
            for l in range(n_layers):
                last = (l == NL - 1)
                if stage >= 1:
                    xnorm(l, b)
                if "A" in phases and stage >= 2:
                    phase_A(l, b, last)
                if "B" in phases:
                    phase_B(l, b, last)
                if "C" in phases:
                    phase_C(l, b, last)
            fence()
            for i in range(16):
                xs, xsk = xstage[i % NXS], ("xs", i % NXS)
                tc = i // 4
                for half in range(2):
                    bank, bkey = next_pj()
                    for kk in range(4):
                        k = half * 4 + kk
                        S.add("pe", (lambda o, i_: (lambda e: e.transpose(out=o, in_=i_, identity=ident[:])))(
                            bank[:, kk * 128:(kk + 1) * 128], xT[:, k, i * 128:(i + 1) * 128]),
                            [("xT", tc), "ident"], [bkey])
                    CP("dve" if half == 0 else "act", xs[:, half * 512:(half + 1) * 512], bank[:, :],
                       r=[bkey, "PH"], w=[xsk])
                final.append(DMA(out_d[b, i * 128:(i + 1) * 128, :], xs, r=[xsk, "PH"], grp="o%d" % (i % NXS)))
        assert stage < 99 or WS.consumed == len(WS.specs), (WS.consumed, len(WS.specs))
        S.emit(nc, final_waits=final[-NXS:])
    return nc


def _f32(a):
    return np.ascontiguousarray(np.asarray(a, dtype=np.float32))


def prep_shared(c_ctx, norm_gain, w_mod, b_mod, w_in, att_q_gain, att_k_gain, pool_w, pool_scale,
                na_q_gain, na_k_gain, na_rpb, w_out):
    sh = {}
    w_mod = _f32(w_mod)
    sh["wmod"] = _f32(w_mod.reshape(NL, 8, 128, 6, 512).transpose(0, 3, 2, 1, 4))
    sh["bmod"] = _f32(_f32(b_mod).reshape(NL, 24, 128).transpose(2, 0, 1))
    sh["ngain"] = _f32(_f32(norm_gain).reshape(NL, 8, 128).transpose(2, 0, 1))
    w_in = _f32(w_in)
    cols = []
    cols.append(np.r_[512:576, 512:576])
    cols.append(np.r_[576:640, 576:640])
    cols.append(np.r_[640:768])
    for c in range(4):
        cols.append(np.r_[c * 128:(c + 1) * 128])
    for c in range(4):
        cols.append(np.r_[768 + c * 128:768 + (c + 1) * 128])
    for base in (1280, 1536, 2048, 2304, 1792, 2560):
        for c in range(2):
            cols.append(np.r_[base + c * 128:base + (c + 1) * 128])
    assert len(cols) == NPAN
    win = np.empty((NL, NPAN, 128, 8, 128), np.float32)
    for pi, cc in enumerate(cols):
        win[:, pi] = w_in[:, :, cc].reshape(NL, 8, 128, 128).transpose(0, 2, 1, 3)
    sh["win"] = win
    sh["wout"] = _f32(_f32(w_out).reshape(NL, 8, 128, 8, 128).transpose(0, 3, 2, 1, 4))
    g = np.stack([_f32(att_q_gain), _f32(att_k_gain), _f32(na_q_gain), _f32(na_k_gain)], axis=-1)
    sh["gains"] = _f32(np.tile(g, (1, 2, 1)).transpose(1, 0, 2))
    p = np.arange(128)
    d = p % 64
    inv_freq = (np.float32(10000.0) ** (-np.arange(16, dtype=np.float32) / np.float32(16))).astype(np.float32)
    f = inv_freq[d % 16]
    sign = np.where((d % 32) < 16, -1.0, 1.0).astype(np.float32)
    isrow = d < 32
    rows = np.arange(32, dtype=np.float32)
    colsv = np.arange(64, dtype=np.float32)
    angA = (rows[None, :] * f[:, None]).astype(np.float32)
    angB = (colsv[None, :] * f[:, None]).astype(np.float32)
    cosA = np.where(isrow[:, None], np.cos(angA), 1.0)
    cosB = np.where(isrow[:, None], 1.0, np.cos(angB))
    sinA = np.where(isrow[:, None], sign[:, None] * np.sin(angA), 1.0)
    sinB = np.where(isrow[:, None], 1.0, sign[:, None] * np.sin(angB))
    sh["ropetab"] = _f32(np.concatenate([cosA, cosB, sinA, sinB], axis=1))
    partner = np.where((p % 32) < 16, p + 16, p - 16)
    rm = np.zeros((128, 128), np.float32)
    rm[partner, p] = 1.0
    sh["rmat"] = rm
    sh["ident"] = np.eye(128, dtype=np.float32)
    pw = _f32(pool_w)
    poolw = np.zeros((NL, 128, 2, 128), np.float32)
    for cz in range(2):
        for j in range(2):
            poolw[:, j * 64:(j + 1) * 64, cz, j * 64:(j + 1) * 64] = pw[:, 2 * cz + j]
    sh["poolw"] = poolw
    sh["pscale"] = _f32(_f32(pool_scale).reshape(NL, 2, 128).transpose(2, 0, 1))
    pt = np.zeros((128, 2, 17), np.float32)
    for cz in range(2):
        for j in range(2):
            w = (2, 4, 8, 16)[2 * cz + j]
            t = np.arange(8)
            cs = np.minimum(w, t + w // 2).astype(np.float32)
            ce = np.minimum(w, (8 - t) + w // 2 - 1 + 0).astype(np.float32)
            ce = np.minimum(w, 8 - t + w // 2).astype(np.float32)
            pt[j * 64:(j + 1) * 64, cz, 0] = 1.0 / w
            pt[j * 64:(j + 1) * 64, cz, 1:9] = 1.0 / cs
            pt[j * 64:(j + 1) * 64, cz, 9:17] = 1.0 / ce
    sh["pooltab"] = pt
    rpb = _f32(na_rpb)
    kc = np.arange(64)[:, None]
    cq = np.arange(64)[None, :]
    c0 = np.clip(cq - 8, 0, 48)
    valid = (kc >= c0) & (kc < c0 + 16)
    dcol = np.clip(kc - cq + 15, 0, 30)
    Dh = np.where(valid[None, None, None], rpb[:, :, :, dcol], np.float32(NEG))
    eb = np.empty((NL, 2, 128, 2, 14, 64), np.float32)
    for c2 in range(2):
        for hh in range(2):
            h = 2 * c2 + hh
            for j in range(2):
                for di in range(14):
                    eb[:, c2, j * 64:(j + 1) * 64, hh, di, :] = Dh[:, h, di + j]
    sh["ebias"] = eb
    return sh


_NC_CACHE = {}


def kernel(x, c, ctx, c_ctx, norm_gain, w_mod, b_mod, w_in, att_q_gain, att_k_gain,
           pool_w, pool_scale, na_q_gain, na_k_gain, na_rpb, w_out):
    x = _f32(x)
    c = _f32(c)
    ctx = _f32(ctx)
    c_ctx = _f32(c_ctx)
    sh = prep_shared(c_ctx, norm_gain, w_mod, b_mod, w_in, att_q_gain, att_k_gain, pool_w, pool_scale,
                     na_q_gain, na_k_gain, na_rpb, w_out)
    if "nc" not in _NC_CACHE:
        _NC_CACHE["nc"] = build_nc()
    nc = _NC_CACHE["nc"]
    in_maps = []
    for i in range(8):
        m = dict(sh)
        m["x"] = np.ascontiguousarray(x[2 * i:2 * i + 2])
        m["ctx"] = np.ascontiguousarray(ctx[2 * i:2 * i + 2])
        vecs = np.stack([c[2 * i], c[2 * i + 1], c_ctx], axis=-1)
        m["cT"] = _f32(vecs.reshape(8, 128, 3).transpose(1, 0, 2))
        in_maps.append(m)
    res = run_bass_kernel_spmd(nc, in_maps, core_ids=list(range(8)))
    return np.concatenate([np.asarray(r["out"], dtype=np.float32) for r in res.results], axis=0)
```

```python
import contextlib
import numpy as np
import concourse.bass as bass
import concourse.mybir as mybir
from concourse.bass_utils import run_bass_kernel_spmd

F32 = mybir.dt.float32
BF16 = mybir.dt.bfloat16
ALU = mybir.AluOpType
AF = mybir.ActivationFunctionType

D = 1024
NL = 4
SEQ = 2048
NCTX = 256
T = SEQ + NCTX
EPS = 1e-6
NEG = -30000.0
TCH = [(0, 512), (512, 512), (1024, 512), (1536, 512), (2048, 256)]
ENGS = ("pe", "act", "dve", "pool", "sp")
PSUM_KEYS = ("PS",)

P_K0, P_K1, P_V = 0, 1, 2
P_Q, P_G, P_Z, P_BG, P_NK, P_NV, P_NQ, P_NG = 3, 7, 11, 13, 15, 17, 19, 21
NPAN = 23


class Op:
    __slots__ = ("eng", "idx", "fn", "waits", "signal", "dma_grp", "dma_cnt", "sigval")

    def __init__(self, eng, idx, fn):
        self.eng = eng
        self.idx = idx
        self.fn = fn
        self.waits = []
        self.signal = False
        self.dma_grp = None
        self.dma_cnt = 0
        self.sigval = 0


class Sched:
    def __init__(self):
        self.q = {e: [] for e in ENGS}
        self.last_w = {}
        self.readers = {}
        self.maxwait = {e: {} for e in ENGS}
        self.dma_cnt = {}

    def add(self, eng, fn, reads=(), writes=(), dma=None):
        op = Op(eng, len(self.q[eng]), fn)
        if dma is not None:
            op.dma_grp = dma
            self.dma_cnt[dma] = self.dma_cnt.get(dma, 0) + 1
            op.dma_cnt = self.dma_cnt[dma]
        best = {}
        for k in reads:
            w = self.last_w.get(k)
            if w is not None:
                self._cand(best, w, eng)
            if isinstance(k, tuple) and k[0] in PSUM_KEYS:
                for r in self.readers.get(k, ()):
                    if r.eng != eng:
                        self._cand(best, r, eng)
        for k in writes:
            w = self.last_w.get(k)
            if w is not None:
                self._cand(best, w, eng)
            for r in self.readers.get(k, ()):
                self._cand(best, r, eng)
        mw = self.maxwait[eng]
        for src, (pos, d) in best.items():
            if mw.get(src, -1) >= pos:
                continue
            mw[src] = pos
            d.signal = True
            op.waits.append(d)
        for k in writes:
            self.last_w[k] = op
            self.readers[k] = []
        for k in reads:
            self.readers.setdefault(k, []).append(op)
        self.q[eng].append(op)
        return op

    @staticmethod
    def _cand(best, d, eng):
        if d.dma_grp is not None:
            src = ("dma", d.dma_grp)
            pos = d.dma_cnt
        else:
            if d.eng == eng and eng == "pe":
                return
            src = d.eng
            pos = d.idx
        cur = best.get(src)
        if cur is None or cur[0] < pos:
            best[src] = (pos, d)

    def emit(self, nc, final_waits=()):
        for op in final_waits:
            op.signal = True
        for e in ENGS:
            n = 0
            for op in self.q[e]:
                if op.dma_grp is None and op.signal:
                    n += 1
                    op.sigval = n
        grps = sorted(self.dma_cnt.keys())
        with contextlib.ExitStack() as st:
            sems = {}
            for e in ENGS:
                sems[e] = st.enter_context(nc.semaphore("sem_" + e))
            for g in grps:
                sems[("dma", g)] = st.enter_context(nc.semaphore("semd_" + str(g)))
            block = st.enter_context(nc.Block())

            def tok(d):
                if d.dma_grp is not None:
                    return sems[("dma", d.dma_grp)], 16 * d.dma_cnt
                return sems[d.eng], d.sigval

            def run(engname, e):
                for op in self.q[engname]:
                    for d in op.waits:
                        s, v = tok(d)
                        e.wait_ge(s, v)
                    ins = op.fn(e)
                    if op.dma_grp is not None:
                        ins.then_inc(sems[("dma", op.dma_grp)], 16)
                    elif op.signal:
                        ins.then_inc(sems[engname], 1)
                if engname == "sp":
                    for d in final_waits:
                        s, v = tok(d)
                        e.wait_ge(s, v)

            @block.tensor
            def _(e):
                run("pe", e)

            @block.scalar
            def _(e):
                run("act", e)

            @block.vector
            def _(e):
                run("dve", e)

            @block.gpsimd
            def _(e):
                run("pool", e)

            @block.sync
            def _(e):
                run("sp", e)


class Ring:
    def __init__(self, name, tiles):
        self.name = name
        self.tiles = tiles
        self.i = 0

    def get(self):
        j = self.i % len(self.tiles)
        self.i += 1
        return self.tiles[j], (self.name, j)


def build_nc(n_layers=NL, n_batch=2, phases="ABC", stage=99):
    nc = bass.Bass("TRN2", target_bir_lowering=False)

    def din(name, shape):
        return nc.dram_tensor(name, list(shape), F32, kind="ExternalInput").ap()

    x_d = din("x", [2, SEQ, D])
    ctx_d = din("ctx", [2, NCTX, D])
    cT_d = din("cT", [128, 8, 3])
    wmod_d = din("wmod", [NL, 6, 128, 8, 512])
    bmod_d = din("bmod", [128, NL, 24])
    ngain_d = din("ngain", [128, NL, 8])
    win_d = din("win", [NL, NPAN, 128, 8, 128])
    wout_d = din("wout", [NL, 8, 128, 8, 128])
    gains_d = din("gains", [128, NL, 4])
    rope_d = din("ropetab", [128, 192])
    rmat_d = din("rmat", [128, 128])
    ident_d = din("ident", [128, 128])
    poolw_d = din("poolw", [NL, 128, 2, 128])
    pscale_d = din("pscale", [128, NL, 2])
    pooltab_d = din("pooltab", [128, 2, 17])
    ebias_d = din("ebias", [NL, 2, 128, 2, 14, 64])
    out_d = nc.dram_tensor("out", [2, SEQ, D], F32, kind="ExternalOutput").ap()

    S = Sched()
    with contextlib.ExitStack() as st:
        def sb(name, shape, dt=F32):
            return st.enter_context(nc.sbuf_tensor(name, list(shape), dt))

        xT = sb("xT", [128, 8, T])
        hT = sb("hT", [128, 8, T], BF16)
        mix = sb("mix", [128, 4, T], BF16)
        PH = sb("PH", [128, 8320])
        stg = [sb("stg%d" % i, [128, 8, 128]) for i in range(2)]
        wb = [sb("wb%d" % i, [128, 8, 128], BF16) for i in range(4)]
        Wt = Ring("W", [sb("wk%d" % i, [128, 512]) for i in range(9)])
        PT = Ring("PT", [sb("pt%d" % i, [128, 1024], BF16) for i in range(3)])
        ident = sb("ident_s", [128, 128])
        rmat = sb("rmat_s", [128, 128])
        ones128 = sb("ones128", [128, 128])
        bones = sb("bones", [128, 128])
        ropet = sb("ropet", [128, 192])
        gains = sb("gains_s", [128, NL, 4])
        ngain = sb("ngain_s", [128, NL, 8])
        bmod = sb("bmod_s", [128, NL, 24])
        pscale = sb("pscale_s", [128, NL, 2])
        pooltab = sb("pooltab_s", [128, 2, 17])
        cT = sb("cT_s", [128, 8, 3])
        scT = sb("scT", [128, 8, 3])
        mod_all = sb("mod_all", [128, NL, 24, 3])
        gs_all = sb("gs_all", [128, NL, 8, 3])
        poolwb = sb("poolwb", [128, 2, 128], BF16)
        dummy = sb("fence_dummy", [128, 8])

        def ps(name):
            return st.enter_context(nc.psum_tensor(name, [128, 512], F32))

        PSALL = st.enter_context(nc.psum_tensor("PSALL", [128, 4096], F32))
        BK = [PSALL[:, 512 * i:512 * (i + 1)] for i in range(8)]
        SC = [BK[2], BK[3]]
        SCK = [("PS", 2), ("PS", 3)]
        OA = [BK[4], BK[5]]
        OAK = [("PS", 4), ("PS", 5)]
        SCD = [PSALL[:, 0:1024], PSALL[:, 1024:2048]]
        SCDK = [[("PS", 0), ("PS", 1)], [("PS", 2), ("PS", 3)]]
        pj_list = [[6, 0]]
        ms_list = [[7, 1]]

        PHb = PH[:, :].bitcast(BF16)
        kz = [[PHb[:, (2 * kv + hh) * T:(2 * kv + hh + 1) * T] for hh in range(2)] for kv in range(2)]
        vaugA = PHb[:, 4 * T:4 * T + 18 * 256].rearrange("p (t k a d) -> p t k a d", t=18, k=2, a=2)
        nkz = [PHb[:, hh * T:(hh + 1) * T] for hh in range(2)]
        nvaug = PHb[:, 2 * T:2 * T + 33 * 256].rearrange("p (t k a d) -> p t k a d", t=33, k=2, a=2)
        e_off = (2 * T + 33 * 256 + 1) // 2
        ebuf = PH[:, e_off:e_off + 2 * 14 * 64].rearrange("p (h d c) -> p h d c", h=2, d=14)
        assert e_off + 2 * 14 * 64 <= 8320
        ZN = 2352
        zbuf = PH[:, 0:ZN]
        tA = PH[:, ZN:2 * ZN]
        tB = PH[:, 2 * ZN:3 * ZN]
        assert 3 * ZN <= 8320
        NXS = 6
        xstage = [PH[:, i * 1024:(i + 1) * 1024] for i in range(NXS)]

        def MM(out, lhsT, rhs, start=True, stop=True, r=(), w=()):
            S.add("pe", lambda e: e.matmul(out, lhsT=lhsT, rhs=rhs, start=start, stop=stop), r, w)

        def ACT(out, in_, func, r=(), w=(), scale=None, bias=None):
            kw = {}
            if scale is not None:
                kw["scale"] = scale
            if bias is not None:
                kw["bias"] = bias
            S.add("act", lambda e: e.activation(out=out, in_=in_, func=func, **kw), r, w)

        def TT(eng, out, in0, in1, op, r=(), w=()):
            S.add(eng, lambda e: e.tensor_tensor(out=out, in0=in0, in1=in1, op=op), r, w)

        def TS(eng, out, in0, s1, op0, r=(), w=(), s2=None, op1=None):
            if op1 is None:
                S.add(eng, lambda e: e.tensor_scalar(out=out, in0=in0, scalar1=s1, scalar2=None, op0=op0), r, w)
            else:
                S.add(eng, lambda e: e.tensor_scalar(out=out, in0=in0, scalar1=s1, scalar2=s2, op0=op0, op1=op1), r, w)

        def STT(out, in0, scalar, in1, op0, op1, r=(), w=()):
            S.add("dve", lambda e: e.scalar_tensor_tensor(out=out, in0=in0, scalar=scalar, in1=in1, op0=op0, op1=op1), r, w)

        def CP(eng, out, in_, r=(), w=()):
            if eng == "act":
                S.add("act", lambda e: e.activation(out=out, in_=in_, func=AF.Copy), r, w)
            else:
                S.add(eng, lambda e: e.tensor_copy(out=out, in_=in_), r, w)

        def MSET(eng, ap, val, r=(), w=()):
            S.add(eng, lambda e: e.memset(ap, val), r, w)

        def DMA(out, in_, r=(), w=(), grp="m"):
            return S.add("sp", lambda e: e.dma_start(out=out, in_=in_), r, w, dma=grp)

        def fence():
            MSET("pool", dummy[:, :], 0.0, r=(), w=["PH", "dummy"])

        class WStream:
            def __init__(self):
                self.specs = []
                self.issued = 0
                self.consumed = 0

            def _issue(self):
                n = self.issued
                ap, nk, _tag = self.specs[n]
                si, wi = n % 2, n % 4
                DMA(stg[si][:, 0:nk, :], ap, w=[("stg", si)], grp="stg%d" % si)
                CP("pool", wb[wi][:, 0:nk, :], stg[si][:, 0:nk, :], r=[("stg", si)], w=[("wb", wi)])
                self.issued += 1

            def next(self, check=None):
                while self.issued < min(len(self.specs), self.consumed + 3 - 1):
                    self._issue()
                n = self.consumed
                if check is not None:
                    assert self.specs[n][2] == check, (self.specs[n][2], check)
                self.consumed += 1
                return wb[n % 4], ("wb", n % 4)

        WS = WStream()

        def add_spec(ap, nk, tag):
            WS.specs.append((ap, nk, tag))

        for b in range(n_batch):
            for l in range(n_layers):
                if "A" in phases:
                    for pid in (P_K0, P_K1, P_V):
                        add_spec(win_d[l, pid], 8, ("in", l, pid))
                    add_spec(win_d[l, P_Q], 8, ("in", l, P_Q))
                    for c in range(4):
                        add_spec(win_d[l, P_G + c], 8, ("in", l, P_G + c))
                        if c < 3:
                            add_spec(win_d[l, P_Q + c + 1], 8, ("in", l, P_Q + c + 1))
                    for m in range(8):
                        add_spec(wout_d[l, m, :, 0:4, :], 4, ("outA", l, m))
                if "B" in phases:
                    for cz in range(2):
                        add_spec(win_d[l, P_Z + cz], 8, ("in", l, P_Z + cz))
                        add_spec(win_d[l, P_BG + cz], 8, ("in", l, P_BG + cz))
                if "C" in phases:
                    for c2 in range(2):
                        for pid in (P_NK, P_NQ, P_NV, P_NG):
                            add_spec(win_d[l, pid + c2], 8, ("in", l, pid + c2))
                    for m in range(8):
                        if "B" in phases:
                            add_spec(wout_d[l, m, :, 4:8, :], 4, ("outC", l, m))
                        else:
                            add_spec(wout_d[l, m, :, 6:8, :], 2, ("outC", l, m))

        DMA(ident[:], ident_d[:], w=["ident"], grp="c_ident")
        DMA(rmat[:], rmat_d[:], w=["rmat"], grp="c_rmat")
        DMA(ropet[:], rope_d[:], w=["ropet"], grp="c_ropet")
        DMA(gains[:], gains_d[:], w=["gains"], grp="c_gains")
        DMA(ngain[:], ngain_d[:], w=["ngain"], grp="c_ngain")
        DMA(bmod[:], bmod_d[:], w=["bmod"], grp="c_bmod")
        DMA(pscale[:], pscale_d[:], w=["pscale"], grp="c_pscale")
        DMA(pooltab[:], pooltab_d[:], w=["pooltab"], grp="c_pooltab")
        DMA(cT[:], cT_d[:], w=["cT"], grp="c_cT")
        MSET("pool", ones128[:], 1.0, w=["ones"])
        MSET("pool", bones[:], 0.0, w=["bones"])
        MSET("pool", bones[0:64, 0:64], 1.0, w=["bones"])
        MSET("pool", bones[64:128, 64:128], 1.0, w=["bones"])
        ACT(scT[:], cT[:], AF.Exp, r=["cT"], w=["scT"], scale=-1.0)
        ACT(scT[:], scT[:], AF.Ln, r=["scT"], w=["scT"], bias=1.0)
        ACT(scT[:], scT[:], AF.Exp, r=["scT"], w=["scT"], scale=-1.0)
        TT("dve", scT[:], scT[:], cT[:], ALU.mult, r=["scT", "cT"], w=["scT"])
        for l in range(n_layers):
            rows = []
            for cc in range(6):
                i = l * 6 + cc
                si = i % 2
                wslot = PH[:, si * 4096:(si + 1) * 4096].rearrange("p (k c) -> p k c", k=8)
                DMA(wslot, wmod_d[l, cc], r=["PH"], w=[("wms", si)], grp="wms%d" % si)
                bank, bkey = BK[7 if cc % 2 == 0 else 6], ("PS", 7 if cc % 2 == 0 else 6)
                for k in range(8):
                    MM(bank[0:3, :], lhsT=scT[:, k, :], rhs=wslot[:, k, :],
                       start=(k == 0), stop=(k == 7), r=[("wms", si), "scT", "PH"], w=[bkey])
                rt, rtk = Wt.get()
                CP("dve", rt[0:3, :], bank[0:3, :], r=[bkey], w=[rtk])
                rows.append((rt, rtk))
            for m in range(24):
                rt, rtk = rows[m // 4]
                off = (m % 4) * 128
                S.add("pe", (lambda o, i_: (lambda e: e.transpose(out=o, in_=i_, identity=ident[0:3, 0:3])))(
                    BK[1][:, m * 4:m * 4 + 3], rt[0:3, off:off + 128]), [rtk, "ident"], [("PS", 1)])
            msv = BK[1][:, 0:96].rearrange("p (m f) -> p m f", f=4)
            for v in range(3):
                TT("dve", mod_all[:, l, :, v], msv[:, :, v], bmod[:, l, :], ALU.add,
                   r=[("PS", 1), "bmod"], w=["mod"])
            for v in range(3):
                STT(gs_all[:, l, :, v], mod_all[:, l, 8:16, v], 1.0, ngain[:, l, :], ALU.add, ALU.mult,
                    r=["mod", "ngain"], w=["mod"])

        pjc = [0]

        def next_pj():
            lst = pj_list[0]
            i = lst[pjc[0] % len(lst)]
            pjc[0] += 1
            return BK[i], ("PS", i)

        msc = [0]

        def next_ms():
            lst = ms_list[0]
            i = lst[msc[0] % len(lst)]
            msc[0] += 1
            return BK[i], ("PS", i)

        def proj(panel, pkey, tc, bank, bkey):
            t0, wd = TCH[tc]
            for k in range(8):
                MM(bank[:, :wd], lhsT=panel[:, k, :], rhs=hT[:, k, t0:t0 + wd],
                   start=(k == 0), stop=(k == 7), r=[pkey, ("hT", tc)], w=[bkey])

        def xnorm(l, b):
            state = {}

            def stage1(tc):
                t0, wd = TCH[tc]
                ms, msk = next_ms()
                for k in range(8):
                    sq, sqk = Wt.get()
                    if k in (0, 3, 6):
                        TT("pool", sq[:, :wd], xT[:, k, t0:t0 + wd], xT[:, k, t0:t0 + wd], ALU.mult,
                           r=[("xT", tc)], w=[sqk])
                    else:
                        ACT(sq[:, :wd], xT[:, k, t0:t0 + wd], AF.Square, r=[("xT", tc)], w=[sqk])
                    MM(ms[:, :wd], lhsT=ones128[:], rhs=sq[:, :wd], start=(k == 0), stop=(k == 7),
                       r=[sqk, "ones"], w=[msk])
                state[tc] = (ms, msk)

            def stage2(tc):
                t0, wd = TCH[tc]
                v = b if tc < 4 else 2
                ms, msk = state[tc]
                ln, lnk = Wt.get()
                ACT(ln[:, :wd], ms[:, :wd], AF.Ln, r=[msk], w=[lnk], scale=1.0 / D, bias=EPS)
                ACT(ms[:, :wd], ln[:, :wd], AF.Exp, r=[lnk], w=[msk], scale=-0.5)
                for k in range(8):
                    t, tk = Wt.get()
                    TT("dve", t[:, :wd], xT[:, k, t0:t0 + wd], ms[:, :wd], ALU.mult,
                       r=[("xT", tc), msk], w=[tk])
                    if k in (1, 4, 7):
                        TS("pool", hT[:, k, t0:t0 + wd], t[:, :wd], gs_all[:, l, k, v:v + 1], ALU.mult,
                           s2=mod_all[:, l, k, v:v + 1], op1=ALU.add, r=[tk, "mod"], w=[("hT", tc)])
                    else:
                        ACT(hT[:, k, t0:t0 + wd], t[:, :wd], AF.Identity, r=[tk, "mod"], w=[("hT", tc)],
                            scale=gs_all[:, l, k, v:v + 1], bias=mod_all[:, l, k, v:v + 1])

            for i in range(6):
                if i < 5:
                    stage1(i)
                if i >= 1:
                    stage2(i - 1)

        def nr_stages(panel, pkey, tc, gcol, dst, dkey, rope, extra_r=(), micro=False, five=False):
            t0, wd = TCH[tc]
            st_ = {}

            hw_ = wd // 2

            def s0():
                st_["bank"], st_["bkey"] = next_pj()
                proj(panel, pkey, tc, st_["bank"], st_["bkey"])

            def pk(k):
                def f():
                    if k == 0:
                        st_["bank"], st_["bkey"] = next_pj()
                    MM(st_["bank"][:, :wd], lhsT=panel[:, k, :], rhs=hT[:, k, t0:t0 + wd],
                       start=(k == 0), stop=(k == 7), r=[pkey, ("hT", tc)], w=[st_["bkey"]])
                return f

            def s1e():
                bank, bkey = st_["bank"], st_["bkey"]
                qg, qgk = Wt.get()
                sq, sqk = Wt.get()
                TS("dve", qg[:, :wd], bank[:, :wd], gcol, ALU.mult, r=[bkey, "gains"], w=[qgk])
                ACT(sq[:, :wd], bank[:, :wd], AF.Square, r=[bkey], w=[sqk])
                st_.update(qg=qg, qgk=qgk, sq=sq, sqk=sqk)

            def s1m():
                ms, msk = next_ms()
                MM(ms[:, :hw_], lhsT=bones[:], rhs=st_["sq"][:, :hw_], r=[st_["sqk"], "bones"], w=[msk])
                st_.update(ms=ms, msk=msk)

            def s1a():
                s1e()
                s1m()

            def s1b():
                MM(st_["ms"][:, hw_:wd], lhsT=bones[:], rhs=st_["sq"][:, hw_:wd], r=[st_["sqk"], "bones"], w=[st_["msk"]])

            def s1():
                s1a()
                s1b()

            def s2e():
                ms, msk = st_["ms"], st_["msk"]
                rs, rsk = st_["sq"], st_["sqk"]
                ACT(rs[:, :wd], ms[:, :wd], AF.Ln, r=[msk], w=[rsk], scale=1.0 / 64, bias=EPS)
                ACT(rs[:, :wd], rs[:, :wd], AF.Exp, r=[rsk], w=[rsk], scale=-0.5)
                st_.update(rs=rs, rsk=rsk)

            def s2m():
                if rope and tc < 4:
                    qg, qgk = st_["qg"], st_["qgk"]
                    ms2, ms2k = next_ms()
                    MM(ms2[:, :hw_], lhsT=rmat[:], rhs=qg[:, :hw_], r=[qgk, "rmat"], w=[ms2k])
                    st_.update(ms2=ms2, ms2k=ms2k)

            def s2a():
                s2e()
                s2m()

            def s2b():
                if rope and tc < 4:
                    MM(st_["ms2"][:, hw_:wd], lhsT=rmat[:], rhs=st_["qg"][:, hw_:wd], r=[st_["qgk"], "rmat"], w=[st_["ms2k"]])

            def s2():
                s2a()
                s2b()

            def s3():
                qg, qgk, rs, rsk = st_["qg"], st_["qgk"], st_["rs"], st_["rsk"]
                if rope and tc < 4:
                    ms2, ms2k = st_["ms2"], st_["ms2k"]
                    nr = wd // 64
                    r0 = t0 // 64
                    cosA = ropet[:, r0:r0 + nr].unsqueeze(2).broadcast_to([128, nr, 64])
                    cosB = ropet[:, 32:96].unsqueeze(1).broadcast_to([128, nr, 64])
                    sinA = ropet[:, 96 + r0:96 + r0 + nr].unsqueeze(2).broadcast_to([128, nr, 64])
                    sinB = ropet[:, 128:192].unsqueeze(1).broadcast_to([128, nr, 64])

                    def v3(ap):
                        return ap.rearrange("p (r c) -> p r c", c=64)
                    t1, t1k = qg, qgk
                    t2, t2k = Wt.get()
                    TT("pool", v3(t1[:, :wd]), v3(qg[:, :wd]), cosA, ALU.mult, r=[qgk, "ropet"], w=[t1k])
                    TT("pool", v3(t1[:, :wd]), v3(t1[:, :wd]), cosB, ALU.mult, r=[t1k, "ropet"], w=[t1k])
                    TT("dve", v3(t2[:, :wd]), v3(ms2[:, :wd]), sinA, ALU.mult, r=[ms2k, "ropet"], w=[t2k])
                    TT("dve", v3(t2[:, :wd]), v3(t2[:, :wd]), sinB, ALU.mult, r=[t2k, "ropet"], w=[t2k])
                    TT("pool", t1[:, :wd], t1[:, :wd], t2[:, :wd], ALU.add, r=[t1k, t2k], w=[t1k])
                    src, srck, eng = t1, t1k, "dve"
                else:
                    src, srck, eng = qg, qgk, "pool"
                dsts = dst if isinstance(dst, list) else [(dst, slice(0, 128))]
                for (dap, prr) in dsts:
                    TT(eng, dap, src[prr, :wd], rs[prr, :wd], ALU.mult, r=[srck, rsk] + list(extra_r), w=[dkey])

            if micro:
                if rope and tc < 4:
                    return [pk(k) for k in range(8)] + [s1e, s1m, s1b, s2e, s2m, s2b, None, s3]
                return [pk(k) for k in range(8)] + [s1e, s1m, s1b, s2e, None, s3]
            if five:
                def s1mb():
                    s1m()
                    s1b()
                return [s0, s1e, s1mb, s2, s3]
            return [s0, s1, s2, s3]

        def place(sched, start, ops, per_step=1):
            for i, f in enumerate(ops):
                if f is not None:
                    sched.setdefault(start + i // per_step, []).append(f)

        def run_all(stages):
            for f in stages:
                f()

        def run_pipelined(units):
            n = len(units)
            ns = len(units[0])
            for step in range(n + ns - 1):
                for si in range(ns - 1, -1, -1):
                    ui = step - si
                    if 0 <= ui < n:
                        units[ui][si]()

        def silu_gate(bank, bkey, wd):
            e1, e1k = Wt.get()
            ACT(e1[:, :wd], bank[:, :wd], AF.Exp, r=[bkey], w=[e1k], scale=-1.0)
            ACT(e1[:, :wd], e1[:, :wd], AF.Ln, r=[e1k], w=[e1k], bias=1.0)
            ACT(e1[:, :wd], e1[:, :wd], AF.Exp, r=[e1k], w=[e1k], scale=-1.0)
            g, gk = Wt.get()
            TT("dve", g[:, :wd], bank[:, :wd], e1[:, :wd], ALU.mult, r=[bkey, e1k], w=[gk])
            return g, gk

        def gate_stages(gp, gpk, tc, holder, micro=False):
            t0, wd = TCH[tc]

            def g0():
                holder["bank"], holder["bkey"] = next_pj()
                proj(gp, gpk, tc, holder["bank"], holder["bkey"])

            def gk_(k):
                def f():
                    if k == 0:
                        holder["bank"], holder["bkey"] = next_pj()
                    MM(holder["bank"][:, :wd], lhsT=gp[:, k, :], rhs=hT[:, k, t0:t0 + wd],
                       start=(k == 0), stop=(k == 7), r=[gpk, ("hT", tc)], w=[holder["bkey"]])
                return f

            def g1():
                holder["g"], holder["gk"] = silu_gate(holder["bank"], holder["bkey"], wd)

            if micro:
                return [gk_(k) for k in range(8)] + [g1]
            return [g0, g1]

        def finalize_norm(tc, mc, g, gk):
            t0, wd = TCH[tc]
            ao, aok = Wt.get()
            for hh in range(2):
                rr, rrk = Wt.get()
                ACT(rr[64:128, :wd], OA[hh][64:128, :wd], AF.Ln, r=[OAK[hh]], w=[rrk])
                ACT(rr[64:128, :wd], rr[64:128, :wd], AF.Exp, r=[rrk], w=[rrk], scale=-1.0)
                TT("dve", ao[64 * hh:64 * hh + 64, :wd], OA[hh][0:64, :wd], rr[64:128, :wd], ALU.mult,
                   r=[OAK[hh], rrk], w=[aok])
            TT("pool", mix[:, mc, t0:t0 + wd], ao[:, :wd], g[:, :wd], ALU.mult, r=[aok, gk], w=[("mix", mc, tc)])

        def finalize_attn(gp, gpk, tc, mc):
            h = {}
            run_all(gate_stages(gp, gpk, tc, h))
            finalize_norm(tc, mc, h["g"], h["gk"])

        def y_update(l, b, mcs, tag, last):
            for m in range(8):
                wp, wpk = WS.next((tag, l, m))
                for tc, (t0, wd) in enumerate(TCH):
                    if tc == 4 and last:
                        continue
                    bank, bkey = next_pj()
                    for i, mc in enumerate(mcs):
                        MM(bank[:, :wd], lhsT=wp[:, i, :], rhs=mix[:, mc, t0:t0 + wd],
                           start=(i == 0), stop=(i == len(mcs) - 1), r=[wpk, ("mix", mc, tc)], w=[bkey])
                    v = b if tc < 4 else 2
                    STT(xT[:, m, t0:t0 + wd], bank[:, :wd], mod_all[:, l, 16 + m, v:v + 1], xT[:, m, t0:t0 + wd],
                        ALU.mult, ALU.add, r=[bkey, "mod", ("xT", tc)], w=[("xT", tc)])

        def phase_A_init():
            fence()
            MSET("dve", vaugA[:, :, :, 1, :], 1.0, r=["PH"], w=["vaug"])
            for kv in range(2):
                for tc_ in range(5):
                    t0_, wd_ = TCH[tc_]
                    MSET("dve", kz[kv][0][64:128, t0_:t0_ + wd_], 0.0, r=["PH"], w=[("kdup", kv, tc_)])
                    MSET("dve", kz[kv][1][0:64, t0_:t0_ + wd_], 0.0, r=["PH"], w=[("kdup", kv, tc_)])

        def phase_A(l, b, last):
            pj_list[0] = [6, 0]
            ms_list[0] = [7, 1]
            kunits = []
            for kv in range(2):
                kp, kpk = WS.next(("in", l, P_K0 + kv))
                for tc in range(5):
                    t0, wd = TCH[tc]
                    kunits.append(nr_stages(kp, kpk, tc, gains[:, l, 1:2],
                                            [(kz[kv][0][0:64, t0:t0 + wd], slice(0, 64)),
                                             (kz[kv][1][64:128, t0:t0 + wd], slice(64, 128))],
                                            ("kdup", kv, tc), True, extra_r=["PH"], five=True))
            run_pipelined(kunits)
            vp, vpk = WS.next(("in", l, P_V))
            for g4 in range(5):
                bank, bkey = next_pj()
                tl = list(range(g4 * 4, min(18, g4 * 4 + 4)))
                for qi, ti in enumerate(tl):
                    for k in range(8):
                        MM(bank[:, qi * 128:(qi + 1) * 128], lhsT=hT[:, k, ti * 128:(ti + 1) * 128], rhs=vp[:, k, :],
                           start=(k == 0), stop=(k == 7), r=[vpk, ("hT", min(ti // 4, 4))], w=[bkey])
                n = len(tl)
                CP("act", vaugA[:, tl[0]:tl[0] + n, :, 0, :],
                   bank[:, 0:n * 128].rearrange("p (t k d) -> p t k d", t=n, k=2),
                   r=[bkey, "PH"], w=["vaug"])

            def q_units(c, qp, qpk, micro=False):
                return [nr_stages(qp, qpk, tc, gains[:, l, 0:1], mix[:, c, TCH[tc][0]:TCH[tc][0] + TCH[tc][1]],
                                  ("mix", c, tc), True, micro=micro, five=not micro) for tc in range(5)]

            qp, qpk = WS.next(("in", l, P_Q + 0))
            run_pipelined(q_units(0, qp, qpk)[:(4 if last else 5)])
            pj_list[0] = [6]
            ms_list[0] = [7]
            for c in range(4):
                kv = c // 2
                gp, gpk = WS.next(("in", l, P_G + c))
                side = []
                if c < 3:
                    qpn, qpnk = WS.next(("in", l, P_Q + c + 1))
                    side = q_units(c + 1, qpn, qpnk, micro=True)
                for tc in range(4 if last else 5):
                    t0, wd = TCH[tc]
                    tiles = list(range(18)) if tc < 4 else [16, 17]
                    nst = len(tiles)
                    sched = {}
                    gh = {}
                    gst = gate_stages(gp, gpk, tc, gh, micro=True)
                    if tc < 4:
                        if side and tc < 3:
                            place(sched, 0, side[tc])
                            place(sched, 9, gst[0:8], per_step=2)
                            place(sched, 14, gst[8:])
                        elif side:
                            place(sched, 0, side[3])
                            if not last:
                                place(sched, 9, side[4][0:8], per_step=2)
                                place(sched, 13, side[4][8:9])
                                place(sched, 16, side[4][9:])
                            place(sched, 14, gst[0:8], per_step=2)
                            place(sched, 18, gst[8:])
                        else:
                            place(sched, 4, gst[0:8])
                            place(sched, 13, gst[8:])
                    else:
                        place(sched, 0, gst[0:8], per_step=8)
                        place(sched, 1, gst[8:])
                    LAG = 2
                    pq = []
                    for s_ in range(nst + LAG):
                        cur = None
                        if s_ < nst:
                            t = tiles[s_]
                            scd, sk = SCD[s_ % 2], SCDK[s_ % 2]
                            for hh in range(2):
                                MM(scd[:, 512 * hh:512 * hh + wd], lhsT=kz[kv][hh][:, t * 128:(t + 1) * 128],
                                   rhs=mix[:, c, t0:t0 + wd],
                                   r=[("kdup", kv, min(t // 4, 4)), ("mix", c, tc), "PH"], w=[sk[hh]])
                            ptd, ptk = PT.get()
                            if wd == 512:
                                ACT(ptd[:, :], scd[:, :], AF.Exp, r=sk, w=[ptk], scale=0.125)
                            else:
                                ACT(ptd[:, :].rearrange("p (h w) -> p h w", h=2)[:, :, 0:wd],
                                    scd.rearrange("p (h w) -> p h w", h=2)[:, :, 0:wd], AF.Exp, r=sk, w=[ptk], scale=0.125)
                            cur = (t, ptd, ptk)
                        pq.append(cur)
                        if len(pq) > LAG and pq[0] is not None:
                            t_, ptd_, ptk_ = pq[0]
                            for hh in range(2):
                                MM(OA[hh][:, :wd], lhsT=vaugA[:, t_, kv, :, :].rearrange("p a d -> p (a d)"),
                                   rhs=ptd_[:, 512 * hh:512 * hh + wd], start=(t_ == tiles[0]), stop=(t_ == tiles[-1]),
                                   r=[ptk_, "vaug", "PH"], w=[OAK[hh]])
                        if len(pq) > LAG:
                            pq.pop(0)
                        for f in sched.get(s_, ()):
                            f()
                    for s_x in sorted(k_ for k_ in sched if k_ >= nst + LAG):
                        for f in sched[s_x]:
                            f()
                    finalize_norm(tc, c, gh["g"], gh["gk"])
            pj_list[0] = [6, 0]
            ms_list[0] = [7, 1]
            y_update(l, b, [0, 1, 2, 3], "outA", last)

        def phase_B(l, b, last):
            fence()
            pw_t, pw_k = Wt.get()
            pw32 = pw_t[:, 0:256].rearrange("p (c d) -> p c d", c=2)
            DMA(pw32, poolw_d[l], w=[pw_k], grp="pw")
            CP("pool", poolwb[:], pw32, r=[pw_k], w=["poolwb"])
            for ap in (zbuf[:, 0:16], zbuf[:, 2064:2080], zbuf[:, 2336:2352]):
                MSET("pool", ap, 0.0, r=["PH"], w=["zbuf"])
            segs = [(16, 0, SEQ), (2080, SEQ, NCTX)]
            for cz in range(2):
                zp, zpk = WS.next(("in", l, P_Z + cz))
                for tc in range(5):
                    t0, wd = TCH[tc]
                    zo = 16 + t0 if tc < 4 else 2080
                    bank, bkey = next_pj()
                    proj(zp, zpk, tc, bank, bkey)
                    CP("act", zbuf[:, zo:zo + wd], bank[:, :wd], r=[bkey, "PH"], w=["zbuf"])
                N = ZN
                TT("pool", tA[:, 1:N], zbuf[:, 1:N], zbuf[:, 0:N - 1], ALU.add, r=["zbuf", "PH"], w=["tA"])
                TT("pool", tB[:, 2:N - 1], tA[:, 3:N], tA[:, 1:N - 2], ALU.add, r=["tA", "PH"], w=["tB"])
                if cz == 1:
                    TT("pool", tA[:, 4:N - 3], tB[:, 6:N - 1], tB[:, 2:N - 5], ALU.add, r=["tB", "PH"], w=["tA"])
                    TT("pool", tB[:, 8:N - 7], tA[:, 12:N - 3], tA[:, 4:N - 11], ALU.add, r=["tA", "PH"], w=["tB"])
                srcs = [(tA, "tA", 0), (tB, "tB", 64)]
                for (src, skey, p0) in srcs:
                    pr = slice(p0, p0 + 64)
                    for (zo, to, ln_) in segs:
                        tcs = [0, 1, 2, 3] if to == 0 else [4]
                        wkeys = [("mix", cz, tc) for tc in tcs]
                        STT(mix[pr, cz, to:to + ln_], src[pr, zo:zo + ln_], pooltab[pr, cz, 0:1], zbuf[pr, zo:zo + ln_],
                            ALU.mult, ALU.subtract, r=[skey, "zbuf", "pooltab", "PH"], w=wkeys)
                        for (eo, tab0) in ((0, 1), (ln_ - 8, 9)):
                            et, etk = Wt.get()
                            TT("pool", et[pr, 0:8], src[pr, zo + eo:zo + eo + 8], pooltab[pr, cz, tab0:tab0 + 8], ALU.mult,
                               r=[skey, "pooltab", "PH"], w=[etk])
                            TT("pool", mix[pr, cz, to + eo:to + eo + 8], et[pr, 0:8], zbuf[pr, zo + eo:zo + eo + 8],
                               ALU.subtract, r=[etk, "zbuf", "PH"], w=[wkeys[0] if eo == 0 else wkeys[-1]])
                gp, gpk = WS.next(("in", l, P_BG + cz))
                for tc in range(5):
                    t0, wd = TCH[tc]
                    bank, bkey = next_pj()
                    MM(bank[:, :wd], lhsT=poolwb[:, cz, :], rhs=mix[:, cz, t0:t0 + wd], r=["poolwb", ("mix", cz, tc)], w=[bkey])
                    bank2, bkey2 = next_pj()
                    proj(gp, gpk, tc, bank2, bkey2)
                    g, gk = silu_gate(bank2, bkey2, wd)
                    STT(mix[:, cz, t0:t0 + wd], bank[:, :wd], pscale[:, l, cz:cz + 1], g[:, :wd], ALU.mult, ALU.mult,
                        r=[bkey, gk, "pscale"], w=[("mix", cz, tc)])

        def phase_C(l, b, last):
            for c2 in range(2):
                mc = 2 + c2
                fence()
                MSET("pool", nvaug[:, :, :, 1, :], 1.0, r=["PH"], w=["nvaug"])
                for tc_ in range(5):
                    t0_, wd_ = TCH[tc_]
                    MSET("pool", nkz[0][64:128, t0_:t0_ + wd_], 0.0, r=["PH"], w=[("nkT", tc_)])
                    MSET("pool", nkz[1][0:64, t0_:t0_ + wd_], 0.0, r=["PH"], w=[("nkT", tc_)])
                DMA(ebuf[:], ebias_d[l, c2], r=["PH"], w=["ebuf"], grp="eb")
                TS("dve", ebuf[:], ebuf[:], 8.0, ALU.mult, r=["ebuf", "PH"], w=["ebuf"])
                kp, kpk = WS.next(("in", l, P_NK + c2))
                qp, qpk = WS.next(("in", l, P_NQ + c2))
                units = []
                for tc in range(5):
                    t0, wd = TCH[tc]
                    units.append(nr_stages(kp, kpk, tc, gains[:, l, 3:4],
                                           [(nkz[0][0:64, t0:t0 + wd], slice(0, 64)), (nkz[1][64:128, t0:t0 + wd], slice(64, 128))],
                                           ("nkT", tc), False, extra_r=["PH"], five=True))
                for tc in range(4 if last else 5):
                    t0, wd = TCH[tc]
                    units.append(nr_stages(qp, qpk, tc, gains[:, l, 2:3], mix[:, mc, t0:t0 + wd], ("mix", mc, tc), False,
                                           five=True))
                run_pipelined(units)
                vp, vpk = WS.next(("in", l, P_NV + c2))
                toks = [ti * 128 for ti in range(18)] + [64 + 128 * o for o in range(15)]
                for g4 in range(9):
                    tl = list(range(g4 * 4, min(33, g4 * 4 + 4)))
                    bank, bkey = next_pj()
                    for qi, ti in enumerate(tl):
                        tk0 = toks[ti]
                        rk = sorted(set([("hT", min(tk0 // 512, 4)), ("hT", min((tk0 + 127) // 512, 4))]))
                        for k in range(8):
                            MM(bank[:, qi * 128:(qi + 1) * 128], lhsT=hT[:, k, tk0:tk0 + 128], rhs=vp[:, k, :],
                               start=(k == 0), stop=(k == 7), r=[vpk] + rk, w=[bkey])
                    n = len(tl)
                    CP("act", nvaug[:, tl[0]:tl[0] + n, :, 0, :],
                       bank[:, 0:n * 128].rearrange("p (t k d) -> p t k d", t=n, k=2),
                       r=[bkey, "PH"], w=["nvaug"])
                gp, gpk = WS.next(("in", l, P_NG + c2))
                ev = ebuf.rearrange("p h (q two) c -> p h q two c", two=2)
                pj_list[0] = [6]
                ms_list[0] = [7]
                LA = 3
                ptc = [0]
                pt_all = [("PT", j_) for j_ in range(3)] + [("PT", j_, h_) for j_ in range(3) for h_ in range(2)]
                MSET("pool", dummy[:, :], 0.0, w=pt_all + ["dummy"])
                for tc in range(4):
                    items = [(hh, rr_) for rr_ in range(8) for hh in (0, 1)]
                    pendq = []
                    for s in range(len(items) + LA):
                        cur = None
                        if s < len(items):
                            hh, rr_ = items[s]
                            r = tc * 8 + rr_
                            r0 = min(max(r - 4, 0), 24)
                            kb = 64 * r0
                            pr = slice(64 * hh, 64 * hh + 64)
                            sc, sck = BK[s % 4], ("PS", s % 4)
                            qap = mix[:, mc, 64 * r:64 * r + 64]
                            rkeys = sorted(set(("nkT", min((kb + 128 * i) // 512, 3)) for i in range(4)) |
                                           set(("nkT", min((kb + 128 * i + 127) // 512, 3)) for i in range(4)))
                            for i in range(4):
                                MM(sc[:, 64 * i:64 * i + 64], lhsT=nkz[hh][:, kb + 128 * i:kb + 128 * i + 128], rhs=qap,
                                   r=list(rkeys) + [("mix", mc, tc), "PH"], w=[sck])
                            for i2 in range(2):
                                MM(sc[:, 256 + 64 * i2:256 + 64 * i2 + 64],
                                   lhsT=nkz[hh][:, SEQ + 128 * i2:SEQ + 128 * i2 + 128], rhs=qap,
                                   r=[("nkT", 4), ("mix", mc, tc), "PH"], w=[sck])
                            s0 = r0 - r + 7
                            esl = ev[:, hh, s0 // 2:s0 // 2 + 4, s0 % 2, :]
                            sc3 = sc[:, 0:256].rearrange("p (i c) -> p i c", c=64)
                            TT("dve", sc3, sc3, esl, ALU.add, r=[sck, "ebuf", "PH"], w=[sck])
                            pj_ = ptc[0] % 6
                            ptc[0] += 1
                            pt, ptk = PT.tiles[pj_ // 2][:, 512 * (pj_ % 2):512 * (pj_ % 2) + 512], ("PT", pj_ // 2, pj_ % 2)
                            ACT(pt[:, 0:384], sc[:, 0:384], AF.Exp, r=[sck], w=[ptk], scale=0.125)
                            if r0 % 2 == 0:
                                vts = [r0 // 2 + i for i in range(4)]
                            else:
                                vts = [18 + (r0 - 1) // 2 + i for i in range(4)]
                            vts += [16, 17]
                            cur = (hh, rr_, pt, ptk, vts)
                        pendq.append(cur)
                        if len(pendq) > LA and pendq[0] is not None:
                            hh_, rr2, pt_, ptk_, vts_ = pendq[0]
                            for idx, vt in enumerate(vts_):
                                MM(OA[hh_][:, 64 * rr2:64 * rr2 + 64],
                                   lhsT=nvaug[:, vt, hh_, :, :].rearrange("p a d -> p (a d)"),
                                   rhs=pt_[:, 64 * idx:64 * idx + 64], start=(idx == 0), stop=(idx == 5),
                                   r=[ptk_, "nvaug", "PH"], w=[OAK[hh_]])
                        if len(pendq) > LA:
                            pendq.pop(0)
                    finalize_attn(gp, gpk, tc, mc)
                pj_list[0] = [6, 0]
                ms_list[0] = [7, 1]
                for hh in (range(2) if not last else ()):
                    pr = slice(64 * hh, 64 * hh + 64)
                    sc, sck = SC[hh], SCK[hh]
                    for i2 in range(2):
                        MM(sc[:, 256 * i2:256 * i2 + 256], lhsT=nkz[hh][:, SEQ + 128 * i2:SEQ + 128 * i2 + 128],
                           rhs=mix[:, mc, SEQ:SEQ + NCTX], r=[("nkT", 4), ("mix", mc, 4), "PH"], w=[sck])
                    pj_ = ptc[0] % 6
                    ptc[0] += 1
                    pt, ptk = PT.tiles[pj_ // 2][:, 512 * (pj_ % 2):512 * (pj_ % 2) + 512], ("PT", pj_ // 2, pj_ % 2)
                    ACT(pt[:, 0:512], sc[:, :], AF.Exp, r=[sck], w=[ptk], scale=0.125)
                    for i2 in range(2):
                        MM(OA[hh][:, 0:256], lhsT=nvaug[:, 16 + i2, hh, :, :].rearrange("p a d -> p (a d)"),
                           rhs=pt[:, 256 * i2:256 * i2 + 256], start=(i2 == 0), stop=(i2 == 1),
                           r=[ptk, "nvaug", "PH"], w=[OAK[hh]])
                if not last:
                    finalize_attn(gp, gpk, 4, mc)
                MSET("pool", dummy[:, :], 0.0, w=pt_all + ["dummy"])
            y_update(l, b, ([0, 1, 2, 3] if "B" in phases else [2, 3]), "outC", last)

        final = []
        for b in range(n_batch):
            fence()
            for i in range(18):
                src = x_d[b, i * 128:(i + 1) * 128, :] if i < 16 else ctx_d[b, (i - 16) * 128:(i - 15) * 128, :]
                xs, xsk = xstage[i % NXS], ("xs", i % NXS)
                DMA(xs, src, r=["PH"], w=[xsk], grp="xs%d" % (i % NXS))
                tc = min(i // 4, 4)
                for half in range(2):
                    bank, bkey = next_pj()
                    for kk in range(4):
                        k = half * 4 + kk
                        S.add("pe", (lambda o, i_: (lambda e: e.transpose(out=o, in_=i_, identity=ident[:])))(
                            bank[:, kk * 128:(kk + 1) * 128], xs[:, k * 128:(k + 1) * 128]),
                            [xsk, "ident", "PH"], [bkey])
                    CP("dve" if half == 0 else "act", xT[:, half * 4:half * 4 + 4, i * 128:(i + 1) * 128],
                       bank[:, :].rearrange("p (k t) -> p k t", t=128), r=[bkey], w=[("xT", tc)])
            for l in range(n_layers):
                last = (l == NL - 1)
                if "A" in phases and stage >= 2:
                    phase_A_init()
                if stage >= 1:
                    xnorm(l, b)
                if "A" in phases and stage >= 2:
                    phase_A(l, b, last)
                if "B" in phases:
                    phase_B(l, b, last)
                if "C" in phases:
                    phase_C(l, b, last)
            fence()
            for i in range(16):
                xs, xsk = xstage[i % NXS], ("xs", i % NXS)
                tc = i // 4
                for half in range(2):
                    bank, bkey = next_pj()
                    for kk in range(4):
                        k = half * 4 + kk
                        S.add("pe", (lambda o, i_: (lambda e: e.transpose(out=o, in_=i_, identity=ident[:])))(
                            bank[:, kk * 128:(kk + 1) * 128], xT[:, k, i * 128:(i + 1) * 128]),
                            [("xT", tc), "ident"], [bkey])
                    CP("dve" if half == 0 else "act", xs[:, half * 512:(half + 1) * 512], bank[:, :],
                       r=[bkey, "PH"], w=[xsk])
                final.append(DMA(out_d[b, i * 128:(i + 1) * 128, :], xs, r=[xsk, "PH"], grp="o%d" % (i % NXS)))
        assert stage < 99 or WS.consumed == len(WS.specs), (WS.consumed, len(WS.specs))
        S.emit(nc, final_waits=final[-NXS:])
    return nc


def _f32(a):
    return np.ascontiguousarray(np.asarray(a, dtype=np.float32))


def prep_shared(c_ctx, norm_gain, w_mod, b_mod, w_in, att_q_gain, att_k_gain, pool_w, pool_scale,
                na_q_gain, na_k_gain, na_rpb, w_out):
    sh = {}
    w_mod = _f32(w_mod)
    sh["wmod"] = _f32(w_mod.reshape(NL, 8, 128, 6, 512).transpose(0, 3, 2, 1, 4))
    sh["bmod"] = _f32(_f32(b_mod).reshape(NL, 24, 128).transpose(2, 0, 1))
    sh["ngain"] = _f32(_f32(norm_gain).reshape(NL, 8, 128).transpose(2, 0, 1))
    w_in = _f32(w_in)
    cols = []
    cols.append(np.r_[512:576, 512:576])
    cols.append(np.r_[576:640, 576:640])
    cols.append(np.r_[640:768])
    for c in range(4):
        cols.append(np.r_[c * 128:(c + 1) * 128])
    for c in range(4):
        cols.append(np.r_[768 + c * 128:768 + (c + 1) * 128])
    for base in (1280, 1536, 2048, 2304, 1792, 2560):
        for c in range(2):
            cols.append(np.r_[base + c * 128:base + (c + 1) * 128])
    assert len(cols) == NPAN
    win = np.empty((NL, NPAN, 128, 8, 128), np.float32)
    for pi, cc in enumerate(cols):
        win[:, pi] = w_in[:, :, cc].reshape(NL, 8, 128, 128).transpose(0, 2, 1, 3)
    sh["win"] = win
    sh["wout"] = _f32(_f32(w_out).reshape(NL, 8, 128, 8, 128).transpose(0, 3, 2, 1, 4))
    g = np.stack([_f32(att_q_gain), _f32(att_k_gain), _f32(na_q_gain), _f32(na_k_gain)], axis=-1)
    sh["gains"] = _f32(np.tile(g, (1, 2, 1)).transpose(1, 0, 2))
    p = np.arange(128)
    d = p % 64
    inv_freq = (np.float32(10000.0) ** (-np.arange(16, dtype=np.float32) / np.float32(16))).astype(np.float32)
    f = inv_freq[d % 16]
    sign = np.where((d % 32) < 16, -1.0, 1.0).astype(np.float32)
    isrow = d < 32
    rows = np.arange(32, dtype=np.float32)
    colsv = np.arange(64, dtype=np.float32)
    angA = (rows[None, :] * f[:, None]).astype(np.float32)
    angB = (colsv[None, :] * f[:, None]).astype(np.float32)
    cosA = np.where(isrow[:, None], np.cos(angA), 1.0)
    cosB = np.where(isrow[:, None], 1.0, np.cos(angB))
    sinA = np.where(isrow[:, None], sign[:, None] * np.sin(angA), 1.0)
    sinB = np.where(isrow[:, None], 1.0, sign[:, None] * np.sin(angB))
    sh["ropetab"] = _f32(np.concatenate([cosA, cosB, sinA, sinB], axis=1))
    partner = np.where((p % 32) < 16, p + 16, p - 16)
    rm = np.zeros((128, 128), np.float32)
    rm[partner, p] = 1.0
    sh["rmat"] = rm
    sh["ident"] = np.eye(128, dtype=np.float32)
    pw = _f32(pool_w)
    poolw = np.zeros((NL, 128, 2, 128), np.float32)
    for cz in range(2):
        for j in range(2):
            poolw[:, j * 64:(j + 1) * 64, cz, j * 64:(j + 1) * 64] = pw[:, 2 * cz + j]
    sh["poolw"] = poolw
    sh["pscale"] = _f32(_f32(pool_scale).reshape(NL, 2, 128).transpose(2, 0, 1))
    pt = np.zeros((128, 2, 17), np.float32)
    for cz in range(2):
        for j in range(2):
            w = (2, 4, 8, 16)[2 * cz + j]
            t = np.arange(8)
            cs = np.minimum(w, t + w // 2).astype(np.float32)
            ce = np.minimum(w, (8 - t) + w // 2 - 1 + 0).astype(np.float32)
            ce = np.minimum(w, 8 - t + w // 2).astype(np.float32)
            pt[j * 64:(j + 1) * 64, cz, 0] = 1.0 / w
            pt[j * 64:(j + 1) * 64, cz, 1:9] = 1.0 / cs
            pt[j * 64:(j + 1) * 64, cz, 9:17] = 1.0 / ce
    sh["pooltab"] = pt
    rpb = _f32(na_rpb)
    kc = np.arange(64)[:, None]
    cq = np.arange(64)[None, :]
    c0 = np.clip(cq - 8, 0, 48)
    valid = (kc >= c0) & (kc < c0 + 16)
    dcol = np.clip(kc - cq + 15, 0, 30)
    Dh = np.where(valid[None, None, None], rpb[:, :, :, dcol], np.float32(NEG))
    eb = np.empty((NL, 2, 128, 2, 14, 64), np.float32)
    for c2 in range(2):
        for hh in range(2):
            h = 2 * c2 + hh
            for j in range(2):
                for di in range(14):
                    eb[:, c2, j * 64:(j + 1) * 64, hh, di, :] = Dh[:, h, di + j]
    sh["ebias"] = eb
    return sh


_NC_CACHE = {}


def kernel(x, c, ctx, c_ctx, norm_gain, w_mod, b_mod, w_in, att_q_gain, att_k_gain,
           pool_w, pool_scale, na_q_gain, na_k_gain, na_rpb, w_out):
    x = _f32(x)
    c = _f32(c)
    ctx = _f32(ctx)
    c_ctx = _f32(c_ctx)
    sh = prep_shared(c_ctx, norm_gain, w_mod, b_mod, w_in, att_q_gain, att_k_gain, pool_w, pool_scale,
                     na_q_gain, na_k_gain, na_rpb, w_out)
    if "nc" not in _NC_CACHE:
        _NC_CACHE["nc"] = build_nc()
    nc = _NC_CACHE["nc"]
    in_maps = []
    for i in range(8):
        m = dict(sh)
        m["x"] = np.ascontiguousarray(x[2 * i:2 * i + 2])
        m["ctx"] = np.ascontiguousarray(ctx[2 * i:2 * i + 2])
        vecs = np.stack([c[2 * i], c[2 * i + 1], c_ctx], axis=-1)
        m["cT"] = _f32(vecs.reshape(8, 128, 3).transpose(1, 0, 2))
        in_maps.append(m)
    res = run_bass_kernel_spmd(nc, in_maps, core_ids=list(range(8)))
    return np.concatenate([np.asarray(r["out"], dtype=np.float32) for r in res.results], axis=0)
```

```python
import contextlib
import numpy as np
import concourse.bass as bass
import concourse.mybir as mybir
from concourse.bass_utils import run_bass_kernel_spmd

F32 = mybir.dt.float32
BF16 = mybir.dt.bfloat16
ALU = mybir.AluOpType
AF = mybir.ActivationFunctionType

D = 1024
NL = 4
SEQ = 2048
NCTX = 256
T = SEQ + NCTX
EPS = 1e-6
NEG = -30000.0
TCH = [(0, 512), (512, 512), (1024, 512), (1536, 512), (2048, 256)]
ENGS = ("pe", "act", "dve", "pool", "sp")
PSUM_KEYS = ("PS",)

P_K0, P_K1, P_V = 0, 1, 2
P_Q, P_G, P_Z, P_BG, P_NK, P_NV, P_NQ, P_NG = 3, 7, 11, 13, 15, 17, 19, 21
NPAN = 23


class Op:
    __slots__ = ("eng", "idx", "fn", "waits", "signal", "dma_grp", "dma_cnt", "sigval")

    def __init__(self, eng, idx, fn):
        self.eng = eng
        self.idx = idx
        self.fn = fn
        self.waits = []
        self.signal = False
        self.dma_grp = None
        self.dma_cnt = 0
        self.sigval = 0


class Sched:
    def __init__(self):
        self.q = {e: [] for e in ENGS}
        self.last_w = {}
        self.readers = {}
        self.maxwait = {e: {} for e in ENGS}
        self.dma_cnt = {}

    def add(self, eng, fn, reads=(), writes=(), dma=None):
        op = Op(eng, len(self.q[eng]), fn)
        if dma is not None:
            op.dma_grp = dma
            self.dma_cnt[dma] = self.dma_cnt.get(dma, 0) + 1
            op.dma_cnt = self.dma_cnt[dma]
        best = {}
        for k in reads:
            w = self.last_w.get(k)
            if w is not None:
                self._cand(best, w, eng)
            if isinstance(k, tuple) and k[0] in PSUM_KEYS:
                for r in self.readers.get(k, ()):
                    if r.eng != eng:
                        self._cand(best, r, eng)
        for k in writes:
            w = self.last_w.get(k)
            if w is not None:
                self._cand(best, w, eng)
            for r in self.readers.get(k, ()):
                self._cand(best, r, eng)
        mw = self.maxwait[eng]
        for src, (pos, d) in best.items():
            if mw.get(src, -1) >= pos:
                continue
            mw[src] = pos
            d.signal = True
            op.waits.append(d)
        for k in writes:
            self.last_w[k] = op
            self.readers[k] = []
        for k in reads:
            self.readers.setdefault(k, []).append(op)
        self.q[eng].append(op)
        return op

    @staticmethod
    def _cand(best, d, eng):
        if d.dma_grp is not None:
            src = ("dma", d.dma_grp)
            pos = d.dma_cnt
        else:
            if d.eng == eng and eng == "pe":
                return
            src = d.eng
            pos = d.idx
        cur = best.get(src)
        if cur is None or cur[0] < pos:
            best[src] = (pos, d)

    def emit(self, nc, final_waits=()):
        for op in final_waits:
            op.signal = True
        for e in ENGS:
            n = 0
            for op in self.q[e]:
                if op.dma_grp is None and op.signal:
                    n += 1
                    op.sigval = n
        grps = sorted(self.dma_cnt.keys())
        with contextlib.ExitStack() as st:
            sems = {}
            for e in ENGS:
                sems[e] = st.enter_context(nc.semaphore("sem_" + e))
            for g in grps:
                sems[("dma", g)] = st.enter_context(nc.semaphore("semd_" + str(g)))
            block = st.enter_context(nc.Block())

            def tok(d):
                if d.dma_grp is not None:
                    return sems[("dma", d.dma_grp)], 16 * d.dma_cnt
                return sems[d.eng], d.sigval

            def run(engname, e):
                for op in self.q[engname]:
                    for d in op.waits:
                        s, v = tok(d)
                        e.wait_ge(s, v)
                    ins = op.fn(e)
                    if op.dma_grp is not None:
                        ins.then_inc(sems[("dma", op.dma_grp)], 16)
                    elif op.signal:
                        ins.then_inc(sems[engname], 1)
                if engname == "sp":
                    for d in final_waits:
                        s, v = tok(d)
                        e.wait_ge(s, v)

            @block.tensor
            def _(e):
                run("pe", e)

            @block.scalar
            def _(e):
                run("act", e)

            @block.vector
            def _(e):
                run("dve", e)

            @block.gpsimd
            def _(e):
                run("pool", e)

            @block.sync
            def _(e):
                run("sp", e)


class Ring:
    def __init__(self, name, tiles):
        self.name = name
        self.tiles = tiles
        self.i = 0

    def get(self):
        j = self.i % len(self.tiles)
        self.i += 1
        return self.tiles[j], (self.name, j)


def build_nc(n_layers=NL, n_batch=2, phases="ABC", stage=99):
    nc = bass.Bass("TRN2", target_bir_lowering=False)

    def din(name, shape):
        return nc.dram_tensor(name, list(shape), F32, kind="ExternalInput").ap()

    x_d = din("x", [2, SEQ, D])
    ctx_d = din("ctx", [2, NCTX, D])
    cT_d = din("cT", [128, 8, 3])
    wmod_d = din("wmod", [NL, 6, 128, 8, 512])
    bmod_d = din("bmod", [128, NL, 24])
    ngain_d = din("ngain", [128, NL, 8])
    win_d = din("win", [NL, NPAN, 128, 8, 128])
    wout_d = din("wout", [NL, 8, 128, 8, 128])
    gains_d = din("gains", [128, NL, 4])
    rope_d = din("ropetab", [128, 192])
    rmat_d = din("rmat", [128, 128])
    ident_d = din("ident", [128, 128])
    poolw_d = din("poolw", [NL, 128, 2, 128])
    pscale_d = din("pscale", [128, NL, 2])
    pooltab_d = din("pooltab", [128, 2, 17])
    ebias_d = din("ebias", [NL, 2, 128, 2, 14, 64])
    out_d = nc.dram_tensor("out", [2, SEQ, D], F32, kind="ExternalOutput").ap()

    S = Sched()
    with contextlib.ExitStack() as st:
        def sb(name, shape, dt=F32):
            return st.enter_context(nc.sbuf_tensor(name, list(shape), dt))

        xT = sb("xT", [128, 8, T])
        hT = sb("hT", [128, 8, T], BF16)
        mix = sb("mix", [128, 4, T], BF16)
        PH = sb("PH", [128, 8320])
        stg = [sb("stg%d" % i, [128, 8, 128]) for i in range(2)]
        wb = [sb("wb%d" % i, [128, 8, 128], BF16) for i in range(4)]
        Wt = Ring("W", [sb("wk%d" % i, [128, 512]) for i in range(9)])
        PT = Ring("PT", [sb("pt%d" % i, [128, 1024], BF16) for i in range(3)])
        ident = sb("ident_s", [128, 128])
        rmat = sb("rmat_s", [128, 128])
        ones128 = sb("ones128", [128, 128])
        bones = sb("bones", [128, 128])
        ropet = sb("ropet", [128, 192])
        gains = sb("gains_s", [128, NL, 4])
        ngain = sb("ngain_s", [128, NL, 8])
        bmod = sb("bmod_s", [128, NL, 24])
        pscale = sb("pscale_s", [128, NL, 2])
        pooltab = sb("pooltab_s", [128, 2, 17])
        cT = sb("cT_s", [128, 8, 3])
        scT = sb("scT", [128, 8, 3])
        mod_all = sb("mod_all", [128, NL, 24, 3])
        gs_all = sb("gs_all", [128, NL, 8, 3])
        poolwb = sb("poolwb", [128, 2, 128], BF16)
        dummy = sb("fence_dummy", [128, 8])

        def ps(name):
            return st.enter_context(nc.psum_tensor(name, [128, 512], F32))

        PSALL = st.enter_context(nc.psum_tensor("PSALL", [128, 4096], F32))
        BK = [PSALL[:, 512 * i:512 * (i + 1)] for i in range(8)]
        SC = [BK[2], BK[3]]
        SCK = [("PS", 2), ("PS", 3)]
        OA = [BK[4], BK[5]]
        OAK = [("PS", 4), ("PS", 5)]
        SCD = [PSALL[:, 0:1024], PSALL[:, 1024:2048]]
        SCDK = [[("PS", 0), ("PS", 1)], [("PS", 2), ("PS", 3)]]
        pj_list = [[6, 0]]
        ms_list = [[7, 1]]

        PHb = PH[:, :].bitcast(BF16)
        kz = [[PHb[:, (2 * kv + hh) * T:(2 * kv + hh + 1) * T] for hh in range(2)] for kv in range(2)]
        vaugA = PHb[:, 4 * T:4 * T + 18 * 256].rearrange("p (t k a d) -> p t k a d", t=18, k=2, a=2)
        nkz = [PHb[:, hh * T:(hh + 1) * T] for hh in range(2)]
        nvaug = PHb[:, 2 * T:2 * T + 33 * 256].rearrange("p (t k a d) -> p t k a d", t=33, k=2, a=2)
        e_off = (2 * T + 33 * 256 + 1) // 2
        ebuf = PH[:, e_off:e_off + 2 * 14 * 64].rearrange("p (h d c) -> p h d c", h=2, d=14)
        assert e_off + 2 * 14 * 64 <= 8320
        ZN = 2352
        zbuf = PH[:, 0:ZN]
        tA = PH[:, ZN:2 * ZN]
        tB = PH[:, 2 * ZN:3 * ZN]
        assert 3 * ZN <= 8320
        NXS = 6
        xstage = [PH[:, i * 1024:(i + 1) * 1024] for i in range(NXS)]

        def MM(out, lhsT, rhs, start=True, stop=True, r=(), w=()):
            S.add("pe", lambda e: e.matmul(out, lhsT=lhsT, rhs=rhs, start=start, stop=stop), r, w)

        def ACT(out, in_, func, r=(), w=(), scale=None, bias=None):
            kw = {}
            if scale is not None:
                kw["scale"] = scale
            if bias is not None:
                kw["bias"] = bias
            S.add("act", lambda e: e.activation(out=out, in_=in_, func=func, **kw), r, w)

        def TT(eng, out, in0, in1, op, r=(), w=()):
            S.add(eng, lambda e: e.tensor_tensor(out=out, in0=in0, in1=in1, op=op), r, w)

        def TS(eng, out, in0, s1, op0, r=(), w=(), s2=None, op1=None):
            if op1 is None:
                S.add(eng, lambda e: e.tensor_scalar(out=out, in0=in0, scalar1=s1, scalar2=None, op0=op0), r, w)
            else:
                S.add(eng, lambda e: e.tensor_scalar(out=out, in0=in0, scalar1=s1, scalar2=s2, op0=op0, op1=op1), r, w)

        def STT(out, in0, scalar, in1, op0, op1, r=(), w=()):
            S.add("dve", lambda e: e.scalar_tensor_tensor(out=out, in0=in0, scalar=scalar, in1=in1, op0=op0, op1=op1), r, w)

        def CP(eng, out, in_, r=(), w=()):
            if eng == "act":
                S.add("act", lambda e: e.activation(out=out, in_=in_, func=AF.Copy), r, w)
            else:
                S.add(eng, lambda e: e.tensor_copy(out=out, in_=in_), r, w)

        def MSET(eng, ap, val, r=(), w=()):
            S.add(eng, lambda e: e.memset(ap, val), r, w)

        def DMA(out, in_, r=(), w=(), grp="m"):
            return S.add("sp", lambda e: e.dma_start(out=out, in_=in_), r, w, dma=grp)

        def fence():
            MSET("pool", dummy[:, :], 0.0, r=(), w=["PH", "dummy"])

        class WStream:
            def __init__(self):
                self.specs = []
                self.issued = 0
                self.consumed = 0

            def _issue(self):
                n = self.issued
                ap, nk, _tag = self.specs[n]
                si, wi = n % 2, n % 4
                DMA(stg[si][:, 0:nk, :], ap, w=[("stg", si)], grp="stg%d" % si)
                in_attn = (_tag[0] == "outA") or (_tag[0] == "in" and P_Q < _tag[2] < P_Z)
                CP("pool" if in_attn else "act", wb[wi][:, 0:nk, :], stg[si][:, 0:nk, :], r=[("stg", si)], w=[("wb", wi)])
                self.issued += 1

            def next(self, check=None):
                while self.issued < min(len(self.specs), self.consumed + 3 - 1):
                    self._issue()
                n = self.consumed
                if check is not None:
                    assert self.specs[n][2] == check, (self.specs[n][2], check)
                self.consumed += 1
                return wb[n % 4], ("wb", n % 4)

        WS = WStream()

        def add_spec(ap, nk, tag):
            WS.specs.append((ap, nk, tag))

        for b in range(n_batch):
            for l in range(n_layers):
                if "A" in phases:
                    for pid in (P_K0, P_K1, P_V):
                        add_spec(win_d[l, pid], 8, ("in", l, pid))
                    add_spec(win_d[l, P_Q], 8, ("in", l, P_Q))
                    for c in range(4):
                        add_spec(win_d[l, P_G + c], 8, ("in", l, P_G + c))
                        if c < 3:
                            add_spec(win_d[l, P_Q + c + 1], 8, ("in", l, P_Q + c + 1))
                    for m in range(8):
                        add_spec(wout_d[l, m, :, 0:4, :], 4, ("outA", l, m))
                if "B" in phases:
                    for cz in range(2):
                        add_spec(win_d[l, P_Z + cz], 8, ("in", l, P_Z + cz))
                        add_spec(win_d[l, P_BG + cz], 8, ("in", l, P_BG + cz))
                if "C" in phases:
                    for c2 in range(2):
                        for pid in (P_NK, P_NQ, P_NV, P_NG):
                            add_spec(win_d[l, pid + c2], 8, ("in", l, pid + c2))
                    for m in range(8):
                        if "B" in phases:
                            add_spec(wout_d[l, m, :, 4:8, :], 4, ("outC", l, m))
                        else:
                            add_spec(wout_d[l, m, :, 6:8, :], 2, ("outC", l, m))

        DMA(ident[:], ident_d[:], w=["ident"], grp="c_ident")
        DMA(rmat[:], rmat_d[:], w=["rmat"], grp="c_rmat")
        DMA(ropet[:], rope_d[:], w=["ropet"], grp="c_ropet")
        DMA(gains[:], gains_d[:], w=["gains"], grp="c_gains")
        DMA(ngain[:], ngain_d[:], w=["ngain"], grp="c_ngain")
        DMA(bmod[:], bmod_d[:], w=["bmod"], grp="c_bmod")
        DMA(pscale[:], pscale_d[:], w=["pscale"], grp="c_pscale")
        DMA(pooltab[:], pooltab_d[:], w=["pooltab"], grp="c_pooltab")
        DMA(cT[:], cT_d[:], w=["cT"], grp="c_cT")
        MSET("pool", ones128[:], 1.0, w=["ones"])
        MSET("pool", bones[:], 0.0, w=["bones"])
        MSET("pool", bones[0:64, 0:64], 1.0, w=["bones"])
        MSET("pool", bones[64:128, 64:128], 1.0, w=["bones"])
        ACT(scT[:], cT[:], AF.Exp, r=["cT"], w=["scT"], scale=-1.0)
        ACT(scT[:], scT[:], AF.Ln, r=["scT"], w=["scT"], bias=1.0)
        ACT(scT[:], scT[:], AF.Exp, r=["scT"], w=["scT"], scale=-1.0)
        TT("dve", scT[:], scT[:], cT[:], ALU.mult, r=["scT", "cT"], w=["scT"])
        for l in range(n_layers):
            rows = []
            for cc in range(6):
                i = l * 6 + cc
                si = i % 2
                wslot = PH[:, si * 4096:(si + 1) * 4096].rearrange("p (k c) -> p k c", k=8)
                DMA(wslot, wmod_d[l, cc], r=["PH"], w=[("wms", si)], grp="wms%d" % si)
                bank, bkey = BK[7 if cc % 2 == 0 else 6], ("PS", 7 if cc % 2 == 0 else 6)
                for k in range(8):
                    MM(bank[0:3, :], lhsT=scT[:, k, :], rhs=wslot[:, k, :],
                       start=(k == 0), stop=(k == 7), r=[("wms", si), "scT", "PH"], w=[bkey])
                rt, rtk = Wt.get()
                CP("dve", rt[0:3, :], bank[0:3, :], r=[bkey], w=[rtk])
                rows.append((rt, rtk))
            for m in range(24):
                rt, rtk = rows[m // 4]
                off = (m % 4) * 128
                S.add("pe", (lambda o, i_: (lambda e: e.transpose(out=o, in_=i_, identity=ident[0:3, 0:3])))(
                    BK[1][:, m * 4:m * 4 + 3], rt[0:3, off:off + 128]), [rtk, "ident"], [("PS", 1)])
            msv = BK[1][:, 0:96].rearrange("p (m f) -> p m f", f=4)
            for v in range(3):
                TT("dve", mod_all[:, l, :, v], msv[:, :, v], bmod[:, l, :], ALU.add,
                   r=[("PS", 1), "bmod"], w=["mod"])
            for v in range(3):
                STT(gs_all[:, l, :, v], mod_all[:, l, 8:16, v], 1.0, ngain[:, l, :], ALU.add, ALU.mult,
                    r=["mod", "ngain"], w=["mod"])

        pjc = [0]

        def next_pj():
            lst = pj_list[0]
            i = lst[pjc[0] % len(lst)]
            pjc[0] += 1
            return BK[i], ("PS", i)

        msc = [0]

        def next_ms():
            lst = ms_list[0]
            i = lst[msc[0] % len(lst)]
            msc[0] += 1
            return BK[i], ("PS", i)

        def proj(panel, pkey, tc, bank, bkey):
            t0, wd = TCH[tc]
            for k in range(8):
                MM(bank[:, :wd], lhsT=panel[:, k, :], rhs=hT[:, k, t0:t0 + wd],
                   start=(k == 0), stop=(k == 7), r=[pkey, ("hT", tc)], w=[bkey])

        def xnorm(l, b):
            state = {}

            def stage1(tc):
                t0, wd = TCH[tc]
                ms, msk = next_ms()
                for k in range(8):
                    sq, sqk = Wt.get()
                    if k in (0, 3, 6):
                        TT("pool", sq[:, :wd], xT[:, k, t0:t0 + wd], xT[:, k, t0:t0 + wd], ALU.mult,
                           r=[("xT", tc)], w=[sqk])
                    else:
                        ACT(sq[:, :wd], xT[:, k, t0:t0 + wd], AF.Square, r=[("xT", tc)], w=[sqk])
                    MM(ms[:, :wd], lhsT=ones128[:], rhs=sq[:, :wd], start=(k == 0), stop=(k == 7),
                       r=[sqk, "ones"], w=[msk])
                state[tc] = (ms, msk)

            def stage2(tc):
                t0, wd = TCH[tc]
                v = b if tc < 4 else 2
                ms, msk = state[tc]
                ln, lnk = Wt.get()
                ACT(ln[:, :wd], ms[:, :wd], AF.Ln, r=[msk], w=[lnk], scale=1.0 / D, bias=EPS)
                ACT(ms[:, :wd], ln[:, :wd], AF.Exp, r=[lnk], w=[msk], scale=-0.5)
                for k in range(8):
                    t, tk = Wt.get()
                    TT("dve", t[:, :wd], xT[:, k, t0:t0 + wd], ms[:, :wd], ALU.mult,
                       r=[("xT", tc), msk], w=[tk])
                    if k in (1, 4, 7):
                        TS("pool", hT[:, k, t0:t0 + wd], t[:, :wd], gs_all[:, l, k, v:v + 1], ALU.mult,
                           s2=mod_all[:, l, k, v:v + 1], op1=ALU.add, r=[tk, "mod"], w=[("hT", tc)])
                    else:
                        ACT(hT[:, k, t0:t0 + wd], t[:, :wd], AF.Identity, r=[tk, "mod"], w=[("hT", tc)],
                            scale=gs_all[:, l, k, v:v + 1], bias=mod_all[:, l, k, v:v + 1])

            for i in range(6):
                if i < 5:
                    stage1(i)
                if i >= 1:
                    stage2(i - 1)

        def nr_stages(panel, pkey, tc, gcol, dst, dkey, rope, extra_r=(), micro=False, five=False):
            t0, wd = TCH[tc]
            st_ = {}

            hw_ = wd // 2

            def s0():
                st_["bank"], st_["bkey"] = next_pj()
                proj(panel, pkey, tc, st_["bank"], st_["bkey"])

            def pk(k):
                def f():
                    if k == 0:
                        st_["bank"], st_["bkey"] = next_pj()
                    MM(st_["bank"][:, :wd], lhsT=panel[:, k, :], rhs=hT[:, k, t0:t0 + wd],
                       start=(k == 0), stop=(k == 7), r=[pkey, ("hT", tc)], w=[st_["bkey"]])
                return f

            def s1e():
                bank, bkey = st_["bank"], st_["bkey"]
                qg, qgk = Wt.get()
                sq, sqk = Wt.get()
                TS("dve", qg[:, :wd], bank[:, :wd], gcol, ALU.mult, r=[bkey, "gains"], w=[qgk])
                ACT(sq[:, :wd], bank[:, :wd], AF.Square, r=[bkey], w=[sqk])
                st_.update(qg=qg, qgk=qgk, sq=sq, sqk=sqk)

            def s1m():
                ms, msk = next_ms()
                MM(ms[:, :hw_], lhsT=bones[:], rhs=st_["sq"][:, :hw_], r=[st_["sqk"], "bones"], w=[msk])
                st_.update(ms=ms, msk=msk)

            def s1a():
                s1e()
                s1m()

            def s1b():
                MM(st_["ms"][:, hw_:wd], lhsT=bones[:], rhs=st_["sq"][:, hw_:wd], r=[st_["sqk"], "bones"], w=[st_["msk"]])

            def s1():
                s1a()
                s1b()

            def s2e():
                ms, msk = st_["ms"], st_["msk"]
                rs, rsk = st_["sq"], st_["sqk"]
                ACT(rs[:, :wd], ms[:, :wd], AF.Ln, r=[msk], w=[rsk], scale=1.0 / 64, bias=EPS)
                ACT(rs[:, :wd], rs[:, :wd], AF.Exp, r=[rsk], w=[rsk], scale=-0.5)
                st_.update(rs=rs, rsk=rsk)

            def s2m():
                if rope and tc < 4:
                    qg, qgk = st_["qg"], st_["qgk"]
                    ms2, ms2k = next_ms()
                    MM(ms2[:, :hw_], lhsT=rmat[:], rhs=qg[:, :hw_], r=[qgk, "rmat"], w=[ms2k])
                    st_.update(ms2=ms2, ms2k=ms2k)

            def s2a():
                s2e()
                s2m()

            def s2b():
                if rope and tc < 4:
                    MM(st_["ms2"][:, hw_:wd], lhsT=rmat[:], rhs=st_["qg"][:, hw_:wd], r=[st_["qgk"], "rmat"], w=[st_["ms2k"]])

            def s2():
                s2a()
                s2b()

            def s3():
                qg, qgk, rs, rsk = st_["qg"], st_["qgk"], st_["rs"], st_["rsk"]
                if rope and tc < 4:
                    ms2, ms2k = st_["ms2"], st_["ms2k"]
                    nr = wd // 64
                    r0 = t0 // 64
                    cosA = ropet[:, r0:r0 + nr].unsqueeze(2).broadcast_to([128, nr, 64])
                    cosB = ropet[:, 32:96].unsqueeze(1).broadcast_to([128, nr, 64])
                    sinA = ropet[:, 96 + r0:96 + r0 + nr].unsqueeze(2).broadcast_to([128, nr, 64])
                    sinB = ropet[:, 128:192].unsqueeze(1).broadcast_to([128, nr, 64])

                    def v3(ap):
                        return ap.rearrange("p (r c) -> p r c", c=64)
                    t1, t1k = qg, qgk
                    t2, t2k = Wt.get()
                    TT("pool", v3(t1[:, :wd]), v3(qg[:, :wd]), cosA, ALU.mult, r=[qgk, "ropet"], w=[t1k])
                    TT("pool", v3(t1[:, :wd]), v3(t1[:, :wd]), cosB, ALU.mult, r=[t1k, "ropet"], w=[t1k])
                    TT("dve", v3(t2[:, :wd]), v3(ms2[:, :wd]), sinA, ALU.mult, r=[ms2k, "ropet"], w=[t2k])
                    TT("dve", v3(t2[:, :wd]), v3(t2[:, :wd]), sinB, ALU.mult, r=[t2k, "ropet"], w=[t2k])
                    TT("pool", t1[:, :wd], t1[:, :wd], t2[:, :wd], ALU.add, r=[t1k, t2k], w=[t1k])
                    src, srck, eng = t1, t1k, "dve"
                else:
                    src, srck, eng = qg, qgk, "pool"
                dsts = dst if isinstance(dst, list) else [(dst, slice(0, 128))]
                for (dap, prr) in dsts:
                    TT(eng, dap, src[prr, :wd], rs[prr, :wd], ALU.mult, r=[srck, rsk] + list(extra_r), w=[dkey])

            if micro:
                if rope and tc < 4:
                    return [pk(k) for k in range(8)] + [s1e, s1m, s1b, s2e, s2m, s2b, None, s3]
                return [pk(k) for k in range(8)] + [s1e, s1m, s1b, s2e, None, s3]
            if five:
                def s1mb():
                    s1m()
                    s1b()
                return [s0, s1e, s1mb, s2, s3]
            return [s0, s1, s2, s3]

        def place(sched, start, ops, per_step=1):
            for i, f in enumerate(ops):
                if f is not None:
                    sched.setdefault(start + i // per_step, []).append(f)

        def run_all(stages):
            for f in stages:
                f()

        def run_pipelined(units):
            n = len(units)
            ns = len(units[0])
            for step in range(n + ns - 1):
                for si in range(ns - 1, -1, -1):
                    ui = step - si
                    if 0 <= ui < n:
                        units[ui][si]()

        def silu_gate(bank, bkey, wd):
            e1, e1k = Wt.get()
            ACT(e1[:, :wd], bank[:, :wd], AF.Exp, r=[bkey], w=[e1k], scale=-1.0)
            ACT(e1[:, :wd], e1[:, :wd], AF.Ln, r=[e1k], w=[e1k], bias=1.0)
            ACT(e1[:, :wd], e1[:, :wd], AF.Exp, r=[e1k], w=[e1k], scale=-1.0)
            g, gk = Wt.get()
            TT("dve", g[:, :wd], bank[:, :wd], e1[:, :wd], ALU.mult, r=[bkey, e1k], w=[gk])
            return g, gk

        def gate_stages(gp, gpk, tc, holder, micro=False):
            t0, wd = TCH[tc]

            def g0():
                holder["bank"], holder["bkey"] = next_pj()
                proj(gp, gpk, tc, holder["bank"], holder["bkey"])

            def gk_(k):
                def f():
                    if k == 0:
                        holder["bank"], holder["bkey"] = next_pj()
                    MM(holder["bank"][:, :wd], lhsT=gp[:, k, :], rhs=hT[:, k, t0:t0 + wd],
                       start=(k == 0), stop=(k == 7), r=[gpk, ("hT", tc)], w=[holder["bkey"]])
                return f

            def g1():
                holder["g"], holder["gk"] = silu_gate(holder["bank"], holder["bkey"], wd)

            if micro:
                return [gk_(k) for k in range(8)] + [g1]
            return [g0, g1]

        def finalize_norm(tc, mc, g, gk):
            t0, wd = TCH[tc]
            ao, aok = Wt.get()
            for hh in range(2):
                rr, rrk = Wt.get()
                ACT(rr[64:128, :wd], OA[hh][64:128, :wd], AF.Ln, r=[OAK[hh]], w=[rrk])
                ACT(rr[64:128, :wd], rr[64:128, :wd], AF.Exp, r=[rrk], w=[rrk], scale=-1.0)
                TT("dve", ao[64 * hh:64 * hh + 64, :wd], OA[hh][0:64, :wd], rr[64:128, :wd], ALU.mult,
                   r=[OAK[hh], rrk], w=[aok])
            TT("pool", mix[:, mc, t0:t0 + wd], ao[:, :wd], g[:, :wd], ALU.mult, r=[aok, gk], w=[("mix", mc, tc)])

        def finalize_attn(gp, gpk, tc, mc):
            h = {}
            run_all(gate_stages(gp, gpk, tc, h))
            finalize_norm(tc, mc, h["g"], h["gk"])

        def y_update(l, b, mcs, tag, last):
            for m in range(8):
                wp, wpk = WS.next((tag, l, m))
                for tc, (t0, wd) in enumerate(TCH):
                    if tc == 4 and last:
                        continue
                    bank, bkey = next_pj()
                    for i, mc in enumerate(mcs):
                        MM(bank[:, :wd], lhsT=wp[:, i, :], rhs=mix[:, mc, t0:t0 + wd],
                           start=(i == 0), stop=(i == len(mcs) - 1), r=[wpk, ("mix", mc, tc)], w=[bkey])
                    v = b if tc < 4 else 2
                    STT(xT[:, m, t0:t0 + wd], bank[:, :wd], mod_all[:, l, 16 + m, v:v + 1], xT[:, m, t0:t0 + wd],
                        ALU.mult, ALU.add, r=[bkey, "mod", ("xT", tc)], w=[("xT", tc)])

        def phase_A_init():
            fence()
            MSET("dve", vaugA[:, :, :, 1, :], 1.0, r=["PH"], w=["vaug"])
            for kv in range(2):
                for tc_ in range(5):
                    t0_, wd_ = TCH[tc_]
                    MSET("dve", kz[kv][0][64:128, t0_:t0_ + wd_], 0.0, r=["PH"], w=[("kdup", kv, tc_)])
                    MSET("dve", kz[kv][1][0:64, t0_:t0_ + wd_], 0.0, r=["PH"], w=[("kdup", kv, tc_)])

        def phase_A(l, b, last):
            pj_list[0] = [6, 0, 2]
            ms_list[0] = [7, 1, 3, 4, 5]
            kunits = []
            for kv in range(2):
                kp, kpk = WS.next(("in", l, P_K0 + kv))
                for tc in range(5):
                    t0, wd = TCH[tc]
                    kunits.append(nr_stages(kp, kpk, tc, gains[:, l, 1:2],
                                            [(kz[kv][0][0:64, t0:t0 + wd], slice(0, 64)),
                                             (kz[kv][1][64:128, t0:t0 + wd], slice(64, 128))],
                                            ("kdup", kv, tc), True, extra_r=["PH"], five=True))
            run_pipelined(kunits)
            vp, vpk = WS.next(("in", l, P_V))
            for g4 in range(5):
                bank, bkey = next_pj()
                tl = list(range(g4 * 4, min(18, g4 * 4 + 4)))
                for qi, ti in enumerate(tl):
                    for k in range(8):
                        MM(bank[:, qi * 128:(qi + 1) * 128], lhsT=hT[:, k, ti * 128:(ti + 1) * 128], rhs=vp[:, k, :],
                           start=(k == 0), stop=(k == 7), r=[vpk, ("hT", min(ti // 4, 4))], w=[bkey])
                n = len(tl)
                CP("act", vaugA[:, tl[0]:tl[0] + n, :, 0, :],
                   bank[:, 0:n * 128].rearrange("p (t k d) -> p t k d", t=n, k=2),
                   r=[bkey, "PH"], w=["vaug"])

            def q_units(c, qp, qpk, micro=False):
                return [nr_stages(qp, qpk, tc, gains[:, l, 0:1], mix[:, c, TCH[tc][0]:TCH[tc][0] + TCH[tc][1]],
                                  ("mix", c, tc), True, micro=micro, five=not micro) for tc in range(5)]

            qp, qpk = WS.next(("in", l, P_Q + 0))
            run_pipelined(q_units(0, qp, qpk)[:(4 if last else 5)])
            pj_list[0] = [6]
            ms_list[0] = [7]
            for c in range(4):
                kv = c // 2
                gp, gpk = WS.next(("in", l, P_G + c))
                side = []
                if c < 3:
                    qpn, qpnk = WS.next(("in", l, P_Q + c + 1))
                    side = q_units(c + 1, qpn, qpnk, micro=True)
                for tc in range(4 if last else 5):
                    t0, wd = TCH[tc]
                    tiles = list(range(18)) if tc < 4 else [16, 17]
                    nst = len(tiles)
                    sched = {}
                    gh = {}
                    gst = gate_stages(gp, gpk, tc, gh, micro=True)
                    if tc < 4:
                        if side and tc < 3:
                            place(sched, 0, side[tc])
                            place(sched, 9, gst[0:8], per_step=2)
                            place(sched, 14, gst[8:])
                        elif side:
                            place(sched, 0, side[3])
                            if not last:
                                place(sched, 9, side[4][0:8], per_step=2)
                                place(sched, 13, side[4][8:9])
                                place(sched, 16, side[4][9:])
                            place(sched, 14, gst[0:8], per_step=2)
                            place(sched, 18, gst[8:])
                        else:
                            place(sched, 4, gst[0:8])
                            place(sched, 13, gst[8:])
                    else:
                        place(sched, 0, gst[0:8], per_step=8)
                        place(sched, 1, gst[8:])
                    LAG = 2
                    pq = []
                    for s_ in range(nst + LAG):
                        cur = None
                        if s_ < nst:
                            t = tiles[s_]
                            scd, sk = SCD[s_ % 2], SCDK[s_ % 2]
                            for hh in range(2):
                                MM(scd[:, 512 * hh:512 * hh + wd], lhsT=kz[kv][hh][:, t * 128:(t + 1) * 128],
                                   rhs=mix[:, c, t0:t0 + wd],
                                   r=[("kdup", kv, min(t // 4, 4)), ("mix", c, tc), "PH"], w=[sk[hh]])
                            ptd, ptk = PT.get()
                            if wd == 512:
                                ACT(ptd[:, :], scd[:, :], AF.Exp, r=sk, w=[ptk], scale=0.125)
                            else:
                                ACT(ptd[:, :].rearrange("p (h w) -> p h w", h=2)[:, :, 0:wd],
                                    scd.rearrange("p (h w) -> p h w", h=2)[:, :, 0:wd], AF.Exp, r=sk, w=[ptk], scale=0.125)
                            cur = (t, ptd, ptk)
                        pq.append(cur)
                        if len(pq) > LAG and pq[0] is not None:
                            t_, ptd_, ptk_ = pq[0]
                            for hh in range(2):
                                MM(OA[hh][:, :wd], lhsT=vaugA[:, t_, kv, :, :].rearrange("p a d -> p (a d)"),
                                   rhs=ptd_[:, 512 * hh:512 * hh + wd], start=(t_ == tiles[0]), stop=(t_ == tiles[-1]),
                                   r=[ptk_, "vaug", "PH"], w=[OAK[hh]])
                        if len(pq) > LAG:
                            pq.pop(0)
                        for f in sched.get(s_, ()):
                            f()
                    for s_x in sorted(k_ for k_ in sched if k_ >= nst + LAG):
                        for f in sched[s_x]:
                            f()
                    finalize_norm(tc, c, gh["g"], gh["gk"])
            pj_list[0] = [6, 0]
            ms_list[0] = [7, 1]
            y_update(l, b, [0, 1, 2, 3], "outA", last)

        def phase_B(l, b, last):
            fence()
            pj_list[0] = [6, 0, 7, 1, 2, 3, 4, 5]
            pw_t, pw_k = Wt.get()
            pw32 = pw_t[:, 0:256].rearrange("p (c d) -> p c d", c=2)
            DMA(pw32, poolw_d[l], w=[pw_k], grp="pw")
            CP("pool", poolwb[:], pw32, r=[pw_k], w=["poolwb"])
            for ap in (zbuf[:, 0:16], zbuf[:, 2064:2080], zbuf[:, 2336:2352]):
                MSET("pool", ap, 0.0, r=["PH"], w=["zbuf"])
            segs = [(16, 0, SEQ), (2080, SEQ, NCTX)]
            for cz in range(2):
                zp, zpk = WS.next(("in", l, P_Z + cz))
                for tc in range(5):
                    t0, wd = TCH[tc]
                    zo = 16 + t0 if tc < 4 else 2080
                    bank, bkey = next_pj()
                    proj(zp, zpk, tc, bank, bkey)
                    CP("act", zbuf[:, zo:zo + wd], bank[:, :wd], r=[bkey, "PH"], w=["zbuf"])
                N = ZN
                TT("pool", tA[:, 1:N], zbuf[:, 1:N], zbuf[:, 0:N - 1], ALU.add, r=["zbuf", "PH"], w=["tA"])
                TT("pool", tB[:, 2:N - 1], tA[:, 3:N], tA[:, 1:N - 2], ALU.add, r=["tA", "PH"], w=["tB"])
                if cz == 1:
                    TT("pool", tA[:, 4:N - 3], tB[:, 6:N - 1], tB[:, 2:N - 5], ALU.add, r=["tB", "PH"], w=["tA"])
                    TT("pool", tB[:, 8:N - 7], tA[:, 12:N - 3], tA[:, 4:N - 11], ALU.add, r=["tA", "PH"], w=["tB"])
                srcs = [(tA, "tA", 0), (tB, "tB", 64)]
                for (src, skey, p0) in srcs:
                    pr = slice(p0, p0 + 64)
                    for (zo, to, ln_) in segs:
                        tcs = [0, 1, 2, 3] if to == 0 else [4]
                        wkeys = [("mix", cz, tc) for tc in tcs]
                        STT(mix[pr, cz, to:to + ln_], src[pr, zo:zo + ln_], pooltab[pr, cz, 0:1], zbuf[pr, zo:zo + ln_],
                            ALU.mult, ALU.subtract, r=[skey, "zbuf", "pooltab", "PH"], w=wkeys)
                        for (eo, tab0) in ((0, 1), (ln_ - 8, 9)):
                            et, etk = Wt.get()
                            TT("pool", et[pr, 0:8], src[pr, zo + eo:zo + eo + 8], pooltab[pr, cz, tab0:tab0 + 8], ALU.mult,
                               r=[skey, "pooltab", "PH"], w=[etk])
                            TT("pool", mix[pr, cz, to + eo:to + eo + 8], et[pr, 0:8], zbuf[pr, zo + eo:zo + eo + 8],
                               ALU.subtract, r=[etk, "zbuf", "PH"], w=[wkeys[0] if eo == 0 else wkeys[-1]])
                gp, gpk = WS.next(("in", l, P_BG + cz))
                for tc in range(5):
                    t0, wd = TCH[tc]
                    bank, bkey = next_pj()
                    MM(bank[:, :wd], lhsT=poolwb[:, cz, :], rhs=mix[:, cz, t0:t0 + wd], r=["poolwb", ("mix", cz, tc)], w=[bkey])
                    bank2, bkey2 = next_pj()
                    proj(gp, gpk, tc, bank2, bkey2)
                    g, gk = silu_gate(bank2, bkey2, wd)
                    STT(mix[:, cz, t0:t0 + wd], bank[:, :wd], pscale[:, l, cz:cz + 1], g[:, :wd], ALU.mult, ALU.mult,
                        r=[bkey, gk, "pscale"], w=[("mix", cz, tc)])
            pj_list[0] = [6, 0]

        def phase_C(l, b, last):
            for c2 in range(2):
                mc = 2 + c2
                fence()
                MSET("pool", nvaug[:, :, :, 1, :], 1.0, r=["PH"], w=["nvaug"])
                for tc_ in range(5):
                    t0_, wd_ = TCH[tc_]
                    MSET("pool", nkz[0][64:128, t0_:t0_ + wd_], 0.0, r=["PH"], w=[("nkT", tc_)])
                    MSET("pool", nkz[1][0:64, t0_:t0_ + wd_], 0.0, r=["PH"], w=[("nkT", tc_)])
                DMA(ebuf[:], ebias_d[l, c2], r=["PH"], w=["ebuf"], grp="eb")
                TS("dve", ebuf[:], ebuf[:], 8.0, ALU.mult, r=["ebuf", "PH"], w=["ebuf"])
                pj_list[0] = [6, 0, 2]
                ms_list[0] = [7, 1, 3, 4, 5]
                kp, kpk = WS.next(("in", l, P_NK + c2))
                qp, qpk = WS.next(("in", l, P_NQ + c2))
                units = []
                for tc in range(5):
                    t0, wd = TCH[tc]
                    units.append(nr_stages(kp, kpk, tc, gains[:, l, 3:4],
                                           [(nkz[0][0:64, t0:t0 + wd], slice(0, 64)), (nkz[1][64:128, t0:t0 + wd], slice(64, 128))],
                                           ("nkT", tc), False, extra_r=["PH"], five=True))
                for tc in range(4 if last else 5):
                    t0, wd = TCH[tc]
                    units.append(nr_stages(qp, qpk, tc, gains[:, l, 2:3], mix[:, mc, t0:t0 + wd], ("mix", mc, tc), False,
                                           five=True))
                run_pipelined(units)
                vp, vpk = WS.next(("in", l, P_NV + c2))
                toks = [ti * 128 for ti in range(18)] + [64 + 128 * o for o in range(15)]
                for g4 in range(9):
                    tl = list(range(g4 * 4, min(33, g4 * 4 + 4)))
                    bank, bkey = next_pj()
                    for qi, ti in enumerate(tl):
                        tk0 = toks[ti]
                        rk = sorted(set([("hT", min(tk0 // 512, 4)), ("hT", min((tk0 + 127) // 512, 4))]))
                        for k in range(8):
                            MM(bank[:, qi * 128:(qi + 1) * 128], lhsT=hT[:, k, tk0:tk0 + 128], rhs=vp[:, k, :],
                               start=(k == 0), stop=(k == 7), r=[vpk] + rk, w=[bkey])
                    n = len(tl)
                    CP("act", nvaug[:, tl[0]:tl[0] + n, :, 0, :],
                       bank[:, 0:n * 128].rearrange("p (t k d) -> p t k d", t=n, k=2),
                       r=[bkey, "PH"], w=["nvaug"])
                gp, gpk = WS.next(("in", l, P_NG + c2))
                ev = ebuf.rearrange("p h (q two) c -> p h q two c", two=2)
                pj_list[0] = [6]
                ms_list[0] = [7]
                LA = 3
                ptc = [0]
                pt_all = [("PT", j_) for j_ in range(3)] + [("PT", j_, h_) for j_ in range(3) for h_ in range(2)]
                MSET("pool", dummy[:, :], 0.0, w=pt_all + ["dummy"])
                for tc in range(4):
                    items = [(hh, rr_) for rr_ in range(8) for hh in (0, 1)]
                    pendq = []
                    for s in range(len(items) + LA):
                        cur = None
                        if s < len(items):
                            hh, rr_ = items[s]
                            r = tc * 8 + rr_
                            r0 = min(max(r - 4, 0), 24)
                            kb = 64 * r0
                            pr = slice(64 * hh, 64 * hh + 64)
                            sc, sck = BK[s % 4], ("PS", s % 4)
                            qap = mix[:, mc, 64 * r:64 * r + 64]
                            rkeys = sorted(set(("nkT", min((kb + 128 * i) // 512, 3)) for i in range(4)) |
                                           set(("nkT", min((kb + 128 * i + 127) // 512, 3)) for i in range(4)))
                            for i in range(4):
                                MM(sc[:, 64 * i:64 * i + 64], lhsT=nkz[hh][:, kb + 128 * i:kb + 128 * i + 128], rhs=qap,
                                   r=list(rkeys) + [("mix", mc, tc), "PH"], w=[sck])
                            for i2 in range(2):
                                MM(sc[:, 256 + 64 * i2:256 + 64 * i2 + 64],
                                   lhsT=nkz[hh][:, SEQ + 128 * i2:SEQ + 128 * i2 + 128], rhs=qap,
                                   r=[("nkT", 4), ("mix", mc, tc), "PH"], w=[sck])
                            s0 = r0 - r + 7
                            esl = ev[:, hh, s0 // 2:s0 // 2 + 4, s0 % 2, :]
                            sc3 = sc[:, 0:256].rearrange("p (i c) -> p i c", c=64)
                            TT("dve", sc3, sc3, esl, ALU.add, r=[sck, "ebuf", "PH"], w=[sck])
                            pj_ = ptc[0] % 6
                            ptc[0] += 1
                            pt, ptk = PT.tiles[pj_ // 2][:, 512 * (pj_ % 2):512 * (pj_ % 2) + 512], ("PT", pj_ // 2, pj_ % 2)
                            ACT(pt[:, 0:384], sc[:, 0:384], AF.Exp, r=[sck], w=[ptk], scale=0.125)
                            if r0 % 2 == 0:
                                vts = [r0 // 2 + i for i in range(4)]
                            else:
                                vts = [18 + (r0 - 1) // 2 + i for i in range(4)]
                            vts += [16, 17]
                            cur = (hh, rr_, pt, ptk, vts)
                        pendq.append(cur)
                        if len(pendq) > LA and pendq[0] is not None:
                            hh_, rr2, pt_, ptk_, vts_ = pendq[0]
                            for idx, vt in enumerate(vts_):
                                MM(OA[hh_][:, 64 * rr2:64 * rr2 + 64],
                                   lhsT=nvaug[:, vt, hh_, :, :].rearrange("p a d -> p (a d)"),
                                   rhs=pt_[:, 64 * idx:64 * idx + 64], start=(idx == 0), stop=(idx == 5),
                                   r=[ptk_, "nvaug", "PH"], w=[OAK[hh_]])
                        if len(pendq) > LA:
                            pendq.pop(0)
                    finalize_attn(gp, gpk, tc, mc)
                pj_list[0] = [6, 0]
                ms_list[0] = [7, 1]
                for hh in (range(2) if not last else ()):
                    pr = slice(64 * hh, 64 * hh + 64)
                    sc, sck = SC[hh], SCK[hh]
                    for i2 in range(2):
                        MM(sc[:, 256 * i2:256 * i2 + 256], lhsT=nkz[hh][:, SEQ + 128 * i2:SEQ + 128 * i2 + 128],
                           rhs=mix[:, mc, SEQ:SEQ + NCTX], r=[("nkT", 4), ("mix", mc, 4), "PH"], w=[sck])
                    pj_ = ptc[0] % 6
                    ptc[0] += 1
                    pt, ptk = PT.tiles[pj_ // 2][:, 512 * (pj_ % 2):512 * (pj_ % 2) + 512], ("PT", pj_ // 2, pj_ % 2)
                    ACT(pt[:, 0:512], sc[:, :], AF.Exp, r=[sck], w=[ptk], scale=0.125)
                    for i2 in range(2):
                        MM(OA[hh][:, 0:256], lhsT=nvaug[:, 16 + i2, hh, :, :].rearrange("p a d -> p (a d)"),
                           rhs=pt[:, 256 * i2:256 * i2 + 256], start=(i2 == 0), stop=(i2 == 1),
                           r=[ptk, "nvaug", "PH"], w=[OAK[hh]])
                if not last:
                    finalize_attn(gp, gpk, 4, mc)
                MSET("pool", dummy[:, :], 0.0, w=pt_all + ["dummy"])
            y_update(l, b, ([0, 1, 2, 3] if "B" in phases else [2, 3]), "outC", last)

        final = []
        for b in range(n_batch):
            fence()
            for i in range(18):
                src = x_d[b, i * 128:(i + 1) * 128, :] if i < 16 else ctx_d[b, (i - 16) * 128:(i - 15) * 128, :]
                xs, xsk = xstage[i % NXS], ("xs", i % NXS)
                DMA(xs, src, r=["PH"], w=[xsk], grp="xs%d" % (i % NXS))
                tc = min(i // 4, 4)
                for half in range(2):
                    bank, bkey = next_pj()
                    for kk in range(4):
                        k = half * 4 + kk
                        S.add("pe", (lambda o, i_: (lambda e: e.transpose(out=o, in_=i_, identity=ident[:])))(
                            bank[:, kk * 128:(kk + 1) * 128], xs[:, k * 128:(k + 1) * 128]),
                            [xsk, "ident", "PH"], [bkey])
                    CP("dve" if half == 0 else "act", xT[:, half * 4:half * 4 + 4, i * 128:(i + 1) * 128],
                       bank[:, :].rearrange("p (k t) -> p k t", t=128), r=[bkey], w=[("xT", tc)])
            for l in range(n_layers):
                last = (l == NL - 1)
                if "A" in phases and stage >= 2:
                    phase_A_init()
                if stage >= 1:
                    xnorm(l, b)
                if "A" in phases and stage >= 2:
                    phase_A(l, b, last)
                if "B" in phases:
                    phase_B(l, b, last)
                if "C" in phases:
                    phase_C(l, b, last)
            fence()
            for i in range(16):
                xs, xsk = xstage[i % NXS], ("xs", i % NXS)
                tc = i // 4
                for half in range(2):
                    bank, bkey = next_pj()
                    for kk in range(4):
                        k = half * 4 + kk
                        S.add("pe", (lambda o, i_: (lambda e: e.transpose(out=o, in_=i_, identity=ident[:])))(
                            bank[:, kk * 128:(kk + 1) * 128], xT[:, k, i * 128:(i + 1) * 128]),
                            [("xT", tc), "ident"], [bkey])
                    CP("dve" if half == 0 else "act", xs[:, half * 512:(half + 1) * 512], bank[:, :],
                       r=[bkey, "PH"], w=[xsk])
                final.append(DMA(out_d[b, i * 128:(i + 1) * 128, :], xs, r=[xsk, "PH"], grp="o%d" % (i % NXS)))
        assert stage < 99 or WS.consumed == len(WS.specs), (WS.consumed, len(WS.specs))
        S.emit(nc, final_waits=final[-NXS:])
    return nc


def _f32(a):
    return np.ascontiguousarray(np.asarray(a, dtype=np.float32))


def prep_shared(c_ctx, norm_gain, w_mod, b_mod, w_in, att_q_gain, att_k_gain, pool_w, pool_scale,
                na_q_gain, na_k_gain, na_rpb, w_out):
    sh = {}
    w_mod = _f32(w_mod)
    sh["wmod"] = _f32(w_mod.reshape(NL, 8, 128, 6, 512).transpose(0, 3, 2, 1, 4))
    sh["bmod"] = _f32(_f32(b_mod).reshape(NL, 24, 128).transpose(2, 0, 1))
    sh["ngain"] = _f32(_f32(norm_gain).reshape(NL, 8, 128).transpose(2, 0, 1))
    w_in = _f32(w_in)
    cols = []
    cols.append(np.r_[512:576, 512:576])
    cols.append(np.r_[576:640, 576:640])
    cols.append(np.r_[640:768])
    for c in range(4):
        cols.append(np.r_[c * 128:(c + 1) * 128])
    for c in range(4):
        cols.append(np.r_[768 + c * 128:768 + (c + 1) * 128])
    for base in (1280, 1536, 2048, 2304, 1792, 2560):
        for c in range(2):
            cols.append(np.r_[base + c * 128:base + (c + 1) * 128])
    assert len(cols) == NPAN
    win = np.empty((NL, NPAN, 128, 8, 128), np.float32)
    for pi, cc in enumerate(cols):
        win[:, pi] = w_in[:, :, cc].reshape(NL, 8, 128, 128).transpose(0, 2, 1, 3)
    sh["win"] = win
    sh["wout"] = _f32(_f32(w_out).reshape(NL, 8, 128, 8, 128).transpose(0, 3, 2, 1, 4))
    g = np.stack([_f32(att_q_gain), _f32(att_k_gain), _f32(na_q_gain), _f32(na_k_gain)], axis=-1)
    sh["gains"] = _f32(np.tile(g, (1, 2, 1)).transpose(1, 0, 2))
    p = np.arange(128)
    d = p % 64
    inv_freq = (np.float32(10000.0) ** (-np.arange(16, dtype=np.float32) / np.float32(16))).astype(np.float32)
    f = inv_freq[d % 16]
    sign = np.where((d % 32) < 16, -1.0, 1.0).astype(np.float32)
    isrow = d < 32
    rows = np.arange(32, dtype=np.float32)
    colsv = np.arange(64, dtype=np.float32)
    angA = (rows[None, :] * f[:, None]).astype(np.float32)
    angB = (colsv[None, :] * f[:, None]).astype(np.float32)
    cosA = np.where(isrow[:, None], np.cos(angA), 1.0)
    cosB = np.where(isrow[:, None], 1.0, np.cos(angB))
    sinA = np.where(isrow[:, None], sign[:, None] * np.sin(angA), 1.0)
    sinB = np.where(isrow[:, None], 1.0, sign[:, None] * np.sin(angB))
    sh["ropetab"] = _f32(np.concatenate([cosA, cosB, sinA, sinB], axis=1))
    partner = np.where((p % 32) < 16, p + 16, p - 16)
    rm = np.zeros((128, 128), np.float32)
    rm[partner, p] = 1.0
    sh["rmat"] = rm
    sh["ident"] = np.eye(128, dtype=np.float32)
    pw = _f32(pool_w)
    poolw = np.zeros((NL, 128, 2, 128), np.float32)
    for cz in range(2):
        for j in range(2):
            poolw[:, j * 64:(j + 1) * 64, cz, j * 64:(j + 1) * 64] = pw[:, 2 * cz + j]
    sh["poolw"] = poolw
    sh["pscale"] = _f32(_f32(pool_scale).reshape(NL, 2, 128).transpose(2, 0, 1))
    pt = np.zeros((128, 2, 17), np.float32)
    for cz in range(2):
        for j in range(2):
            w = (2, 4, 8, 16)[2 * cz + j]
            t = np.arange(8)
            cs = np.minimum(w, t + w // 2).astype(np.float32)
            ce = np.minimum(w, (8 - t) + w // 2 - 1 + 0).astype(np.float32)
            ce = np.minimum(w, 8 - t + w // 2).astype(np.float32)
            pt[j * 64:(j + 1) * 64, cz, 0] = 1.0 / w
            pt[j * 64:(j + 1) * 64, cz, 1:9] = 1.0 / cs
            pt[j * 64:(j + 1) * 64, cz, 9:17] = 1.0 / ce
    sh["pooltab"] = pt
    rpb = _f32(na_rpb)
    kc = np.arange(64)[:, None]
    cq = np.arange(64)[None, :]
    c0 = np.clip(cq - 8, 0, 48)
    valid = (kc >= c0) & (kc < c0 + 16)
    dcol = np.clip(kc - cq + 15, 0, 30)
    Dh = np.where(valid[None, None, None], rpb[:, :, :, dcol], np.float32(NEG))
    eb = np.empty((NL, 2, 128, 2, 14, 64), np.float32)
    for c2 in range(2):
        for hh in range(2):
            h = 2 * c2 + hh
            for j in range(2):
                for di in range(14):
                    eb[:, c2, j * 64:(j + 1) * 64, hh, di, :] = Dh[:, h, di + j]
    sh["ebias"] = eb
    return sh


_NC_CACHE = {}


def kernel(x, c, ctx, c_ctx, norm_gain, w_mod, b_mod, w_in, att_q_gain, att_k_gain,
           pool_w, pool_scale, na_q_gain, na_k_gain, na_rpb, w_out):
    x = _f32(x)
    c = _f32(c)
    ctx = _f32(ctx)
    c_ctx = _f32(c_ctx)
    sh = prep_shared(c_ctx, norm_gain, w_mod, b_mod, w_in, att_q_gain, att_k_gain, pool_w, pool_scale,
                     na_q_gain, na_k_gain, na_rpb, w_out)
    if "nc" not in _NC_CACHE:
        _NC_CACHE["nc"] = build_nc()
    nc = _NC_CACHE["nc"]
    in_maps = []
    for i in range(8):
        m = dict(sh)
        m["x"] = np.ascontiguousarray(x[2 * i:2 * i + 2])
        m["ctx"] = np.ascontiguousarray(ctx[2 * i:2 * i + 2])
        vecs = np.stack([c[2 * i], c[2 * i + 1], c_ctx], axis=-1)
        m["cT"] = _f32(vecs.reshape(8, 128, 3).transpose(1, 0, 2))
        in_maps.append(m)
    res = run_bass_kernel_spmd(nc, in_maps, core_ids=list(range(8)))
    return np.concatenate([np.asarray(r["out"], dtype=np.float32) for r in res.results], axis=0)
```

```python
import contextlib
import numpy as np
import concourse.bass as bass
import concourse.mybir as mybir
from concourse.bass_utils import run_bass_kernel_spmd

F32 = mybir.dt.float32
BF16 = mybir.dt.bfloat16
ALU = mybir.AluOpType
AF = mybir.ActivationFunctionType

D = 1024
NL = 4
SEQ = 2048
NCTX = 256
T = SEQ + NCTX
EPS = 1e-6
NEG = -30000.0
TCH = [(0, 512), (512, 512), (1024, 512), (1536, 512), (2048, 256)]
ENGS = ("pe", "act", "dve", "pool", "sp")
PSUM_KEYS = ("PS",)

P_K0, P_K1, P_V = 0, 1, 2
P_Q, P_G, P_Z, P_BG, P_NK, P_NV, P_NQ, P_NG = 3, 7, 11, 13, 15, 17, 19, 21
NPAN = 23


class Op:
    __slots__ = ("eng", "idx", "fn", "waits", "signal", "dma_grp", "dma_cnt", "sigval")

    def __init__(self, eng, idx, fn):
        self.eng = eng
        self.idx = idx
        self.fn = fn
        self.waits = []
        self.signal = False
        self.dma_grp = None
        self.dma_cnt = 0
        self.sigval = 0


class Sched:
    def __init__(self):
        self.q = {e: [] for e in ENGS}
        self.last_w = {}
        self.readers = {}
        self.maxwait = {e: {} for e in ENGS}
        self.dma_cnt = {}

    def add(self, eng, fn, reads=(), writes=(), dma=None):
        op = Op(eng, len(self.q[eng]), fn)
        if dma is not None:
            op.dma_grp = dma
            self.dma_cnt[dma] = self.dma_cnt.get(dma, 0) + 1
            op.dma_cnt = self.dma_cnt[dma]
        best = {}
        for k in reads:
            w = self.last_w.get(k)
            if w is not None:
                self._cand(best, w, eng)
            if isinstance(k, tuple) and k[0] in PSUM_KEYS:
                for r in self.readers.get(k, ()):
                    if r.eng != eng:
                        self._cand(best, r, eng)
        for k in writes:
            w = self.last_w.get(k)
            if w is not None:
                self._cand(best, w, eng)
            for r in self.readers.get(k, ()):
                self._cand(best, r, eng)
        mw = self.maxwait[eng]
        for src, (pos, d) in best.items():
            if mw.get(src, -1) >= pos:
                continue
            mw[src] = pos
            d.signal = True
            op.waits.append(d)
        for k in writes:
            self.last_w[k] = op
            self.readers[k] = []
        for k in reads:
            self.readers.setdefault(k, []).append(op)
        self.q[eng].append(op)
        return op

    @staticmethod
    def _cand(best, d, eng):
        if d.dma_grp is not None:
            src = ("dma", d.dma_grp)
            pos = d.dma_cnt
        else:
            if d.eng == eng and eng == "pe":
                return
            src = d.eng
            pos = d.idx
        cur = best.get(src)
        if cur is None or cur[0] < pos:
            best[src] = (pos, d)

    def emit(self, nc, final_waits=()):
        for op in final_waits:
            op.signal = True
        for e in ENGS:
            n = 0
            for op in self.q[e]:
                if op.dma_grp is None and op.signal:
                    n += 1
                    op.sigval = n
        grps = sorted(self.dma_cnt.keys())
        with contextlib.ExitStack() as st:
            sems = {}
            for e in ENGS:
                sems[e] = st.enter_context(nc.semaphore("sem_" + e))
            for g in grps:
                sems[("dma", g)] = st.enter_context(nc.semaphore("semd_" + str(g)))
            block = st.enter_context(nc.Block())

            def tok(d):
                if d.dma_grp is not None:
                    return sems[("dma", d.dma_grp)], 16 * d.dma_cnt
                return sems[d.eng], d.sigval

            def run(engname, e):
                for op in self.q[engname]:
                    for d in op.waits:
                        s, v = tok(d)
                        e.wait_ge(s, v)
                    ins = op.fn(e)
                    if op.dma_grp is not None:
                        ins.then_inc(sems[("dma", op.dma_grp)], 16)
                    elif op.signal:
                        ins.then_inc(sems[engname], 1)
                if engname == "sp":
                    for d in final_waits:
                        s, v = tok(d)
                        e.wait_ge(s, v)

            @block.tensor
            def _(e):
                run("pe", e)

            @block.scalar
            def _(e):
                run("act", e)

            @block.vector
            def _(e):
                run("dve", e)

            @block.gpsimd
            def _(e):
                run("pool", e)

            @block.sync
            def _(e):
                run("sp", e)


class Ring:
    def __init__(self, name, tiles):
        self.name = name
        self.tiles = tiles
        self.i = 0

    def get(self):
        j = self.i % len(self.tiles)
        self.i += 1
        return self.tiles[j], (self.name, j)


def build_nc(n_layers=NL, n_batch=2, phases="ABC", stage=99):
    nc = bass.Bass("TRN2", target_bir_lowering=False)

    def din(name, shape):
        return nc.dram_tensor(name, list(shape), F32, kind="ExternalInput").ap()

    x_d = din("x", [2, SEQ, D])
    ctx_d = din("ctx", [2, NCTX, D])
    cT_d = din("cT", [128, 8, 3])
    wmod_d = din("wmod", [NL, 6, 128, 8, 512])
    bmod_d = din("bmod", [128, NL, 24])
    ngain_d = din("ngain", [128, NL, 8])
    win_d = din("win", [NL, NPAN, 128, 8, 128])
    wout_d = din("wout", [NL, 8, 128, 8, 128])
    gains_d = din("gains", [128, NL, 4])
    rope_d = din("ropetab", [128, 192])
    rmat_d = din("rmat", [128, 128])
    ident_d = din("ident", [128, 128])
    poolw_d = din("poolw", [NL, 128, 2, 128])
    pscale_d = din("pscale", [128, NL, 2])
    pooltab_d = din("pooltab", [128, 2, 17])
    ebias_d = din("ebias", [NL, 2, 128, 2, 14, 64])
    out_d = nc.dram_tensor("out", [2, SEQ, D], F32, kind="ExternalOutput").ap()

    S = Sched()
    with contextlib.ExitStack() as st:
        def sb(name, shape, dt=F32):
            return st.enter_context(nc.sbuf_tensor(name, list(shape), dt))

        xT = sb("xT", [128, 8, T])
        hT = sb("hT", [128, 8, T], BF16)
        mix = sb("mix", [128, 4, T], BF16)
        PH = sb("PH", [128, 8320])
        stg = [sb("stg%d" % i, [128, 8, 128]) for i in range(2)]
        wb = [sb("wb%d" % i, [128, 8, 128], BF16) for i in range(4)]
        Wt = Ring("W", [sb("wk%d" % i, [128, 512]) for i in range(9)])
        PT = Ring("PT", [sb("pt%d" % i, [128, 1024], BF16) for i in range(3)])
        ident = sb("ident_s", [128, 128])
        rmat = sb("rmat_s", [128, 128])
        ones128 = sb("ones128", [128, 128])
        bones = sb("bones", [128, 128])
        ropet = sb("ropet", [128, 192])
        gains = sb("gains_s", [128, NL, 4])
        ngain = sb("ngain_s", [128, NL, 8])
        bmod = sb("bmod_s", [128, NL, 24])
        pscale = sb("pscale_s", [128, NL, 2])
        pooltab = sb("pooltab_s", [128, 2, 17])
        cT = sb("cT_s", [128, 8, 3])
        scT = sb("scT", [128, 8, 3])
        mod_all = sb("mod_all", [128, NL, 24, 3])
        gs_all = sb("gs_all", [128, NL, 8, 3])
        poolwb = sb("poolwb", [128, 2, 128], BF16)
        dummy = sb("fence_dummy", [128, 8])

        def ps(name):
            return st.enter_context(nc.psum_tensor(name, [128, 512], F32))

        PSALL = st.enter_context(nc.psum_tensor("PSALL", [128, 4096], F32))
        BK = [PSALL[:, 512 * i:512 * (i + 1)] for i in range(8)]
        SC = [BK[2], BK[3]]
        SCK = [("PS", 2), ("PS", 3)]
        OA = [BK[4], BK[5]]
        OAK = [("PS", 4), ("PS", 5)]
        SCD = [PSALL[:, 0:1024], PSALL[:, 1024:2048]]
        SCDK = [[("PS", 0), ("PS", 1)], [("PS", 2), ("PS", 3)]]
        pj_list = [[6, 0]]
        ms_list = [[7, 1]]

        PHb = PH[:, :].bitcast(BF16)
        kz = [[PHb[:, (2 * kv + hh) * T:(2 * kv + hh + 1) * T] for hh in range(2)] for kv in range(2)]
        vaugA = PHb[:, 4 * T:4 * T + 18 * 256].rearrange("p (t k a d) -> p t k a d", t=18, k=2, a=2)
        nkz = [PHb[:, hh * T:(hh + 1) * T] for hh in range(2)]
        nvaug = PHb[:, 2 * T:2 * T + 33 * 256].rearrange("p (t k a d) -> p t k a d", t=33, k=2, a=2)
        e_off = (2 * T + 33 * 256 + 1) // 2
        ebuf = PH[:, e_off:e_off + 2 * 14 * 64].rearrange("p (h d c) -> p h d c", h=2, d=14)
        assert e_off + 2 * 14 * 64 <= 8320
        ZN = 2352
        zbuf = PH[:, 0:ZN]
        tA = PH[:, ZN:2 * ZN]
        tB = PH[:, 2 * ZN:3 * ZN]
        assert 3 * ZN <= 8320
        NXS = 6
        xstage = [PH[:, i * 1024:(i + 1) * 1024] for i in range(NXS)]

        def MM(out, lhsT, rhs, start=True, stop=True, r=(), w=()):
            S.add("pe", lambda e: e.matmul(out, lhsT=lhsT, rhs=rhs, start=start, stop=stop), r, w)

        def ACT(out, in_, func, r=(), w=(), scale=None, bias=None):
            kw = {}
            if scale is not None:
                kw["scale"] = scale
            if bias is not None:
                kw["bias"] = bias
            S.add("act", lambda e: e.activation(out=out, in_=in_, func=func, **kw), r, w)

        def TT(eng, out, in0, in1, op, r=(), w=()):
            S.add(eng, lambda e: e.tensor_tensor(out=out, in0=in0, in1=in1, op=op), r, w)

        def TS(eng, out, in0, s1, op0, r=(), w=(), s2=None, op1=None):
            if op1 is None:
                S.add(eng, lambda e: e.tensor_scalar(out=out, in0=in0, scalar1=s1, scalar2=None, op0=op0), r, w)
            else:
                S.add(eng, lambda e: e.tensor_scalar(out=out, in0=in0, scalar1=s1, scalar2=s2, op0=op0, op1=op1), r, w)

        def STT(out, in0, scalar, in1, op0, op1, r=(), w=()):
            S.add("dve", lambda e: e.scalar_tensor_tensor(out=out, in0=in0, scalar=scalar, in1=in1, op0=op0, op1=op1), r, w)

        def CP(eng, out, in_, r=(), w=()):
            if eng == "act":
                S.add("act", lambda e: e.activation(out=out, in_=in_, func=AF.Copy), r, w)
            else:
                S.add(eng, lambda e: e.tensor_copy(out=out, in_=in_), r, w)

        def MSET(eng, ap, val, r=(), w=()):
            S.add(eng, lambda e: e.memset(ap, val), r, w)

        def DMA(out, in_, r=(), w=(), grp="m"):
            return S.add("sp", lambda e: e.dma_start(out=out, in_=in_), r, w, dma=grp)

        def fence():
            MSET("pool", dummy[:, :], 0.0, r=(), w=["PH", "dummy"])

        class WStream:
            def __init__(self):
                self.specs = []
                self.issued = 0
                self.consumed = 0

            def _issue(self):
                n = self.issued
                ap, nk, _tag = self.specs[n]
                si, wi = n % 2, n % 4
                DMA(stg[si][:, 0:nk, :], ap, w=[("stg", si)], grp="stg%d" % si)
                in_attn = (_tag[0] == "outA") or (_tag[0] == "in" and P_Q < _tag[2] < P_Z)
                CP("pool" if in_attn else "act", wb[wi][:, 0:nk, :], stg[si][:, 0:nk, :], r=[("stg", si)], w=[("wb", wi)])
                self.issued += 1

            def next(self, check=None):
                while self.issued < min(len(self.specs), self.consumed + 3 - 1):
                    self._issue()
                n = self.consumed
                if check is not None:
                    assert self.specs[n][2] == check, (self.specs[n][2], check)
                self.consumed += 1
                return wb[n % 4], ("wb", n % 4)

        WS = WStream()

        def add_spec(ap, nk, tag):
            WS.specs.append((ap, nk, tag))

        for b in range(n_batch):
            for l in range(n_layers):
                if "A" in phases:
                    for pid in (P_K0, P_K1, P_V):
                        add_spec(win_d[l, pid], 8, ("in", l, pid))
                    add_spec(win_d[l, P_Q], 8, ("in", l, P_Q))
                    for c in range(4):
                        add_spec(win_d[l, P_G + c], 8, ("in", l, P_G + c))
                        if c < 3:
                            add_spec(win_d[l, P_Q + c + 1], 8, ("in", l, P_Q + c + 1))
                    for m in range(8):
                        add_spec(wout_d[l, m, :, 0:4, :], 4, ("outA", l, m))
                if "B" in phases:
                    for cz in range(2):
                        add_spec(win_d[l, P_Z + cz], 8, ("in", l, P_Z + cz))
                        add_spec(win_d[l, P_BG + cz], 8, ("in", l, P_BG + cz))
                if "C" in phases:
                    for c2 in range(2):
                        for pid in (P_NK, P_NQ, P_NV, P_NG):
                            add_spec(win_d[l, pid + c2], 8, ("in", l, pid + c2))
                    for m in range(8):
                        if "B" in phases:
                            add_spec(wout_d[l, m, :, 4:8, :], 4, ("outC", l, m))
                        else:
                            add_spec(wout_d[l, m, :, 6:8, :], 2, ("outC", l, m))

        DMA(ident[:], ident_d[:], w=["ident"], grp="c_ident")
        DMA(rmat[:], rmat_d[:], w=["rmat"], grp="c_rmat")
        DMA(ropet[:], rope_d[:], w=["ropet"], grp="c_ropet")
        DMA(gains[:], gains_d[:], w=["gains"], grp="c_gains")
        DMA(ngain[:], ngain_d[:], w=["ngain"], grp="c_ngain")
        DMA(bmod[:], bmod_d[:], w=["bmod"], grp="c_bmod")
        DMA(pscale[:], pscale_d[:], w=["pscale"], grp="c_pscale")
        DMA(pooltab[:], pooltab_d[:], w=["pooltab"], grp="c_pooltab")
        DMA(cT[:], cT_d[:], w=["cT"], grp="c_cT")
        MSET("pool", ones128[:], 1.0, w=["ones"])
        MSET("pool", bones[:], 0.0, w=["bones"])
        MSET("pool", bones[0:64, 0:64], 1.0, w=["bones"])
        MSET("pool", bones[64:128, 64:128], 1.0, w=["bones"])
        ACT(scT[:], cT[:], AF.Exp, r=["cT"], w=["scT"], scale=-1.0)
        ACT(scT[:], scT[:], AF.Ln, r=["scT"], w=["scT"], bias=1.0)
        ACT(scT[:], scT[:], AF.Exp, r=["scT"], w=["scT"], scale=-1.0)
        TT("dve", scT[:], scT[:], cT[:], ALU.mult, r=["scT", "cT"], w=["scT"])
        for l in range(n_layers):
            rows = []
            for cc in range(6):
                i = l * 6 + cc
                si = i % 2
                wslot = PH[:, si * 4096:(si + 1) * 4096].rearrange("p (k c) -> p k c", k=8)
                DMA(wslot, wmod_d[l, cc], r=["PH"], w=[("wms", si)], grp="wms%d" % si)
                bank, bkey = BK[7 if cc % 2 == 0 else 6], ("PS", 7 if cc % 2 == 0 else 6)
                for k in range(8):
                    MM(bank[0:3, :], lhsT=scT[:, k, :], rhs=wslot[:, k, :],
                       start=(k == 0), stop=(k == 7), r=[("wms", si), "scT", "PH"], w=[bkey])
                rt, rtk = Wt.get()
                CP("dve", rt[0:3, :], bank[0:3, :], r=[bkey], w=[rtk])
                rows.append((rt, rtk))
            for m in range(24):
                rt, rtk = rows[m // 4]
                off = (m % 4) * 128
                S.add("pe", (lambda o, i_: (lambda e: e.transpose(out=o, in_=i_, identity=ident[0:3, 0:3])))(
                    BK[1][:, m * 4:m * 4 + 3], rt[0:3, off:off + 128]), [rtk, "ident"], [("PS", 1)])
            msv = BK[1][:, 0:96].rearrange("p (m f) -> p m f", f=4)
            for v in range(3):
                TT("dve", mod_all[:, l, :, v], msv[:, :, v], bmod[:, l, :], ALU.add,
                   r=[("PS", 1), "bmod"], w=["mod"])
            for v in range(3):
                STT(gs_all[:, l, :, v], mod_all[:, l, 8:16, v], 1.0, ngain[:, l, :], ALU.add, ALU.mult,
                    r=["mod", "ngain"], w=["mod"])

        pjc = [0]

        def next_pj():
            lst = pj_list[0]
            i = lst[pjc[0] % len(lst)]
            pjc[0] += 1
            return BK[i], ("PS", i)

        msc = [0]

        def next_ms():
            lst = ms_list[0]
            i = lst[msc[0] % len(lst)]
            msc[0] += 1
            return BK[i], ("PS", i)

        def proj(panel, pkey, tc, bank, bkey):
            t0, wd = TCH[tc]
            for k in range(8):
                MM(bank[:, :wd], lhsT=panel[:, k, :], rhs=hT[:, k, t0:t0 + wd],
                   start=(k == 0), stop=(k == 7), r=[pkey, ("hT", tc)], w=[bkey])

        def xnorm(l, b):
            state = {}

            def stage1(tc):
                t0, wd = TCH[tc]
                ms, msk = next_ms()
                for k in range(8):
                    sq, sqk = Wt.get()
                    if k in (0, 3, 6):
                        TT("pool", sq[:, :wd], xT[:, k, t0:t0 + wd], xT[:, k, t0:t0 + wd], ALU.mult,
                           r=[("xT", tc)], w=[sqk])
                    else:
                        ACT(sq[:, :wd], xT[:, k, t0:t0 + wd], AF.Square, r=[("xT", tc)], w=[sqk])
                    MM(ms[:, :wd], lhsT=ones128[:], rhs=sq[:, :wd], start=(k == 0), stop=(k == 7),
                       r=[sqk, "ones"], w=[msk])
                state[tc] = (ms, msk)

            def stage2(tc):
                t0, wd = TCH[tc]
                v = b if tc < 4 else 2
                ms, msk = state[tc]
                ln, lnk = Wt.get()
                ACT(ln[:, :wd], ms[:, :wd], AF.Ln, r=[msk], w=[lnk], scale=1.0 / D, bias=EPS)
                ACT(ms[:, :wd], ln[:, :wd], AF.Exp, r=[lnk], w=[msk], scale=-0.5)
                for k in range(8):
                    t, tk = Wt.get()
                    TT("dve", t[:, :wd], xT[:, k, t0:t0 + wd], ms[:, :wd], ALU.mult,
                       r=[("xT", tc), msk], w=[tk])
                    if k in (1, 4, 7):
                        TS("pool", hT[:, k, t0:t0 + wd], t[:, :wd], gs_all[:, l, k, v:v + 1], ALU.mult,
                           s2=mod_all[:, l, k, v:v + 1], op1=ALU.add, r=[tk, "mod"], w=[("hT", tc)])
                    else:
                        ACT(hT[:, k, t0:t0 + wd], t[:, :wd], AF.Identity, r=[tk, "mod"], w=[("hT", tc)],
                            scale=gs_all[:, l, k, v:v + 1], bias=mod_all[:, l, k, v:v + 1])

            for i in range(6):
                if i < 5:
                    stage1(i)
                if i >= 1:
                    stage2(i - 1)

        def nr_stages(panel, pkey, tc, gcol, dst, dkey, rope, extra_r=(), micro=False, five=False):
            t0, wd = TCH[tc]
            st_ = {}

            hw_ = wd // 2

            def s0():
                st_["bank"], st_["bkey"] = next_pj()
                proj(panel, pkey, tc, st_["bank"], st_["bkey"])

            def pk(k):
                def f():
                    if k == 0:
                        st_["bank"], st_["bkey"] = next_pj()
                    MM(st_["bank"][:, :wd], lhsT=panel[:, k, :], rhs=hT[:, k, t0:t0 + wd],
                       start=(k == 0), stop=(k == 7), r=[pkey, ("hT", tc)], w=[st_["bkey"]])
                return f

            def s1e():
                bank, bkey = st_["bank"], st_["bkey"]
                qg, qgk = Wt.get()
                sq, sqk = Wt.get()
                TS("dve", qg[:, :wd], bank[:, :wd], gcol, ALU.mult, r=[bkey, "gains"], w=[qgk])
                ACT(sq[:, :wd], bank[:, :wd], AF.Square, r=[bkey], w=[sqk])
                st_.update(qg=qg, qgk=qgk, sq=sq, sqk=sqk)

            def s1m():
                ms, msk = next_ms()
                MM(ms[:, :hw_], lhsT=bones[:], rhs=st_["sq"][:, :hw_], r=[st_["sqk"], "bones"], w=[msk])
                st_.update(ms=ms, msk=msk)

            def s1a():
                s1e()
                s1m()

            def s1b():
                MM(st_["ms"][:, hw_:wd], lhsT=bones[:], rhs=st_["sq"][:, hw_:wd], r=[st_["sqk"], "bones"], w=[st_["msk"]])

            def s1():
                s1a()
                s1b()

            def s2e():
                ms, msk = st_["ms"], st_["msk"]
                rs, rsk = st_["sq"], st_["sqk"]
                ACT(rs[:, :wd], ms[:, :wd], AF.Ln, r=[msk], w=[rsk], scale=1.0 / 64, bias=EPS)
                ACT(rs[:, :wd], rs[:, :wd], AF.Exp, r=[rsk], w=[rsk], scale=-0.5)
                st_.update(rs=rs, rsk=rsk)

            def s2m():
                if rope and tc < 4:
                    qg, qgk = st_["qg"], st_["qgk"]
                    ms2, ms2k = next_ms()
                    MM(ms2[:, :hw_], lhsT=rmat[:], rhs=qg[:, :hw_], r=[qgk, "rmat"], w=[ms2k])
                    st_.update(ms2=ms2, ms2k=ms2k)

            def s2a():
                s2e()
                s2m()

            def s2b():
                if rope and tc < 4:
                    MM(st_["ms2"][:, hw_:wd], lhsT=rmat[:], rhs=st_["qg"][:, hw_:wd], r=[st_["qgk"], "rmat"], w=[st_["ms2k"]])

            def s2():
                s2a()
                s2b()

            def s3():
                qg, qgk, rs, rsk = st_["qg"], st_["qgk"], st_["rs"], st_["rsk"]
                if rope and tc < 4:
                    ms2, ms2k = st_["ms2"], st_["ms2k"]
                    nr = wd // 64
                    r0 = t0 // 64
                    cosA = ropet[:, r0:r0 + nr].unsqueeze(2).broadcast_to([128, nr, 64])
                    cosB = ropet[:, 32:96].unsqueeze(1).broadcast_to([128, nr, 64])
                    sinA = ropet[:, 96 + r0:96 + r0 + nr].unsqueeze(2).broadcast_to([128, nr, 64])
                    sinB = ropet[:, 128:192].unsqueeze(1).broadcast_to([128, nr, 64])

                    def v3(ap):
                        return ap.rearrange("p (r c) -> p r c", c=64)
                    t1, t1k = qg, qgk
                    t2, t2k = Wt.get()
                    TT("pool", v3(t1[:, :wd]), v3(qg[:, :wd]), cosA, ALU.mult, r=[qgk, "ropet"], w=[t1k])
                    TT("pool", v3(t1[:, :wd]), v3(t1[:, :wd]), cosB, ALU.mult, r=[t1k, "ropet"], w=[t1k])
                    TT("dve", v3(t2[:, :wd]), v3(ms2[:, :wd]), sinA, ALU.mult, r=[ms2k, "ropet"], w=[t2k])
                    TT("dve", v3(t2[:, :wd]), v3(t2[:, :wd]), sinB, ALU.mult, r=[t2k, "ropet"], w=[t2k])
                    TT("pool", t1[:, :wd], t1[:, :wd], t2[:, :wd], ALU.add, r=[t1k, t2k], w=[t1k])
                    src, srck, eng = t1, t1k, "dve"
                else:
                    src, srck, eng = qg, qgk, "pool"
                dsts = dst if isinstance(dst, list) else [(dst, slice(0, 128))]
                for (dap, prr) in dsts:
                    TT(eng, dap, src[prr, :wd], rs[prr, :wd], ALU.mult, r=[srck, rsk] + list(extra_r), w=[dkey])

            if micro:
                if rope and tc < 4:
                    return [pk(k) for k in range(8)] + [s1e, s1m, s1b, s2e, s2m, s2b, None, s3]
                return [pk(k) for k in range(8)] + [s1e, s1m, s1b, s2e, None, s3]
            if five:
                def s1mb():
                    s1m()
                    s1b()
                return [s0, s1e, s1mb, s2, s3]
            return [s0, s1, s2, s3]

        def place(sched, start, ops, per_step=1):
            for i, f in enumerate(ops):
                if f is not None:
                    sched.setdefault(start + i // per_step, []).append(f)

        def run_all(stages):
            for f in stages:
                f()

        def run_pipelined(units):
            n = len(units)
            ns = len(units[0])
            for step in range(n + ns - 1):
                for si in range(ns - 1, -1, -1):
                    ui = step - si
                    if 0 <= ui < n:
                        units[ui][si]()

        def silu_gate(bank, bkey, wd):
            e1, e1k = Wt.get()
            ACT(e1[:, :wd], bank[:, :wd], AF.Exp, r=[bkey], w=[e1k], scale=-1.0)
            ACT(e1[:, :wd], e1[:, :wd], AF.Ln, r=[e1k], w=[e1k], bias=1.0)
            ACT(e1[:, :wd], e1[:, :wd], AF.Exp, r=[e1k], w=[e1k], scale=-1.0)
            g, gk = Wt.get()
            TT("dve", g[:, :wd], bank[:, :wd], e1[:, :wd], ALU.mult, r=[bkey, e1k], w=[gk])
            return g, gk

        def gate_stages(gp, gpk, tc, holder, micro=False):
            t0, wd = TCH[tc]

            def g0():
                holder["bank"], holder["bkey"] = next_pj()
                proj(gp, gpk, tc, holder["bank"], holder["bkey"])

            def gk_(k):
                def f():
                    if k == 0:
                        holder["bank"], holder["bkey"] = next_pj()
                    MM(holder["bank"][:, :wd], lhsT=gp[:, k, :], rhs=hT[:, k, t0:t0 + wd],
                       start=(k == 0), stop=(k == 7), r=[gpk, ("hT", tc)], w=[holder["bkey"]])
                return f

            def g1():
                holder["g"], holder["gk"] = silu_gate(holder["bank"], holder["bkey"], wd)

            if micro:
                return [gk_(k) for k in range(8)] + [g1]
            return [g0, g1]

        def finalize_norm(tc, mc, g, gk):
            t0, wd = TCH[tc]
            ao, aok = Wt.get()
            for hh in range(2):
                rr, rrk = Wt.get()
                ACT(rr[64:128, :wd], OA[hh][64:128, :wd], AF.Ln, r=[OAK[hh]], w=[rrk])
                ACT(rr[64:128, :wd], rr[64:128, :wd], AF.Exp, r=[rrk], w=[rrk], scale=-1.0)
                TT("dve", ao[64 * hh:64 * hh + 64, :wd], OA[hh][0:64, :wd], rr[64:128, :wd], ALU.mult,
                   r=[OAK[hh], rrk], w=[aok])
            TT("pool", mix[:, mc, t0:t0 + wd], ao[:, :wd], g[:, :wd], ALU.mult, r=[aok, gk], w=[("mix", mc, tc)])

        def finalize_attn(gp, gpk, tc, mc):
            h = {}
            run_all(gate_stages(gp, gpk, tc, h))
            finalize_norm(tc, mc, h["g"], h["gk"])

        def y_update(l, b, mcs, tag, last):
            for m in range(8):
                wp, wpk = WS.next((tag, l, m))
                for tc, (t0, wd) in enumerate(TCH):
                    if tc == 4 and last:
                        continue
                    bank, bkey = next_pj()
                    for i, mc in enumerate(mcs):
                        MM(bank[:, :wd], lhsT=wp[:, i, :], rhs=mix[:, mc, t0:t0 + wd],
                           start=(i == 0), stop=(i == len(mcs) - 1), r=[wpk, ("mix", mc, tc)], w=[bkey])
                    v = b if tc < 4 else 2
                    STT(xT[:, m, t0:t0 + wd], bank[:, :wd], mod_all[:, l, 16 + m, v:v + 1], xT[:, m, t0:t0 + wd],
                        ALU.mult, ALU.add, r=[bkey, "mod", ("xT", tc)], w=[("xT", tc)])

        def phase_A_init():
            fence()
            MSET("dve", vaugA[:, :, :, 1, :], 1.0, r=["PH"], w=["vaug"])
            for kv in range(2):
                for tc_ in range(5):
                    t0_, wd_ = TCH[tc_]
                    MSET("dve", kz[kv][0][64:128, t0_:t0_ + wd_], 0.0, r=["PH"], w=[("kdup", kv, tc_)])
                    MSET("dve", kz[kv][1][0:64, t0_:t0_ + wd_], 0.0, r=["PH"], w=[("kdup", kv, tc_)])

        def phase_A(l, b, last):
            pj_list[0] = [6, 0, 2]
            ms_list[0] = [7, 1, 3, 4, 5]
            kunits = []
            for kv in range(2):
                kp, kpk = WS.next(("in", l, P_K0 + kv))
                for tc in range(5):
                    t0, wd = TCH[tc]
                    kunits.append(nr_stages(kp, kpk, tc, gains[:, l, 1:2],
                                            [(kz[kv][0][0:64, t0:t0 + wd], slice(0, 64)),
                                             (kz[kv][1][64:128, t0:t0 + wd], slice(64, 128))],
                                            ("kdup", kv, tc), True, extra_r=["PH"], five=True))
            run_pipelined(kunits)
            vp, vpk = WS.next(("in", l, P_V))
            for g4 in range(5):
                bank, bkey = next_pj()
                tl = list(range(g4 * 4, min(18, g4 * 4 + 4)))
                for qi, ti in enumerate(tl):
                    for k in range(8):
                        MM(bank[:, qi * 128:(qi + 1) * 128], lhsT=hT[:, k, ti * 128:(ti + 1) * 128], rhs=vp[:, k, :],
                           start=(k == 0), stop=(k == 7), r=[vpk, ("hT", min(ti // 4, 4))], w=[bkey])
                n = len(tl)
                CP("act", vaugA[:, tl[0]:tl[0] + n, :, 0, :],
                   bank[:, 0:n * 128].rearrange("p (t k d) -> p t k d", t=n, k=2),
                   r=[bkey, "PH"], w=["vaug"])

            def q_units(c, qp, qpk, micro=False):
                return [nr_stages(qp, qpk, tc, gains[:, l, 0:1], mix[:, c, TCH[tc][0]:TCH[tc][0] + TCH[tc][1]],
                                  ("mix", c, tc), True, micro=micro, five=not micro) for tc in range(5)]

            qp, qpk = WS.next(("in", l, P_Q + 0))
            run_pipelined(q_units(0, qp, qpk)[:(4 if last else 5)])
            pj_list[0] = [6]
            ms_list[0] = [7]
            for c in range(4):
                kv = c // 2
                gp, gpk = WS.next(("in", l, P_G + c))
                side = []
                if c < 3:
                    qpn, qpnk = WS.next(("in", l, P_Q + c + 1))
                    side = q_units(c + 1, qpn, qpnk, micro=True)
                for tc in range(4 if last else 5):
                    t0, wd = TCH[tc]
                    tiles = list(range(18)) if tc < 4 else [16, 17]
                    nst = len(tiles)
                    sched = {}
                    gh = {}
                    gst = gate_stages(gp, gpk, tc, gh, micro=True)
                    if tc < 4:
                        if side and tc < 3:
                            place(sched, 0, side[tc])
                            place(sched, 9, gst[0:8], per_step=2)
                            place(sched, 14, gst[8:])
                        elif side:
                            place(sched, 0, side[3])
                            if not last:
                                place(sched, 9, side[4][0:8], per_step=2)
                                place(sched, 13, side[4][8:9])
                                place(sched, 16, side[4][9:])
                            place(sched, 14, gst[0:8], per_step=2)
                            place(sched, 18, gst[8:])
                        else:
                            place(sched, 4, gst[0:8])
                            place(sched, 13, gst[8:])
                    else:
                        place(sched, 0, gst[0:8], per_step=8)
                        place(sched, 1, gst[8:])
                    LAG = 2
                    pq = []
                    for s_ in range(nst + LAG):
                        cur = None
                        if s_ < nst:
                            t = tiles[s_]
                            scd, sk = SCD[s_ % 2], SCDK[s_ % 2]
                            for hh in range(2):
                                MM(scd[:, 512 * hh:512 * hh + wd], lhsT=kz[kv][hh][:, t * 128:(t + 1) * 128],
                                   rhs=mix[:, c, t0:t0 + wd],
                                   r=[("kdup", kv, min(t // 4, 4)), ("mix", c, tc), "PH"], w=[sk[hh]])
                            ptd, ptk = PT.get()
                            if wd == 512:
                                ACT(ptd[:, :], scd[:, :], AF.Exp, r=sk, w=[ptk], scale=0.125)
                            else:
                                ACT(ptd[:, :].rearrange("p (h w) -> p h w", h=2)[:, :, 0:wd],
                                    scd.rearrange("p (h w) -> p h w", h=2)[:, :, 0:wd], AF.Exp, r=sk, w=[ptk], scale=0.125)
                            cur = (t, ptd, ptk)
                        pq.append(cur)
                        if len(pq) > LAG and pq[0] is not None:
                            t_, ptd_, ptk_ = pq[0]
                            for hh in range(2):
                                MM(OA[hh][:, :wd], lhsT=vaugA[:, t_, kv, :, :].rearrange("p a d -> p (a d)"),
                                   rhs=ptd_[:, 512 * hh:512 * hh + wd], start=(t_ == tiles[0]), stop=(t_ == tiles[-1]),
                                   r=[ptk_, "vaug", "PH"], w=[OAK[hh]])
                        if len(pq) > LAG:
                            pq.pop(0)
                        for f in sched.get(s_, ()):
                            f()
                    for s_x in sorted(k_ for k_ in sched if k_ >= nst + LAG):
                        for f in sched[s_x]:
                            f()
                    finalize_norm(tc, c, gh["g"], gh["gk"])
            pj_list[0] = [6, 0]
            ms_list[0] = [7, 1]
            y_update(l, b, [0, 1, 2, 3], "outA", last)

        def phase_B(l, b, last):
            fence()
            pj_list[0] = [6, 0, 7, 1, 2, 3, 4, 5]
            pw_t, pw_k = Wt.get()
            pw32 = pw_t[:, 0:256].rearrange("p (c d) -> p c d", c=2)
            DMA(pw32, poolw_d[l], w=[pw_k], grp="pw")
            CP("pool", poolwb[:], pw32, r=[pw_k], w=["poolwb"])
            for ap in (zbuf[:, 0:16], zbuf[:, 2064:2080], zbuf[:, 2336:2352]):
                MSET("pool", ap, 0.0, r=["PH"], w=["zbuf"])
            segs = [(16, 0, SEQ), (2080, SEQ, NCTX)]
            for cz in range(2):
                zp, zpk = WS.next(("in", l, P_Z + cz))
                for tc in range(5):
                    t0, wd = TCH[tc]
                    zo = 16 + t0 if tc < 4 else 2080
                    bank, bkey = next_pj()
                    proj(zp, zpk, tc, bank, bkey)
                    CP("act", zbuf[:, zo:zo + wd], bank[:, :wd], r=[bkey, "PH"], w=["zbuf"])
                N = ZN
                def lvl(dst, dkey, src, skey, a, b_, up, dn):
                    mid = (a + b_) // 2
                    rk = [skey, (skey, 0), (skey, 1), "PH"]
                    TT("pool", dst[:, a:mid], src[:, a + up:mid + up], src[:, a - dn:mid - dn], ALU.add,
                       r=rk, w=[(dkey, 0)])
                    TT("dve", dst[:, mid:b_], src[:, mid + up:b_ + up], src[:, mid - dn:b_ - dn], ALU.add,
                       r=rk, w=[(dkey, 1)])

                lvl(tA, "tA", zbuf, "zbuf", 1, N, 0, 1)
                lvl(tB, "tB", tA, "tA", 2, N - 1, 1, 1)
                if cz == 1:
                    lvl(tA, "tA", tB, "tB", 4, N - 3, 2, 2)
                    lvl(tB, "tB", tA, "tA", 8, N - 7, 4, 4)
                srcs = [(tA, "tA", 0), (tB, "tB", 64)]
                hk = lambda k_: [(k_, 0), (k_, 1)]
                for (src, skey, p0) in srcs:
                    pr = slice(p0, p0 + 64)
                    for (zo, to, ln_) in segs:
                        tcs = [0, 1, 2, 3] if to == 0 else [4]
                        wkeys = [("mix", cz, tc) for tc in tcs]
                        STT(mix[pr, cz, to:to + ln_], src[pr, zo:zo + ln_], pooltab[pr, cz, 0:1], zbuf[pr, zo:zo + ln_],
                            ALU.mult, ALU.subtract, r=hk(skey) + ["zbuf", "pooltab", "PH"], w=wkeys)
                        for (eo, tab0) in ((0, 1), (ln_ - 8, 9)):
                            et, etk = Wt.get()
                            TT("pool", et[pr, 0:8], src[pr, zo + eo:zo + eo + 8], pooltab[pr, cz, tab0:tab0 + 8], ALU.mult,
                               r=hk(skey) + ["pooltab", "PH"], w=[etk])
                            TT("pool", mix[pr, cz, to + eo:to + eo + 8], et[pr, 0:8], zbuf[pr, zo + eo:zo + eo + 8],
                               ALU.subtract, r=[etk, "zbuf", "PH"], w=[wkeys[0] if eo == 0 else wkeys[-1]])
                gp, gpk = WS.next(("in", l, P_BG + cz))
                for tc in range(5):
                    t0, wd = TCH[tc]
                    bank, bkey = next_pj()
                    MM(bank[:, :wd], lhsT=poolwb[:, cz, :], rhs=mix[:, cz, t0:t0 + wd], r=["poolwb", ("mix", cz, tc)], w=[bkey])
                    bank2, bkey2 = next_pj()
                    proj(gp, gpk, tc, bank2, bkey2)
                    g, gk = silu_gate(bank2, bkey2, wd)
                    STT(mix[:, cz, t0:t0 + wd], bank[:, :wd], pscale[:, l, cz:cz + 1], g[:, :wd], ALU.mult, ALU.mult,
                        r=[bkey, gk, "pscale"], w=[("mix", cz, tc)])
            pj_list[0] = [6, 0]

        def phase_C(l, b, last):
            for c2 in range(2):
                mc = 2 + c2
                fence()
                MSET("pool", nvaug[:, :, :, 1, :], 1.0, r=["PH"], w=["nvaug"])
                for tc_ in range(5):
                    t0_, wd_ = TCH[tc_]
                    MSET("pool", nkz[0][64:128, t0_:t0_ + wd_], 0.0, r=["PH"], w=[("nkT", tc_)])
                    MSET("pool", nkz[1][0:64, t0_:t0_ + wd_], 0.0, r=["PH"], w=[("nkT", tc_)])
                DMA(ebuf[:], ebias_d[l, c2], r=["PH"], w=["ebuf"], grp="eb")
                TS("dve", ebuf[:], ebuf[:], 8.0, ALU.mult, r=["ebuf", "PH"], w=["ebuf"])
                pj_list[0] = [6, 0, 2]
                ms_list[0] = [7, 1, 3, 4, 5]
                kp, kpk = WS.next(("in", l, P_NK + c2))
                qp, qpk = WS.next(("in", l, P_NQ + c2))
                units = []
                for tc in range(5):
                    t0, wd = TCH[tc]
                    units.append(nr_stages(kp, kpk, tc, gains[:, l, 3:4],
                                           [(nkz[0][0:64, t0:t0 + wd], slice(0, 64)), (nkz[1][64:128, t0:t0 + wd], slice(64, 128))],
                                           ("nkT", tc), False, extra_r=["PH"], five=True))
                for tc in range(4 if last else 5):
                    t0, wd = TCH[tc]
                    units.append(nr_stages(qp, qpk, tc, gains[:, l, 2:3], mix[:, mc, t0:t0 + wd], ("mix", mc, tc), False,
                                           five=True))
                run_pipelined(units)
                vp, vpk = WS.next(("in", l, P_NV + c2))
                toks = [ti * 128 for ti in range(18)] + [64 + 128 * o for o in range(15)]
                for g4 in range(9):
                    tl = list(range(g4 * 4, min(33, g4 * 4 + 4)))
                    bank, bkey = next_pj()
                    for qi, ti in enumerate(tl):
                        tk0 = toks[ti]
                        rk = sorted(set([("hT", min(tk0 // 512, 4)), ("hT", min((tk0 + 127) // 512, 4))]))
                        for k in range(8):
                            MM(bank[:, qi * 128:(qi + 1) * 128], lhsT=hT[:, k, tk0:tk0 + 128], rhs=vp[:, k, :],
                               start=(k == 0), stop=(k == 7), r=[vpk] + rk, w=[bkey])
                    n = len(tl)
                    CP("act", nvaug[:, tl[0]:tl[0] + n, :, 0, :],
                       bank[:, 0:n * 128].rearrange("p (t k d) -> p t k d", t=n, k=2),
                       r=[bkey, "PH"], w=["nvaug"])
                gp, gpk = WS.next(("in", l, P_NG + c2))
                ev = ebuf.rearrange("p h (q two) c -> p h q two c", two=2)
                pj_list[0] = [6]
                ms_list[0] = [7]
                LA = 3
                ptc = [0]
                pt_all = [("PT", j_) for j_ in range(3)] + [("PT", j_, h_) for j_ in range(3) for h_ in range(2)]
                MSET("pool", dummy[:, :], 0.0, w=pt_all + ["dummy"])
                for tc in range(4):
                    items = [(hh, rr_) for rr_ in range(8) for hh in (0, 1)]
                    pendq = []
                    for s in range(len(items) + LA):
                        cur = None
                        if s < len(items):
                            hh, rr_ = items[s]
                            r = tc * 8 + rr_
                            r0 = min(max(r - 4, 0), 24)
                            kb = 64 * r0
                            pr = slice(64 * hh, 64 * hh + 64)
                            sc, sck = BK[s % 4], ("PS", s % 4)
                            qap = mix[:, mc, 64 * r:64 * r + 64]
                            rkeys = sorted(set(("nkT", min((kb + 128 * i) // 512, 3)) for i in range(4)) |
                                           set(("nkT", min((kb + 128 * i + 127) // 512, 3)) for i in range(4)))
                            for i in range(4):
                                MM(sc[:, 64 * i:64 * i + 64], lhsT=nkz[hh][:, kb + 128 * i:kb + 128 * i + 128], rhs=qap,
                                   r=list(rkeys) + [("mix", mc, tc), "PH"], w=[sck])
                            for i2 in range(2):
                                MM(sc[:, 256 + 64 * i2:256 + 64 * i2 + 64],
                                   lhsT=nkz[hh][:, SEQ + 128 * i2:SEQ + 128 * i2 + 128], rhs=qap,
                                   r=[("nkT", 4), ("mix", mc, tc), "PH"], w=[sck])
                            s0 = r0 - r + 7
                            esl = ev[:, hh, s0 // 2:s0 // 2 + 4, s0 % 2, :]
                            sc3 = sc[:, 0:256].rearrange("p (i c) -> p i c", c=64)
                            TT("dve", sc3, sc3, esl, ALU.add, r=[sck, "ebuf", "PH"], w=[sck])
                            pj_ = ptc[0] % 6
                            ptc[0] += 1
                            pt, ptk = PT.tiles[pj_ // 2][:, 512 * (pj_ % 2):512 * (pj_ % 2) + 512], ("PT", pj_ // 2, pj_ % 2)
                            ACT(pt[:, 0:384], sc[:, 0:384], AF.Exp, r=[sck], w=[ptk], scale=0.125)
                            if r0 % 2 == 0:
                                vts = [r0 // 2 + i for i in range(4)]
                            else:
                                vts = [18 + (r0 - 1) // 2 + i for i in range(4)]
                            vts += [16, 17]
                            cur = (hh, rr_, pt, ptk, vts)
                        pendq.append(cur)
                        if len(pendq) > LA and pendq[0] is not None:
                            hh_, rr2, pt_, ptk_, vts_ = pendq[0]
                            for idx, vt in enumerate(vts_):
                                MM(OA[hh_][:, 64 * rr2:64 * rr2 + 64],
                                   lhsT=nvaug[:, vt, hh_, :, :].rearrange("p a d -> p (a d)"),
                                   rhs=pt_[:, 64 * idx:64 * idx + 64], start=(idx == 0), stop=(idx == 5),
                                   r=[ptk_, "nvaug", "PH"], w=[OAK[hh_]])
                        if len(pendq) > LA:
                            pendq.pop(0)
                    finalize_attn(gp, gpk, tc, mc)
                pj_list[0] = [6, 0]
                ms_list[0] = [7, 1]
                for hh in (range(2) if not last else ()):
                    pr = slice(64 * hh, 64 * hh + 64)
                    sc, sck = SC[hh], SCK[hh]
                    for i2 in range(2):
                        MM(sc[:, 256 * i2:256 * i2 + 256], lhsT=nkz[hh][:, SEQ + 128 * i2:SEQ + 128 * i2 + 128],
                           rhs=mix[:, mc, SEQ:SEQ + NCTX], r=[("nkT", 4), ("mix", mc, 4), "PH"], w=[sck])
                    pj_ = ptc[0] % 6
                    ptc[0] += 1
                    pt, ptk = PT.tiles[pj_ // 2][:, 512 * (pj_ % 2):512 * (pj_ % 2) + 512], ("PT", pj_ // 2, pj_ % 2)
                    ACT(pt[:, 0:512], sc[:, :], AF.Exp, r=[sck], w=[ptk], scale=0.125)
                    for i2 in range(2):
                        MM(OA[hh][:, 0:256], lhsT=nvaug[:, 16 + i2, hh, :, :].rearrange("p a d -> p (a d)"),
                           rhs=pt[:, 256 * i2:256 * i2 + 256], start=(i2 == 0), stop=(i2 == 1),
                           r=[ptk, "nvaug", "PH"], w=[OAK[hh]])
                if not last:
                    finalize_attn(gp, gpk, 4, mc)
                MSET("pool", dummy[:, :], 0.0, w=pt_all + ["dummy"])
            y_update(l, b, ([0, 1, 2, 3] if "B" in phases else [2, 3]), "outC", last)

        final = []
        for b in range(n_batch):
            fence()
            for i in range(18):
                src = x_d[b, i * 128:(i + 1) * 128, :] if i < 16 else ctx_d[b, (i - 16) * 128:(i - 15) * 128, :]
                xs, xsk = xstage[i % NXS], ("xs", i % NXS)
                DMA(xs, src, r=["PH"], w=[xsk], grp="xs%d" % (i % NXS))
                tc = min(i // 4, 4)
                for half in range(2):
                    bank, bkey = next_pj()
                    for kk in range(4):
                        k = half * 4 + kk
                        S.add("pe", (lambda o, i_: (lambda e: e.transpose(out=o, in_=i_, identity=ident[:])))(
                            bank[:, kk * 128:(kk + 1) * 128], xs[:, k * 128:(k + 1) * 128]),
                            [xsk, "ident", "PH"], [bkey])
                    CP("dve" if half == 0 else "act", xT[:, half * 4:half * 4 + 4, i * 128:(i + 1) * 128],
                       bank[:, :].rearrange("p (k t) -> p k t", t=128), r=[bkey], w=[("xT", tc)])
            for l in range(n_layers):
                last = (l == NL - 1)
                if "A" in phases and stage >= 2:
                    phase_A_init()
                if stage >= 1:
                    xnorm(l, b)
                if "A" in phases and stage >= 2:
                    phase_A(l, b, last)
                if "B" in phases:
                    phase_B(l, b, last)
                if "C" in phases:
                    phase_C(l, b, last)
            fence()
            for i in range(16):
                xs, xsk = xstage[i % NXS], ("xs", i % NXS)
                tc = i // 4
                for half in range(2):
                    bank, bkey = next_pj()
                    for kk in range(4):
                        k = half * 4 + kk
                        S.add("pe", (lambda o, i_: (lambda e: e.transpose(out=o, in_=i_, identity=ident[:])))(
                            bank[:, kk * 128:(kk + 1) * 128], xT[:, k, i * 128:(i + 1) * 128]),
                            [("xT", tc), "ident"], [bkey])
                    CP("dve" if half == 0 else "act", xs[:, half * 512:(half + 1) * 512], bank[:, :],
                       r=[bkey, "PH"], w=[xsk])
                final.append(DMA(out_d[b, i * 128:(i + 1) * 128, :], xs, r=[xsk, "PH"], grp="o%d" % (i % NXS)))
        assert stage < 99 or WS.consumed == len(WS.specs), (WS.consumed, len(WS.specs))
        S.emit(nc, final_waits=final[-NXS:])
    return nc


def _f32(a):
    return np.ascontiguousarray(np.asarray(a, dtype=np.float32))


def prep_shared(c_ctx, norm_gain, w_mod, b_mod, w_in, att_q_gain, att_k_gain, pool_w, pool_scale,
                na_q_gain, na_k_gain, na_rpb, w_out):
    sh = {}
    w_mod = _f32(w_mod)
    sh["wmod"] = _f32(w_mod.reshape(NL, 8, 128, 6, 512).transpose(0, 3, 2, 1, 4))
    sh["bmod"] = _f32(_f32(b_mod).reshape(NL, 24, 128).transpose(2, 0, 1))
    sh["ngain"] = _f32(_f32(norm_gain).reshape(NL, 8, 128).transpose(2, 0, 1))
    w_in = _f32(w_in)
    cols = []
    cols.append(np.r_[512:576, 512:576])
    cols.append(np.r_[576:640, 576:640])
    cols.append(np.r_[640:768])
    for c in range(4):
        cols.append(np.r_[c * 128:(c + 1) * 128])
    for c in range(4):
        cols.append(np.r_[768 + c * 128:768 + (c + 1) * 128])
    for base in (1280, 1536, 2048, 2304, 1792, 2560):
        for c in range(2):
            cols.append(np.r_[base + c * 128:base + (c + 1) * 128])
    assert len(cols) == NPAN
    win = np.empty((NL, NPAN, 128, 8, 128), np.float32)
    for pi, cc in enumerate(cols):
        win[:, pi] = w_in[:, :, cc].reshape(NL, 8, 128, 128).transpose(0, 2, 1, 3)
    sh["win"] = win
    sh["wout"] = _f32(_f32(w_out).reshape(NL, 8, 128, 8, 128).transpose(0, 3, 2, 1, 4))
    g = np.stack([_f32(att_q_gain), _f32(att_k_gain), _f32(na_q_gain), _f32(na_k_gain)], axis=-1)
    sh["gains"] = _f32(np.tile(g, (1, 2, 1)).transpose(1, 0, 2))
    p = np.arange(128)
    d = p % 64
    inv_freq = (np.float32(10000.0) ** (-np.arange(16, dtype=np.float32) / np.float32(16))).astype(np.float32)
    f = inv_freq[d % 16]
    sign = np.where((d % 32) < 16, -1.0, 1.0).astype(np.float32)
    isrow = d < 32
    rows = np.arange(32, dtype=np.float32)
    colsv = np.arange(64, dtype=np.float32)
    angA = (rows[None, :] * f[:, None]).astype(np.float32)
    angB = (colsv[None, :] * f[:, None]).astype(np.float32)
    cosA = np.where(isrow[:, None], np.cos(angA), 1.0)
    cosB = np.where(isrow[:, None], 1.0, np.cos(angB))
    sinA = np.where(isrow[:, None], sign[:, None] * np.sin(angA), 1.0)
    sinB = np.where(isrow[:, None], 1.0, sign[:, None] * np.sin(angB))
    sh["ropetab"] = _f32(np.concatenate([cosA, cosB, sinA, sinB], axis=1))
    partner = np.where((p % 32) < 16, p + 16, p - 16)
    rm = np.zeros((128, 128), np.float32)
    rm[partner, p] = 1.0
    sh["rmat"] = rm
    sh["ident"] = np.eye(128, dtype=np.float32)
    pw = _f32(pool_w)
    poolw = np.zeros((NL, 128, 2, 128), np.float32)
    for cz in range(2):
        for j in range(2):
            poolw[:, j * 64:(j + 1) * 64, cz, j * 64:(j + 1) * 64] = pw[:, 2 * cz + j]
    sh["poolw"] = poolw
    sh["pscale"] = _f32(_f32(pool_scale).reshape(NL, 2, 128).transpose(2, 0, 1))
    pt = np.zeros((128, 2, 17), np.float32)
    for cz in range(2):
        for j in range(2):
            w = (2, 4, 8, 16)[2 * cz + j]
            t = np.arange(8)
            cs = np.minimum(w, t + w // 2).astype(np.float32)
            ce = np.minimum(w, (8 - t) + w // 2 - 1 + 0).astype(np.float32)
            ce = np.minimum(w, 8 - t + w // 2).astype(np.float32)
            pt[j * 64:(j + 1) * 64, cz, 0] = 1.0 / w
            pt[j * 64:(j + 1) * 64, cz, 1:9] = 1.0 / cs
            pt[j * 64:(j + 1) * 64, cz, 9:17] = 1.0 / ce
    sh["pooltab"] = pt
    rpb = _f32(na_rpb)
    kc = np.arange(64)[:, None]
    cq = np.arange(64)[None, :]
    c0 = np.clip(cq - 8, 0, 48)
    valid = (kc >= c0) & (kc < c0 + 16)
    dcol = np.clip(kc - cq + 15, 0, 30)
    Dh = np.where(valid[None, None, None], rpb[:, :, :, dcol], np.float32(NEG))
    eb = np.empty((NL, 2, 128, 2, 14, 64), np.float32)
    for c2 in range(2):
        for hh in range(2):
            h = 2 * c2 + hh
            for j in range(2):
                for di in range(14):
                    eb[:, c2, j * 64:(j + 1) * 64, hh, di, :] = Dh[:, h, di + j]
    sh["ebias"] = eb
    return sh


_NC_CACHE = {}


def kernel(x, c, ctx, c_ctx, norm_gain, w_mod, b_mod, w_in, att_q_gain, att_k_gain,
           pool_w, pool_scale, na_q_gain, na_k_gain, na_rpb, w_out):
    x = _f32(x)
    c = _f32(c)
    ctx = _f32(ctx)
    c_ctx = _f32(c_ctx)
    sh = prep_shared(c_ctx, norm_gain, w_mod, b_mod, w_in, att_q_gain, att_k_gain, pool_w, pool_scale,
                     na_q_gain, na_k_gain, na_rpb, w_out)
    if "nc" not in _NC_CACHE:
        _NC_CACHE["nc"] = build_nc()
    nc = _NC_CACHE["nc"]
    in_maps = []
    for i in range(8):
        m = dict(sh)
        m["x"] = np.ascontiguousarray(x[2 * i:2 * i + 2])
        m["ctx"] = np.ascontiguousarray(ctx[2 * i:2 * i + 2])
        vecs = np.stack([c[2 * i], c[2 * i + 1], c_ctx], axis=-1)
        m["cT"] = _f32(vecs.reshape(8, 128, 3).transpose(1, 0, 2))
        in_maps.append(m)
    res = run_bass_kernel_spmd(nc, in_maps, core_ids=list(range(8)))
    return np.concatenate([np.asarray(r["out"], dtype=np.float32) for r in res.results], axis=0)
```

```python
import contextlib
import numpy as np
import concourse.bass as bass
import concourse.mybir as mybir
from concourse.bass_utils import run_bass_kernel_spmd

F32 = mybir.dt.float32
BF16 = mybir.dt.bfloat16
ALU = mybir.AluOpType
AF = mybir.ActivationFunctionType

D = 1024
NL = 4
SEQ = 2048
NCTX = 256
T = SEQ + NCTX
EPS = 1e-6
NEG = -30000.0
TCH = [(0, 512), (512, 512), (1024, 512), (1536, 512), (2048, 256)]
ENGS = ("pe", "act", "dve", "pool", "sp")
PSUM_KEYS = ("PS",)

P_K0, P_K1, P_V = 0, 1, 2
P_Q, P_G, P_Z, P_BG, P_NK, P_NV, P_NQ, P_NG = 3, 7, 11, 13, 15, 17, 19, 21
NPAN = 23


class Op:
    __slots__ = ("eng", "idx", "fn", "waits", "signal", "dma_grp", "dma_cnt", "sigval")

    def __init__(self, eng, idx, fn):
        self.eng = eng
        self.idx = idx
        self.fn = fn
        self.waits = []
        self.signal = False
        self.dma_grp = None
        self.dma_cnt = 0
        self.sigval = 0


class Sched:
    def __init__(self):
        self.q = {e: [] for e in ENGS}
        self.last_w = {}
        self.readers = {}
        self.maxwait = {e: {} for e in ENGS}
        self.dma_cnt = {}

    def add(self, eng, fn, reads=(), writes=(), dma=None):
        op = Op(eng, len(self.q[eng]), fn)
        if dma is not None:
            op.dma_grp = dma
            self.dma_cnt[dma] = self.dma_cnt.get(dma, 0) + 1
            op.dma_cnt = self.dma_cnt[dma]
        best = {}
        for k in reads:
            w = self.last_w.get(k)
            if w is not None:
                self._cand(best, w, eng)
            if isinstance(k, tuple) and k[0] in PSUM_KEYS:
                for r in self.readers.get(k, ()):
                    if r.eng != eng:
                        self._cand(best, r, eng)
        for k in writes:
            w = self.last_w.get(k)
            if w is not None:
                self._cand(best, w, eng)
            for r in self.readers.get(k, ()):
                self._cand(best, r, eng)
        mw = self.maxwait[eng]
        for src, (pos, d) in best.items():
            if mw.get(src, -1) >= pos:
                continue
            mw[src] = pos
            d.signal = True
            op.waits.append(d)
        for k in writes:
            self.last_w[k] = op
            self.readers[k] = []
        for k in reads:
            self.readers.setdefault(k, []).append(op)
        self.q[eng].append(op)
        return op

    @staticmethod
    def _cand(best, d, eng):
        if d.dma_grp is not None:
            src = ("dma", d.dma_grp)
            pos = d.dma_cnt
        else:
            if d.eng == eng and eng == "pe":
                return
            src = d.eng
            pos = d.idx
        cur = best.get(src)
        if cur is None or cur[0] < pos:
            best[src] = (pos, d)

    def emit(self, nc, final_waits=()):
        for op in final_waits:
            op.signal = True
        for e in ENGS:
            n = 0
            for op in self.q[e]:
                if op.dma_grp is None and op.signal:
                    n += 1
                    op.sigval = n
        grps = sorted(self.dma_cnt.keys())
        with contextlib.ExitStack() as st:
            sems = {}
            for e in ENGS:
                sems[e] = st.enter_context(nc.semaphore("sem_" + e))
            for g in grps:
                sems[("dma", g)] = st.enter_context(nc.semaphore("semd_" + str(g)))
            block = st.enter_context(nc.Block())

            def tok(d):
                if d.dma_grp is not None:
                    return sems[("dma", d.dma_grp)], 16 * d.dma_cnt
                return sems[d.eng], d.sigval

            def run(engname, e):
                for op in self.q[engname]:
                    for d in op.waits:
                        s, v = tok(d)
                        e.wait_ge(s, v)
                    ins = op.fn(e)
                    if op.dma_grp is not None:
                        ins.then_inc(sems[("dma", op.dma_grp)], 16)
                    elif op.signal:
                        ins.then_inc(sems[engname], 1)
                if engname == "sp":
                    for d in final_waits:
                        s, v = tok(d)
                        e.wait_ge(s, v)

            @block.tensor
            def _(e):
                run("pe", e)

            @block.scalar
            def _(e):
                run("act", e)

            @block.vector
            def _(e):
                run("dve", e)

            @block.gpsimd
            def _(e):
                run("pool", e)

            @block.sync
            def _(e):
                run("sp", e)


class Ring:
    def __init__(self, name, tiles):
        self.name = name
        self.tiles = tiles
        self.i = 0

    def get(self):
        j = self.i % len(self.tiles)
        self.i += 1
        return self.tiles[j], (self.name, j)


def build_nc(n_layers=NL, n_batch=2, phases="ABC", stage=99):
    nc = bass.Bass("TRN2", target_bir_lowering=False)

    def din(name, shape):
        return nc.dram_tensor(name, list(shape), F32, kind="ExternalInput").ap()

    x_d = din("x", [2, SEQ, D])
    ctx_d = din("ctx", [2, NCTX, D])
    cT_d = din("cT", [128, 8, 3])
    wmod_d = din("wmod", [NL, 6, 128, 8, 512])
    bmod_d = din("bmod", [128, NL, 24])
    ngain_d = din("ngain", [128, NL, 8])
    win_d = din("win", [NL, NPAN, 128, 8, 128])
    wout_d = din("wout", [NL, 8, 128, 8, 128])
    gains_d = din("gains", [128, NL, 4])
    rope_d = din("ropetab", [128, 192])
    rmat_d = din("rmat", [128, 128])
    ident_d = din("ident", [128, 128])
    poolw_d = din("poolw", [NL, 128, 2, 128])
    pscale_d = din("pscale", [128, NL, 2])
    pooltab_d = din("pooltab", [128, 2, 17])
    ebias_d = din("ebias", [NL, 2, 128, 2, 14, 64])
    out_d = nc.dram_tensor("out", [2, SEQ, D], F32, kind="ExternalOutput").ap()

    S = Sched()
    with contextlib.ExitStack() as st:
        def sb(name, shape, dt=F32):
            return st.enter_context(nc.sbuf_tensor(name, list(shape), dt))

        xT = sb("xT", [128, 8, T])
        hT = sb("hT", [128, 8, T], BF16)
        mix = sb("mix", [128, 4, T], BF16)
        PH = sb("PH", [128, 8320])
        stg = [sb("stg%d" % i, [128, 8, 128]) for i in range(2)]
        wb = [sb("wb%d" % i, [128, 8, 128], BF16) for i in range(4)]
        Wt = Ring("W", [sb("wk%d" % i, [128, 512]) for i in range(9)])
        PT = Ring("PT", [sb("pt%d" % i, [128, 1024], BF16) for i in range(3)])
        ident = sb("ident_s", [128, 128])
        rmat = sb("rmat_s", [128, 128])
        ones128 = sb("ones128", [128, 128])
        bones = sb("bones", [128, 128])
        ropet = sb("ropet", [128, 192])
        gains = sb("gains_s", [128, NL, 4])
        ngain = sb("ngain_s", [128, NL, 8])
        bmod = sb("bmod_s", [128, NL, 24])
        pscale = sb("pscale_s", [128, NL, 2])
        pooltab = sb("pooltab_s", [128, 2, 17])
        cT = sb("cT_s", [128, 8, 3])
        scT = sb("scT", [128, 8, 3])
        mod_all = sb("mod_all", [128, NL, 24, 3])
        gs_all = sb("gs_all", [128, NL, 8, 3])
        poolwb = sb("poolwb", [128, 2, 128], BF16)
        dummy = sb("fence_dummy", [128, 8])

        def ps(name):
            return st.enter_context(nc.psum_tensor(name, [128, 512], F32))

        PSALL = st.enter_context(nc.psum_tensor("PSALL", [128, 4096], F32))
        BK = [PSALL[:, 512 * i:512 * (i + 1)] for i in range(8)]
        SC = [BK[2], BK[3]]
        SCK = [("PS", 2), ("PS", 3)]
        OA = [BK[4], BK[5]]
        OAK = [("PS", 4), ("PS", 5)]
        SCD = [PSALL[:, 0:1024], PSALL[:, 1024:2048]]
        SCDK = [[("PS", 0), ("PS", 1)], [("PS", 2), ("PS", 3)]]
        pj_list = [[6, 0]]
        ms_list = [[7, 1]]

        PHb = PH[:, :].bitcast(BF16)
        kz = [[PHb[:, (2 * kv + hh) * T:(2 * kv + hh + 1) * T] for hh in range(2)] for kv in range(2)]
        vaugA = PHb[:, 4 * T:4 * T + 18 * 256].rearrange("p (t k a d) -> p t k a d", t=18, k=2, a=2)
        nkz = [PHb[:, hh * T:(hh + 1) * T] for hh in range(2)]
        nvaug = PHb[:, 2 * T:2 * T + 33 * 256].rearrange("p (t k a d) -> p t k a d", t=33, k=2, a=2)
        e_off = (2 * T + 33 * 256 + 1) // 2
        ebuf = PH[:, e_off:e_off + 2 * 14 * 64].rearrange("p (h d c) -> p h d c", h=2, d=14)
        assert e_off + 2 * 14 * 64 <= 8320
        ZN = 2352
        zbuf = PH[:, 0:ZN]
        tA = PH[:, ZN:2 * ZN]
        tB = PH[:, 2 * ZN:3 * ZN]
        assert 3 * ZN <= 8320
        NXS = 6
        xstage = [PH[:, i * 1024:(i + 1) * 1024] for i in range(NXS)]

        def MM(out, lhsT, rhs, start=True, stop=True, r=(), w=()):
            S.add("pe", lambda e: e.matmul(out, lhsT=lhsT, rhs=rhs, start=start, stop=stop), r, w)

        def ACT(out, in_, func, r=(), w=(), scale=None, bias=None):
            kw = {}
            if scale is not None:
                kw["scale"] = scale
            if bias is not None:
                kw["bias"] = bias
            S.add("act", lambda e: e.activation(out=out, in_=in_, func=func, **kw), r, w)

        def TT(eng, out, in0, in1, op, r=(), w=()):
            S.add(eng, lambda e: e.tensor_tensor(out=out, in0=in0, in1=in1, op=op), r, w)

        def TS(eng, out, in0, s1, op0, r=(), w=(), s2=None, op1=None):
            if op1 is None:
                S.add(eng, lambda e: e.tensor_scalar(out=out, in0=in0, scalar1=s1, scalar2=None, op0=op0), r, w)
            else:
                S.add(eng, lambda e: e.tensor_scalar(out=out, in0=in0, scalar1=s1, scalar2=s2, op0=op0, op1=op1), r, w)

        def STT(out, in0, scalar, in1, op0, op1, r=(), w=()):
            S.add("dve", lambda e: e.scalar_tensor_tensor(out=out, in0=in0, scalar=scalar, in1=in1, op0=op0, op1=op1), r, w)

        def CP(eng, out, in_, r=(), w=()):
            if eng == "act":
                S.add("act", lambda e: e.activation(out=out, in_=in_, func=AF.Copy), r, w)
            else:
                S.add(eng, lambda e: e.tensor_copy(out=out, in_=in_), r, w)

        def MSET(eng, ap, val, r=(), w=()):
            S.add(eng, lambda e: e.memset(ap, val), r, w)

        def DMA(out, in_, r=(), w=(), grp="m"):
            return S.add("sp", lambda e: e.dma_start(out=out, in_=in_), r, w, dma=grp)

        def fence():
            MSET("pool", dummy[:, :], 0.0, r=(), w=["PH", "dummy"])

        class WStream:
            def __init__(self):
                self.specs = []
                self.issued = 0
                self.consumed = 0

            def _issue(self):
                n = self.issued
                ap, nk, _tag = self.specs[n]
                si, wi = n % 2, n % 4
                DMA(stg[si][:, 0:nk, :], ap, w=[("stg", si)], grp="stg%d" % si)
                in_attn = (_tag[0] == "outA") or (_tag[0] == "in" and P_Q < _tag[2] < P_Z)
                CP("pool" if in_attn else "act", wb[wi][:, 0:nk, :], stg[si][:, 0:nk, :], r=[("stg", si)], w=[("wb", wi)])
                self.issued += 1

            def next(self, check=None):
                while self.issued < min(len(self.specs), self.consumed + 3 - 1):
                    self._issue()
                n = self.consumed
                if check is not None:
                    assert self.specs[n][2] == check, (self.specs[n][2], check)
                self.consumed += 1
                return wb[n % 4], ("wb", n % 4)

        WS = WStream()

        def add_spec(ap, nk, tag):
            WS.specs.append((ap, nk, tag))

        for b in range(n_batch):
            for l in range(n_layers):
                if "A" in phases:
                    for pid in (P_K0, P_K1, P_V):
                        add_spec(win_d[l, pid], 8, ("in", l, pid))
                    add_spec(win_d[l, P_Q], 8, ("in", l, P_Q))
                    for c in range(4):
                        add_spec(win_d[l, P_G + c], 8, ("in", l, P_G + c))
                        if c < 3:
                            add_spec(win_d[l, P_Q + c + 1], 8, ("in", l, P_Q + c + 1))
                    for m in range(8):
                        add_spec(wout_d[l, m, :, 0:4, :], 4, ("outA", l, m))
                if "B" in phases:
                    for cz in range(2):
                        add_spec(win_d[l, P_Z + cz], 8, ("in", l, P_Z + cz))
                        add_spec(win_d[l, P_BG + cz], 8, ("in", l, P_BG + cz))
                if "C" in phases:
                    for c2 in range(2):
                        for pid in (P_NK, P_NQ, P_NV, P_NG):
                            add_spec(win_d[l, pid + c2], 8, ("in", l, pid + c2))
                    for m in range(8):
                        if "B" in phases:
                            add_spec(wout_d[l, m, :, 4:8, :], 4, ("outC", l, m))
                        else:
                            add_spec(wout_d[l, m, :, 6:8, :], 2, ("outC", l, m))

        DMA(ident[:], ident_d[:], w=["ident"], grp="c_ident")
        DMA(rmat[:], rmat_d[:], w=["rmat"], grp="c_rmat")
        DMA(ropet[:], rope_d[:], w=["ropet"], grp="c_ropet")
        DMA(gains[:], gains_d[:], w=["gains"], grp="c_gains")
        DMA(ngain[:], ngain_d[:], w=["ngain"], grp="c_ngain")
        DMA(bmod[:], bmod_d[:], w=["bmod"], grp="c_bmod")
        DMA(pscale[:], pscale_d[:], w=["pscale"], grp="c_pscale")
        DMA(pooltab[:], pooltab_d[:], w=["pooltab"], grp="c_pooltab")
        DMA(cT[:], cT_d[:], w=["cT"], grp="c_cT")
        MSET("pool", ones128[:], 1.0, w=["ones"])
        MSET("pool", bones[:], 0.0, w=["bones"])
        MSET("pool", bones[0:64, 0:64], 1.0, w=["bones"])
        MSET("pool", bones[64:128, 64:128], 1.0, w=["bones"])
        ACT(scT[:], cT[:], AF.Exp, r=["cT"], w=["scT"], scale=-1.0)
        ACT(scT[:], scT[:], AF.Ln, r=["scT"], w=["scT"], bias=1.0)
        ACT(scT[:], scT[:], AF.Exp, r=["scT"], w=["scT"], scale=-1.0)
        TT("dve", scT[:], scT[:], cT[:], ALU.mult, r=["scT", "cT"], w=["scT"])
        for l in range(n_layers):
            rows = []
            for cc in range(6):
                i = l * 6 + cc
                si = i % 2
                wslot = PH[:, si * 4096:(si + 1) * 4096].rearrange("p (k c) -> p k c", k=8)
                DMA(wslot, wmod_d[l, cc], r=["PH"], w=[("wms", si)], grp="wms%d" % si)
                bank, bkey = BK[7 if cc % 2 == 0 else 6], ("PS", 7 if cc % 2 == 0 else 6)
                for k in range(8):
                    MM(bank[0:3, :], lhsT=scT[:, k, :], rhs=wslot[:, k, :],
                       start=(k == 0), stop=(k == 7), r=[("wms", si), "scT", "PH"], w=[bkey])
                rt, rtk = Wt.get()
                CP("dve", rt[0:3, :], bank[0:3, :], r=[bkey], w=[rtk])
                rows.append((rt, rtk))
            for m in range(24):
                rt, rtk = rows[m // 4]
                off = (m % 4) * 128
                S.add("pe", (lambda o, i_: (lambda e: e.transpose(out=o, in_=i_, identity=ident[0:3, 0:3])))(
                    BK[1][:, m * 4:m * 4 + 3], rt[0:3, off:off + 128]), [rtk, "ident"], [("PS", 1)])
            msv = BK[1][:, 0:96].rearrange("p (m f) -> p m f", f=4)
            for v in range(3):
                TT("dve", mod_all[:, l, :, v], msv[:, :, v], bmod[:, l, :], ALU.add,
                   r=[("PS", 1), "bmod"], w=["mod"])
            for v in range(3):
                STT(gs_all[:, l, :, v], mod_all[:, l, 8:16, v], 1.0, ngain[:, l, :], ALU.add, ALU.mult,
                    r=["mod", "ngain"], w=["mod"])

        pjc = [0]

        def next_pj():
            lst = pj_list[0]
            i = lst[pjc[0] % len(lst)]
            pjc[0] += 1
            return BK[i], ("PS", i)

        msc = [0]

        def next_ms():
            lst = ms_list[0]
            i = lst[msc[0] % len(lst)]
            msc[0] += 1
            return BK[i], ("PS", i)

        def proj(panel, pkey, tc, bank, bkey):
            t0, wd = TCH[tc]
            for k in range(8):
                MM(bank[:, :wd], lhsT=panel[:, k, :], rhs=hT[:, k, t0:t0 + wd],
                   start=(k == 0), stop=(k == 7), r=[pkey, ("hT", tc)], w=[bkey])

        def xnorm(l, b):
            state = {}

            def stage1(tc):
                t0, wd = TCH[tc]
                ms, msk = next_ms()
                for k in range(8):
                    sq, sqk = Wt.get()
                    if k in (0, 3, 6):
                        TT("pool", sq[:, :wd], xT[:, k, t0:t0 + wd], xT[:, k, t0:t0 + wd], ALU.mult,
                           r=[("xT", tc)], w=[sqk])
                    else:
                        ACT(sq[:, :wd], xT[:, k, t0:t0 + wd], AF.Square, r=[("xT", tc)], w=[sqk])
                    MM(ms[:, :wd], lhsT=ones128[:], rhs=sq[:, :wd], start=(k == 0), stop=(k == 7),
                       r=[sqk, "ones"], w=[msk])
                state[tc] = (ms, msk)

            def stage2(tc):
                t0, wd = TCH[tc]
                v = b if tc < 4 else 2
                ms, msk = state[tc]
                ln, lnk = Wt.get()
                ACT(ln[:, :wd], ms[:, :wd], AF.Ln, r=[msk], w=[lnk], scale=1.0 / D, bias=EPS)
                ACT(ms[:, :wd], ln[:, :wd], AF.Exp, r=[lnk], w=[msk], scale=-0.5)
                for k in range(8):
                    t, tk = Wt.get()
                    TT("dve", t[:, :wd], xT[:, k, t0:t0 + wd], ms[:, :wd], ALU.mult,
                       r=[("xT", tc), msk], w=[tk])
                    if k in (1, 4, 7):
                        TS("pool", hT[:, k, t0:t0 + wd], t[:, :wd], gs_all[:, l, k, v:v + 1], ALU.mult,
                           s2=mod_all[:, l, k, v:v + 1], op1=ALU.add, r=[tk, "mod"], w=[("hT", tc)])
                    else:
                        ACT(hT[:, k, t0:t0 + wd], t[:, :wd], AF.Identity, r=[tk, "mod"], w=[("hT", tc)],
                            scale=gs_all[:, l, k, v:v + 1], bias=mod_all[:, l, k, v:v + 1])

            for i in range(6):
                if i < 5:
                    stage1(i)
                if i >= 1:
                    stage2(i - 1)

        def nr_stages(panel, pkey, tc, gcol, dst, dkey, rope, extra_r=(), micro=False, five=False):
            t0, wd = TCH[tc]
            st_ = {}

            hw_ = wd // 2

            def s0():
                st_["bank"], st_["bkey"] = next_pj()
                proj(panel, pkey, tc, st_["bank"], st_["bkey"])

            def pk(k):
                def f():
                    if k == 0:
                        st_["bank"], st_["bkey"] = next_pj()
                    MM(st_["bank"][:, :wd], lhsT=panel[:, k, :], rhs=hT[:, k, t0:t0 + wd],
                       start=(k == 0), stop=(k == 7), r=[pkey, ("hT", tc)], w=[st_["bkey"]])
                return f

            def s1e():
                bank, bkey = st_["bank"], st_["bkey"]
                qg, qgk = Wt.get()
                sq, sqk = Wt.get()
                TS("dve", qg[:, :wd], bank[:, :wd], gcol, ALU.mult, r=[bkey, "gains"], w=[qgk])
                ACT(sq[:, :wd], bank[:, :wd], AF.Square, r=[bkey], w=[sqk])
                st_.update(qg=qg, qgk=qgk, sq=sq, sqk=sqk)

            def s1m():
                ms, msk = next_ms()
                MM(ms[:, :hw_], lhsT=bones[:], rhs=st_["sq"][:, :hw_], r=[st_["sqk"], "bones"], w=[msk])
                st_.update(ms=ms, msk=msk)

            def s1a():
                s1e()
                s1m()

            def s1b():
                MM(st_["ms"][:, hw_:wd], lhsT=bones[:], rhs=st_["sq"][:, hw_:wd], r=[st_["sqk"], "bones"], w=[st_["msk"]])

            def s1():
                s1a()
                s1b()

            def s2e():
                ms, msk = st_["ms"], st_["msk"]
                rs, rsk = st_["sq"], st_["sqk"]
                ACT(rs[:, :wd], ms[:, :wd], AF.Ln, r=[msk], w=[rsk], scale=1.0 / 64, bias=EPS)
                ACT(rs[:, :wd], rs[:, :wd], AF.Exp, r=[rsk], w=[rsk], scale=-0.5)
                st_.update(rs=rs, rsk=rsk)

            def s2m():
                if rope and tc < 4:
                    qg, qgk = st_["qg"], st_["qgk"]
                    ms2, ms2k = next_ms()
                    MM(ms2[:, :hw_], lhsT=rmat[:], rhs=qg[:, :hw_], r=[qgk, "rmat"], w=[ms2k])
                    st_.update(ms2=ms2, ms2k=ms2k)

            def s2a():
                s2e()
                s2m()

            def s2b():
                if rope and tc < 4:
                    MM(st_["ms2"][:, hw_:wd], lhsT=rmat[:], rhs=st_["qg"][:, hw_:wd], r=[st_["qgk"], "rmat"], w=[st_["ms2k"]])

            def s2():
                s2a()
                s2b()

            def s3():
                qg, qgk, rs, rsk = st_["qg"], st_["qgk"], st_["rs"], st_["rsk"]
                if rope and tc < 4:
                    ms2, ms2k = st_["ms2"], st_["ms2k"]
                    nr = wd // 64
                    r0 = t0 // 64
                    cosA = ropet[:, r0:r0 + nr].unsqueeze(2).broadcast_to([128, nr, 64])
                    cosB = ropet[:, 32:96].unsqueeze(1).broadcast_to([128, nr, 64])
                    sinA = ropet[:, 96 + r0:96 + r0 + nr].unsqueeze(2).broadcast_to([128, nr, 64])
                    sinB = ropet[:, 128:192].unsqueeze(1).broadcast_to([128, nr, 64])

                    def v3(ap):
                        return ap.rearrange("p (r c) -> p r c", c=64)
                    t1, t1k = qg, qgk
                    t2, t2k = Wt.get()
                    TT("pool", v3(t1[:, :wd]), v3(qg[:, :wd]), cosA, ALU.mult, r=[qgk, "ropet"], w=[t1k])
                    TT("pool", v3(t1[:, :wd]), v3(t1[:, :wd]), cosB, ALU.mult, r=[t1k, "ropet"], w=[t1k])
                    TT("dve", v3(t2[:, :wd]), v3(ms2[:, :wd]), sinA, ALU.mult, r=[ms2k, "ropet"], w=[t2k])
                    TT("dve", v3(t2[:, :wd]), v3(t2[:, :wd]), sinB, ALU.mult, r=[t2k, "ropet"], w=[t2k])
                    TT("pool", t1[:, :wd], t1[:, :wd], t2[:, :wd], ALU.add, r=[t1k, t2k], w=[t1k])
                    src, srck, eng = t1, t1k, "dve"
                else:
                    src, srck, eng = qg, qgk, "pool"
                dsts = dst if isinstance(dst, list) else [(dst, slice(0, 128))]
                for (dap, prr) in dsts:
                    TT(eng, dap, src[prr, :wd], rs[prr, :wd], ALU.mult, r=[srck, rsk] + list(extra_r), w=[dkey])

            if micro:
                if rope and tc < 4:
                    return [pk(k) for k in range(8)] + [s1e, s1m, s1b, s2e, s2m, s2b, None, s3]
                return [pk(k) for k in range(8)] + [s1e, s1m, s1b, s2e, None, s3]
            if five:
                def s1mb():
                    s1m()
                    s1b()
                return [s0, s1e, s1mb, s2, s3]
            return [s0, s1, s2, s3]

        def place(sched, start, ops, per_step=1):
            for i, f in enumerate(ops):
                if f is not None:
                    sched.setdefault(start + i // per_step, []).append(f)

        def run_all(stages):
            for f in stages:
                f()

        def run_pipelined(units):
            n = len(units)
            ns = len(units[0])
            for step in range(n + ns - 1):
                for si in range(ns - 1, -1, -1):
                    ui = step - si
                    if 0 <= ui < n:
                        units[ui][si]()

        def silu_gate(bank, bkey, wd):
            e1, e1k = Wt.get()
            ACT(e1[:, :wd], bank[:, :wd], AF.Exp, r=[bkey], w=[e1k], scale=-1.0)
            ACT(e1[:, :wd], e1[:, :wd], AF.Ln, r=[e1k], w=[e1k], bias=1.0)
            ACT(e1[:, :wd], e1[:, :wd], AF.Exp, r=[e1k], w=[e1k], scale=-1.0)
            g, gk = Wt.get()
            TT("dve", g[:, :wd], bank[:, :wd], e1[:, :wd], ALU.mult, r=[bkey, e1k], w=[gk])
            return g, gk

        def gate_stages(gp, gpk, tc, holder, micro=False):
            t0, wd = TCH[tc]

            def g0():
                holder["bank"], holder["bkey"] = next_pj()
                proj(gp, gpk, tc, holder["bank"], holder["bkey"])

            def gk_(k):
                def f():
                    if k == 0:
                        holder["bank"], holder["bkey"] = next_pj()
                    MM(holder["bank"][:, :wd], lhsT=gp[:, k, :], rhs=hT[:, k, t0:t0 + wd],
                       start=(k == 0), stop=(k == 7), r=[gpk, ("hT", tc)], w=[holder["bkey"]])
                return f

            def g1():
                holder["g"], holder["gk"] = silu_gate(holder["bank"], holder["bkey"], wd)

            if micro:
                return [gk_(k) for k in range(8)] + [g1]
            return [g0, g1]

        def finalize_norm(tc, mc, g, gk):
            t0, wd = TCH[tc]
            ao, aok = Wt.get()
            for hh in range(2):
                rr, rrk = Wt.get()
                ACT(rr[64:128, :wd], OA[hh][64:128, :wd], AF.Ln, r=[OAK[hh]], w=[rrk])
                ACT(rr[64:128, :wd], rr[64:128, :wd], AF.Exp, r=[rrk], w=[rrk], scale=-1.0)
                TT("dve", ao[64 * hh:64 * hh + 64, :wd], OA[hh][0:64, :wd], rr[64:128, :wd], ALU.mult,
                   r=[OAK[hh], rrk], w=[aok])
            TT("pool", mix[:, mc, t0:t0 + wd], ao[:, :wd], g[:, :wd], ALU.mult, r=[aok, gk], w=[("mix", mc, tc)])

        def finalize_attn(gp, gpk, tc, mc):
            h = {}
            run_all(gate_stages(gp, gpk, tc, h))
            finalize_norm(tc, mc, h["g"], h["gk"])

        def y_update(l, b, mcs, tag, last):
            for m in range(8):
                wp, wpk = WS.next((tag, l, m))
                for tc, (t0, wd) in enumerate(TCH):
                    if tc == 4 and last:
                        continue
                    bank, bkey = next_pj()
                    for i, mc in enumerate(mcs):
                        MM(bank[:, :wd], lhsT=wp[:, i, :], rhs=mix[:, mc, t0:t0 + wd],
                           start=(i == 0), stop=(i == len(mcs) - 1), r=[wpk, ("mix", mc, tc)], w=[bkey])
                    v = b if tc < 4 else 2
                    STT(xT[:, m, t0:t0 + wd], bank[:, :wd], mod_all[:, l, 16 + m, v:v + 1], xT[:, m, t0:t0 + wd],
                        ALU.mult, ALU.add, r=[bkey, "mod", ("xT", tc)], w=[("xT", tc)])

        def phase_A_init():
            fence()
            MSET("dve", vaugA[:, :, :, 1, :], 1.0, r=["PH"], w=["vaug"])
            for kv in range(2):
                for tc_ in range(5):
                    t0_, wd_ = TCH[tc_]
                    MSET("dve", kz[kv][0][64:128, t0_:t0_ + wd_], 0.0, r=["PH"], w=[("kdup", kv, tc_)])
                    MSET("dve", kz[kv][1][0:64, t0_:t0_ + wd_], 0.0, r=["PH"], w=[("kdup", kv, tc_)])

        def phase_A(l, b, last):
            pj_list[0] = [6, 0, 2]
            ms_list[0] = [7, 1, 3, 4, 5]
            kunits = []
            for kv in range(2):
                kp, kpk = WS.next(("in", l, P_K0 + kv))
                for tc in range(5):
                    t0, wd = TCH[tc]
                    kunits.append(nr_stages(kp, kpk, tc, gains[:, l, 1:2],
                                            [(kz[kv][0][0:64, t0:t0 + wd], slice(0, 64)),
                                             (kz[kv][1][64:128, t0:t0 + wd], slice(64, 128))],
                                            ("kdup", kv, tc), True, extra_r=["PH"], five=True))
            run_pipelined(kunits)
            vp, vpk = WS.next(("in", l, P_V))
            for g4 in range(5):
                bank, bkey = next_pj()
                tl = list(range(g4 * 4, min(18, g4 * 4 + 4)))
                for qi, ti in enumerate(tl):
                    for k in range(8):
                        MM(bank[:, qi * 128:(qi + 1) * 128], lhsT=hT[:, k, ti * 128:(ti + 1) * 128], rhs=vp[:, k, :],
                           start=(k == 0), stop=(k == 7), r=[vpk, ("hT", min(ti // 4, 4))], w=[bkey])
                n = len(tl)
                CP("act", vaugA[:, tl[0]:tl[0] + n, :, 0, :],
                   bank[:, 0:n * 128].rearrange("p (t k d) -> p t k d", t=n, k=2),
                   r=[bkey, "PH"], w=["vaug"])

            def q_units(c, qp, qpk, micro=False):
                return [nr_stages(qp, qpk, tc, gains[:, l, 0:1], mix[:, c, TCH[tc][0]:TCH[tc][0] + TCH[tc][1]],
                                  ("mix", c, tc), True, micro=micro, five=not micro) for tc in range(5)]

            qp, qpk = WS.next(("in", l, P_Q + 0))
            run_pipelined(q_units(0, qp, qpk)[:(4 if last else 5)])
            pj_list[0] = [6]
            ms_list[0] = [7]
            for c in range(4):
                kv = c // 2
                gp, gpk = WS.next(("in", l, P_G + c))
                side = []
                if c < 3:
                    qpn, qpnk = WS.next(("in", l, P_Q + c + 1))
                    side = q_units(c + 1, qpn, qpnk, micro=True)
                for tc in range(4 if last else 5):
                    t0, wd = TCH[tc]
                    tiles = list(range(18)) if tc < 4 else [16, 17]
                    nst = len(tiles)
                    sched = {}
                    gh = {}
                    gst = gate_stages(gp, gpk, tc, gh, micro=True)
                    if tc < 4:
                        if side and tc < 3:
                            place(sched, 0, side[tc])
                            place(sched, 9, gst[0:8], per_step=2)
                            place(sched, 14, gst[8:])
                        elif side:
                            place(sched, 0, side[3])
                            if not last:
                                place(sched, 9, side[4][0:8], per_step=2)
                                place(sched, 13, side[4][8:9])
                                place(sched, 16, side[4][9:])
                            place(sched, 14, gst[0:8], per_step=2)
                            place(sched, 18, gst[8:])
                        else:
                            place(sched, 4, gst[0:8])
                            place(sched, 13, gst[8:])
                    else:
                        place(sched, 0, gst[0:8], per_step=8)
                        place(sched, 1, gst[8:])
                    LAG = 2
                    pq = []
                    for s_ in range(nst + LAG):
                        cur = None
                        if s_ < nst:
                            t = tiles[s_]
                            scd, sk = SCD[s_ % 2], SCDK[s_ % 2]
                            for hh in range(2):
                                MM(scd[:, 512 * hh:512 * hh + wd], lhsT=kz[kv][hh][:, t * 128:(t + 1) * 128],
                                   rhs=mix[:, c, t0:t0 + wd],
                                   r=[("kdup", kv, min(t // 4, 4)), ("mix", c, tc), "PH"], w=[sk[hh]])
                            ptd, ptk = PT.get()
                            if wd == 512:
                                ACT(ptd[:, :], scd[:, :], AF.Exp, r=sk, w=[ptk], scale=0.125)
                            else:
                                ACT(ptd[:, :].rearrange("p (h w) -> p h w", h=2)[:, :, 0:wd],
                                    scd.rearrange("p (h w) -> p h w", h=2)[:, :, 0:wd], AF.Exp, r=sk, w=[ptk], scale=0.125)
                            cur = (t, ptd, ptk)
                        pq.append(cur)
                        if len(pq) > LAG and pq[0] is not None:
                            t_, ptd_, ptk_ = pq[0]
                            for hh in range(2):
                                MM(OA[hh][:, :wd], lhsT=vaugA[:, t_, kv, :, :].rearrange("p a d -> p (a d)"),
                                   rhs=ptd_[:, 512 * hh:512 * hh + wd], start=(t_ == tiles[0]), stop=(t_ == tiles[-1]),
                                   r=[ptk_, "vaug", "PH"], w=[OAK[hh]])
                        if len(pq) > LAG:
                            pq.pop(0)
                        for f in sched.get(s_, ()):
                            f()
                    for s_x in sorted(k_ for k_ in sched if k_ >= nst + LAG):
                        for f in sched[s_x]:
                            f()
                    finalize_norm(tc, c, gh["g"], gh["gk"])
            pj_list[0] = [6, 0]
            ms_list[0] = [7, 1]
            y_update(l, b, [0, 1, 2, 3], "outA", last)

        def phase_B(l, b, last):
            fence()
            pj_list[0] = [6, 0, 7, 1, 2, 3, 4, 5]
            pw_t, pw_k = Wt.get()
            pw32 = pw_t[:, 0:256].rearrange("p (c d) -> p c d", c=2)
            DMA(pw32, poolw_d[l], w=[pw_k], grp="pw")
            CP("pool", poolwb[:], pw32, r=[pw_k], w=["poolwb"])
            for ap in (zbuf[:, 0:16], zbuf[:, 2064:2080], zbuf[:, 2336:2352]):
                MSET("pool", ap, 0.0, r=["PH"], w=["zbuf"])
            segs = [(16, 0, SEQ), (2080, SEQ, NCTX)]
            for cz in range(2):
                zp, zpk = WS.next(("in", l, P_Z + cz))
                for tc in range(5):
                    t0, wd = TCH[tc]
                    zo = 16 + t0 if tc < 4 else 2080
                    bank, bkey = next_pj()
                    proj(zp, zpk, tc, bank, bkey)
                    CP("act", zbuf[:, zo:zo + wd], bank[:, :wd], r=[bkey, "PH"], w=["zbuf"])
                N = ZN
                def lvl(dst, dkey, src, skey, a, b_, up, dn):
                    mid = (a + b_) // 2
                    rk = [skey, (skey, 0), (skey, 1), "PH"]
                    TT("pool", dst[:, a:mid], src[:, a + up:mid + up], src[:, a - dn:mid - dn], ALU.add,
                       r=rk, w=[(dkey, 0)])
                    TT("dve", dst[:, mid:b_], src[:, mid + up:b_ + up], src[:, mid - dn:b_ - dn], ALU.add,
                       r=rk, w=[(dkey, 1)])

                lvl(tA, "tA", zbuf, "zbuf", 1, N, 0, 1)
                lvl(tB, "tB", tA, "tA", 2, N - 1, 1, 1)
                if cz == 1:
                    lvl(tA, "tA", tB, "tB", 4, N - 3, 2, 2)
                    lvl(tB, "tB", tA, "tA", 8, N - 7, 4, 4)
                srcs = [(tA, "tA", 0), (tB, "tB", 64)]
                hk = lambda k_: [(k_, 0), (k_, 1)]
                for (src, skey, p0) in srcs:
                    pr = slice(p0, p0 + 64)
                    for (zo, to, ln_) in segs:
                        tcs = [0, 1, 2, 3] if to == 0 else [4]
                        wkeys = [("mix", cz, tc) for tc in tcs]
                        STT(mix[pr, cz, to:to + ln_], src[pr, zo:zo + ln_], pooltab[pr, cz, 0:1], zbuf[pr, zo:zo + ln_],
                            ALU.mult, ALU.subtract, r=hk(skey) + ["zbuf", "pooltab", "PH"], w=wkeys)
                        for (eo, tab0) in ((0, 1), (ln_ - 8, 9)):
                            et, etk = Wt.get()
                            TT("pool", et[pr, 0:8], src[pr, zo + eo:zo + eo + 8], pooltab[pr, cz, tab0:tab0 + 8], ALU.mult,
                               r=hk(skey) + ["pooltab", "PH"], w=[etk])
                            TT("pool", mix[pr, cz, to + eo:to + eo + 8], et[pr, 0:8], zbuf[pr, zo + eo:zo + eo + 8],
                               ALU.subtract, r=[etk, "zbuf", "PH"], w=[wkeys[0] if eo == 0 else wkeys[-1]])
                gp, gpk = WS.next(("in", l, P_BG + cz))
                for tc in range(5):
                    t0, wd = TCH[tc]
                    bank, bkey = next_pj()
                    MM(bank[:, :wd], lhsT=poolwb[:, cz, :], rhs=mix[:, cz, t0:t0 + wd], r=["poolwb", ("mix", cz, tc)], w=[bkey])
                    bank2, bkey2 = next_pj()
                    proj(gp, gpk, tc, bank2, bkey2)
                    g, gk = silu_gate(bank2, bkey2, wd)
                    STT(mix[:, cz, t0:t0 + wd], bank[:, :wd], pscale[:, l, cz:cz + 1], g[:, :wd], ALU.mult, ALU.mult,
                        r=[bkey, gk, "pscale"], w=[("mix", cz, tc)])
            pj_list[0] = [6, 0]

        def phase_C(l, b, last):
            for c2 in range(2):
                mc = 2 + c2
                if c2 == 0:
                    fence()
                    MSET("pool", nvaug[:, :, :, 1, :], 1.0, r=["PH"], w=["nvaug"])
                    for tc_ in range(5):
                        t0_, wd_ = TCH[tc_]
                        MSET("pool", nkz[0][64:128, t0_:t0_ + wd_], 0.0, r=["PH"], w=[("nkT", tc_)])
                        MSET("pool", nkz[1][0:64, t0_:t0_ + wd_], 0.0, r=["PH"], w=[("nkT", tc_)])
                DMA(ebuf[:], ebias_d[l, c2], r=["PH"], w=["ebuf"], grp="eb")
                TS("dve", ebuf[:], ebuf[:], 8.0, ALU.mult, r=["ebuf", "PH"], w=["ebuf"])
                pj_list[0] = [6, 0, 2]
                ms_list[0] = [7, 1, 3, 4, 5]
                kp, kpk = WS.next(("in", l, P_NK + c2))
                qp, qpk = WS.next(("in", l, P_NQ + c2))
                units = []
                for tc in range(5):
                    t0, wd = TCH[tc]
                    units.append(nr_stages(kp, kpk, tc, gains[:, l, 3:4],
                                           [(nkz[0][0:64, t0:t0 + wd], slice(0, 64)), (nkz[1][64:128, t0:t0 + wd], slice(64, 128))],
                                           ("nkT", tc), False, extra_r=["PH"], five=True))
                for tc in range(4 if last else 5):
                    t0, wd = TCH[tc]
                    units.append(nr_stages(qp, qpk, tc, gains[:, l, 2:3], mix[:, mc, t0:t0 + wd], ("mix", mc, tc), False,
                                           five=True))
                run_pipelined(units)
                vp, vpk = WS.next(("in", l, P_NV + c2))
                toks = [ti * 128 for ti in range(18)] + [64 + 128 * o for o in range(15)]
                for g4 in range(9):
                    tl = list(range(g4 * 4, min(33, g4 * 4 + 4)))
                    bank, bkey = next_pj()
                    for qi, ti in enumerate(tl):
                        tk0 = toks[ti]
                        rk = sorted(set([("hT", min(tk0 // 512, 4)), ("hT", min((tk0 + 127) // 512, 4))]))
                        for k in range(8):
                            MM(bank[:, qi * 128:(qi + 1) * 128], lhsT=hT[:, k, tk0:tk0 + 128], rhs=vp[:, k, :],
                               start=(k == 0), stop=(k == 7), r=[vpk] + rk, w=[bkey])
                    n = len(tl)
                    CP("act", nvaug[:, tl[0]:tl[0] + n, :, 0, :],
                       bank[:, 0:n * 128].rearrange("p (t k d) -> p t k d", t=n, k=2),
                       r=[bkey, "PH"], w=["nvaug"])
                gp, gpk = WS.next(("in", l, P_NG + c2))
                ev = ebuf.rearrange("p h (q two) c -> p h q two c", two=2)
                pj_list[0] = [6]
                ms_list[0] = [7]
                LA = 3
                ptc = [0]
                pt_all = [("PT", j_) for j_ in range(3)] + [("PT", j_, h_) for j_ in range(3) for h_ in range(2)]
                MSET("pool", dummy[:, :], 0.0, w=pt_all + ["dummy"])
                for tc in range(4):
                    items = [(hh, rr_) for rr_ in range(8) for hh in (0, 1)]
                    pendq = []
                    for s in range(len(items) + LA):
                        cur = None
                        if s < len(items):
                            hh, rr_ = items[s]
                            r = tc * 8 + rr_
                            r0 = min(max(r - 4, 0), 24)
                            kb = 64 * r0
                            pr = slice(64 * hh, 64 * hh + 64)
                            sc, sck = BK[s % 4], ("PS", s % 4)
                            qap = mix[:, mc, 64 * r:64 * r + 64]
                            rkeys = sorted(set(("nkT", min((kb + 128 * i) // 512, 3)) for i in range(4)) |
                                           set(("nkT", min((kb + 128 * i + 127) // 512, 3)) for i in range(4)))
                            for i in range(4):
                                MM(sc[:, 64 * i:64 * i + 64], lhsT=nkz[hh][:, kb + 128 * i:kb + 128 * i + 128], rhs=qap,
                                   r=list(rkeys) + [("mix", mc, tc), "PH"], w=[sck])
                            for i2 in range(2):
                                MM(sc[:, 256 + 64 * i2:256 + 64 * i2 + 64],
                                   lhsT=nkz[hh][:, SEQ + 128 * i2:SEQ + 128 * i2 + 128], rhs=qap,
                                   r=[("nkT", 4), ("mix", mc, tc), "PH"], w=[sck])
                            s0 = r0 - r + 7
                            esl = ev[:, hh, s0 // 2:s0 // 2 + 4, s0 % 2, :]
                            sc3 = sc[:, 0:256].rearrange("p (i c) -> p i c", c=64)
                            TT("dve", sc3, sc3, esl, ALU.add, r=[sck, "ebuf", "PH"], w=[sck])
                            pj_ = ptc[0] % 6
                            ptc[0] += 1
                            pt, ptk = PT.tiles[pj_ // 2][:, 512 * (pj_ % 2):512 * (pj_ % 2) + 512], ("PT", pj_ // 2, pj_ % 2)
                            ACT(pt[:, 0:384], sc[:, 0:384], AF.Exp, r=[sck], w=[ptk], scale=0.125)
                            if r0 % 2 == 0:
                                vts = [r0 // 2 + i for i in range(4)]
                            else:
                                vts = [18 + (r0 - 1) // 2 + i for i in range(4)]
                            vts += [16, 17]
                            cur = (hh, rr_, pt, ptk, vts)
                        pendq.append(cur)
                        if len(pendq) > LA and pendq[0] is not None:
                            hh_, rr2, pt_, ptk_, vts_ = pendq[0]
                            for idx, vt in enumerate(vts_):
                                MM(OA[hh_][:, 64 * rr2:64 * rr2 + 64],
                                   lhsT=nvaug[:, vt, hh_, :, :].rearrange("p a d -> p (a d)"),
                                   rhs=pt_[:, 64 * idx:64 * idx + 64], start=(idx == 0), stop=(idx == 5),
                                   r=[ptk_, "nvaug", "PH"], w=[OAK[hh_]])
                        if len(pendq) > LA:
                            pendq.pop(0)
                    finalize_attn(gp, gpk, tc, mc)
                pj_list[0] = [6, 0]
                ms_list[0] = [7, 1]
                for hh in (range(2) if not last else ()):
                    pr = slice(64 * hh, 64 * hh + 64)
                    sc, sck = SC[hh], SCK[hh]
                    for i2 in range(2):
                        MM(sc[:, 256 * i2:256 * i2 + 256], lhsT=nkz[hh][:, SEQ + 128 * i2:SEQ + 128 * i2 + 128],
                           rhs=mix[:, mc, SEQ:SEQ + NCTX], r=[("nkT", 4), ("mix", mc, 4), "PH"], w=[sck])
                    pj_ = ptc[0] % 6
                    ptc[0] += 1
                    pt, ptk = PT.tiles[pj_ // 2][:, 512 * (pj_ % 2):512 * (pj_ % 2) + 512], ("PT", pj_ // 2, pj_ % 2)
                    ACT(pt[:, 0:512], sc[:, :], AF.Exp, r=[sck], w=[ptk], scale=0.125)
                    for i2 in range(2):
                        MM(OA[hh][:, 0:256], lhsT=nvaug[:, 16 + i2, hh, :, :].rearrange("p a d -> p (a d)"),
                           rhs=pt[:, 256 * i2:256 * i2 + 256], start=(i2 == 0), stop=(i2 == 1),
                           r=[ptk, "nvaug", "PH"], w=[OAK[hh]])
                if not last:
                    finalize_attn(gp, gpk, 4, mc)
                MSET("pool", dummy[:, :], 0.0, w=pt_all + ["dummy"])
            y_update(l, b, ([0, 1, 2, 3] if "B" in phases else [2, 3]), "outC", last)

        final = []
        for b in range(n_batch):
            fence()
            for i in range(18):
                src = x_d[b, i * 128:(i + 1) * 128, :] if i < 16 else ctx_d[b, (i - 16) * 128:(i - 15) * 128, :]
                xs, xsk = xstage[i % NXS], ("xs", i % NXS)
                DMA(xs, src, r=["PH"], w=[xsk], grp="xs%d" % (i % NXS))
                tc = min(i // 4, 4)
                for half in range(2):
                    bank, bkey = next_pj()
                    for kk in range(4):
                        k = half * 4 + kk
                        S.add("pe", (lambda o, i_: (lambda e: e.transpose(out=o, in_=i_, identity=ident[:])))(
                            bank[:, kk * 128:(kk + 1) * 128], xs[:, k * 128:(k + 1) * 128]),
                            [xsk, "ident", "PH"], [bkey])
                    CP("dve" if half == 0 else "act", xT[:, half * 4:half * 4 + 4, i * 128:(i + 1) * 128],
                       bank[:, :].rearrange("p (k t) -> p k t", t=128), r=[bkey], w=[("xT", tc)])
            for l in range(n_layers):
                last = (l == NL - 1)
                if "A" in phases and stage >= 2:
                    phase_A_init()
                if stage >= 1:
                    xnorm(l, b)
                if "A" in phases and stage >= 2:
                    phase_A(l, b, last)
                if "B" in phases:
                    phase_B(l, b, last)
                if "C" in phases:
                    phase_C(l, b, last)
            fence()
            for i in range(16):
                xs, xsk = xstage[i % NXS], ("xs", i % NXS)
                tc = i // 4
                for half in range(2):
                    bank, bkey = next_pj()
                    for kk in range(4):
                        k = half * 4 + kk
                        S.add("pe", (lambda o, i_: (lambda e: e.transpose(out=o, in_=i_, identity=ident[:])))(
                            bank[:, kk * 128:(kk + 1) * 128], xT[:, k, i * 128:(i + 1) * 128]),
                            [("xT", tc), "ident"], [bkey])
                    CP("dve" if half == 0 else "act", xs[:, half * 512:(half + 1) * 512], bank[:, :],
                       r=[bkey, "PH"], w=[xsk])
                final.append(DMA(out_d[b, i * 128:(i + 1) * 128, :], xs, r=[xsk, "PH"], grp="o%d" % (i % NXS)))
        assert stage < 99 or WS.consumed == len(WS.specs), (WS.consumed, len(WS.specs))
        S.emit(nc, final_waits=final[-NXS:])
    return nc


def _f32(a):
    return np.ascontiguousarray(np.asarray(a, dtype=np.float32))


def prep_shared(c_ctx, norm_gain, w_mod, b_mod, w_in, att_q_gain, att_k_gain, pool_w, pool_scale,
                na_q_gain, na_k_gain, na_rpb, w_out):
    sh = {}
    w_mod = _f32(w_mod)
    sh["wmod"] = _f32(w_mod.reshape(NL, 8, 128, 6, 512).transpose(0, 3, 2, 1, 4))
    sh["bmod"] = _f32(_f32(b_mod).reshape(NL, 24, 128).transpose(2, 0, 1))
    sh["ngain"] = _f32(_f32(norm_gain).reshape(NL, 8, 128).transpose(2, 0, 1))
    w_in = _f32(w_in)
    cols = []
    cols.append(np.r_[512:576, 512:576])
    cols.append(np.r_[576:640, 576:640])
    cols.append(np.r_[640:768])
    for c in range(4):
        cols.append(np.r_[c * 128:(c + 1) * 128])
    for c in range(4):
        cols.append(np.r_[768 + c * 128:768 + (c + 1) * 128])
    for base in (1280, 1536, 2048, 2304, 1792, 2560):
        for c in range(2):
            cols.append(np.r_[base + c * 128:base + (c + 1) * 128])
    assert len(cols) == NPAN
    win = np.empty((NL, NPAN, 128, 8, 128), np.float32)
    for pi, cc in enumerate(cols):
        win[:, pi] = w_in[:, :, cc].reshape(NL, 8, 128, 128).transpose(0, 2, 1, 3)
    sh["win"] = win
    sh["wout"] = _f32(_f32(w_out).reshape(NL, 8, 128, 8, 128).transpose(0, 3, 2, 1, 4))
    g = np.stack([_f32(att_q_gain), _f32(att_k_gain), _f32(na_q_gain), _f32(na_k_gain)], axis=-1)
    sh["gains"] = _f32(np.tile(g, (1, 2, 1)).transpose(1, 0, 2))
    p = np.arange(128)
    d = p % 64
    inv_freq = (np.float32(10000.0) ** (-np.arange(16, dtype=np.float32) / np.float32(16))).astype(np.float32)
    f = inv_freq[d % 16]
    sign = np.where((d % 32) < 16, -1.0, 1.0).astype(np.float32)
    isrow = d < 32
    rows = np.arange(32, dtype=np.float32)
    colsv = np.arange(64, dtype=np.float32)
    angA = (rows[None, :] * f[:, None]).astype(np.float32)
    angB = (colsv[None, :] * f[:, None]).astype(np.float32)
    cosA = np.where(isrow[:, None], np.cos(angA), 1.0)
    cosB = np.where(isrow[:, None], 1.0, np.cos(angB))
    sinA = np.where(isrow[:, None], sign[:, None] * np.sin(angA), 1.0)
    sinB = np.where(isrow[:, None], 1.0, sign[:, None] * np.sin(angB))
    sh["ropetab"] = _f32(np.concatenate([cosA, cosB, sinA, sinB], axis=1))
    partner = np.where((p % 32) < 16, p + 16, p - 16)
    rm = np.zeros((128, 128), np.float32)
    rm[partner, p] = 1.0
    sh["rmat"] = rm
    sh["ident"] = np.eye(128, dtype=np.float32)
    pw = _f32(pool_w)
    poolw = np.zeros((NL, 128, 2, 128), np.float32)
    for cz in range(2):
        for j in range(2):
            poolw[:, j * 64:(j + 1) * 64, cz, j * 64:(j + 1) * 64] = pw[:, 2 * cz + j]
    sh["poolw"] = poolw
    sh["pscale"] = _f32(_f32(pool_scale).reshape(NL, 2, 128).transpose(2, 0, 1))
    pt = np.zeros((128, 2, 17), np.float32)
    for cz in range(2):
        for j in range(2):
            w = (2, 4, 8, 16)[2 * cz + j]
            t = np.arange(8)
            cs = np.minimum(w, t + w // 2).astype(np.float32)
            ce = np.minimum(w, (8 - t) + w // 2 - 1 + 0).astype(np.float32)
            ce = np.minimum(w, 8 - t + w // 2).astype(np.float32)
            pt[j * 64:(j + 1) * 64, cz, 0] = 1.0 / w
            pt[j * 64:(j + 1) * 64, cz, 1:9] = 1.0 / cs
            pt[j * 64:(j + 1) * 64, cz, 9:17] = 1.0 / ce
    sh["pooltab"] = pt
    rpb = _f32(na_rpb)
    kc = np.arange(64)[:, None]
    cq = np.arange(64)[None, :]
    c0 = np.clip(cq - 8, 0, 48)
    valid = (kc >= c0) & (kc < c0 + 16)
    dcol = np.clip(kc - cq + 15, 0, 30)
    Dh = np.where(valid[None, None, None], rpb[:, :, :, dcol], np.float32(NEG))
    eb = np.empty((NL, 2, 128, 2, 14, 64), np.float32)
    for c2 in range(2):
        for hh in range(2):
            h = 2 * c2 + hh
            for j in range(2):
                for di in range(14):
                    eb[:, c2, j * 64:(j + 1) * 64, hh, di, :] = Dh[:, h, di + j]
    sh["ebias"] = eb
    return sh


_NC_CACHE = {}


def kernel(x, c, ctx, c_ctx, norm_gain, w_mod, b_mod, w_in, att_q_gain, att_k_gain,
           pool_w, pool_scale, na_q_gain, na_k_gain, na_rpb, w_out):
    x = _f32(x)
    c = _f32(c)
    ctx = _f32(ctx)
    c_ctx = _f32(c_ctx)
    sh = prep_shared(c_ctx, norm_gain, w_mod, b_mod, w_in, att_q_gain, att_k_gain, pool_w, pool_scale,
                     na_q_gain, na_k_gain, na_rpb, w_out)
    if "nc" not in _NC_CACHE:
        _NC_CACHE["nc"] = build_nc()
    nc = _NC_CACHE["nc"]
    in_maps = []
    for i in range(8):
        m = dict(sh)
        m["x"] = np.ascontiguousarray(x[2 * i:2 * i + 2])
        m["ctx"] = np.ascontiguousarray(ctx[2 * i:2 * i + 2])
        vecs = np.stack([c[2 * i], c[2 * i + 1], c_ctx], axis=-1)
        m["cT"] = _f32(vecs.reshape(8, 128, 3).transpose(1, 0, 2))
        in_maps.append(m)
    res = run_bass_kernel_spmd(nc, in_maps, core_ids=list(range(8)))
    return np.concatenate([np.asarray(r["out"], dtype=np.float32) for r in res.results], axis=0)
```

```python
import contextlib
import numpy as np
import concourse.bass as bass
import concourse.mybir as mybir
from concourse.bass_utils import run_bass_kernel_spmd

F32 = mybir.dt.float32
BF16 = mybir.dt.bfloat16
ALU = mybir.AluOpType
AF = mybir.ActivationFunctionType

D = 1024
NL = 4
SEQ = 2048
NCTX = 256
T = SEQ + NCTX
EPS = 1e-6
NEG = -30000.0
TCH = [(0, 512), (512, 512), (1024, 512), (1536, 512), (2048, 256)]
ENGS = ("pe", "act", "dve", "pool", "sp")
PSUM_KEYS = ("PS",)

P_K0, P_K1, P_V = 0, 1, 2
P_Q, P_G, P_Z, P_BG, P_NK, P_NV, P_NQ, P_NG = 3, 7, 11, 13, 15, 17, 19, 21
NPAN = 23


class Op:
    __slots__ = ("eng", "idx", "fn", "waits", "signal", "dma_grp", "dma_cnt", "sigval")

    def __init__(self, eng, idx, fn):
        self.eng = eng
        self.idx = idx
        self.fn = fn
        self.waits = []
        self.signal = False
        self.dma_grp = None
        self.dma_cnt = 0
        self.sigval = 0


class Sched:
    def __init__(self):
        self.q = {e: [] for e in ENGS}
        self.last_w = {}
        self.readers = {}
        self.maxwait = {e: {} for e in ENGS}
        self.dma_cnt = {}

    def add(self, eng, fn, reads=(), writes=(), dma=None):
        op = Op(eng, len(self.q[eng]), fn)
        if dma is not None:
            op.dma_grp = dma
            self.dma_cnt[dma] = self.dma_cnt.get(dma, 0) + 1
            op.dma_cnt = self.dma_cnt[dma]
        best = {}
        for k in reads:
            w = self.last_w.get(k)
            if w is not None:
                self._cand(best, w, eng)
            if isinstance(k, tuple) and k[0] in PSUM_KEYS:
                for r in self.readers.get(k, ()):
                    if r.eng != eng:
                        self._cand(best, r, eng)
        for k in writes:
            w = self.last_w.get(k)
            if w is not None:
                self._cand(best, w, eng)
            for r in self.readers.get(k, ()):
                self._cand(best, r, eng)
        mw = self.maxwait[eng]
        for src, (pos, d) in best.items():
            if mw.get(src, -1) >= pos:
                continue
            mw[src] = pos
            d.signal = True
            op.waits.append(d)
        for k in writes:
            self.last_w[k] = op
            self.readers[k] = []
        for k in reads:
            self.readers.setdefault(k, []).append(op)
        self.q[eng].append(op)
        return op

    @staticmethod
    def _cand(best, d, eng):
        if d.dma_grp is not None:
            src = ("dma", d.dma_grp)
            pos = d.dma_cnt
        else:
            if d.eng == eng and eng == "pe":
                return
            src = d.eng
            pos = d.idx
        cur = best.get(src)
        if cur is None or cur[0] < pos:
            best[src] = (pos, d)

    def emit(self, nc, final_waits=()):
        for op in final_waits:
            op.signal = True
        for e in ENGS:
            n = 0
            for op in self.q[e]:
                if op.dma_grp is None and op.signal:
                    n += 1
                    op.sigval = n
        grps = sorted(self.dma_cnt.keys())
        with contextlib.ExitStack() as st:
            sems = {}
            for e in ENGS:
                sems[e] = st.enter_context(nc.semaphore("sem_" + e))
            for g in grps:
                sems[("dma", g)] = st.enter_context(nc.semaphore("semd_" + str(g)))
            block = st.enter_context(nc.Block())

            def tok(d):
                if d.dma_grp is not None:
                    return sems[("dma", d.dma_grp)], 16 * d.dma_cnt
                return sems[d.eng], d.sigval

            def run(engname, e):
                for op in self.q[engname]:
                    for d in op.waits:
                        s, v = tok(d)
                        e.wait_ge(s, v)
                    ins = op.fn(e)
                    if op.dma_grp is not None:
                        ins.then_inc(sems[("dma", op.dma_grp)], 16)
                    elif op.signal:
                        ins.then_inc(sems[engname], 1)
                if engname == "sp":
                    for d in final_waits:
                        s, v = tok(d)
                        e.wait_ge(s, v)

            @block.tensor
            def _(e):
                run("pe", e)

            @block.scalar
            def _(e):
                run("act", e)

            @block.vector
            def _(e):
                run("dve", e)

            @block.gpsimd
            def _(e):
                run("pool", e)

            @block.sync
            def _(e):
                run("sp", e)


class Ring:
    def __init__(self, name, tiles):
        self.name = name
        self.tiles = tiles
        self.i = 0

    def get(self):
        j = self.i % len(self.tiles)
        self.i += 1
        return self.tiles[j], (self.name, j)


def build_nc(n_layers=NL, n_batch=2, phases="ABC", stage=99):
    nc = bass.Bass("TRN2", target_bir_lowering=False)

    def din(name, shape):
        return nc.dram_tensor(name, list(shape), F32, kind="ExternalInput").ap()

    x_d = din("x", [2, SEQ, D])
    ctx_d = din("ctx", [2, NCTX, D])
    cT_d = din("cT", [128, 8, 3])
    wmod_d = din("wmod", [NL, 6, 128, 8, 512])
    bmod_d = din("bmod", [128, NL, 24])
    ngain_d = din("ngain", [128, NL, 8])
    win_d = din("win", [NL, NPAN, 128, 8, 128])
    wout_d = din("wout", [NL, 8, 128, 8, 128])
    gains_d = din("gains", [128, NL, 4])
    rope_d = din("ropetab", [128, 192])
    rmat_d = din("rmat", [128, 128])
    ident_d = din("ident", [128, 128])
    poolw_d = din("poolw", [NL, 128, 2, 128])
    pscale_d = din("pscale", [128, NL, 2])
    pooltab_d = din("pooltab", [128, 2, 17])
    ebias_d = din("ebias", [NL, 2, 128, 2, 14, 64])
    out_d = nc.dram_tensor("out", [2, SEQ, D], F32, kind="ExternalOutput").ap()

    S = Sched()
    with contextlib.ExitStack() as st:
        def sb(name, shape, dt=F32):
            return st.enter_context(nc.sbuf_tensor(name, list(shape), dt))

        xT = sb("xT", [128, 8, T])
        hT = sb("hT", [128, 8, T], BF16)
        mix = sb("mix", [128, 4, T], BF16)
        PH = sb("PH", [128, 8320])
        stg = [sb("stg%d" % i, [128, 8, 128]) for i in range(2)]
        wb = [sb("wb%d" % i, [128, 8, 128], BF16) for i in range(4)]
        Wt = Ring("W", [sb("wk%d" % i, [128, 512]) for i in range(9)])
        PT = Ring("PT", [sb("pt%d" % i, [128, 1024], BF16) for i in range(3)])
        ident = sb("ident_s", [128, 128])
        rmat = sb("rmat_s", [128, 128])
        ones128 = sb("ones128", [128, 128])
        bones = sb("bones", [128, 128])
        ropet = sb("ropet", [128, 192])
        gains = sb("gains_s", [128, NL, 4])
        ngain = sb("ngain_s", [128, NL, 8])
        bmod = sb("bmod_s", [128, NL, 24])
        pscale = sb("pscale_s", [128, NL, 2])
        pooltab = sb("pooltab_s", [128, 2, 17])
        cT = sb("cT_s", [128, 8, 3])
        scT = sb("scT", [128, 8, 3])
        mod_all = sb("mod_all", [128, NL, 24, 3])
        gs_all = sb("gs_all", [128, NL, 8, 3])
        poolwb = sb("poolwb", [128, 2, 128], BF16)
        dummy = sb("fence_dummy", [128, 8])

        def ps(name):
            return st.enter_context(nc.psum_tensor(name, [128, 512], F32))

        PSALL = st.enter_context(nc.psum_tensor("PSALL", [128, 4096], F32))
        BK = [PSALL[:, 512 * i:512 * (i + 1)] for i in range(8)]
        SC = [BK[2], BK[3]]
        SCK = [("PS", 2), ("PS", 3)]
        OA = [BK[4], BK[5]]
        OAK = [("PS", 4), ("PS", 5)]
        SCD = [PSALL[:, 0:1024], PSALL[:, 1024:2048]]
        SCDK = [[("PS", 0), ("PS", 1)], [("PS", 2), ("PS", 3)]]
        pj_list = [[6, 0]]
        ms_list = [[7, 1]]

        PHb = PH[:, :].bitcast(BF16)
        kz = [[PHb[:, (2 * kv + hh) * T:(2 * kv + hh + 1) * T] for hh in range(2)] for kv in range(2)]
        vaugA = PHb[:, 4 * T:4 * T + 18 * 256].rearrange("p (t k a d) -> p t k a d", t=18, k=2, a=2)
        nkz = [PHb[:, hh * T:(hh + 1) * T] for hh in range(2)]
        nvaug = PHb[:, 2 * T:2 * T + 33 * 256].rearrange("p (t k a d) -> p t k a d", t=33, k=2, a=2)
        e_off = (2 * T + 33 * 256 + 1) // 2
        ebuf = PH[:, e_off:e_off + 2 * 14 * 64].rearrange("p (h d c) -> p h d c", h=2, d=14)
        assert e_off + 2 * 14 * 64 <= 8320
        ZN = 2352
        zbuf = PH[:, 0:ZN]
        tA = PH[:, ZN:2 * ZN]
        tB = PH[:, 2 * ZN:3 * ZN]
        assert 3 * ZN <= 8320
        NXS = 6
        xstage = [PH[:, i * 1024:(i + 1) * 1024] for i in range(NXS)]

        def MM(out, lhsT, rhs, start=True, stop=True, r=(), w=()):
            S.add("pe", lambda e: e.matmul(out, lhsT=lhsT, rhs=rhs, start=start, stop=stop), r, w)

        def ACT(out, in_, func, r=(), w=(), scale=None, bias=None):
            kw = {}
            if scale is not None:
                kw["scale"] = scale
            if bias is not None:
                kw["bias"] = bias
            S.add("act", lambda e: e.activation(out=out, in_=in_, func=func, **kw), r, w)

        def TT(eng, out, in0, in1, op, r=(), w=()):
            S.add(eng, lambda e: e.tensor_tensor(out=out, in0=in0, in1=in1, op=op), r, w)

        def TS(eng, out, in0, s1, op0, r=(), w=(), s2=None, op1=None):
            if op1 is None:
                S.add(eng, lambda e: e.tensor_scalar(out=out, in0=in0, scalar1=s1, scalar2=None, op0=op0), r, w)
            else:
                S.add(eng, lambda e: e.tensor_scalar(out=out, in0=in0, scalar1=s1, scalar2=s2, op0=op0, op1=op1), r, w)

        def STT(out, in0, scalar, in1, op0, op1, r=(), w=()):
            S.add("dve", lambda e: e.scalar_tensor_tensor(out=out, in0=in0, scalar=scalar, in1=in1, op0=op0, op1=op1), r, w)

        def CP(eng, out, in_, r=(), w=()):
            if eng == "act":
                S.add("act", lambda e: e.activation(out=out, in_=in_, func=AF.Copy), r, w)
            else:
                S.add(eng, lambda e: e.tensor_copy(out=out, in_=in_), r, w)

        def MSET(eng, ap, val, r=(), w=()):
            S.add(eng, lambda e: e.memset(ap, val), r, w)

        def DMA(out, in_, r=(), w=(), grp="m"):
            return S.add("sp", lambda e: e.dma_start(out=out, in_=in_), r, w, dma=grp)

        def fence():
            MSET("pool", dummy[:, :], 0.0, r=(), w=["PH", "dummy"])

        class WStream:
            def __init__(self):
                self.specs = []
                self.issued = 0
                self.consumed = 0

            def _issue(self):
                n = self.issued
                ap, nk, _tag = self.specs[n]
                si, wi = n % 2, n % 4
                DMA(stg[si][:, 0:nk, :], ap, w=[("stg", si)], grp="stg%d" % si)
                in_attn = (_tag[0] == "outA") or (_tag[0] == "in" and P_Q < _tag[2] < P_Z)
                CP("pool" if in_attn else "act", wb[wi][:, 0:nk, :], stg[si][:, 0:nk, :], r=[("stg", si)], w=[("wb", wi)])
                self.issued += 1

            def next(self, check=None):
                while self.issued < min(len(self.specs), self.consumed + 3 - 1):
                    self._issue()
                n = self.consumed
                if check is not None:
                    assert self.specs[n][2] == check, (self.specs[n][2], check)
                self.consumed += 1
                return wb[n % 4], ("wb", n % 4)

        WS = WStream()

        def add_spec(ap, nk, tag):
            WS.specs.append((ap, nk, tag))

        for b in range(n_batch):
            for l in range(n_layers):
                if "A" in phases:
                    for pid in (P_K0, P_V):
                        add_spec(win_d[l, pid], 8, ("in", l, pid))
                    add_spec(win_d[l, P_Q], 8, ("in", l, P_Q))
                    for c in range(4):
                        add_spec(win_d[l, P_G + c], 8, ("in", l, P_G + c))
                        if c < 3:
                            add_spec(win_d[l, P_Q + c + 1], 8, ("in", l, P_Q + c + 1))
                    for m in range(8):
                        add_spec(wout_d[l, m, :, 0:4, :], 4, ("outA", l, m))
                if "B" in phases:
                    for cz in range(2):
                        add_spec(win_d[l, P_Z + cz], 8, ("in", l, P_Z + cz))
                        add_spec(win_d[l, P_BG + cz], 8, ("in", l, P_BG + cz))
                if "C" in phases:
                    for c2 in range(2):
                        for pid in (P_NK, P_NQ, P_NV, P_NG):
                            add_spec(win_d[l, pid + c2], 8, ("in", l, pid + c2))
                    for m in range(8):
                        if "B" in phases:
                            add_spec(wout_d[l, m, :, 4:8, :], 4, ("outC", l, m))
                        else:
                            add_spec(wout_d[l, m, :, 6:8, :], 2, ("outC", l, m))

        DMA(ident[:], ident_d[:], w=["ident"], grp="c_ident")
        DMA(rmat[:], rmat_d[:], w=["rmat"], grp="c_rmat")
        DMA(ropet[:], rope_d[:], w=["ropet"], grp="c_ropet")
        DMA(gains[:], gains_d[:], w=["gains"], grp="c_gains")
        DMA(ngain[:], ngain_d[:], w=["ngain"], grp="c_ngain")
        DMA(bmod[:], bmod_d[:], w=["bmod"], grp="c_bmod")
        DMA(pscale[:], pscale_d[:], w=["pscale"], grp="c_pscale")
        DMA(pooltab[:], pooltab_d[:], w=["pooltab"], grp="c_pooltab")
        DMA(cT[:], cT_d[:], w=["cT"], grp="c_cT")
        MSET("pool", ones128[:], 1.0, w=["ones"])
        MSET("pool", bones[:], 0.0, w=["bones"])
        MSET("pool", bones[0:64, 0:64], 1.0, w=["bones"])
        MSET("pool", bones[64:128, 64:128], 1.0, w=["bones"])
        ACT(scT[:], cT[:], AF.Exp, r=["cT"], w=["scT"], scale=-1.0)
        ACT(scT[:], scT[:], AF.Ln, r=["scT"], w=["scT"], bias=1.0)
        ACT(scT[:], scT[:], AF.Exp, r=["scT"], w=["scT"], scale=-1.0)
        TT("dve", scT[:], scT[:], cT[:], ALU.mult, r=["scT", "cT"], w=["scT"])
        for l in range(n_layers):
            rows = []
            for cc in range(6):
                i = l * 6 + cc
                si = i % 2
                wslot = PH[:, si * 4096:(si + 1) * 4096].rearrange("p (k c) -> p k c", k=8)
                DMA(wslot, wmod_d[l, cc], r=["PH"], w=[("wms", si)], grp="wms%d" % si)
                bank, bkey = BK[7 if cc % 2 == 0 else 6], ("PS", 7 if cc % 2 == 0 else 6)
                for k in range(8):
                    MM(bank[0:3, :], lhsT=scT[:, k, :], rhs=wslot[:, k, :],
                       start=(k == 0), stop=(k == 7), r=[("wms", si), "scT", "PH"], w=[bkey])
                rt, rtk = Wt.get()
                CP("dve", rt[0:3, :], bank[0:3, :], r=[bkey], w=[rtk])
                rows.append((rt, rtk))
            for m in range(24):
                rt, rtk = rows[m // 4]
                off = (m % 4) * 128
                S.add("pe", (lambda o, i_: (lambda e: e.transpose(out=o, in_=i_, identity=ident[0:3, 0:3])))(
                    BK[1][:, m * 4:m * 4 + 3], rt[0:3, off:off + 128]), [rtk, "ident"], [("PS", 1)])
            msv = BK[1][:, 0:96].rearrange("p (m f) -> p m f", f=4)
            for v in range(3):
                TT("dve", mod_all[:, l, :, v], msv[:, :, v], bmod[:, l, :], ALU.add,
                   r=[("PS", 1), "bmod"], w=["mod"])
            for v in range(3):
                STT(gs_all[:, l, :, v], mod_all[:, l, 8:16, v], 1.0, ngain[:, l, :], ALU.add, ALU.mult,
                    r=["mod", "ngain"], w=["mod"])

        pjc = [0]

        def next_pj():
            lst = pj_list[0]
            i = lst[pjc[0] % len(lst)]
            pjc[0] += 1
            return BK[i], ("PS", i)

        msc = [0]

        def next_ms():
            lst = ms_list[0]
            i = lst[msc[0] % len(lst)]
            msc[0] += 1
            return BK[i], ("PS", i)

        def proj(panel, pkey, tc, bank, bkey):
            t0, wd = TCH[tc]
            for k in range(8):
                MM(bank[:, :wd], lhsT=panel[:, k, :], rhs=hT[:, k, t0:t0 + wd],
                   start=(k == 0), stop=(k == 7), r=[pkey, ("hT", tc)], w=[bkey])

        def xnorm(l, b):
            state = {}

            def stage1(tc):
                t0, wd = TCH[tc]
                ms, msk = next_ms()
                for k in range(8):
                    sq, sqk = Wt.get()
                    if k in (0, 3, 6):
                        TT("pool", sq[:, :wd], xT[:, k, t0:t0 + wd], xT[:, k, t0:t0 + wd], ALU.mult,
                           r=[("xT", tc)], w=[sqk])
                    else:
                        ACT(sq[:, :wd], xT[:, k, t0:t0 + wd], AF.Square, r=[("xT", tc)], w=[sqk])
                    MM(ms[:, :wd], lhsT=ones128[:], rhs=sq[:, :wd], start=(k == 0), stop=(k == 7),
                       r=[sqk, "ones"], w=[msk])
                state[tc] = (ms, msk)

            def stage2(tc):
                t0, wd = TCH[tc]
                v = b if tc < 4 else 2
                ms, msk = state[tc]
                ln, lnk = Wt.get()
                ACT(ln[:, :wd], ms[:, :wd], AF.Ln, r=[msk], w=[lnk], scale=1.0 / D, bias=EPS)
                ACT(ms[:, :wd], ln[:, :wd], AF.Exp, r=[lnk], w=[msk], scale=-0.5)
                for k in range(8):
                    t, tk = Wt.get()
                    TT("dve", t[:, :wd], xT[:, k, t0:t0 + wd], ms[:, :wd], ALU.mult,
                       r=[("xT", tc), msk], w=[tk])
                    if k in (1, 4, 7):
                        TS("pool", hT[:, k, t0:t0 + wd], t[:, :wd], gs_all[:, l, k, v:v + 1], ALU.mult,
                           s2=mod_all[:, l, k, v:v + 1], op1=ALU.add, r=[tk, "mod"], w=[("hT", tc)])
                    else:
                        ACT(hT[:, k, t0:t0 + wd], t[:, :wd], AF.Identity, r=[tk, "mod"], w=[("hT", tc)],
                            scale=gs_all[:, l, k, v:v + 1], bias=mod_all[:, l, k, v:v + 1])

            for i in range(6):
                if i < 5:
                    stage1(i)
                if i >= 1:
                    stage2(i - 1)

        def nr_stages(panel, pkey, tc, gcol, dst, dkey, rope, extra_r=(), micro=False, five=False):
            t0, wd = TCH[tc]
            st_ = {}

            hw_ = wd // 2

            def s0():
                st_["bank"], st_["bkey"] = next_pj()
                proj(panel, pkey, tc, st_["bank"], st_["bkey"])

            def pk(k):
                def f():
                    if k == 0:
                        st_["bank"], st_["bkey"] = next_pj()
                    MM(st_["bank"][:, :wd], lhsT=panel[:, k, :], rhs=hT[:, k, t0:t0 + wd],
                       start=(k == 0), stop=(k == 7), r=[pkey, ("hT", tc)], w=[st_["bkey"]])
                return f

            def s1e():
                bank, bkey = st_["bank"], st_["bkey"]
                qg, qgk = Wt.get()
                sq, sqk = Wt.get()
                TS("dve", qg[:, :wd], bank[:, :wd], gcol, ALU.mult, r=[bkey, "gains"], w=[qgk])
                ACT(sq[:, :wd], bank[:, :wd], AF.Square, r=[bkey], w=[sqk])
                st_.update(qg=qg, qgk=qgk, sq=sq, sqk=sqk)

            def s1m():
                ms, msk = next_ms()
                MM(ms[:, :hw_], lhsT=bones[:], rhs=st_["sq"][:, :hw_], r=[st_["sqk"], "bones"], w=[msk])
                st_.update(ms=ms, msk=msk)

            def s1a():
                s1e()
                s1m()

            def s1b():
                MM(st_["ms"][:, hw_:wd], lhsT=bones[:], rhs=st_["sq"][:, hw_:wd], r=[st_["sqk"], "bones"], w=[st_["msk"]])

            def s1():
                s1a()
                s1b()

            def s2e():
                ms, msk = st_["ms"], st_["msk"]
                rs, rsk = st_["sq"], st_["sqk"]
                ACT(rs[:, :wd], ms[:, :wd], AF.Ln, r=[msk], w=[rsk], scale=1.0 / 64, bias=EPS)
                ACT(rs[:, :wd], rs[:, :wd], AF.Exp, r=[rsk], w=[rsk], scale=-0.5)
                st_.update(rs=rs, rsk=rsk)

            def s2m():
                if rope and tc < 4:
                    qg, qgk = st_["qg"], st_["qgk"]
                    ms2, ms2k = next_ms()
                    MM(ms2[:, :hw_], lhsT=rmat[:], rhs=qg[:, :hw_], r=[qgk, "rmat"], w=[ms2k])
                    st_.update(ms2=ms2, ms2k=ms2k)

            def s2a():
                s2e()
                s2m()

            def s2b():
                if rope and tc < 4:
                    MM(st_["ms2"][:, hw_:wd], lhsT=rmat[:], rhs=st_["qg"][:, hw_:wd], r=[st_["qgk"], "rmat"], w=[st_["ms2k"]])

            def s2():
                s2a()
                s2b()

            def s3():
                qg, qgk, rs, rsk = st_["qg"], st_["qgk"], st_["rs"], st_["rsk"]
                if rope and tc < 4:
                    ms2, ms2k = st_["ms2"], st_["ms2k"]
                    nr = wd // 64
                    r0 = t0 // 64
                    cosA = ropet[:, r0:r0 + nr].unsqueeze(2).broadcast_to([128, nr, 64])
                    cosB = ropet[:, 32:96].unsqueeze(1).broadcast_to([128, nr, 64])
                    sinA = ropet[:, 96 + r0:96 + r0 + nr].unsqueeze(2).broadcast_to([128, nr, 64])
                    sinB = ropet[:, 128:192].unsqueeze(1).broadcast_to([128, nr, 64])

                    def v3(ap):
                        return ap.rearrange("p (r c) -> p r c", c=64)
                    t1, t1k = qg, qgk
                    t2, t2k = Wt.get()
                    TT("pool", v3(t1[:, :wd]), v3(qg[:, :wd]), cosA, ALU.mult, r=[qgk, "ropet"], w=[t1k])
                    TT("pool", v3(t1[:, :wd]), v3(t1[:, :wd]), cosB, ALU.mult, r=[t1k, "ropet"], w=[t1k])
                    TT("dve", v3(t2[:, :wd]), v3(ms2[:, :wd]), sinA, ALU.mult, r=[ms2k, "ropet"], w=[t2k])
                    TT("dve", v3(t2[:, :wd]), v3(t2[:, :wd]), sinB, ALU.mult, r=[t2k, "ropet"], w=[t2k])
                    TT("pool", t1[:, :wd], t1[:, :wd], t2[:, :wd], ALU.add, r=[t1k, t2k], w=[t1k])
                    src, srck, eng = t1, t1k, "dve"
                else:
                    src, srck, eng = qg, qgk, "pool"
                dsts = dst if isinstance(dst, list) else [(dst, slice(0, 128))]
                dkeys = dkey if isinstance(dkey, list) else [dkey]
                for (dap, prr) in dsts:
                    TT(eng, dap, src[prr, :wd], rs[prr, :wd], ALU.mult, r=[srck, rsk] + list(extra_r), w=dkeys)

            if micro:
                if rope and tc < 4:
                    return [pk(k) for k in range(8)] + [s1e, s1m, s1b, s2e, s2m, s2b, None, s3]
                return [pk(k) for k in range(8)] + [s1e, s1m, s1b, s2e, None, s3]
            if five:
                def s1mb():
                    s1m()
                    s1b()
                return [s0, s1e, s1mb, s2, s3]
            return [s0, s1, s2, s3]

        def place(sched, start, ops, per_step=1):
            for i, f in enumerate(ops):
                if f is not None:
                    sched.setdefault(start + i // per_step, []).append(f)

        def run_all(stages):
            for f in stages:
                f()

        def run_pipelined(units):
            n = len(units)
            ns = len(units[0])
            for step in range(n + ns - 1):
                for si in range(ns - 1, -1, -1):
                    ui = step - si
                    if 0 <= ui < n:
                        units[ui][si]()

        def silu_gate(bank, bkey, wd):
            e1, e1k = Wt.get()
            ACT(e1[:, :wd], bank[:, :wd], AF.Exp, r=[bkey], w=[e1k], scale=-1.0)
            ACT(e1[:, :wd], e1[:, :wd], AF.Ln, r=[e1k], w=[e1k], bias=1.0)
            ACT(e1[:, :wd], e1[:, :wd], AF.Exp, r=[e1k], w=[e1k], scale=-1.0)
            g, gk = Wt.get()
            TT("dve", g[:, :wd], bank[:, :wd], e1[:, :wd], ALU.mult, r=[bkey, e1k], w=[gk])
            return g, gk

        def gate_stages(gp, gpk, tc, holder, micro=False):
            t0, wd = TCH[tc]

            def g0():
                holder["bank"], holder["bkey"] = next_pj()
                proj(gp, gpk, tc, holder["bank"], holder["bkey"])

            def gk_(k):
                def f():
                    if k == 0:
                        holder["bank"], holder["bkey"] = next_pj()
                    MM(holder["bank"][:, :wd], lhsT=gp[:, k, :], rhs=hT[:, k, t0:t0 + wd],
                       start=(k == 0), stop=(k == 7), r=[gpk, ("hT", tc)], w=[holder["bkey"]])
                return f

            def g1():
                holder["g"], holder["gk"] = silu_gate(holder["bank"], holder["bkey"], wd)

            if micro:
                return [gk_(k) for k in range(8)] + [g1]
            return [g0, g1]

        def finalize_norm(tc, mc, g, gk):
            t0, wd = TCH[tc]
            ao, aok = Wt.get()
            for hh in range(2):
                rr, rrk = Wt.get()
                ACT(rr[64:128, :wd], OA[hh][64:128, :wd], AF.Ln, r=[OAK[hh]], w=[rrk])
                ACT(rr[64:128, :wd], rr[64:128, :wd], AF.Exp, r=[rrk], w=[rrk], scale=-1.0)
                TT("dve", ao[64 * hh:64 * hh + 64, :wd], OA[hh][0:64, :wd], rr[64:128, :wd], ALU.mult,
                   r=[OAK[hh], rrk], w=[aok])
            TT("pool", mix[:, mc, t0:t0 + wd], ao[:, :wd], g[:, :wd], ALU.mult, r=[aok, gk], w=[("mix", mc, tc)])

        def finalize_attn(gp, gpk, tc, mc):
            h = {}
            run_all(gate_stages(gp, gpk, tc, h))
            finalize_norm(tc, mc, h["g"], h["gk"])

        def y_update(l, b, mcs, tag, last):
            for m in range(8):
                wp, wpk = WS.next((tag, l, m))
                for tc, (t0, wd) in enumerate(TCH):
                    if tc == 4 and last:
                        continue
                    bank, bkey = next_pj()
                    for i, mc in enumerate(mcs):
                        MM(bank[:, :wd], lhsT=wp[:, i, :], rhs=mix[:, mc, t0:t0 + wd],
                           start=(i == 0), stop=(i == len(mcs) - 1), r=[wpk, ("mix", mc, tc)], w=[bkey])
                    v = b if tc < 4 else 2
                    STT(xT[:, m, t0:t0 + wd], bank[:, :wd], mod_all[:, l, 16 + m, v:v + 1], xT[:, m, t0:t0 + wd],
                        ALU.mult, ALU.add, r=[bkey, "mod", ("xT", tc)], w=[("xT", tc)])

        def phase_A_init():
            fence()
            MSET("dve", vaugA[:, :, :, 1, :], 1.0, r=["PH"], w=["vaug"])
            for kv in range(2):
                for tc_ in range(5):
                    t0_, wd_ = TCH[tc_]
                    MSET("dve", kz[kv][0][64:128, t0_:t0_ + wd_], 0.0, r=["PH"], w=[("kdup", kv, tc_)])
                    MSET("dve", kz[kv][1][0:64, t0_:t0_ + wd_], 0.0, r=["PH"], w=[("kdup", kv, tc_)])

        def phase_A(l, b, last):
            pj_list[0] = [6, 0, 2]
            ms_list[0] = [7, 1, 3, 4, 5]
            kunits = []
            kp, kpk = WS.next(("in", l, P_K0))
            for tc in range(5):
                t0, wd = TCH[tc]
                kunits.append(nr_stages(kp, kpk, tc, gains[:, l, 1:2],
                                        [(kz[0][0][0:64, t0:t0 + wd], slice(0, 64)),
                                         (kz[0][1][64:128, t0:t0 + wd], slice(0, 64)),
                                         (kz[1][0][0:64, t0:t0 + wd], slice(64, 128)),
                                         (kz[1][1][64:128, t0:t0 + wd], slice(64, 128))],
                                        [("kdup", 0, tc), ("kdup", 1, tc)], True, extra_r=["PH"], five=True))
            run_pipelined(kunits)
            vp, vpk = WS.next(("in", l, P_V))
            for g4 in range(5):
                bank, bkey = next_pj()
                tl = list(range(g4 * 4, min(18, g4 * 4 + 4)))
                for qi, ti in enumerate(tl):
                    for k in range(8):
                        MM(bank[:, qi * 128:(qi + 1) * 128], lhsT=hT[:, k, ti * 128:(ti + 1) * 128], rhs=vp[:, k, :],
                           start=(k == 0), stop=(k == 7), r=[vpk, ("hT", min(ti // 4, 4))], w=[bkey])
                n = len(tl)
                CP("act", vaugA[:, tl[0]:tl[0] + n, :, 0, :],
                   bank[:, 0:n * 128].rearrange("p (t k d) -> p t k d", t=n, k=2),
                   r=[bkey, "PH"], w=["vaug"])

            def q_units(c, qp, qpk, micro=False):
                return [nr_stages(qp, qpk, tc, gains[:, l, 0:1], mix[:, c, TCH[tc][0]:TCH[tc][0] + TCH[tc][1]],
                                  ("mix", c, tc), True, micro=micro, five=not micro) for tc in range(5)]

            qp, qpk = WS.next(("in", l, P_Q + 0))
            run_pipelined(q_units(0, qp, qpk)[:(4 if last else 5)])
            pj_list[0] = [6]
            ms_list[0] = [7]
            for c in range(4):
                kv = c // 2
                gp, gpk = WS.next(("in", l, P_G + c))
                side = []
                if c < 3:
                    qpn, qpnk = WS.next(("in", l, P_Q + c + 1))
                    side = q_units(c + 1, qpn, qpnk, micro=True)
                for tc in range(4 if last else 5):
                    t0, wd = TCH[tc]
                    tiles = list(range(18)) if tc < 4 else [16, 17]
                    nst = len(tiles)
                    sched = {}
                    gh = {}
                    gst = gate_stages(gp, gpk, tc, gh, micro=True)
                    if tc < 4:
                        if side and tc < 3:
                            place(sched, 0, side[tc])
                            place(sched, 9, gst[0:8], per_step=2)
                            place(sched, 14, gst[8:])
                        elif side:
                            place(sched, 0, side[3])
                            if not last:
                                place(sched, 9, side[4][0:8], per_step=2)
                                place(sched, 13, side[4][8:9])
                                place(sched, 16, side[4][9:])
                            place(sched, 14, gst[0:8], per_step=2)
                            place(sched, 18, gst[8:])
                        else:
                            place(sched, 4, gst[0:8])
                            place(sched, 13, gst[8:])
                    else:
                        place(sched, 0, gst[0:8], per_step=8)
                        place(sched, 1, gst[8:])
                    LAG = 2
                    pq = []
                    for s_ in range(nst + LAG):
                        cur = None
                        if s_ < nst:
                            t = tiles[s_]
                            scd, sk = SCD[s_ % 2], SCDK[s_ % 2]
                            for hh in range(2):
                                MM(scd[:, 512 * hh:512 * hh + wd], lhsT=kz[kv][hh][:, t * 128:(t + 1) * 128],
                                   rhs=mix[:, c, t0:t0 + wd],
                                   r=[("kdup", kv, min(t // 4, 4)), ("mix", c, tc), "PH"], w=[sk[hh]])
                            ptd, ptk = PT.get()
                            if wd == 512:
                                ACT(ptd[:, :], scd[:, :], AF.Exp, r=sk, w=[ptk], scale=0.125)
                            else:
                                ACT(ptd[:, :].rearrange("p (h w) -> p h w", h=2)[:, :, 0:wd],
                                    scd.rearrange("p (h w) -> p h w", h=2)[:, :, 0:wd], AF.Exp, r=sk, w=[ptk], scale=0.125)
                            cur = (t, ptd, ptk)
                        pq.append(cur)
                        if len(pq) > LAG and pq[0] is not None:
                            t_, ptd_, ptk_ = pq[0]
                            for hh in range(2):
                                MM(OA[hh][:, :wd], lhsT=vaugA[:, t_, kv, :, :].rearrange("p a d -> p (a d)"),
                                   rhs=ptd_[:, 512 * hh:512 * hh + wd], start=(t_ == tiles[0]), stop=(t_ == tiles[-1]),
                                   r=[ptk_, "vaug", "PH"], w=[OAK[hh]])
                        if len(pq) > LAG:
                            pq.pop(0)
                        for f in sched.get(s_, ()):
                            f()
                    for s_x in sorted(k_ for k_ in sched if k_ >= nst + LAG):
                        for f in sched[s_x]:
                            f()
                    finalize_norm(tc, c, gh["g"], gh["gk"])
            pj_list[0] = [6, 0]
            ms_list[0] = [7, 1]
            y_update(l, b, [0, 1, 2, 3], "outA", last)

        def phase_B(l, b, last):
            fence()
            pj_list[0] = [6, 0, 7, 1, 2, 3, 4, 5]
            pw_t, pw_k = Wt.get()
            pw32 = pw_t[:, 0:256].rearrange("p (c d) -> p c d", c=2)
            DMA(pw32, poolw_d[l], w=[pw_k], grp="pw")
            CP("pool", poolwb[:], pw32, r=[pw_k], w=["poolwb"])
            for ap in (zbuf[:, 0:16], zbuf[:, 2064:2080], zbuf[:, 2336:2352]):
                MSET("pool", ap, 0.0, r=["PH"], w=["zbuf"])
            segs = [(16, 0, SEQ), (2080, SEQ, NCTX)]
            for cz in range(2):
                zp, zpk = WS.next(("in", l, P_Z + cz))
                for tc in range(5):
                    t0, wd = TCH[tc]
                    zo = 16 + t0 if tc < 4 else 2080
                    bank, bkey = next_pj()
                    proj(zp, zpk, tc, bank, bkey)
                    CP("act", zbuf[:, zo:zo + wd], bank[:, :wd], r=[bkey, "PH"], w=["zbuf"])
                N = ZN
                def lvl(dst, dkey, src, skey, a, b_, up, dn):
                    mid = (a + b_) // 2
                    rk = [skey, (skey, 0), (skey, 1), "PH"]
                    TT("pool", dst[:, a:mid], src[:, a + up:mid + up], src[:, a - dn:mid - dn], ALU.add,
                       r=rk, w=[(dkey, 0)])
                    TT("dve", dst[:, mid:b_], src[:, mid + up:b_ + up], src[:, mid - dn:b_ - dn], ALU.add,
                       r=rk, w=[(dkey, 1)])

                lvl(tA, "tA", zbuf, "zbuf", 1, N, 0, 1)
                lvl(tB, "tB", tA, "tA", 2, N - 1, 1, 1)
                if cz == 1:
                    lvl(tA, "tA", tB, "tB", 4, N - 3, 2, 2)
                    lvl(tB, "tB", tA, "tA", 8, N - 7, 4, 4)
                srcs = [(tA, "tA", 0), (tB, "tB", 64)]
                hk = lambda k_: [(k_, 0), (k_, 1)]
                for (src, skey, p0) in srcs:
                    pr = slice(p0, p0 + 64)
                    for (zo, to, ln_) in segs:
                        tcs = [0, 1, 2, 3] if to == 0 else [4]
                        wkeys = [("mix", cz, tc) for tc in tcs]
                        STT(mix[pr, cz, to:to + ln_], src[pr, zo:zo + ln_], pooltab[pr, cz, 0:1], zbuf[pr, zo:zo + ln_],
                            ALU.mult, ALU.subtract, r=hk(skey) + ["zbuf", "pooltab", "PH"], w=wkeys)
                        for (eo, tab0) in ((0, 1), (ln_ - 8, 9)):
                            et, etk = Wt.get()
                            TT("pool", et[pr, 0:8], src[pr, zo + eo:zo + eo + 8], pooltab[pr, cz, tab0:tab0 + 8], ALU.mult,
                               r=hk(skey) + ["pooltab", "PH"], w=[etk])
                            TT("pool", mix[pr, cz, to + eo:to + eo + 8], et[pr, 0:8], zbuf[pr, zo + eo:zo + eo + 8],
                               ALU.subtract, r=[etk, "zbuf", "PH"], w=[wkeys[0] if eo == 0 else wkeys[-1]])
                gp, gpk = WS.next(("in", l, P_BG + cz))
                for tc in range(5):
                    t0, wd = TCH[tc]
                    bank, bkey = next_pj()
                    MM(bank[:, :wd], lhsT=poolwb[:, cz, :], rhs=mix[:, cz, t0:t0 + wd], r=["poolwb", ("mix", cz, tc)], w=[bkey])
                    bank2, bkey2 = next_pj()
                    proj(gp, gpk, tc, bank2, bkey2)
                    g, gk = silu_gate(bank2, bkey2, wd)
                    STT(mix[:, cz, t0:t0 + wd], bank[:, :wd], pscale[:, l, cz:cz + 1], g[:, :wd], ALU.mult, ALU.mult,
                        r=[bkey, gk, "pscale"], w=[("mix", cz, tc)])
            pj_list[0] = [6, 0]

        def phase_C(l, b, last):
            for c2 in range(2):
                mc = 2 + c2
                if c2 == 0:
                    fence()
                    MSET("pool", nvaug[:, :, :, 1, :], 1.0, r=["PH"], w=["nvaug"])
                    for tc_ in range(5):
                        t0_, wd_ = TCH[tc_]
                        MSET("pool", nkz[0][64:128, t0_:t0_ + wd_], 0.0, r=["PH"], w=[("nkT", tc_)])
                        MSET("pool", nkz[1][0:64, t0_:t0_ + wd_], 0.0, r=["PH"], w=[("nkT", tc_)])
                DMA(ebuf[:], ebias_d[l, c2], r=["PH"], w=["ebuf"], grp="eb")
                TS("dve", ebuf[:], ebuf[:], 8.0, ALU.mult, r=["ebuf", "PH"], w=["ebuf"])
                pj_list[0] = [6, 0, 2]
                ms_list[0] = [7, 1, 3, 4, 5]
                kp, kpk = WS.next(("in", l, P_NK + c2))
                qp, qpk = WS.next(("in", l, P_NQ + c2))
                units = []
                for tc in range(5):
                    t0, wd = TCH[tc]
                    units.append(nr_stages(kp, kpk, tc, gains[:, l, 3:4],
                                           [(nkz[0][0:64, t0:t0 + wd], slice(0, 64)), (nkz[1][64:128, t0:t0 + wd], slice(64, 128))],
                                           ("nkT", tc), False, extra_r=["PH"], five=True))
                for tc in range(4 if last else 5):
                    t0, wd = TCH[tc]
                    units.append(nr_stages(qp, qpk, tc, gains[:, l, 2:3], mix[:, mc, t0:t0 + wd], ("mix", mc, tc), False,
                                           five=True))
                run_pipelined(units)
                vp, vpk = WS.next(("in", l, P_NV + c2))
                toks = [ti * 128 for ti in range(18)] + [64 + 128 * o for o in range(15)]
                for g4 in range(9):
                    tl = list(range(g4 * 4, min(33, g4 * 4 + 4)))
                    bank, bkey = next_pj()
                    for qi, ti in enumerate(tl):
                        tk0 = toks[ti]
                        rk = sorted(set([("hT", min(tk0 // 512, 4)), ("hT", min((tk0 + 127) // 512, 4))]))
                        for k in range(8):
                            MM(bank[:, qi * 128:(qi + 1) * 128], lhsT=hT[:, k, tk0:tk0 + 128], rhs=vp[:, k, :],
                               start=(k == 0), stop=(k == 7), r=[vpk] + rk, w=[bkey])
                    n = len(tl)
                    CP("act", nvaug[:, tl[0]:tl[0] + n, :, 0, :],
                       bank[:, 0:n * 128].rearrange("p (t k d) -> p t k d", t=n, k=2),
                       r=[bkey, "PH"], w=["nvaug"])
                gp, gpk = WS.next(("in", l, P_NG + c2))
                ev = ebuf.rearrange("p h (q two) c -> p h q two c", two=2)
                pj_list[0] = [6]
                ms_list[0] = [7]
                LA = 3
                ptc = [0]
                pt_all = [("PT", j_) for j_ in range(3)] + [("PT", j_, h_) for j_ in range(3) for h_ in range(2)]
                MSET("pool", dummy[:, :], 0.0, w=pt_all + ["dummy"])
                for tc in range(4):
                    items = [(hh, rr_) for rr_ in range(8) for hh in (0, 1)]
                    pendq = []
                    for s in range(len(items) + LA):
                        cur = None
                        if s < len(items):
                            hh, rr_ = items[s]
                            r = tc * 8 + rr_
                            r0 = min(max(r - 4, 0), 24)
                            kb = 64 * r0
                            pr = slice(64 * hh, 64 * hh + 64)
                            sc, sck = BK[s % 4], ("PS", s % 4)
                            qap = mix[:, mc, 64 * r:64 * r + 64]
                            rkeys = sorted(set(("nkT", min((kb + 128 * i) // 512, 3)) for i in range(4)) |
                                           set(("nkT", min((kb + 128 * i + 127) // 512, 3)) for i in range(4)))
                            for i in range(4):
                                MM(sc[:, 64 * i:64 * i + 64], lhsT=nkz[hh][:, kb + 128 * i:kb + 128 * i + 128], rhs=qap,
                                   r=list(rkeys) + [("mix", mc, tc), "PH"], w=[sck])
                            for i2 in range(2):
                                MM(sc[:, 256 + 64 * i2:256 + 64 * i2 + 64],
                                   lhsT=nkz[hh][:, SEQ + 128 * i2:SEQ + 128 * i2 + 128], rhs=qap,
                                   r=[("nkT", 4), ("mix", mc, tc), "PH"], w=[sck])
                            s0 = r0 - r + 7
                            esl = ev[:, hh, s0 // 2:s0 // 2 + 4, s0 % 2, :]
                            sc3 = sc[:, 0:256].rearrange("p (i c) -> p i c", c=64)
                            TT("dve", sc3, sc3, esl, ALU.add, r=[sck, "ebuf", "PH"], w=[sck])
                            pj_ = ptc[0] % 6
                            ptc[0] += 1
                            pt, ptk = PT.tiles[pj_ // 2][:, 512 * (pj_ % 2):512 * (pj_ % 2) + 512], ("PT", pj_ // 2, pj_ % 2)
                            ACT(pt[:, 0:384], sc[:, 0:384], AF.Exp, r=[sck], w=[ptk], scale=0.125)
                            if r0 % 2 == 0:
                                vts = [r0 // 2 + i for i in range(4)]
                            else:
                                vts = [18 + (r0 - 1) // 2 + i for i in range(4)]
                            vts += [16, 17]
                            cur = (hh, rr_, pt, ptk, vts)
                        pendq.append(cur)
                        if len(pendq) > LA and pendq[0] is not None:
                            hh_, rr2, pt_, ptk_, vts_ = pendq[0]
                            for idx, vt in enumerate(vts_):
                                MM(OA[hh_][:, 64 * rr2:64 * rr2 + 64],
                                   lhsT=nvaug[:, vt, hh_, :, :].rearrange("p a d -> p (a d)"),
                                   rhs=pt_[:, 64 * idx:64 * idx + 64], start=(idx == 0), stop=(idx == 5),
                                   r=[ptk_, "nvaug", "PH"], w=[OAK[hh_]])
                        if len(pendq) > LA:
                            pendq.pop(0)
                    finalize_attn(gp, gpk, tc, mc)
                pj_list[0] = [6, 0]
                ms_list[0] = [7, 1]
                for hh in (range(2) if not last else ()):
                    pr = slice(64 * hh, 64 * hh + 64)
                    sc, sck = SC[hh], SCK[hh]
                    for i2 in range(2):
                        MM(sc[:, 256 * i2:256 * i2 + 256], lhsT=nkz[hh][:, SEQ + 128 * i2:SEQ + 128 * i2 + 128],
                           rhs=mix[:, mc, SEQ:SEQ + NCTX], r=[("nkT", 4), ("mix", mc, 4), "PH"], w=[sck])
                    pj_ = ptc[0] % 6
                    ptc[0] += 1
                    pt, ptk = PT.tiles[pj_ // 2][:, 512 * (pj_ % 2):512 * (pj_ % 2) + 512], ("PT", pj_ // 2, pj_ % 2)
                    ACT(pt[:, 0:512], sc[:, :], AF.Exp, r=[sck], w=[ptk], scale=0.125)
                    for i2 in range(2):
                        MM(OA[hh][:, 0:256], lhsT=nvaug[:, 16 + i2, hh, :, :].rearrange("p a d -> p (a d)"),
                           rhs=pt[:, 256 * i2:256 * i2 + 256], start=(i2 == 0), stop=(i2 == 1),
                           r=[ptk, "nvaug", "PH"], w=[OAK[hh]])
                if not last:
                    finalize_attn(gp, gpk, 4, mc)
                MSET("pool", dummy[:, :], 0.0, w=pt_all + ["dummy"])
            y_update(l, b, ([0, 1, 2, 3] if "B" in phases else [2, 3]), "outC", last)

        final = []
        for b in range(n_batch):
            fence()
            for i in range(18):
                src = x_d[b, i * 128:(i + 1) * 128, :] if i < 16 else ctx_d[b, (i - 16) * 128:(i - 15) * 128, :]
                xs, xsk = xstage[i % NXS], ("xs", i % NXS)
                DMA(xs, src, r=["PH"], w=[xsk], grp="xs%d" % (i % NXS))
                tc = min(i // 4, 4)
                for half in range(2):
                    bank, bkey = next_pj()
                    for kk in range(4):
                        k = half * 4 + kk
                        S.add("pe", (lambda o, i_: (lambda e: e.transpose(out=o, in_=i_, identity=ident[:])))(
                            bank[:, kk * 128:(kk + 1) * 128], xs[:, k * 128:(k + 1) * 128]),
                            [xsk, "ident", "PH"], [bkey])
                    CP("dve" if half == 0 else "act", xT[:, half * 4:half * 4 + 4, i * 128:(i + 1) * 128],
                       bank[:, :].rearrange("p (k t) -> p k t", t=128), r=[bkey], w=[("xT", tc)])
            for l in range(n_layers):
                last = (l == NL - 1)
                if "A" in phases and stage >= 2:
                    phase_A_init()
                if stage >= 1:
                    xnorm(l, b)
                if "A" in phases and stage >= 2:
                    phase_A(l, b, last)
                if "B" in phases:
                    phase_B(l, b, last)
                if "C" in phases:
                    phase_C(l, b, last)
            fence()
            for i in range(16):
                xs, xsk = xstage[i % NXS], ("xs", i % NXS)
                tc = i // 4
                for half in range(2):
                    bank, bkey = next_pj()
                    for kk in range(4):
                        k = half * 4 + kk
                        S.add("pe", (lambda o, i_: (lambda e: e.transpose(out=o, in_=i_, identity=ident[:])))(
                            bank[:, kk * 128:(kk + 1) * 128], xT[:, k, i * 128:(i + 1) * 128]),
                            [("xT", tc), "ident"], [bkey])
                    CP("dve" if half == 0 else "act", xs[:, half * 512:(half + 1) * 512], bank[:, :],
                       r=[bkey, "PH"], w=[xsk])
                final.append(DMA(out_d[b, i * 128:(i + 1) * 128, :], xs, r=[xsk, "PH"], grp="o%d" % (i % NXS)))
        assert stage < 99 or WS.consumed == len(WS.specs), (WS.consumed, len(WS.specs))
        S.emit(nc, final_waits=final[-NXS:])
    return nc


def _f32(a):
    return np.ascontiguousarray(np.asarray(a, dtype=np.float32))


def prep_shared(c_ctx, norm_gain, w_mod, b_mod, w_in, att_q_gain, att_k_gain, pool_w, pool_scale,
                na_q_gain, na_k_gain, na_rpb, w_out):
    sh = {}
    w_mod = _f32(w_mod)
    sh["wmod"] = _f32(w_mod.reshape(NL, 8, 128, 6, 512).transpose(0, 3, 2, 1, 4))
    sh["bmod"] = _f32(_f32(b_mod).reshape(NL, 24, 128).transpose(2, 0, 1))
    sh["ngain"] = _f32(_f32(norm_gain).reshape(NL, 8, 128).transpose(2, 0, 1))
    w_in = _f32(w_in)
    cols = []
    cols.append(np.r_[512:640])
    cols.append(np.r_[512:640])
    cols.append(np.r_[640:768])
    for c in range(4):
        cols.append(np.r_[c * 128:(c + 1) * 128])
    for c in range(4):
        cols.append(np.r_[768 + c * 128:768 + (c + 1) * 128])
    for base in (1280, 1536, 2048, 2304, 1792, 2560):
        for c in range(2):
            cols.append(np.r_[base + c * 128:base + (c + 1) * 128])
    assert len(cols) == NPAN
    win = np.empty((NL, NPAN, 128, 8, 128), np.float32)
    for pi, cc in enumerate(cols):
        win[:, pi] = w_in[:, :, cc].reshape(NL, 8, 128, 128).transpose(0, 2, 1, 3)
    sh["win"] = win
    sh["wout"] = _f32(_f32(w_out).reshape(NL, 8, 128, 8, 128).transpose(0, 3, 2, 1, 4))
    g = np.stack([_f32(att_q_gain), _f32(att_k_gain), _f32(na_q_gain), _f32(na_k_gain)], axis=-1)
    sh["gains"] = _f32(np.tile(g, (1, 2, 1)).transpose(1, 0, 2))
    p = np.arange(128)
    d = p % 64
    inv_freq = (np.float32(10000.0) ** (-np.arange(16, dtype=np.float32) / np.float32(16))).astype(np.float32)
    f = inv_freq[d % 16]
    sign = np.where((d % 32) < 16, -1.0, 1.0).astype(np.float32)
    isrow = d < 32
    rows = np.arange(32, dtype=np.float32)
    colsv = np.arange(64, dtype=np.float32)
    angA = (rows[None, :] * f[:, None]).astype(np.float32)
    angB = (colsv[None, :] * f[:, None]).astype(np.float32)
    cosA = np.where(isrow[:, None], np.cos(angA), 1.0)
    cosB = np.where(isrow[:, None], 1.0, np.cos(angB))
    sinA = np.where(isrow[:, None], sign[:, None] * np.sin(angA), 1.0)
    sinB = np.where(isrow[:, None], 1.0, sign[:, None] * np.sin(angB))
    sh["ropetab"] = _f32(np.concatenate([cosA, cosB, sinA, sinB], axis=1))
    partner = np.where((p % 32) < 16, p + 16, p - 16)
    rm = np.zeros((128, 128), np.float32)
    rm[partner, p] = 1.0
    sh["rmat"] = rm
    sh["ident"] = np.eye(128, dtype=np.float32)
    pw = _f32(pool_w)
    poolw = np.zeros((NL, 128, 2, 128), np.float32)
    for cz in range(2):
        for j in range(2):
            poolw[:, j * 64:(j + 1) * 64, cz, j * 64:(j + 1) * 64] = pw[:, 2 * cz + j]
    sh["poolw"] = poolw
    sh["pscale"] = _f32(_f32(pool_scale).reshape(NL, 2, 128).transpose(2, 0, 1))
    pt = np.zeros((128, 2, 17), np.float32)
    for cz in range(2):
        for j in range(2):
            w = (2, 4, 8, 16)[2 * cz + j]
            t = np.arange(8)
            cs = np.minimum(w, t + w // 2).astype(np.float32)
            ce = np.minimum(w, (8 - t) + w // 2 - 1 + 0).astype(np.float32)
            ce = np.minimum(w, 8 - t + w // 2).astype(np.float32)
            pt[j * 64:(j + 1) * 64, cz, 0] = 1.0 / w
            pt[j * 64:(j + 1) * 64, cz, 1:9] = 1.0 / cs
            pt[j * 64:(j + 1) * 64, cz, 9:17] = 1.0 / ce
    sh["pooltab"] = pt
    rpb = _f32(na_rpb)
    kc = np.arange(64)[:, None]
    cq = np.arange(64)[None, :]
    c0 = np.clip(cq - 8, 0, 48)
    valid = (kc >= c0) & (kc < c0 + 16)
    dcol = np.clip(kc - cq + 15, 0, 30)
    Dh = np.where(valid[None, None, None], rpb[:, :, :, dcol], np.float32(NEG))
    eb = np.empty((NL, 2, 128, 2, 14, 64), np.float32)
    for c2 in range(2):
        for hh in range(2):
            h = 2 * c2 + hh
            for j in range(2):
                for di in range(14):
                    eb[:, c2, j * 64:(j + 1) * 64, hh, di, :] = Dh[:, h, di + j]
    sh["ebias"] = eb
    return sh


_NC_CACHE = {}


def kernel(x, c, ctx, c_ctx, norm_gain, w_mod, b_mod, w_in, att_q_gain, att_k_gain,
           pool_w, pool_scale, na_q_gain, na_k_gain, na_rpb, w_out):
    x = _f32(x)
    c = _f32(c)
    ctx = _f32(ctx)
    c_ctx = _f32(c_ctx)
    sh = prep_shared(c_ctx, norm_gain, w_mod, b_mod, w_in, att_q_gain, att_k_gain, pool_w, pool_scale,
                     na_q_gain, na_k_gain, na_rpb, w_out)
    if "nc" not in _NC_CACHE:
        _NC_CACHE["nc"] = build_nc()
    nc = _NC_CACHE["nc"]
    in_maps = []
    for i in range(8):
        m = dict(sh)
        m["x"] = np.ascontiguousarray(x[2 * i:2 * i + 2])
        m["ctx"] = np.ascontiguousarray(ctx[2 * i:2 * i + 2])
        vecs = np.stack([c[2 * i], c[2 * i + 1], c_ctx], axis=-1)
        m["cT"] = _f32(vecs.reshape(8, 128, 3).transpose(1, 0, 2))
        in_maps.append(m)
    res = run_bass_kernel_spmd(nc, in_maps, core_ids=list(range(8)))
    return np.concatenate([np.asarray(r["out"], dtype=np.float32) for r in res.results], axis=0)
```
